# Optimizing a Trainium2 kernel written in Bass

```python
import jax, jax.numpy as jnp
from jax import lax
import numpy as np

D_MODEL = 1024
BATCH = 16
SEQ = 2048
DEPTH = 2

GRID_W = 64
CTX_LEN = 256
EPS = 1e-6
NEG_INF = -1e30
MIX_WIDTH = D_MODEL
A_GROUPS = 4
A_CHUNK = 128
A_WIDTH = MIX_WIDTH // 2
A_GDIM = A_WIDTH // A_GROUPS
B_HEADS = 8
B_WIDTH = MIX_WIDTH - A_WIDTH
B_HDIM = B_WIDTH // B_HEADS
NA_ROWS = 8
NA_COLS = 16
NA_QCB = NA_COLS
NA_KCB = 2 * NA_COLS
C_HEADS = 16
C_KV_HEADS = 4
C_HDIM = D_MODEL // C_HEADS
C_QBLOCK = 128
ROPE_THETA = 10000.0
D_FF = 2816
CONV_W = 3
N_EVEN = (DEPTH + 1) // 2
N_ODD = DEPTH // 2
IN_AB = 2 * A_WIDTH + 3 * B_WIDTH
IN_C = (C_HEADS + 2 * C_KV_HEADS) * C_HDIM

kernel_name = "hybrid_gmlp_natten_gqa_convffn_dit"


def rms_norm(x, g):
    xf = x.astype(jnp.float32)
    y = xf * lax.rsqrt(jnp.mean(xf * xf, axis=-1, keepdims=True) + EPS)
    return (y * g.astype(jnp.float32)).astype(x.dtype)


def layer_norm(x, g):
    xf = x.astype(jnp.float32)
    mu = jnp.mean(xf, axis=-1, keepdims=True)
    var = jnp.mean(jnp.square(xf - mu), axis=-1, keepdims=True)
    return ((xf - mu) * lax.rsqrt(var + EPS) * g.astype(jnp.float32)).astype(x.dtype)


def ada_chunks(cvec, w, b):
    m = jax.nn.silu(cvec) @ w + b
    return jnp.split(m[..., None, :], 6, axis=-1)


def modulate(x, g, shift, scale):
    return rms_norm(x, g) * (1 + scale) + shift


def axial_rope(n_tokens, head_dim):
    t = jnp.arange(n_tokens)
    quarter = head_dim // 4
    inv = ROPE_THETA ** (-jnp.arange(quarter, dtype=jnp.float32) / quarter)
    rows = (t // GRID_W).astype(jnp.float32)[:, None] * inv
    cols = (t % GRID_W).astype(jnp.float32)[:, None] * inv
    ang = jnp.concatenate([rows, cols], axis=-1)
    return jnp.cos(ang), jnp.sin(ang)


def apply_rope(x, cos, sin):
    xf = x.astype(jnp.float32)
    x1, x2 = jnp.split(xf, 2, axis=-1)
    cs, sn = cos[:, None, :], sin[:, None, :]
    return jnp.concatenate([x1 * cs - x2 * sn, x1 * sn + x2 * cs], axis=-1).astype(x.dtype)


def gqa_attend(q, k, v):
    bn, lq, hq, dh = q.shape
    hkv = k.shape[2]
    qg = q.reshape(bn, lq, hkv, hq // hkv, dh)
    s = jnp.einsum('bqhgd,bkhd->bhgqk', qg, k).astype(jnp.float32) * (dh ** -0.5)
    p = jax.nn.softmax(s, axis=-1).astype(v.dtype)
    return jnp.einsum('bhgqk,bkhd->bqhgd', p, v).reshape(bn, lq, hq * dh)


def gqa_blocked(q, k, v, k_ctx, v_ctx):
    bn, s_len, hq, dh = q.shape
    keys = jnp.concatenate([k, k_ctx], axis=1)
    vals = jnp.concatenate([v, v_ctx], axis=1)
    qb = q.reshape(bn, s_len // C_QBLOCK, C_QBLOCK, hq, dh).transpose(1, 0, 2, 3, 4)
    ob = lax.map(lambda qi: gqa_attend(qi, keys, vals), qb)
    return ob.transpose(1, 0, 2, 3).reshape(bn, s_len, hq * dh)


def chunk_gmlp(u, v, w_s, b_s, g_v):
    bn, length, _ = u.shape
    u = jax.nn.gelu(u)
    v = layer_norm(jax.nn.gelu(v), g_v)
    v = v.reshape(bn, length // A_CHUNK, A_CHUNK, A_GROUPS, A_GDIM)
    s = jnp.einsum('gij,bnjgc->bnigc', w_s, v) + b_s.T[None, None, :, :, None]
    return u * s.reshape(bn, length, A_WIDTH)


def neighbourhood_attention(q, k, v, k_ctx, v_ctx, rpb):
    bn, s_len, nh, dh = q.shape
    rows = s_len // GRID_W
    kr = min(NA_ROWS, rows)
    ncb = GRID_W // NA_QCB
    qcol = np.arange(GRID_W).reshape(ncb, NA_QCB)
    kstart = np.clip(np.arange(ncb) * NA_QCB - NA_COLS // 2, 0, GRID_W - NA_KCB)
    kcol = kstart[:, None] + np.arange(NA_KCB)
    c0 = np.clip(qcol - NA_COLS // 2, 0, GRID_W - NA_COLS)
    kc3 = kcol[:, None, :]
    in_win = (kc3 >= c0[..., None]) & (kc3 < c0[..., None] + NA_COLS)
    dc_idx = np.clip(kc3 - qcol[..., None], -(NA_COLS - 1), NA_COLS - 1) + NA_COLS - 1
    mask = in_win[:, :, None, :]
    qg = q.reshape(bn, rows, ncb, NA_QCB, nh, dh)
    kgc = k.reshape(bn, rows, GRID_W, nh, dh)[:, :, kcol]
    vgc = v.reshape(bn, rows, GRID_W, nh, dh)[:, :, kcol]
    scale = dh ** -0.5
    s_ctx_all = None
    n_loc = kr * NA_KCB

    def row_block(r):
        r0 = jnp.clip(r - kr // 2, 0, rows - kr)
        q_r = lax.dynamic_index_in_dim(qg, r, axis=1, keepdims=False)
        k_r = lax.dynamic_slice_in_dim(kgc, r0, kr, axis=1)
        v_r = lax.dynamic_slice_in_dim(vgc, r0, kr, axis=1)
        s_loc = jnp.einsum('bnqhd,brnkhd->bhnqrk', q_r, k_r).astype(jnp.float32) * scale
        dr_idx = r0 + jnp.arange(kr) - r + NA_ROWS - 1
        bias = rpb[:, dr_idx][:, :, dc_idx].transpose(0, 2, 3, 1, 4)
        s_loc = jnp.where(mask, s_loc + bias.astype(jnp.float32)[None], NEG_INF)
        s_ctx = jnp.einsum('bnqhd,bchd->bhnqc', q_r, k_ctx).astype(jnp.float32) * scale
        s = jnp.concatenate([s_loc.reshape(bn, nh, ncb, NA_QCB, n_loc), s_ctx], axis=-1)
        p = jax.nn.softmax(s, axis=-1).astype(v.dtype)
        p_loc = p[..., :n_loc].reshape(bn, nh, ncb, NA_QCB, kr, NA_KCB)
        o = jnp.einsum('bhnqrk,brnkhd->bnqhd', p_loc, v_r)
        return o + jnp.einsum('bhnqc,bchd->bnqhd', p[..., n_loc:], v_ctx)

    o = lax.map(row_block, jnp.arange(rows))
    return o.transpose(1, 0, 2, 3, 4, 5).reshape(bn, s_len, nh * dh)


def even_mixer(xm, cm, w_in, w_s, b_s, g_v, rpb, w_out, with_ctx):
    splits = [A_WIDTH, 2 * A_WIDTH, 2 * A_WIDTH + B_WIDTH, 2 * A_WIDTH + 2 * B_WIDTH]

    def proj(t):
        u, va, q, k, vb = jnp.split(t @ w_in, splits, axis=-1)
        hd = lambda z: z.reshape(*z.shape[:-1], B_HEADS, B_HDIM)
        return u, va, hd(q), hd(k), hd(vb)

    u, va, q, k, vb = proj(xm)
    uc, vac, qc, kc, vbc = proj(cm)
    y_a = chunk_gmlp(u, va, w_s, b_s, g_v)
    y_b = neighbourhood_attention(q, k, vb, kc, vbc, rpb)
    y = jnp.concatenate([y_a, y_b], axis=-1) @ w_out
    yc = None
    if with_ctx:
        yc_a = chunk_gmlp(uc, vac, w_s, b_s, g_v)
        yc_b = gqa_attend(qc, kc, vbc)
        yc = jnp.concatenate([yc_a, yc_b], axis=-1) @ w_out
    return y, yc


def odd_mixer(xm, cm, w_qkv, q_g, k_g, w_out, cos, sin, with_ctx):
    splits = [C_HEADS * C_HDIM, (C_HEADS + C_KV_HEADS) * C_HDIM]

    def proj(t):
        q, k, v = jnp.split(t @ w_qkv, splits, axis=-1)
        q = rms_norm(q.reshape(*q.shape[:-1], C_HEADS, C_HDIM), q_g)
        k = rms_norm(k.reshape(*k.shape[:-1], C_KV_HEADS, C_HDIM), k_g)
        return q, k, v.reshape(*v.shape[:-1], C_KV_HEADS, C_HDIM)

    q, k, v = proj(xm)
    qc, kc, vc = proj(cm)
    q = apply_rope(q, cos, sin)
    k = apply_rope(k, cos, sin)
    y = gqa_blocked(q, k, v, kc, vc) @ w_out
    yc = gqa_attend(qc, kc, vc) @ w_out if with_ctx else None
    return y, yc


def conv_ffn(x, w_up, conv_w, conv_b, w_down):
    h = x @ w_up
    h = lax.conv_general_dilated(h, conv_w[:, None, :], window_strides=(1,),
                                 padding=((CONV_W // 2, CONV_W // 2),),
                                 dimension_numbers=('NWC', 'WIO', 'NWC'),
                                 feature_group_count=h.shape[-1]) + conv_b
    a, g = jnp.split(h, 2, axis=-1)
    return (jax.nn.silu(g) * a) @ w_down


def setup_inputs(seed: int = 0) -> dict:
    key = jax.random.key(seed)
    ks = jax.random.split(key, 24)
    nrm = lambda k, shape, s: jax.random.normal(k, shape, jnp.float32) * s
    return {
        "x": nrm(ks[0], (BATCH, SEQ, D_MODEL), 1.0),
        "c": nrm(ks[1], (BATCH, D_MODEL), 1.0),
        "ctx": nrm(ks[2], (BATCH, CTX_LEN, D_MODEL), 1.0),
        "c_ctx": nrm(ks[3], (D_MODEL,), 1.0),
        "w_ada": nrm(ks[4], (DEPTH, D_MODEL, 6 * D_MODEL), 0.5 * D_MODEL ** -0.5),
        "b_ada": nrm(ks[5], (DEPTH, 6 * D_MODEL), 0.02),
        "norm_g": 1.0 + nrm(ks[6], (DEPTH, 4, D_MODEL), 0.05),
        "w_in_ab": nrm(ks[7], (N_EVEN, D_MODEL, IN_AB), D_MODEL ** -0.5),
        "a_w_s": nrm(ks[8], (N_EVEN, A_GROUPS, A_CHUNK, A_CHUNK), A_CHUNK ** -0.5),
        "a_b_s": 1.0 + nrm(ks[9], (N_EVEN, A_GROUPS, A_CHUNK), 0.1),
        "a_v_g": 1.0 + nrm(ks[10], (N_EVEN, A_WIDTH), 0.05),
        "b_rpb": nrm(ks[11], (N_EVEN, B_HEADS, 2 * NA_ROWS - 1, 2 * NA_COLS - 1), 0.5),
        "w_out_ab": nrm(ks[12], (N_EVEN, MIX_WIDTH, D_MODEL), MIX_WIDTH ** -0.5),
        "w_qkv_c": nrm(ks[13], (N_ODD, D_MODEL, IN_C), D_MODEL ** -0.5),
        "c_q_g": 1.0 + nrm(ks[14], (N_ODD, C_HDIM), 0.05),
        "c_k_g": 1.0 + nrm(ks[15], (N_ODD, C_HDIM), 0.05),
        "w_out_c": nrm(ks[16], (N_ODD, C_HEADS * C_HDIM, D_MODEL), (C_HEADS * C_HDIM) ** -0.5),
        "w_up": nrm(ks[17], (DEPTH, D_MODEL, 2 * D_FF), D_MODEL ** -0.5),
        "conv_w": nrm(ks[18], (DEPTH, CONV_W, 2 * D_FF), CONV_W ** -0.5),
        "conv_b": nrm(ks[19], (DEPTH, 2 * D_FF), 0.02),
        "w_down": nrm(ks[20], (DEPTH, D_FF, D_MODEL), D_FF ** -0.5),
    }


def reference(x, c, ctx, c_ctx, w_ada, b_ada, norm_g, w_in_ab, a_w_s, a_b_s, a_v_g, b_rpb,
              w_out_ab, w_qkv_c, c_q_g, c_k_g, w_out_c, w_up, conv_w, conv_b, w_down):
    cos, sin = axial_rope(x.shape[1], C_HDIM)
    h, hc = x, ctx
    for layer in range(DEPTH):
        with_ctx = layer < DEPTH - 1
        sh_m, sc_m, gt_m, sh_f, sc_f, gt_f = ada_chunks(c, w_ada[layer], b_ada[layer])
        csh_m, csc_m, cgt_m, csh_f, csc_f, cgt_f = ada_chunks(c_ctx, w_ada[layer], b_ada[layer])
        g_pre_m, g_post_m, g_pre_f, g_post_f = norm_g[layer]
        xm = modulate(h, g_pre_m, sh_m, sc_m)
        cm = modulate(hc, g_pre_m, csh_m, csc_m)
        if layer % 2 == 0:
            e = layer // 2
            y, yc = even_mixer(xm, cm, w_in_ab[e], a_w_s[e], a_b_s[e], a_v_g[e], b_rpb[e],
                               w_out_ab[e], with_ctx)
        else:
            o = layer // 2
            y, yc = odd_mixer(xm, cm, w_qkv_c[o], c_q_g[o], c_k_g[o], w_out_c[o], cos, sin, with_ctx)
        h = h + gt_m * rms_norm(y, g_post_m)
        f = conv_ffn(modulate(h, g_pre_f, sh_f, sc_f), w_up[layer], conv_w[layer], conv_b[layer], w_down[layer])
        h = h + gt_f * rms_norm(f, g_post_f)
        if with_ctx:
            hc = hc + cgt_m * rms_norm(yc, g_post_m)
            fc = conv_ffn(modulate(hc, g_pre_f, csh_f, csc_f), w_up[layer], conv_w[layer], conv_b[layer], w_down[layer])
            hc = hc + cgt_f * rms_norm(fc, g_post_f)
    return h
```

```python
import numpy as np
from contextlib import ExitStack
import concourse.bass as bass
import concourse.mybir as mybir
from concourse.bass_utils import run_bass_kernel_spmd

F32 = mybir.dt.float32
BF16 = mybir.dt.bfloat16
AF = mybir.ActivationFunctionType
ALU = mybir.AluOpType

D = 1024
S = 2048
LC = 256
NT = S + LC
DFF = 2816
EPS = 1e-6
NCH = 22


import types


def _freeze(fn, depth=0):
    if not isinstance(fn, types.FunctionType) or fn.__closure__ is None or depth > 4:
        return fn
    cells = []
    for c in fn.__closure__:
        try:
            v = c.cell_contents
        except ValueError:
            cells.append(c)
            continue
        if isinstance(v, types.FunctionType):
            v = _freeze(v, depth + 1)
        cells.append(types.CellType(v))
    g = types.FunctionType(fn.__code__, fn.__globals__, fn.__name__, fn.__defaults__, tuple(cells))
    g.__kwdefaults__ = fn.__kwdefaults__
    return g


class Prog:
    def __init__(self, nc):
        self.nc = nc
        self.eng = {"pe": nc.tensor, "act": nc.scalar, "dve": nc.vector, "pool": nc.gpsimd, "sp": nc.sync}
        self.sem = {e: nc.alloc_semaphore(name=f"sem_{e}") for e in self.eng}
        self.cnt = {e: 0 for e in self.eng}
        self.dsem = {}
        self.waited = {e: {} for e in self.eng}
        self.ops = []
        self.last_w = {}
        self.readers = {}
        self.last_of_eng = {}
        self.last_dma = {}

    def op(self, eng, fn, r=(), w=(), dma=None):
        idx = len(self.ops)
        deps = {}
        for k in r:
            d = self.last_w.get(k)
            if d is not None:
                deps[d] = "raw"
        for k in w:
            d = self.last_w.get(k)
            if d is not None:
                deps[d] = "raw"
            lastr = {}
            for rd in self.readers.get(k, ()):
                o_ = self.ops[rd]
                if o_["dma"] is not None:
                    lastr[("d", rd)] = rd
                else:
                    lastr[o_["eng"]] = rd
            for rd in lastr.values():
                if rd not in deps:
                    deps[rd] = "war"
        deps.pop(idx, None)
        self.ops.append(dict(eng=eng, fn=_freeze(fn), deps=deps, dma=dma, sig=None))
        for k in r:
            self.readers.setdefault(k, []).append(idx)
        for k in w:
            self.last_w[k] = idx
            self.readers[k] = []
        if dma is None:
            self.last_of_eng[eng] = idx
        else:
            self.last_dma[dma] = idx
        return idx

    def dma(self, eng, out, in_, r=(), w=(), slot=None):
        if slot is None:
            slot = w[0] if w else "out"
        return self.op(eng, lambda e: e.dma_start(out=out, in_=in_), r=r, w=w, dma=slot)

    def flush(self):
        alld = {}
        for e, i in self.last_of_eng.items():
            alld[i] = "raw"
        for s, i in self.last_dma.items():
            alld[i] = "raw"
        for e in self.eng:
            self.ops.append(dict(eng=e, fn=None, deps=dict(alld), dma=None, sig=None, barrier=True))
        ops = self.ops
        need = [False] * len(ops)
        for i, o in enumerate(ops):
            fd = []
            for d, kind in o["deps"].items():
                od = ops[d]
                if od["dma"] is None and od["eng"] == o["eng"] and not o.get("barrier"):
                    if o["eng"] == "pe" or kind == "war":
                        continue
                if od["dma"] is None and od["eng"] == o["eng"] and o.get("barrier"):
                    continue
                fd.append(d)
                if od["dma"] is None:
                    need[d] = True
            o["fd"] = fd
        for i, o in enumerate(ops):
            E = o["eng"]
            eo = self.eng[E]
            waits = {}
            for d in o["fd"]:
                key, val = ops[d]["sig"]
                if waits.get(key, 0) < val:
                    waits[key] = val
            for key, val in waits.items():
                if self.waited[E].get(key, 0) < val:
                    h = self.sem[key[1]] if key[0] == "e" else self.dsem[key[1]][0]
                    eo.wait_ge(h, val)
                    self.waited[E][key] = val
            if o["fn"] is None:
                continue
            ins = o["fn"](eo)
            if o["dma"] is not None:
                slot = o["dma"]
                if slot not in self.dsem:
                    self.dsem[slot] = [self.nc.alloc_semaphore(name=f"dma_{len(self.dsem)}"), 0]
                s = self.dsem[slot]
                s[1] += 16
                ins.then_inc(s[0], 16)
                o["sig"] = (("d", slot), s[1])
            elif need[i]:
                self.cnt[E] += 1
                ins.then_inc(self.sem[E], 1)
                o["sig"] = (("e", E), self.cnt[E])
        self.ops = []
        self.last_w = {}
        self.readers = {}
        self.last_of_eng = {}
        self.last_dma = {}


def _na_tile_plan():
    plan = []
    for A in range(4):
        rows = list(range(8 * A, 8 * A + 8))
        lo = min(max(r - 4, 0) if r - 4 <= 24 else 24 for r in rows)
        lo = min(min(max(r - 4, 0), 24) for r in rows)
        hi = max(min(max(r - 4, 0), 24) + 7 for r in rows)
        t0, t1 = lo // 2, hi // 2
        plan.append(list(range(t0, t1 + 1)))
    return plan


def _build_na_bias(rpb):
    plan = _na_tile_plan()
    NEG = np.float32(-30000.0)
    tiles = []
    index = {}
    qr_l = np.arange(8)[:, None]
    qc = np.arange(64)[None, :]
    for A in range(4):
        qr = (8 * A + qr_l) + 0 * qc
        qcc = 0 * qr_l + qc
        r0 = np.clip(qr - 4, 0, 24)
        c0 = np.clip(qcc - 8, 0, 48)
        qr_f, qc_f, r0_f, c0_f = [a.reshape(-1) for a in (qr, qcc, r0, c0)]
        for h in range(8):
            for j, t in enumerate(plan[A]):
                kr = np.repeat(np.arange(2 * t, 2 * t + 2), 64)
                kc = np.tile(np.arange(64), 2)
                valid = ((kr[:, None] >= r0_f[None, :]) & (kr[:, None] < r0_f[None, :] + 8)
                         & (kc[:, None] >= c0_f[None, :]) & (kc[:, None] < c0_f[None, :] + 16))
                dr = np.clip(kr[:, None] - qr_f[None, :] + 7, 0, 14)
                dc = np.clip(kc[:, None] - qc_f[None, :], -15, 15) + 15
                b = rpb[h][dr, dc]
                tiles.append(np.where(valid, b, NEG).astype(np.float32))
                index[(A, h, j)] = len(tiles) - 1
    return np.stack(tiles, 0), index, plan


def _rope_tables():
    t = np.arange(S)
    inv = (10000.0 ** (-np.arange(16, dtype=np.float32) / 16)).astype(np.float32)
    rows = (t // 64).astype(np.float32)[:, None] * inv
    cols = (t % 64).astype(np.float32)[:, None] * inv
    ang = np.concatenate([rows, cols], -1)
    cos = np.cos(ang).astype(np.float32).T
    sin = np.sin(ang).astype(np.float32).T
    cosT = np.tile(cos, (4, 1))
    sinT = np.concatenate([-sin, sin, -sin, sin], 0)
    perm = np.zeros((128, 128), np.float32)
    for m in range(128):
        k = m + 32 if (m % 64) < 32 else m - 32
        perm[k, m] = 1.0
    return np.ascontiguousarray(cosT), np.ascontiguousarray(sinT), perm


HMAP = [[(4 * (2 * (c // 4)) + (c % 4)), (4 * (2 * (c // 4) + 1) + (c % 4))] for c in range(8)]


def _lay_w(w):
    K, N = w.shape
    return np.ascontiguousarray(w.reshape(K // 128, 128, N).transpose(1, 0, 2))


def build_program(n_bias_tiles, bias_index, na_plan, nb=2, stages=("mix0", "ffn0", "mix1", "ffn1"), dbg=False):
    nc = bass.Bass("TRN2", target_bir_lowering=False)
    P = Prog(nc)

    def din(name, shape):
        return nc.dram_tensor(name, list(shape), F32, kind="ExternalInput").ap()

    xT_d = din("xT", (nb, 128, 8, S))
    ctxT_d = din("ctxT", (nb, 128, 8, LC))
    out_d = nc.dram_tensor("outT", [nb, 128, 8, S], F32, kind="ExternalOutput").ap()
    cT_d = din("cT", (128, 8, 3))
    wada_d = din("w_ada", (2, 128, 8, 6 * D))
    badaT_d = din("b_adaT", (2, 128, 48))
    gT_d = din("gT", (2, 4, 128, 8))
    win_d = din("w_in", (128, 8, 2560))
    woab_d = din("w_out_ab", (128, 8, D))
    wq_d = din("w_q", (128, 8, D))
    wk_d = din("w_k", (128, 8, 256))
    wv_d = din("w_v", (128, 8, 256))
    woc_d = din("w_out_c", (128, 8, D))
    wsT_d = din("w_sT", (128, 4, 128))
    bs_d = din("b_s_bc", (128, 512))
    gv_d = din("gv_bc", (128, 512))
    bias_d = din("bias_na", (n_bias_tiles, 128, 512))
    qg_d = din("qg2", (128, 1))
    kg_d = din("kg2", (128, 1))
    wup_d = din("w_up", (2, 2 * NCH, 128, 8, 128))
    cw_d = din("conv_wT", (2, 128, 2 * NCH, 3))
    cb_d = din("conv_bT", (2, 128, 2 * NCH))
    wdn_d = din("w_down", (2, 8, 128, NCH, 128))
    cos_d = din("cosT", (128, S))
    sin_d = din("sinT", (128, S))
    perm_d = din("perm", (128, 128))

    ES = ExitStack()
    hc_out = nc.dram_tensor("hcT_out", [nb, 128, 8, LC], F32, kind="ExternalOutput").ap() if dbg else None

    uid = [0]

    def sb(es, name, shape, dt=F32):
        uid[0] += 1
        return es.enter_context(nc.sbuf_tensor(f"{name}_{uid[0]}", list(shape), dt))

    def pst(es, name, shape):
        uid[0] += 1
        return es.enter_context(nc.psum_tensor(f"{name}_{uid[0]}", list(shape), F32))

    hT = sb(ES, "hT", (128, 8, S))
    hcT = sb(ES, "hcT", (128, 8, LC))
    ones_bf = sb(ES, "ones_bf", (128, 128), BF16)
    bones_bf = sb(ES, "bones_bf", (128, 128), BF16)
    perm_bf = sb(ES, "perm_bf", (128, 128), BF16)
    modA = sb(ES, "modA", (128, 2, 2, 3, 8))
    modB = sb(ES, "modB", (128, 2, 2, 3, 8))
    modG = sb(ES, "modG", (128, 2, 2, 3, 8))
    qkg = sb(ES, "qkg", (128, 2))
    cwT = sb(ES, "cwT", (128, 2, 2 * NCH, 3))
    cbT = sb(ES, "cbT", (128, 2, 2 * NCH))
    sq = [sb(ES, f"sq{i}", (128, 512), BF16) for i in range(2)]
    rstd = sb(ES, "rstd", (128, 512))
    tmp = [sb(ES, f"tmp{i}", (128, 512)) for i in range(2)]
    xt = sb(ES, "xt", (128, 8, 512), BF16)
    xhalo = sb(ES, "xhalo", (128, 8, 1), BF16)

    def hsrc(tile):
        if tile < 4:
            return (lambda kc, a=0, n=512: hT[:, kc, tile * 512 + a: tile * 512 + a + n]), 512
        return (lambda kc, a=0, n=256: hcT[:, kc, a:a + n]), 256

    def hkey(tile, kc):
        return ("h", tile, kc)

    def norm_mod(src, skeys, n, A, B, dst, dkeys, ss_ps, ss_key):
        for kc in range(8):
            s_ = sq[kc % 2]
            P.op("act", lambda e, kc=kc, s_=s_: e.activation(out=s_[:, :n], in_=src(kc), func=AF.Square),
                 r=[skeys[kc]], w=[("sq", kc % 2)])
            P.op("pe", lambda e, kc=kc, s_=s_: e.matmul(ss_ps[:, :n], ones_bf[:, :], s_[:, :n], start=(kc == 0), stop=(kc == 7)),
                 r=[("sq", kc % 2)], w=[ss_key])
        P.op("act", lambda e: e.activation(out=rstd[:, :n], in_=ss_ps[:, :n], func=AF.Sqrt, scale=1.0 / D, bias=eps_t[:, 0:1]),
             r=[ss_key], w=["rstd"])
        P.op("dve", lambda e: e.reciprocal(out=rstd[:, :n], in_=rstd[:, :n]), r=["rstd"], w=["rstd"])
        for kc in range(8):
            t_ = tmp[kc % 2]
            P.op("dve", lambda e, kc=kc, t_=t_: e.tensor_tensor(out=t_[:, :n], in0=src(kc), in1=rstd[:, :n], op=ALU.mult),
                 r=[skeys[kc], "rstd"], w=[("tmp", kc % 2)])
            P.op("act", lambda e, kc=kc, t_=t_: e.activation(out=dst(kc), in_=t_[:, :n], func=AF.Identity,
                                                              scale=A[:, kc:kc + 1], bias=B[:, kc:kc + 1]),
                 r=[("tmp", kc % 2)], w=[dkeys[kc]])

    def tile_norm(tile, l, mf, b, ss_ps, ss_key):
        src, n = hsrc(tile)
        v = b if tile < 4 else 2
        norm_mod(lambda kc: src(kc), [hkey(tile, kc) for kc in range(8)], n,
                 modA[:, l, mf, v, :], modB[:, l, mf, v, :],
                 lambda kc: xt[:, kc, :n], [("xt", kc) for kc in range(8)], ss_ps, ss_key)
        return n

    def fm_proj(ps, pkey, w_ap, wkeys, rhs, rkeys, n, nk=8):
        for kc in range(nk):
            P.op("pe", lambda e, kc=kc: e.matmul(ps[:, :n], w_ap(kc), rhs(kc), start=(kc == 0), stop=(kc == nk - 1)),
                 r=list(wkeys) + [rkeys[kc]], w=[pkey])

    def post_norm_residual(tile, l, mf, b, get_ps, ss_ps, ss_key, yo):
        src, n = hsrc(tile)
        v = b if tile < 4 else 2
        G = modG[:, l, mf, v, :]
        for dc in range(8):
            ps, pkey = get_ps(dc)
            s_ = sq[dc % 2]
            P.op("act", lambda e, ps=ps, dc=dc: e.activation(out=yo[:, dc, :n], in_=ps, func=AF.Copy), r=[pkey], w=[("yo", dc)])
            P.op("act", lambda e, ps=ps, s_=s_: e.activation(out=s_[:, :n], in_=ps, func=AF.Square), r=[pkey], w=[("sq", dc % 2)])
            P.op("pe", lambda e, dc=dc, s_=s_: e.matmul(ss_ps[:, :n], ones_bf[:, :], s_[:, :n], start=(dc == 0), stop=(dc == 7)),
                 r=[("sq", dc % 2)], w=[ss_key])
        P.op("act", lambda e: e.activation(out=rstd[:, :n], in_=ss_ps[:, :n], func=AF.Sqrt, scale=1.0 / D, bias=eps_t[:, 0:1]),
             r=[ss_key], w=["rstd"])
        P.op("dve", lambda e: e.reciprocal(out=rstd[:, :n], in_=rstd[:, :n]), r=["rstd"], w=["rstd"])
        for dc in range(8):
            t_ = tmp[dc % 2]
            P.op("dve", lambda e, dc=dc, t_=t_: e.tensor_tensor(out=t_[:, :n], in0=yo[:, dc, :n], in1=rstd[:, :n], op=ALU.mult),
                 r=[("yo", dc), "rstd"], w=[("tmp", dc % 2)])
            P.op("dve", lambda e, dc=dc, t_=t_: e.scalar_tensor_tensor(out=src(dc), in0=t_[:, :n], scalar=G[:, dc:dc + 1], in1=src(dc),
                                                                        op0=ALU.mult, op1=ALU.add),
                 r=[("tmp", dc % 2), hkey(tile, dc)], w=[hkey(tile, dc)])

    def attention(q_ap, qkey, n_q, keys, scale, s_ps, o_ps, pT, sbt, rec, out_ap, okey, hp, tagc):
        nk = len(keys)
        ob = tagc[0] % 2
        tagc[0] += 1
        ops_ = o_ps[ob]
        for i, (k_ap, kkey, v_ap, vkey, b_ap, bkey) in enumerate(keys):
            sbk = tagc[1] % 2
            tagc[1] += 1
            sp = s_ps[sbk]
            P.op("pe", lambda e, k_ap=k_ap, sp=sp: e.matmul(sp[:, :n_q], k_ap, q_ap, start=True, stop=True),
                 r=[kkey, qkey], w=[("sps", sbk)])
            p_ = pT[sbk]
            if b_ap is not None:
                P.op("dve", lambda e, sp=sp, b_ap=b_ap: e.scalar_tensor_tensor(out=sbt[:, :n_q], in0=sp[:, :n_q], scalar=float(scale), in1=b_ap,
                                                                               op0=ALU.mult, op1=ALU.add),
                     r=[("sps", sbk), bkey], w=["sbt"])
                P.op("act", lambda e, p_=p_: e.activation(out=p_[:, :n_q], in_=sbt[:, :n_q], func=AF.Exp), r=["sbt"], w=[("pT", sbk)])
            else:
                P.op("act", lambda e, p_=p_, sp=sp: e.activation(out=p_[:, :n_q], in_=sp[:, :n_q], func=AF.Exp, scale=float(scale)),
                     r=[("sps", sbk)], w=[("pT", sbk)])
            P.op("pe", lambda e, v_ap=v_ap, p_=p_, i=i: e.matmul(ops_[:, :n_q], v_ap, p_[:, :n_q], start=(i == 0), stop=(i == nk - 1)),
                 r=[vkey, ("pT", sbk)], w=[("ops", ob)])
        P.op("dve", lambda e: e.reciprocal(out=rec[64:128, :n_q], in_=ops_[64:128, :n_q]), r=[("ops", ob)], w=["rec"])
        P.op("dve", lambda e: e.tensor_tensor(out=out_ap, in0=ops_[0:64, :n_q], in1=rec[64:128, :n_q], op=ALU.mult),
             r=[("ops", ob), "rec"], w=[okey])

    eps_t = sb(ES, "eps_t", (128, 1))
    with ExitStack() as es:
        wa = [sb(es, f"wa{i}", (128, 8, 1024)) for i in range(2)]
        scT = sb(es, "scT", (128, 8, 3))
        sgm = sb(es, "sgm", (128, 8, 3))
        mod = sb(es, "mod", (128, 2, 48, 3))
        bT = sb(es, "bT", (128, 2, 48))
        gTt = sb(es, "gTt", (128, 2, 4, 8))
        permf = sb(es, "permf", (128, 128))
        aps = pst(es, "aps", (128, 8, 3))
        P.op("dve", lambda e: e.memset(ones_bf[:, :], 1.0), w=["ones"])
        P.op("dve", lambda e: e.memset(bones_bf[:, :], 0.0), w=["bones"])
        P.op("dve", lambda e: e.memset(bones_bf[0:64, 0:64], 1.0), r=[], w=["bones"])
        P.op("dve", lambda e: e.memset(bones_bf[64:128, 64:128], 1.0), r=[], w=["bones"])
        P.op("dve", lambda e: e.memset(eps_t[:, :], EPS), w=["eps"])
        P.dma("sp", permf[:, :], perm_d, w=["permf"])
        P.op("dve", lambda e: e.tensor_copy(out=perm_bf[:, :], in_=permf[:, :]), r=["permf"], w=["perm"])
        P.dma("sp", scT[:, :, :], cT_d, w=["scT"])
        P.dma("sp", qkg[:, 0:1], qg_d, w=["qg"])
        P.dma("sp", qkg[:, 1:2], kg_d, w=["kg"])
        P.dma("sp", cwT[:, 0], cw_d[0], w=["cw0"])
        P.dma("sp", cwT[:, 1], cw_d[1], w=["cw1"])
        P.dma("sp", cbT[:, 0], cb_d[0], w=["cb0"])
        P.dma("sp", cbT[:, 1], cb_d[1], w=["cb1"])
        for l in range(2):
            P.dma("sp", bT[:, l, :], badaT_d[l], w=[("bT", l)])
            for i in range(4):
                P.dma("sp", gTt[:, l, i, :], gT_d[l, i], w=[("gT", l, i)])
        P.op("act", lambda e: e.activation(out=sgm[:, :, :], in_=scT[:, :, :], func=AF.Sigmoid), r=["scT"], w=["sgm"])
        P.op("dve", lambda e: e.tensor_tensor(out=scT[:, :, :], in0=scT[:, :, :], in1=sgm[:, :, :], op=ALU.mult), r=["scT", "sgm"], w=["scT"])
        it = 0
        for l in range(2):
            for grp in range(6):
                wbuf = wa[it % 2]
                P.dma("sp", wbuf[:, :, :], wada_d[l][:, :, grp * 1024:(grp + 1) * 1024], w=[("wa", it % 2)])
                for j in range(8):
                    for kc in range(8):
                        P.op("pe", lambda e, wbuf=wbuf, j=j, kc=kc: e.matmul(aps[:, j, :], wbuf[:, kc, j * 128:(j + 1) * 128], scT[:, kc, :],
                                                                               start=(kc == 0), stop=(kc == 7)),
                             r=[("wa", it % 2), "scT"], w=[("aps", j)])
                for v in range(3):
                    P.op("dve", lambda e, l=l, grp=grp, v=v: e.tensor_tensor(out=mod[:, l, grp * 8:(grp + 1) * 8, v], in0=aps[:, :, v],
                                                                              in1=bT[:, l, grp * 8:(grp + 1) * 8], op=ALU.add),
                         r=[("aps", j) for j in range(8)] + [("bT", l)], w=[("mod", l, grp, v)])
                it += 1
        for l in range(2):
            for mf in range(2):
                for v in range(3):
                    ish, isc, igt = 3 * mf, 3 * mf + 1, 3 * mf + 2
                    P.op("dve", lambda e, l=l, mf=mf, v=v, isc=isc: e.scalar_tensor_tensor(
                        out=modA[:, l, mf, v, :], in0=mod[:, l, isc * 8:(isc + 1) * 8, v], scalar=1.0, in1=gTt[:, l, 2 * mf, :],
                        op0=ALU.add, op1=ALU.mult), r=[("mod", l, isc, v), ("gT", l, 2 * mf)], w=[("modA", l, mf, v)])
                    P.op("dve", lambda e, l=l, mf=mf, v=v, ish=ish: e.tensor_copy(out=modB[:, l, mf, v, :], in_=mod[:, l, ish * 8:(ish + 1) * 8, v]),
                         r=[("mod", l, ish, v)], w=[("modB", l, mf, v)])
                    P.op("dve", lambda e, l=l, mf=mf, v=v, igt=igt: e.tensor_tensor(
                        out=modG[:, l, mf, v, :], in0=mod[:, l, igt * 8:(igt + 1) * 8, v], in1=gTt[:, l, 2 * mf + 1, :], op=ALU.mult),
                        r=[("mod", l, igt, v), ("gT", l, 2 * mf + 1)], w=[("modG", l, mf, v)])
        P.flush()

    def mixer_even(b, l, upto="c", halves=(0, 1)):
        with ExitStack() as es0:
            ybT = sb(es0, "ybT", (128, 4, NT), BF16)
            for hh in halves:
                with ExitStack() as es1:
                    kT = sb(es1, "kT", (128, 2, NT), BF16)
                    vaug = sb(es1, "vaug", (128, 18, 4, 128), BF16)
                    with ExitStack() as es:
                        wk = sb(es, "wk", (128, 8, 256), BF16)
                        wv = sb(es, "wv", (128, 8, 256), BF16)
                        ss_ps = pst(es, "ss_ps", (128, 512))
                        pp = [pst(es, f"pp{i}", (128, 512)) for i in range(2)]
                        P.dma("pool", wk[:, :, :], win_d[:, :, 1536 + hh * 256:1536 + (hh + 1) * 256], w=["wk"])
                        P.dma("pool", wv[:, :, :], win_d[:, :, 2048 + hh * 256:2048 + (hh + 1) * 256], w=["wv"])
                        P.op("dve", lambda e: e.memset(vaug[:, :, :, :], 1.0), w=[("vaug", t) for t in range(18)])
                        pi = 0
                        for tile in range(5):
                            n = tile_norm(tile, l, 0, b, ss_ps, "ss")
                            t0 = tile * 512
                            for c in range(2):
                                ps = pp[pi % 2]; pk = ("pp", pi % 2); pi += 1
                                fm_proj(ps, pk, lambda kc, c=c: wk[:, kc, c * 128:(c + 1) * 128], ["wk"],
                                        lambda kc: xt[:, kc, :n], [("xt", kc) for kc in range(8)], n)
                                P.op("act", lambda e, ps=ps, c=c: e.activation(out=kT[:, c, t0:t0 + n], in_=ps[:, :n], func=AF.Copy),
                                     r=[pk], w=[("kT", tile, c)])
                            for s_ in range(n // 128):
                                ps = pp[pi % 2]; pk = ("pp", pi % 2); pi += 1
                                ti = tile * 4 + s_
                                for kc in range(8):
                                    P.op("pe", lambda e, ps=ps, kc=kc, s_=s_: e.matmul(ps[:, 0:256], xt[:, kc, s_ * 128:(s_ + 1) * 128], wv[:, kc, :],
                                                                                         start=(kc == 0), stop=(kc == 7)),
                                         r=[("xt", kc), "wv"], w=[pk])
                                for hq in range(4):
                                    P.op("dve", lambda e, ps=ps, ti=ti, hq=hq: e.tensor_copy(out=vaug[:, ti, hq, 0:64], in_=ps[:, hq * 64:(hq + 1) * 64]),
                                         r=[pk], w=[("vaug", ti)])
                        P.flush()
                    if upto == "a":
                        continue
                    with ExitStack() as es:
                        wq = sb(es, "wq", (128, 8, 256), BF16)
                        qT = sb(es, "qT", (128, 2, 512), BF16)
                        bias_sb = [sb(es, f"bias{i}", (128, 4, 512)) for i in range(2)]
                        pT = [sb(es, f"pT{i}", (128, 512), BF16) for i in range(2)]
                        sbt = sb(es, "sbt", (128, 512))
                        rec = sb(es, "rec", (128, 512))
                        ss_ps = pst(es, "ss_ps", (128, 512))
                        pp = [pst(es, f"pp{i}", (128, 512)) for i in range(2)]
                        s_ps = [pst(es, f"s_ps{i}", (128, 512)) for i in range(2)]
                        o_ps = [pst(es, f"o_ps{i}", (128, 512)) for i in range(2)]
                        P.dma("pool", wq[:, :, :], win_d[:, :, 1024 + hh * 256:1024 + (hh + 1) * 256], w=["wq"])
                        tagc = [0, 0]
                        pi = 0
                        bi = 0
                        for tile in range(5):
                            n = tile_norm(tile, l, 0, b, ss_ps, "ss")
                            t0 = tile * 512
                            for c in range(2):
                                ps = pp[pi % 2]; pk = ("pp", pi % 2); pi += 1
                                fm_proj(ps, pk, lambda kc, c=c: wq[:, kc, c * 128:(c + 1) * 128], ["wq"],
                                        lambda kc: xt[:, kc, :n], [("xt", kc) for kc in range(8)], n)
                                P.op("act", lambda e, ps=ps, c=c: e.activation(out=qT[:, c, :n], in_=ps[:, :n], func=AF.Copy), r=[pk], w=[("qT", c)])
                            for hq in range(4):
                                h = 4 * hh + hq
                                c, hp = hq // 2, (hq % 2) * 64
                                keys = []
                                if tile < 4:
                                    kts = na_plan[tile]
                                    for j0 in range(0, len(kts), 4):
                                        grp = kts[j0:j0 + 4]
                                        bb = bias_sb[bi % 2]; bkey = ("bias", bi % 2); bi += 1
                                        i0 = bias_index[(tile, h, j0)]
                                        P.dma("sp", bb[:, 0:len(grp), :], bias_d[i0:i0 + len(grp)].rearrange("t p q -> p t q"), w=[bkey])
                                        for jj, t in enumerate(grp):
                                            keys.append((kT[hp:hp + 64, c, t * 128:(t + 1) * 128], ("kT", t // 4, c), vaug[:, t, hq, :], ("vaug", t),
                                                         bb[:, jj, :], bkey))
                                for t in (16, 17):
                                    keys.append((kT[hp:hp + 64, c, t * 128:(t + 1) * 128], ("kT", 4, c), vaug[:, t, hq, :], ("vaug", t), None, None))
                                attention(qT[hp:hp + 64, c, :n], ("qT", c), n, keys, 0.125, s_ps, o_ps, pT, sbt, rec,
                                          ybT[hp:hp + 64, 2 * hh + c, t0:t0 + n], ("ybT", tile, 2 * hh + c, hp), hp, tagc)
                        P.flush()
            if upto in ("a", "b"):
                return
            with ExitStack() as es:
                wu = sb(es, "wu", (128, 8, 512), BF16)
                wva = sb(es, "wva", (128, 8, 512), BF16)
                wo = sb(es, "wo", (128, 8, D), BF16)
                wsT = sb(es, "wsT", (128, 4, 128), BF16)
                bsb = sb(es, "bsb", (128, 512))
                gvb = sb(es, "gvb", (128, 512))
                uT = sb(es, "uT", (128, 4, 512), BF16)
                vg = sb(es, "vg", (128, 512))
                vn = sb(es, "vn", (128, 512))
                vln = sb(es, "vln", (128, 512), BF16)
                tt = sb(es, "tt", (128, 4, 128))
                yaT = sb(es, "yaT", (128, 4, 512), BF16)
                yo = sb(es, "yo", (128, 8, 512))
                stats = sb(es, "stats", (128, 6))
                mv = sb(es, "mv", (128, 2))
                ss_ps = pst(es, "ss_ps", (128, 512))
                pp = [pst(es, f"pp{i}", (128, 512)) for i in range(2)]
                g_ps = pst(es, "g_ps", (128, 4, 128))
                P.dma("pool", wu[:, :, :], win_d[:, :, 0:512], w=["wu"])
                P.dma("pool", wva[:, :, :], win_d[:, :, 512:1024], w=["wva"])
                P.dma("pool", wo[:, :, :], woab_d, w=["wo"])
                P.dma("pool", wsT[:, :, :], wsT_d, w=["wsT"])
                P.dma("sp", bsb[:, :], bs_d, w=["bsb"])
                P.dma("sp", gvb[:, :], gv_d, w=["gvb"])
                pi = 0
                for tile in range(5):
                    n = tile_norm(tile, l, 0, b, ss_ps, "ss")
                    t0 = tile * 512
                    for c in range(4):
                        ps = pp[pi % 2]; pk = ("pp", pi % 2); pi += 1
                        fm_proj(ps, pk, lambda kc, c=c: wu[:, kc, c * 128:(c + 1) * 128], ["wu"],
                                lambda kc: xt[:, kc, :n], [("xt", kc) for kc in range(8)], n)
                        P.op("act", lambda e, ps=ps, c=c: e.activation(out=uT[:, c, :n], in_=ps[:, :n], func=AF.Gelu), r=[pk], w=[("uT", c)])
                    if upto == "c_u":
                        P.flush(); return
                    for s_ in range(n // 128):
                        ps = pp[pi % 2]; pk = ("pp", pi % 2); pi += 1
                        for kc in range(8):
                            P.op("pe", lambda e, ps=ps, kc=kc, s_=s_: e.matmul(ps[:, :], xt[:, kc, s_ * 128:(s_ + 1) * 128], wva[:, kc, :],
                                                                                 start=(kc == 0), stop=(kc == 7)),
                                 r=[("xt", kc), "wva"], w=[pk])
                        P.op("act", lambda e, ps=ps: e.activation(out=vg[:, :], in_=ps[:, :], func=AF.Gelu), r=[pk], w=["vg"])
                        P.op("dve", lambda e: e.bn_stats(out=stats[:, :], in_=vg[:, :]), r=["vg"], w=["stats"])
                        P.op("dve", lambda e: e.bn_aggr(out=mv[:, :], in_=stats[:, :]), r=["stats"], w=["mv"])
                        P.op("act", lambda e: e.activation(out=mv[:, 1:2], in_=mv[:, 1:2], func=AF.Sqrt, scale=1.0, bias=eps_t[:, 0:1]), r=["mv"], w=["mv"])
                        P.op("dve", lambda e: e.reciprocal(out=mv[:, 1:2], in_=mv[:, 1:2]), r=["mv"], w=["mv"])
                        P.op("dve", lambda e: e.tensor_scalar(out=vn[:, :], in0=vg[:, :], scalar1=mv[:, 0:1], scalar2=mv[:, 1:2],
                                                              op0=ALU.subtract, op1=ALU.mult), r=["vg", "mv"], w=["vn"])
                        P.op("dve", lambda e: e.tensor_tensor(out=vln[:, :], in0=vn[:, :], in1=gvb[:, :], op=ALU.mult), r=["vn", "gvb"], w=["vln"])
                        if upto == "c_va":
                            P.flush(); return
                        for g in range(4):
                            P.op("pe", lambda e, g=g: e.matmul(g_ps[:, g, :], vln[:, g * 128:(g + 1) * 128], wsT[:, g, :], start=True, stop=True),
                                 r=["vln", "wsT"], w=["g_ps"])
                        P.op("dve", lambda e: e.tensor_tensor(out=tt[:, :, :], in0=g_ps[:, :, :], in1=bsb[:, :].rearrange("p (g i) -> p g i", g=4), op=ALU.add),
                             r=["g_ps", "bsb"], w=["tt"])
                        P.op("dve", lambda e, s_=s_: e.tensor_tensor(out=yaT[:, :, s_ * 128:(s_ + 1) * 128], in0=tt[:, :, :],
                                                                     in1=uT[:, :, s_ * 128:(s_ + 1) * 128], op=ALU.mult),
                             r=["tt"] + [("uT", c) for c in range(4)], w=[("yaT", s_)])
                        if upto == "c_gate":
                            P.flush(); return

                    def get_ps(dc, tile=tile, n=n, t0=t0):
                        nonlocal pi
                        ps = pp[pi % 2]; pk = ("pp", pi % 2); pi += 1
                        rk = [("yaT", s_) for s_ in range(n // 128)]
                        for kc in range(8):
                            rhs = yaT[:, kc, :n] if kc < 4 else ybT[:, kc - 4, t0:t0 + n]
                            rr = rk if kc < 4 else [("ybT", tile, kc - 4, 0), ("ybT", tile, kc - 4, 64)]
                            P.op("pe", lambda e, ps=ps, kc=kc, rhs=rhs, dc=dc: e.matmul(ps[:, :n], wo[:, kc, dc * 128:(dc + 1) * 128], rhs,
                                                                                         start=(kc == 0), stop=(kc == 7)),
                                 r=["wo"] + rr, w=[pk])
                        return ps[:, :n], pk
                    post_norm_residual(tile, l, 0, b, get_ps, ss_ps, "ss", yo)
                P.flush()

    def qk_process(ps, pkey, n, gcol, rope, t0, dst, dkey, bps, sw_ps, cs, qn, qg, bkey="bps", skey="sw_ps"):
        P.op("act", lambda e: e.activation(out=sq[0][:, :n], in_=ps, func=AF.Square), r=[pkey], w=[("sq", 0)])
        P.op("pe", lambda e: e.matmul(bps[:, :n], bones_bf[:, :], sq[0][:, :n], start=True, stop=True), r=[("sq", 0), "bones"], w=[bkey])
        P.op("act", lambda e: e.activation(out=rstd[:, :n], in_=bps[:, :n], func=AF.Sqrt, scale=1.0 / 64, bias=eps_t[:, 0:1]), r=[bkey], w=["rstd"])
        P.op("dve", lambda e: e.reciprocal(out=rstd[:, :n], in_=rstd[:, :n]), r=["rstd"], w=["rstd"])
        P.op("dve", lambda e: e.tensor_tensor(out=qn[:, :n], in0=ps, in1=rstd[:, :n], op=ALU.mult), r=[pkey, "rstd"], w=["qn"])
        if not rope:
            P.op("act", lambda e: e.activation(out=dst, in_=qn[:, :n], func=AF.Identity, scale=qkg[:, gcol:gcol + 1]), r=["qn"], w=[dkey])
            return
        P.op("act", lambda e: e.activation(out=qg[:, :n], in_=qn[:, :n], func=AF.Identity, scale=qkg[:, gcol:gcol + 1]), r=["qn"], w=["qg"])
        P.op("pe", lambda e: e.matmul(sw_ps[:, :n], perm_bf[:, :], qg[:, :n], start=True, stop=True), r=["qg", "perm"], w=[skey])
        P.op("dve", lambda e: e.tensor_tensor(out=tmp[0][:, :n], in0=qg[:, :n], in1=cs[:, 0, :n], op=ALU.mult), r=["qg", "cs"], w=[("tmp", 0)])
        P.op("dve", lambda e: e.tensor_tensor(out=tmp[1][:, :n], in0=sw_ps[:, :n], in1=cs[:, 1, :n], op=ALU.mult), r=[skey, "cs"], w=[("tmp", 1)])
        P.op("dve", lambda e: e.tensor_tensor(out=dst, in0=tmp[0][:, :n], in1=tmp[1][:, :n], op=ALU.add), r=[("tmp", 0), ("tmp", 1)], w=[dkey])

    def mixer_odd(b, l):
        with ExitStack() as es0:
            kT = sb(es0, "kT", (128, 2, NT), BF16)
            vaug = sb(es0, "vaug", (128, 18, 4, 128), BF16)
            cs = sb(es0, "cs", (128, 2, 512))
            qn = sb(es0, "qn", (128, 512))
            qg = sb(es0, "qg", (128, 512), BF16)
            with ExitStack() as es:
                wk = sb(es, "wk", (128, 8, 256), BF16)
                wv = sb(es, "wv", (128, 8, 256), BF16)
                ss_ps = pst(es, "ss_ps", (128, 512))
                pp = [pst(es, f"pp{i}", (128, 512)) for i in range(2)]
                bps = pst(es, "bps", (128, 512))
                sw_ps = pst(es, "sw_ps", (128, 512))
                P.dma("pool", wk[:, :, :], wk_d, w=["wk"])
                P.dma("pool", wv[:, :, :], wv_d, w=["wv"])
                P.op("dve", lambda e: e.memset(vaug[:, :, :, :], 1.0), w=[("vaug", t) for t in range(18)])
                pi = 0
                for tile in range(5):
                    n = tile_norm(tile, l, 0, b, ss_ps, "ss")
                    t0 = tile * 512
                    if tile < 4:
                        P.dma("sp", cs[:, 0, :], cos_d[:, t0:t0 + 512], w=["cs"], slot="cs0")
                        P.dma("sp", cs[:, 1, :], sin_d[:, t0:t0 + 512], w=["cs"], slot="cs1")
                    for c in range(2):
                        ps = pp[pi % 2]; pk = ("pp", pi % 2); pi += 1
                        fm_proj(ps, pk, lambda kc, c=c: wk[:, kc, c * 128:(c + 1) * 128], ["wk"],
                                lambda kc: xt[:, kc, :n], [("xt", kc) for kc in range(8)], n)
                        qk_process(ps[:, :n], pk, n, 1, tile < 4, t0, kT[:, c, t0:t0 + n], ("kT", tile, c), bps, sw_ps, cs, qn, qg)
                    for s_ in range(n // 128):
                        ps = pp[pi % 2]; pk = ("pp", pi % 2); pi += 1
                        ti = tile * 4 + s_
                        for kc in range(8):
                            P.op("pe", lambda e, ps=ps, kc=kc, s_=s_: e.matmul(ps[:, 0:256], xt[:, kc, s_ * 128:(s_ + 1) * 128], wv[:, kc, :],
                                                                                 start=(kc == 0), stop=(kc == 7)),
                                 r=[("xt", kc), "wv"], w=[pk])
                        for g in range(4):
                            P.op("dve", lambda e, ps=ps, ti=ti, g=g: e.tensor_copy(out=vaug[:, ti, g, 0:64], in_=ps[:, g * 64:(g + 1) * 64]),
                                 r=[pk], w=[("vaug", ti)])
                P.flush()
            with ExitStack() as es:
                wq = sb(es, "wq", (128, 8, D), BF16)
                wo = sb(es, "wo", (128, 8, D), BF16)
                qT = sb(es, "qT", (128, 8, 512), BF16)
                yT = sb(es, "yT", (128, 8, 512), BF16)
                yo = sb(es, "yo", (128, 8, 512))
                pT = [sb(es, f"pT{i}", (128, 512), BF16) for i in range(2)]
                rec = sb(es, "rec", (128, 512))
                ss_ps = pst(es, "ss_ps", (128, 512))
                pp = [pst(es, f"pp{i}", (128, 512)) for i in range(2)]
                xps = pst(es, "xps", (128, 512))
                s_ps = [pst(es, f"s_ps{i}", (128, 512)) for i in range(2)]
                o_ps = [pst(es, f"o_ps{i}", (128, 512)) for i in range(2)]
                P.dma("pool", wq[:, :, :], wq_d, w=["wq"])
                P.dma("pool", wo[:, :, :], woc_d, w=["wo"])
                tagc = [0, 0]
                pi = 0
                for tile in range(4):
                    n = tile_norm(tile, l, 0, b, ss_ps, "ss")
                    t0 = tile * 512
                    P.dma("sp", cs[:, 0, :], cos_d[:, t0:t0 + 512], w=["cs"], slot="cs0")
                    P.dma("sp", cs[:, 1, :], sin_d[:, t0:t0 + 512], w=["cs"], slot="cs1")
                    for c in range(8):
                        ps = pp[pi % 2]; pk = ("pp", pi % 2); pi += 1
                        fm_proj(ps, pk, lambda kc, c=c: wq[:, kc, c * 128:(c + 1) * 128], ["wq"],
                                lambda kc: xt[:, kc, :n], [("xt", kc) for kc in range(8)], n)
                        qk_process(ps[:, :n], pk, n, 0, True, t0, qT[:, c, :n], ("qT", c), xps, xps, cs, qn, qg, bkey="xps", skey="xps")
                    for c in range(8):
                        for s2 in range(2):
                            hp = s2 * 64
                            g = 2 * (c // 4) + s2
                            kc_, = [g // 2]
                            keys = []
                            for t in range(18):
                                keys.append((kT[hp:hp + 64, kc_, t * 128:(t + 1) * 128], ("kT", t // 4, kc_), vaug[:, t, g, :], ("vaug", t), None, None))
                            attention(qT[hp:hp + 64, c, :n], ("qT", c), n, keys, 0.125, s_ps, o_ps, pT, None, rec,
                                      yT[hp:hp + 64, c, :n], ("yT", c, hp), hp, tagc)

                    def get_ps(dc, n=n):
                        nonlocal pi
                        ps = pp[pi % 2]; pk = ("pp", pi % 2); pi += 1
                        for kc in range(8):
                            P.op("pe", lambda e, ps=ps, kc=kc, dc=dc: e.matmul(ps[:, :n], wo[:, kc, dc * 128:(dc + 1) * 128], yT[:, kc, :n],
                                                                               start=(kc == 0), stop=(kc == 7)),
                                 r=["wo", ("yT", kc, 0), ("yT", kc, 64)], w=[pk])
                        return ps[:, :n], pk
                    post_norm_residual(tile, l, 0, b, get_ps, ss_ps, "ss", yo)
                P.flush()

    def ffn(b, l, with_ctx, upto=None):
        passes = [(0, 0, 1024), (0, 1024, 1024)] + ([(1, 0, 256)] if with_ctx else [])
        for (isctx, s0, T) in passes:
            with ExitStack() as es:
                xf = sb(es, "xf", (128, 8, 1026), BF16)
                actT = sb(es, "actT", (128, NCH, 1024), BF16)
                ca = sb(es, "ca", (128, 1024))
                cg = sb(es, "cg", (128, 1024))
                wub = [sb(es, f"wub{i}", (128, 2, 8, 128), BF16) for i in range(3)]
                wdb = [sb(es, f"wdb{i}", (128, NCH, 128), BF16) for i in range(2)]
                yo = sb(es, "yo", (128, 8, 512))
                ss_ps = pst(es, "ss_ps", (128, 512))
                a_ps = pst(es, "a_ps", (128, 1536))
                g_ps = pst(es, "g_ps", (128, 1536))
                pp = [pst(es, "ppd", (128, 512))]
                v = 2 if isctx else b
                A_, B_ = modA[:, l, 1, v, :], modB[:, l, 1, v, :]
                Sq = LC if isctx else S

                def hap(kc, a, n):
                    return hcT[:, kc, a:a + n] if isctx else hT[:, kc, a:a + n]

                def hk(a):
                    return 4 if isctx else a // 512
                segs = []
                if s0 > 0:
                    P.op("dve", lambda e: e.tensor_copy(out=xf[:, :, 0:1], in_=xhalo[:, :, 0:1]), r=["xhalo"], w=[("xf", "l")])
                else:
                    P.op("dve", lambda e: e.memset(xf[:, :, 0:1], 0.0), w=[("xf", "l")])
                for a in range(s0, s0 + T, 512):
                    nn = min(512, s0 + T - a)
                    segs.append((a, nn, 1 + a - s0))
                if s0 + T < Sq:
                    segs.append((s0 + T, 1, 1 + T))
                else:
                    P.op("dve", lambda e: e.memset(xf[:, :, 1 + T:2 + T], 0.0), w=[("xf", "r")])
                xkeys = []
                for (a, nn, col) in segs:
                    key = ("xf", a)
                    xkeys.append(key)
                    norm_mod(lambda kc, a=a, nn=nn: hap(kc, a, nn), [hkey(hk(a), kc) for kc in range(8)], nn, A_, B_,
                             lambda kc, col=col, nn=nn: xf[:, kc, col:col + nn], [key] * 8, ss_ps, "ss")
                xall = xkeys + [("xf", "l"), ("xf", "r")]
                if s0 + T < Sq:
                    P.op("dve", lambda e: e.tensor_copy(out=xhalo[:, :, 0:1], in_=xf[:, :, T:T + 1]), r=xkeys, w=["xhalo"])
                if upto == "f_norm":
                    P.flush(); return
                for j in range(NCH):
                    wb_ = wub[j % 3]
                    wkey = ("wub", j % 3)
                    P.dma("pool", wb_[:, 0, :, :], wup_d[l, j], w=[wkey], slot=("wub", j % 3, 0))
                    P.dma("pool", wb_[:, 1, :, :], wup_d[l, NCH + j], w=[wkey], slot=("wub", j % 3, 1))
                    for part, (ps_, pkey, cdst, ckey) in enumerate(((a_ps, "a_ps", ca, "ca"), (g_ps, "g_ps", cg, "cg"))):
                        ch = part * NCH + j
                        for c0 in range(0, T + 2, 512):
                            c1 = min(c0 + 512, T + 2)
                            for kc in range(8):
                                P.op("pe", lambda e, ps_=ps_, kc=kc, c0=c0, c1=c1, part=part, wb_=wb_: e.matmul(
                                    ps_[:, c0:c1], wb_[:, part, kc, :], xf[:, kc, c0:c1], start=(kc == 0), stop=(kc == 7)),
                                    r=[wkey] + xall, w=[pkey])
                        w0, w1, w2 = (cwT[:, l, ch, i:i + 1] for i in range(3))
                        P.op("act", lambda e, ps_=ps_, cdst=cdst, w1=w1, ch=ch: e.activation(out=cdst[:, :T], in_=ps_[:, 1:T + 1], func=AF.Identity,
                                                                                               scale=w1, bias=cbT[:, l, ch:ch + 1]),
                             r=[pkey], w=[ckey])
                        P.op("dve", lambda e, ps_=ps_, cdst=cdst, w0=w0: e.scalar_tensor_tensor(out=cdst[:, 0:T], in0=ps_[:, 0:T], scalar=w0,
                                                                                                  in1=cdst[:, 0:T], op0=ALU.mult, op1=ALU.add),
                             r=[pkey, ckey], w=[ckey])
                        P.op("dve", lambda e, ps_=ps_, cdst=cdst, w2=w2: e.scalar_tensor_tensor(out=cdst[:, 0:T], in0=ps_[:, 2:T + 2], scalar=w2,
                                                                                                  in1=cdst[:, 0:T], op0=ALU.mult, op1=ALU.add),
                             r=[pkey, ckey], w=[ckey])
                    P.op("act", lambda e: e.activation(out=cg[:, :T], in_=cg[:, :T], func=AF.Silu), r=["cg"], w=["cg"])
                    P.op("dve", lambda e, j=j: e.tensor_tensor(out=actT[:, j, :T], in0=cg[:, :T], in1=ca[:, :T], op=ALU.mult),
                         r=["cg", "ca"], w=[("actT", j)])
                    if upto == "f_up1":
                        P.flush(); return
                if upto == "f_up":
                    P.flush(); return
                pi = 0
                di = 0
                for a in range(0, T, 512):
                    nn = min(512, T - a)
                    tile = 4 if isctx else (s0 + a) // 512

                    def get_ps(dc, a=a, nn=nn):
                        nonlocal pi, di
                        wd_ = wdb[di % 2]; wdk = ("wdb", di % 2); di += 1
                        P.dma("pool", wd_[:, 0:11, :], wdn_d[l, dc][:, 0:11, :], w=[wdk], slot=("wdb", (di - 1) % 2, 0))
                        P.dma("pool", wd_[:, 11:22, :], wdn_d[l, dc][:, 11:22, :], w=[wdk], slot=("wdb", (di - 1) % 2, 1))
                        ps = pp[0]; pk = ("pp", 0); pi += 1
                        for kc in range(NCH):
                            P.op("pe", lambda e, ps=ps, kc=kc, wd_=wd_: e.matmul(ps[:, :nn], wd_[:, kc, :], actT[:, kc, a:a + nn],
                                                                                 start=(kc == 0), stop=(kc == NCH - 1)),
                                 r=[wdk, ("actT", kc)], w=[pk])
                        return ps[:, :nn], pk
                    post_norm_residual(tile, l, 1, b, get_ps, ss_ps, "ss", yo)
                P.flush()

    for b in range(nb):
        for kc in range(8):
            P.dma("sp", hT[:, kc, :], xT_d[b, :, kc, :], w=[hkey(t, kc) for t in range(4)], slot=("ld", kc))
        P.dma("sp", hcT[:, :, :], ctxT_d[b], w=[hkey(4, kc) for kc in range(8)], slot="ldc")
        P.flush()
        if "mix0a" in stages:
            mixer_even(b, 0, upto="a", halves=(0,))
        if "mix0b" in stages:
            mixer_even(b, 0, upto="b", halves=(0,))
        for st_ in ("c_u", "c_va", "c_gate"):
            if st_ in stages:
                mixer_even(b, 0, upto=st_, halves=())
        if "mix0ab" in stages:
            mixer_even(b, 0, upto="b")
        if "mix0c" in stages:
            mixer_even(b, 0, halves=())
        if "mix0" in stages:
            mixer_even(b, 0)
        for st_ in ("f_norm", "f_up1", "f_up"):
            if st_ in stages:
                ffn(b, 0, with_ctx=True, upto=st_)
        if "ffn0" in stages:
            ffn(b, 0, with_ctx=True)
        if "mix1" in stages:
            mixer_odd(b, 1)
        if "ffn1" in stages:
            ffn(b, 1, with_ctx=False)
        if dbg:
            P.dma("sp", hc_out[b], hcT[:, :, :], r=[hkey(4, kc) for kc in range(8)], slot="sthc")
        for kc in range(8):
            P.dma("sp", out_d[b, :, kc, :], hT[:, kc, :], r=[hkey(t, kc) for t in range(4)], slot=("st", kc))
        P.flush()
    ES.close()
    return nc


def _prep_shared(inp):
    f = lambda a: np.ascontiguousarray(np.asarray(a, dtype=np.float32))
    sh = {}
    w_ada = f(inp["w_ada"])
    sh["w_ada"] = np.ascontiguousarray(w_ada.reshape(2, 8, 128, 6 * D).transpose(0, 2, 1, 3))
    sh["b_adaT"] = np.ascontiguousarray(f(inp["b_ada"]).reshape(2, 48, 128).transpose(0, 2, 1))
    sh["gT"] = np.ascontiguousarray(f(inp["norm_g"]).reshape(2, 4, 8, 128).transpose(0, 1, 3, 2))
    sh["w_in"] = _lay_w(f(inp["w_in_ab"])[0])
    sh["w_out_ab"] = _lay_w(f(inp["w_out_ab"])[0])
    wqkv = f(inp["w_qkv_c"])[0]
    qcols = []
    for c in range(8):
        for s in range(2):
            h = HMAP[c][s]
            qcols.extend(range(h * 64, (h + 1) * 64))
    sh["w_q"] = _lay_w(wqkv[:, qcols])
    sh["w_k"] = _lay_w(wqkv[:, 1024:1280])
    sh["w_v"] = _lay_w(wqkv[:, 1280:1536])
    sh["w_out_c"] = _lay_w(f(inp["w_out_c"])[0][qcols, :])
    sh["w_sT"] = np.ascontiguousarray(f(inp["a_w_s"])[0].transpose(2, 0, 1))
    sh["b_s_bc"] = np.ascontiguousarray(np.broadcast_to(f(inp["a_b_s"])[0].reshape(1, 512), (128, 512)))
    sh["gv_bc"] = np.ascontiguousarray(np.broadcast_to(f(inp["a_v_g"])[0].reshape(1, 512), (128, 512)))
    bias, index, plan = _build_na_bias(f(inp["b_rpb"])[0])
    sh["bias_na"] = bias
    sh["qg2"] = np.ascontiguousarray(np.tile(f(inp["c_q_g"])[0], 2).reshape(128, 1))
    sh["kg2"] = np.ascontiguousarray(np.tile(f(inp["c_k_g"])[0], 2).reshape(128, 1))
    wup = f(inp["w_up"])
    sh["w_up"] = np.ascontiguousarray(wup.reshape(2, 8, 128, 2 * NCH, 128).transpose(0, 3, 2, 1, 4))
    sh["conv_wT"] = np.ascontiguousarray(f(inp["conv_w"]).reshape(2, 3, 2 * NCH, 128).transpose(0, 3, 2, 1))
    sh["conv_bT"] = np.ascontiguousarray(f(inp["conv_b"]).reshape(2, 2 * NCH, 128).transpose(0, 2, 1))
    wdn = f(inp["w_down"])
    sh["w_down"] = np.ascontiguousarray(wdn.reshape(2, NCH, 128, 8, 128).transpose(0, 3, 2, 1, 4))
    cosT, sinT, perm = _rope_tables()
    sh["cosT"], sh["sinT"], sh["perm"] = cosT, sinT, perm
    return sh, index, plan


def _lay_act(a):
    nb_, L, _ = a.shape
    return np.ascontiguousarray(a.transpose(0, 2, 1).reshape(nb_, 8, 128, L).transpose(0, 2, 1, 3))


def kernel(**inp):
    x = np.asarray(inp["x"], np.float32)
    c = np.asarray(inp["c"], np.float32)
    ctx = np.asarray(inp["ctx"], np.float32)
    c_ctx = np.asarray(inp["c_ctx"], np.float32)
    sh, index, plan = _prep_shared(inp)
    n_cores = 8
    nb = x.shape[0] // n_cores
    nc = build_program(sh["bias_na"].shape[0], index, plan, nb=nb)
    in_maps = []
    for i in range(n_cores):
        m = dict(sh)
        sl = slice(i * nb, (i + 1) * nb)
        m["xT"] = _lay_act(x[sl])
        m["ctxT"] = _lay_act(ctx[sl])
        cv = np.stack([c[i * nb], c[i * nb + 1], c_ctx], -1)
        m["cT"] = np.ascontiguousarray(cv.reshape(8, 128, 3).transpose(1, 0, 2))
        in_maps.append(m)
    res = run_bass_kernel_spmd(nc, in_maps, core_ids=list(range(n_cores)))
    out = np.empty_like(x)
    for i in range(n_cores):
        o = res.results[i]["outT"]
        out[i * nb:(i + 1) * nb] = o.transpose(0, 3, 2, 1).reshape(nb, S, D)
    return out
```

```python
import numpy as np
from contextlib import ExitStack
import concourse.bass as bass
import concourse.mybir as mybir
from concourse.bass_utils import run_bass_kernel_spmd

F32 = mybir.dt.float32
BF16 = mybir.dt.bfloat16
AF = mybir.ActivationFunctionType
ALU = mybir.AluOpType

D = 1024
S = 2048
LC = 256
NT = S + LC
DFF = 2816
EPS = 1e-6
NCH = 22


import types


def _freeze(fn, depth=0):
    if not isinstance(fn, types.FunctionType) or fn.__closure__ is None or depth > 4:
        return fn
    cells = []
    for c in fn.__closure__:
        try:
            v = c.cell_contents
        except ValueError:
            cells.append(c)
            continue
        if isinstance(v, types.FunctionType):
            v = _freeze(v, depth + 1)
        cells.append(types.CellType(v))
    g = types.FunctionType(fn.__code__, fn.__globals__, fn.__name__, fn.__defaults__, tuple(cells))
    g.__kwdefaults__ = fn.__kwdefaults__
    return g


class Prog:
    def __init__(self, nc):
        self.nc = nc
        self.eng = {"pe": nc.tensor, "act": nc.scalar, "dve": nc.vector, "pool": nc.gpsimd, "sp": nc.sync}
        self.sem = {e: nc.alloc_semaphore(name=f"sem_{e}") for e in self.eng}
        self.cnt = {e: 0 for e in self.eng}
        self.dsem = {}
        self.waited = {e: {} for e in self.eng}
        self.ops = []
        self.last_w = {}
        self.readers = {}
        self.last_of_eng = {}
        self.last_dma = {}

    def op(self, eng, fn, r=(), w=(), dma=None):
        idx = len(self.ops)
        deps = {}
        for k in r:
            d = self.last_w.get(k)
            if d is not None:
                deps[d] = "raw"
        for k in w:
            d = self.last_w.get(k)
            if d is not None:
                deps[d] = "raw"
            lastr = {}
            for rd in self.readers.get(k, ()):
                o_ = self.ops[rd]
                if o_["dma"] is not None:
                    lastr[("d", rd)] = rd
                else:
                    lastr[o_["eng"]] = rd
            for rd in lastr.values():
                if rd not in deps:
                    deps[rd] = "war"
        deps.pop(idx, None)
        self.ops.append(dict(eng=eng, fn=_freeze(fn), deps=deps, dma=dma, sig=None))
        for k in r:
            self.readers.setdefault(k, []).append(idx)
        for k in w:
            self.last_w[k] = idx
            self.readers[k] = []
        if dma is None:
            self.last_of_eng[eng] = idx
        else:
            self.last_dma[dma] = idx
        return idx

    def dma(self, eng, out, in_, r=(), w=(), slot=None):
        if slot is None:
            slot = w[0] if w else "out"
        return self.op(eng, lambda e: e.dma_start(out=out, in_=in_), r=r, w=w, dma=slot)

    def flush(self):
        alld = {}
        for e, i in self.last_of_eng.items():
            alld[i] = "raw"
        for s, i in self.last_dma.items():
            alld[i] = "raw"
        for e in self.eng:
            self.ops.append(dict(eng=e, fn=None, deps=dict(alld), dma=None, sig=None, barrier=True))
        ops = self.ops
        need = [False] * len(ops)
        for i, o in enumerate(ops):
            fd = []
            for d, kind in o["deps"].items():
                od = ops[d]
                if od["dma"] is None and od["eng"] == o["eng"] and not o.get("barrier"):
                    if o["eng"] == "pe" or kind == "war":
                        continue
                if od["dma"] is None and od["eng"] == o["eng"] and o.get("barrier"):
                    continue
                fd.append(d)
                if od["dma"] is None:
                    need[d] = True
            o["fd"] = fd
        for i, o in enumerate(ops):
            E = o["eng"]
            eo = self.eng[E]
            waits = {}
            for d in o["fd"]:
                key, val = ops[d]["sig"]
                if waits.get(key, 0) < val:
                    waits[key] = val
            for key, val in waits.items():
                if self.waited[E].get(key, 0) < val:
                    h = self.sem[key[1]] if key[0] == "e" else self.dsem[key[1]][0]
                    eo.wait_ge(h, val)
                    self.waited[E][key] = val
            if o["fn"] is None:
                continue
            ins = o["fn"](eo)
            if o["dma"] is not None:
                slot = o["dma"]
                if slot not in self.dsem:
                    self.dsem[slot] = [self.nc.alloc_semaphore(name=f"dma_{len(self.dsem)}"), 0]
                s = self.dsem[slot]
                s[1] += 16
                ins.then_inc(s[0], 16)
                o["sig"] = (("d", slot), s[1])
            elif need[i]:
                self.cnt[E] += 1
                ins.then_inc(self.sem[E], 1)
                o["sig"] = (("e", E), self.cnt[E])
        self.ops = []
        self.last_w = {}
        self.readers = {}
        self.last_of_eng = {}
        self.last_dma = {}


def _na_tile_plan():
    plan = []
    for A in range(4):
        rows = list(range(8 * A, 8 * A + 8))
        lo = min(max(r - 4, 0) if r - 4 <= 24 else 24 for r in rows)
        lo = min(min(max(r - 4, 0), 24) for r in rows)
        hi = max(min(max(r - 4, 0), 24) + 7 for r in rows)
        t0, t1 = lo // 2, hi // 2
        plan.append(list(range(t0, t1 + 1)))
    return plan


def _build_na_bias(rpb):
    plan = _na_tile_plan()
    NEG = np.float32(-30000.0)
    tiles = []
    index = {}
    qr_l = np.arange(8)[:, None]
    qc = np.arange(64)[None, :]
    for A in range(4):
        qr = (8 * A + qr_l) + 0 * qc
        qcc = 0 * qr_l + qc
        r0 = np.clip(qr - 4, 0, 24)
        c0 = np.clip(qcc - 8, 0, 48)
        qr_f, qc_f, r0_f, c0_f = [a.reshape(-1) for a in (qr, qcc, r0, c0)]
        for h in range(8):
            for j, t in enumerate(plan[A]):
                kr = np.repeat(np.arange(2 * t, 2 * t + 2), 64)
                kc = np.tile(np.arange(64), 2)
                valid = ((kr[:, None] >= r0_f[None, :]) & (kr[:, None] < r0_f[None, :] + 8)
                         & (kc[:, None] >= c0_f[None, :]) & (kc[:, None] < c0_f[None, :] + 16))
                dr = np.clip(kr[:, None] - qr_f[None, :] + 7, 0, 14)
                dc = np.clip(kc[:, None] - qc_f[None, :], -15, 15) + 15
                b = rpb[h][dr, dc]
                tiles.append(np.where(valid, b, NEG).astype(np.float32))
                index[(A, h, j)] = len(tiles) - 1
    return np.stack(tiles, 0), index, plan


def _rope_tables():
    t = np.arange(S)
    inv = (10000.0 ** (-np.arange(16, dtype=np.float32) / 16)).astype(np.float32)
    rows = (t // 64).astype(np.float32)[:, None] * inv
    cols = (t % 64).astype(np.float32)[:, None] * inv
    ang = np.concatenate([rows, cols], -1)
    cos = np.cos(ang).astype(np.float32).T
    sin = np.sin(ang).astype(np.float32).T
    cosT = np.tile(cos, (4, 1))
    sinT = np.concatenate([-sin, sin, -sin, sin], 0)
    perm = np.zeros((128, 128), np.float32)
    for m in range(128):
        k = m + 32 if (m % 64) < 32 else m - 32
        perm[k, m] = 1.0
    return np.ascontiguousarray(cosT), np.ascontiguousarray(sinT), perm


HMAP = [[(4 * (2 * (c // 4)) + (c % 4)), (4 * (2 * (c // 4) + 1) + (c % 4))] for c in range(8)]


def _lay_w(w):
    K, N = w.shape
    return np.ascontiguousarray(w.reshape(K // 128, 128, N).transpose(1, 0, 2))


def build_program(n_bias_tiles, bias_index, na_plan, nb=2, stages=("mix0", "ffn0", "mix1", "ffn1"), dbg=False):
    nc = bass.Bass("TRN2", target_bir_lowering=False)
    P = Prog(nc)

    def din(name, shape):
        return nc.dram_tensor(name, list(shape), F32, kind="ExternalInput").ap()

    xT_d = din("xT", (nb, 128, 8, S))
    ctxT_d = din("ctxT", (nb, 128, 8, LC))
    out_d = nc.dram_tensor("outT", [nb, 128, 8, S], F32, kind="ExternalOutput").ap()
    cT_d = din("cT", (128, 8, 3))
    wada_d = din("w_ada", (2, 128, 8, 6 * D))
    badaT_d = din("b_adaT", (2, 128, 48))
    gT_d = din("gT", (2, 4, 128, 8))
    win_d = din("w_in", (128, 8, 2560))
    woab_d = din("w_out_ab", (128, 8, D))
    wq_d = din("w_q", (128, 8, D))
    wk_d = din("w_k", (128, 8, 256))
    wv_d = din("w_v", (128, 8, 256))
    woc_d = din("w_out_c", (128, 8, D))
    wsT_d = din("w_sT", (128, 4, 128))
    bs_d = din("b_s_bc", (128, 512))
    gv_d = din("gv_bc", (128, 512))
    bias_d = din("bias_na", (n_bias_tiles, 128, 512))
    qg_d = din("qg2", (128, 1))
    kg_d = din("kg2", (128, 1))
    wup_d = din("w_up", (2, 2 * NCH, 128, 8, 128))
    cw_d = din("conv_wT", (2, 128, 2 * NCH, 3))
    cb_d = din("conv_bT", (2, 128, 2 * NCH))
    wdn_d = din("w_down", (2, 8, 128, NCH, 128))
    cos_d = din("cosT", (128, S))
    sin_d = din("sinT", (128, S))
    perm_d = din("perm", (128, 128))

    ES = ExitStack()
    hc_out = nc.dram_tensor("hcT_out", [nb, 128, 8, LC], F32, kind="ExternalOutput").ap() if dbg else None

    uid = [0]

    def sb(es, name, shape, dt=F32):
        uid[0] += 1
        return es.enter_context(nc.sbuf_tensor(f"{name}_{uid[0]}", list(shape), dt))

    def pst(es, name, shape):
        uid[0] += 1
        return es.enter_context(nc.psum_tensor(f"{name}_{uid[0]}", list(shape), F32))

    hT = sb(ES, "hT", (128, 8, S))
    hcT = sb(ES, "hcT", (128, 8, LC))
    ones_bf = sb(ES, "ones_bf", (128, 128), BF16)
    bones_bf = sb(ES, "bones_bf", (128, 128), BF16)
    perm_bf = sb(ES, "perm_bf", (128, 128), BF16)
    modA = sb(ES, "modA", (128, 2, 2, 3, 8))
    modB = sb(ES, "modB", (128, 2, 2, 3, 8))
    modG = sb(ES, "modG", (128, 2, 2, 3, 8))
    qkg = sb(ES, "qkg", (128, 2))
    cwT = sb(ES, "cwT", (128, 2, 2 * NCH, 3))
    cbT = sb(ES, "cbT", (128, 2, 2 * NCH))
    sq = [sb(ES, f"sq{i}", (128, 512), BF16) for i in range(2)]
    rstd = sb(ES, "rstd", (128, 512))
    tmp = [sb(ES, f"tmp{i}", (128, 512)) for i in range(2)]
    xt = sb(ES, "xt", (128, 8, 512), BF16)
    xhalo = sb(ES, "xhalo", (128, 8, 1), BF16)

    def hsrc(tile):
        if tile < 4:
            return (lambda kc, a=0, n=512: hT[:, kc, tile * 512 + a: tile * 512 + a + n]), 512
        return (lambda kc, a=0, n=256: hcT[:, kc, a:a + n]), 256

    def hkey(tile, kc):
        return ("h", tile, kc)

    def norm_mod(src, skeys, n, A, B, dst, dkeys, ss_ps, ss_key):
        for kc in range(8):
            s_ = sq[kc % 2]
            P.op("act", lambda e, kc=kc, s_=s_: e.activation(out=s_[:, :n], in_=src(kc), func=AF.Square),
                 r=[skeys[kc]], w=[("sq", kc % 2)])
            P.op("pe", lambda e, kc=kc, s_=s_: e.matmul(ss_ps[:, :n], ones_bf[:, :], s_[:, :n], start=(kc == 0), stop=(kc == 7)),
                 r=[("sq", kc % 2)], w=[ss_key])
        P.op("act", lambda e: e.activation(out=rstd[:, :n], in_=ss_ps[:, :n], func=AF.Sqrt, scale=1.0 / D, bias=eps_t[:, 0:1]),
             r=[ss_key], w=["rstd"])
        P.op("dve", lambda e: e.reciprocal(out=rstd[:, :n], in_=rstd[:, :n]), r=["rstd"], w=["rstd"])
        for kc in range(8):
            t_ = tmp[kc % 2]
            P.op("dve", lambda e, kc=kc, t_=t_: e.tensor_tensor(out=t_[:, :n], in0=src(kc), in1=rstd[:, :n], op=ALU.mult),
                 r=[skeys[kc], "rstd"], w=[("tmp", kc % 2)])
            P.op("act", lambda e, kc=kc, t_=t_: e.activation(out=dst(kc), in_=t_[:, :n], func=AF.Identity,
                                                              scale=A[:, kc:kc + 1], bias=B[:, kc:kc + 1]),
                 r=[("tmp", kc % 2)], w=[dkeys[kc]])

    def tile_norm(tile, l, mf, b, ss_ps, ss_key):
        src, n = hsrc(tile)
        v = b if tile < 4 else 2
        norm_mod(lambda kc: src(kc), [hkey(tile, kc) for kc in range(8)], n,
                 modA[:, l, mf, v, :], modB[:, l, mf, v, :],
                 lambda kc: xt[:, kc, :n], [("xt", kc) for kc in range(8)], ss_ps, ss_key)
        return n

    def fm_proj(ps, pkey, w_ap, wkeys, rhs, rkeys, n, nk=8):
        for kc in range(nk):
            P.op("pe", lambda e, kc=kc: e.matmul(ps[:, :n], w_ap(kc), rhs(kc), start=(kc == 0), stop=(kc == nk - 1)),
                 r=list(wkeys) + [rkeys[kc]], w=[pkey])

    def post_norm_residual(tile, l, mf, b, get_ps, ss_ps, ss_key, yo):
        src, n = hsrc(tile)
        v = b if tile < 4 else 2
        G = modG[:, l, mf, v, :]
        for dc in range(8):
            ps, pkey = get_ps(dc)
            s_ = sq[dc % 2]
            P.op("act", lambda e, ps=ps, dc=dc: e.activation(out=yo[:, dc, :n], in_=ps, func=AF.Copy), r=[pkey], w=[("yo", dc)])
            P.op("act", lambda e, ps=ps, s_=s_: e.activation(out=s_[:, :n], in_=ps, func=AF.Square), r=[pkey], w=[("sq", dc % 2)])
            P.op("pe", lambda e, dc=dc, s_=s_: e.matmul(ss_ps[:, :n], ones_bf[:, :], s_[:, :n], start=(dc == 0), stop=(dc == 7)),
                 r=[("sq", dc % 2)], w=[ss_key])
        P.op("act", lambda e: e.activation(out=rstd[:, :n], in_=ss_ps[:, :n], func=AF.Sqrt, scale=1.0 / D, bias=eps_t[:, 0:1]),
             r=[ss_key], w=["rstd"])
        P.op("dve", lambda e: e.reciprocal(out=rstd[:, :n], in_=rstd[:, :n]), r=["rstd"], w=["rstd"])
        for dc in range(8):
            t_ = tmp[dc % 2]
            P.op("dve", lambda e, dc=dc, t_=t_: e.tensor_tensor(out=t_[:, :n], in0=yo[:, dc, :n], in1=rstd[:, :n], op=ALU.mult),
                 r=[("yo", dc), "rstd"], w=[("tmp", dc % 2)])
            P.op("dve", lambda e, dc=dc, t_=t_: e.scalar_tensor_tensor(out=src(dc), in0=t_[:, :n], scalar=G[:, dc:dc + 1], in1=src(dc),
                                                                        op0=ALU.mult, op1=ALU.add),
                 r=[("tmp", dc % 2), hkey(tile, dc)], w=[hkey(tile, dc)])

    def attention(q_ap, qkey, n_q, keys, scale, s_ps, o_ps, pT, sbt, rec, out_ap, okey, hp, tagc):
        tagc.append(dict(q_ap=q_ap, qkey=qkey, n_q=n_q, keys=keys, scale=scale, out_ap=out_ap, okey=okey))

    def attn_flush(jobs, s_ps, o_ps, pT, sbt, rec, cnt, bias_sb=None):
        units = []
        for ji, J in enumerate(jobs):
            J["ob"] = (cnt[0] + ji) % 2
            for i in range(len(J["keys"])):
                units.append((J, i))
        cnt[0] += len(jobs)
        groups = []
        for k, (J, i) in enumerate(units):
            bsp = J["keys"][i][4]
            if bsp is not None and (not groups or groups[-1][0] != bsp[0]):
                groups.append((bsp[0], k))
        gpos = {g[0]: n_ for n_, g in enumerate(groups)}
        issued = [0]

        def issue_groups(upto):
            while issued[0] < min(upto, len(groups)):
                (bi_, i0, ln), _ = groups[issued[0]]
                bb = bias_sb[bi_ % 3]
                P.dma("sp", bb[:, 0:ln, :], bias_d[i0:i0 + ln].rearrange("t p q -> p t q"), w=[("bias", bi_ % 3)])
                issued[0] += 1

        def emit_S(k):
            J, i = units[k]
            k_ap, kkey = J["keys"][i][0], J["keys"][i][1]
            sbk = (cnt[1] + k) % 2
            sp = s_ps[sbk]
            n_q = J["n_q"]
            q_ap = J["q_ap"]
            P.op("pe", lambda e: e.matmul(sp[:, :n_q], k_ap, q_ap, start=True, stop=True), r=[kkey, J["qkey"]], w=[("sps", sbk)])

        def emit_post_pv(k):
            J, i = units[k]
            _, _, v_ap, vkey, b_ap, bkey = J["keys"][i]
            if b_ap is not None:
                (gid, jj) = b_ap
                issue_groups(gpos[gid] + 3)
                b_ap = bias_sb[gid[0] % 3][:, jj, :]
                bkey = ("bias", gid[0] % 3)
            sbk = (cnt[1] + k) % 2
            sp, p_ = s_ps[sbk], pT[sbk]
            n_q, scale, nk, ob = J["n_q"], J["scale"], len(J["keys"]), J["ob"]
            ops_ = o_ps[ob]
            if b_ap is not None:
                sb_ = sbt[sbk]
                P.op("dve", lambda e: e.scalar_tensor_tensor(out=sb_[:, :n_q], in0=sp[:, :n_q], scalar=float(scale), in1=b_ap,
                                                             op0=ALU.mult, op1=ALU.add),
                     r=[("sps", sbk), bkey], w=[("sbt", sbk)])
                P.op("act", lambda e: e.activation(out=p_[:, :n_q], in_=sb_[:, :n_q], func=AF.Exp), r=[("sbt", sbk)], w=[("pT", sbk)])
            else:
                P.op("act", lambda e: e.activation(out=p_[:, :n_q], in_=sp[:, :n_q], func=AF.Exp, scale=float(scale)),
                     r=[("sps", sbk)], w=[("pT", sbk)])
            P.op("pe", lambda e: e.matmul(ops_[:, :n_q], v_ap, p_[:, :n_q], start=(i == 0), stop=(i == nk - 1)),
                 r=[vkey, ("pT", sbk)], w=[("ops", ob)])
            if i == nk - 1:
                out_ap = J["out_ap"]
                P.op("dve", lambda e: e.reciprocal(out=rec[64:128, :n_q], in_=ops_[64:128, :n_q]), r=[("ops", ob)], w=["rec"])
                P.op("dve", lambda e: e.tensor_tensor(out=out_ap, in0=ops_[0:64, :n_q], in1=rec[64:128, :n_q], op=ALU.mult),
                     r=[("ops", ob), "rec"], w=[J["okey"]])

        issue_groups(2)
        emit_S(0)
        for k in range(len(units)):
            if k + 1 < len(units):
                emit_S(k + 1)
            emit_post_pv(k)
        cnt[1] += len(units)

    eps_t = sb(ES, "eps_t", (128, 1))
    with ExitStack() as es:
        wa = [sb(es, f"wa{i}", (128, 8, 1024)) for i in range(2)]
        scT = sb(es, "scT", (128, 8, 3))
        sgm = sb(es, "sgm", (128, 8, 3))
        mod = sb(es, "mod", (128, 2, 48, 3))
        bT = sb(es, "bT", (128, 2, 48))
        gTt = sb(es, "gTt", (128, 2, 4, 8))
        permf = sb(es, "permf", (128, 128))
        aps = pst(es, "aps", (128, 8, 3))
        P.op("dve", lambda e: e.memset(ones_bf[:, :], 1.0), w=["ones"])
        P.op("dve", lambda e: e.memset(bones_bf[:, :], 0.0), w=["bones"])
        P.op("dve", lambda e: e.memset(bones_bf[0:64, 0:64], 1.0), r=[], w=["bones"])
        P.op("dve", lambda e: e.memset(bones_bf[64:128, 64:128], 1.0), r=[], w=["bones"])
        P.op("dve", lambda e: e.memset(eps_t[:, :], EPS), w=["eps"])
        P.dma("sp", permf[:, :], perm_d, w=["permf"])
        P.op("dve", lambda e: e.tensor_copy(out=perm_bf[:, :], in_=permf[:, :]), r=["permf"], w=["perm"])
        P.dma("sp", scT[:, :, :], cT_d, w=["scT"])
        P.dma("sp", qkg[:, 0:1], qg_d, w=["qg"])
        P.dma("sp", qkg[:, 1:2], kg_d, w=["kg"])
        P.dma("sp", cwT[:, 0], cw_d[0], w=["cw0"])
        P.dma("sp", cwT[:, 1], cw_d[1], w=["cw1"])
        P.dma("sp", cbT[:, 0], cb_d[0], w=["cb0"])
        P.dma("sp", cbT[:, 1], cb_d[1], w=["cb1"])
        for l in range(2):
            P.dma("sp", bT[:, l, :], badaT_d[l], w=[("bT", l)])
            for i in range(4):
                P.dma("sp", gTt[:, l, i, :], gT_d[l, i], w=[("gT", l, i)])
        P.op("act", lambda e: e.activation(out=sgm[:, :, :], in_=scT[:, :, :], func=AF.Sigmoid), r=["scT"], w=["sgm"])
        P.op("dve", lambda e: e.tensor_tensor(out=scT[:, :, :], in0=scT[:, :, :], in1=sgm[:, :, :], op=ALU.mult), r=["scT", "sgm"], w=["scT"])
        it = 0
        for l in range(2):
            for grp in range(6):
                wbuf = wa[it % 2]
                P.dma("sp", wbuf[:, :, :], wada_d[l][:, :, grp * 1024:(grp + 1) * 1024], w=[("wa", it % 2)])
                for j in range(8):
                    for kc in range(8):
                        P.op("pe", lambda e, wbuf=wbuf, j=j, kc=kc: e.matmul(aps[:, j, :], wbuf[:, kc, j * 128:(j + 1) * 128], scT[:, kc, :],
                                                                               start=(kc == 0), stop=(kc == 7)),
                             r=[("wa", it % 2), "scT"], w=[("aps", j)])
                for v in range(3):
                    P.op("dve", lambda e, l=l, grp=grp, v=v: e.tensor_tensor(out=mod[:, l, grp * 8:(grp + 1) * 8, v], in0=aps[:, :, v],
                                                                              in1=bT[:, l, grp * 8:(grp + 1) * 8], op=ALU.add),
                         r=[("aps", j) for j in range(8)] + [("bT", l)], w=[("mod", l, grp, v)])
                it += 1
        for l in range(2):
            for mf in range(2):
                for v in range(3):
                    ish, isc, igt = 3 * mf, 3 * mf + 1, 3 * mf + 2
                    P.op("dve", lambda e, l=l, mf=mf, v=v, isc=isc: e.scalar_tensor_tensor(
                        out=modA[:, l, mf, v, :], in0=mod[:, l, isc * 8:(isc + 1) * 8, v], scalar=1.0, in1=gTt[:, l, 2 * mf, :],
                        op0=ALU.add, op1=ALU.mult), r=[("mod", l, isc, v), ("gT", l, 2 * mf)], w=[("modA", l, mf, v)])
                    P.op("dve", lambda e, l=l, mf=mf, v=v, ish=ish: e.tensor_copy(out=modB[:, l, mf, v, :], in_=mod[:, l, ish * 8:(ish + 1) * 8, v]),
                         r=[("mod", l, ish, v)], w=[("modB", l, mf, v)])
                    P.op("dve", lambda e, l=l, mf=mf, v=v, igt=igt: e.tensor_tensor(
                        out=modG[:, l, mf, v, :], in0=mod[:, l, igt * 8:(igt + 1) * 8, v], in1=gTt[:, l, 2 * mf + 1, :], op=ALU.mult),
                        r=[("mod", l, igt, v), ("gT", l, 2 * mf + 1)], w=[("modG", l, mf, v)])
        P.flush()

    def mixer_even(b, l, upto="c", halves=(0, 1)):
        with ExitStack() as es0:
            ybT = sb(es0, "ybT", (128, 4, NT), BF16)
            for hh in halves:
                with ExitStack() as es1:
                    kT = sb(es1, "kT", (128, 2, NT), BF16)
                    vaug = sb(es1, "vaug", (128, 18, 4, 128), BF16)
                    with ExitStack() as es:
                        wk = sb(es, "wk", (128, 8, 256), BF16)
                        wv = sb(es, "wv", (128, 8, 256), BF16)
                        ss_ps = pst(es, "ss_ps", (128, 512))
                        pp = [pst(es, f"pp{i}", (128, 512)) for i in range(2)]
                        P.dma("pool", wk[:, :, :], win_d[:, :, 1536 + hh * 256:1536 + (hh + 1) * 256], w=["wk"])
                        P.dma("pool", wv[:, :, :], win_d[:, :, 2048 + hh * 256:2048 + (hh + 1) * 256], w=["wv"])
                        P.op("dve", lambda e: e.memset(vaug[:, :, :, :], 1.0), w=[("vaug", t) for t in range(18)])
                        pi = 0
                        for tile in range(5):
                            n = tile_norm(tile, l, 0, b, ss_ps, "ss")
                            t0 = tile * 512
                            for c in range(2):
                                ps = pp[pi % 2]; pk = ("pp", pi % 2); pi += 1
                                fm_proj(ps, pk, lambda kc, c=c: wk[:, kc, c * 128:(c + 1) * 128], ["wk"],
                                        lambda kc: xt[:, kc, :n], [("xt", kc) for kc in range(8)], n)
                                P.op("act", lambda e, ps=ps, c=c: e.activation(out=kT[:, c, t0:t0 + n], in_=ps[:, :n], func=AF.Copy),
                                     r=[pk], w=[("kT", tile, c)])
                            for s_ in range(n // 128):
                                ps = pp[pi % 2]; pk = ("pp", pi % 2); pi += 1
                                ti = tile * 4 + s_
                                for kc in range(8):
                                    P.op("pe", lambda e, ps=ps, kc=kc, s_=s_: e.matmul(ps[:, 0:256], xt[:, kc, s_ * 128:(s_ + 1) * 128], wv[:, kc, :],
                                                                                         start=(kc == 0), stop=(kc == 7)),
                                         r=[("xt", kc), "wv"], w=[pk])
                                for hq in range(4):
                                    P.op("dve", lambda e, ps=ps, ti=ti, hq=hq: e.tensor_copy(out=vaug[:, ti, hq, 0:64], in_=ps[:, hq * 64:(hq + 1) * 64]),
                                         r=[pk], w=[("vaug", ti)])
                        P.flush()
                    if upto == "a":
                        continue
                    with ExitStack() as es:
                        wq = sb(es, "wq", (128, 8, 256), BF16)
                        qT = sb(es, "qT", (128, 2, 512), BF16)
                        bias_sb = [sb(es, f"bias{i}", (128, 4, 512)) for i in range(3)]
                        pT = [sb(es, f"pT{i}", (128, 512), BF16) for i in range(2)]
                        sbt = [sb(es, f"sbt{i}", (128, 512)) for i in range(2)]
                        rec = sb(es, "rec", (128, 512))
                        ss_ps = pst(es, "ss_ps", (128, 512))
                        pp = [pst(es, f"pp{i}", (128, 512)) for i in range(2)]
                        s_ps = [pst(es, f"s_ps{i}", (128, 512)) for i in range(2)]
                        o_ps = [pst(es, f"o_ps{i}", (128, 512)) for i in range(2)]
                        P.dma("pool", wq[:, :, :], win_d[:, :, 1024 + hh * 256:1024 + (hh + 1) * 256], w=["wq"])
                        acnt = [0, 0]
                        pi = 0
                        bi = 0
                        for tile in range(5):
                            n = tile_norm(tile, l, 0, b, ss_ps, "ss")
                            t0 = tile * 512
                            for c in range(2):
                                ps = pp[pi % 2]; pk = ("pp", pi % 2); pi += 1
                                fm_proj(ps, pk, lambda kc, c=c: wq[:, kc, c * 128:(c + 1) * 128], ["wq"],
                                        lambda kc: xt[:, kc, :n], [("xt", kc) for kc in range(8)], n)
                                P.op("act", lambda e, ps=ps, c=c: e.activation(out=qT[:, c, :n], in_=ps[:, :n], func=AF.Copy), r=[pk], w=[("qT", c)])
                            tagc = []
                            for hq in range(4):
                                h = 4 * hh + hq
                                c, hp = hq // 2, (hq % 2) * 64
                                keys = []
                                if tile < 4:
                                    kts = na_plan[tile]
                                    for j0 in range(0, len(kts), 4):
                                        grp = kts[j0:j0 + 4]
                                        i0 = bias_index[(tile, h, j0)]
                                        gid = (bi, i0, len(grp)); bi += 1
                                        for jj, t in enumerate(grp):
                                            keys.append((kT[hp:hp + 64, c, t * 128:(t + 1) * 128], ("kT", t // 4, c), vaug[:, t, hq, :], ("vaug", t),
                                                         (gid, jj), None))
                                for t in (16, 17):
                                    keys.append((kT[hp:hp + 64, c, t * 128:(t + 1) * 128], ("kT", 4, c), vaug[:, t, hq, :], ("vaug", t), None, None))
                                attention(qT[hp:hp + 64, c, :n], ("qT", c), n, keys, 0.125, s_ps, o_ps, pT, sbt, rec,
                                          ybT[hp:hp + 64, 2 * hh + c, t0:t0 + n], ("ybT", tile, 2 * hh + c, hp), hp, tagc)
                            attn_flush(tagc, s_ps, o_ps, pT, sbt, rec, acnt, bias_sb)
                        P.flush()
            if upto in ("a", "b"):
                return
            with ExitStack() as es:
                wu = sb(es, "wu", (128, 8, 512), BF16)
                wva = sb(es, "wva", (128, 8, 512), BF16)
                wo = sb(es, "wo", (128, 8, D), BF16)
                wsT = sb(es, "wsT", (128, 4, 128), BF16)
                bsb = sb(es, "bsb", (128, 512))
                gvb = sb(es, "gvb", (128, 512))
                uT = sb(es, "uT", (128, 4, 512), BF16)
                vg = sb(es, "vg", (128, 512))
                vn = sb(es, "vn", (128, 512))
                vln = sb(es, "vln", (128, 512), BF16)
                tt = sb(es, "tt", (128, 4, 128))
                yaT = sb(es, "yaT", (128, 4, 512), BF16)
                yo = sb(es, "yo", (128, 8, 512))
                stats = sb(es, "stats", (128, 6))
                mv = sb(es, "mv", (128, 2))
                ss_ps = pst(es, "ss_ps", (128, 512))
                pp = [pst(es, f"pp{i}", (128, 512)) for i in range(2)]
                g_ps = pst(es, "g_ps", (128, 4, 128))
                P.dma("pool", wu[:, :, :], win_d[:, :, 0:512], w=["wu"])
                P.dma("pool", wva[:, :, :], win_d[:, :, 512:1024], w=["wva"])
                P.dma("pool", wo[:, :, :], woab_d, w=["wo"])
                P.dma("pool", wsT[:, :, :], wsT_d, w=["wsT"])
                P.dma("sp", bsb[:, :], bs_d, w=["bsb"])
                P.dma("sp", gvb[:, :], gv_d, w=["gvb"])
                pi = 0
                for tile in range(5):
                    n = tile_norm(tile, l, 0, b, ss_ps, "ss")
                    t0 = tile * 512
                    for c in range(4):
                        ps = pp[pi % 2]; pk = ("pp", pi % 2); pi += 1
                        fm_proj(ps, pk, lambda kc, c=c: wu[:, kc, c * 128:(c + 1) * 128], ["wu"],
                                lambda kc: xt[:, kc, :n], [("xt", kc) for kc in range(8)], n)
                        P.op("act", lambda e, ps=ps, c=c: e.activation(out=uT[:, c, :n], in_=ps[:, :n], func=AF.Gelu), r=[pk], w=[("uT", c)])
                    if upto == "c_u":
                        P.flush(); return
                    for s_ in range(n // 128):
                        ps = pp[pi % 2]; pk = ("pp", pi % 2); pi += 1
                        for kc in range(8):
                            P.op("pe", lambda e, ps=ps, kc=kc, s_=s_: e.matmul(ps[:, :], xt[:, kc, s_ * 128:(s_ + 1) * 128], wva[:, kc, :],
                                                                                 start=(kc == 0), stop=(kc == 7)),
                                 r=[("xt", kc), "wva"], w=[pk])
                        P.op("act", lambda e, ps=ps: e.activation(out=vg[:, :], in_=ps[:, :], func=AF.Gelu), r=[pk], w=["vg"])
                        P.op("dve", lambda e: e.bn_stats(out=stats[:, :], in_=vg[:, :]), r=["vg"], w=["stats"])
                        P.op("dve", lambda e: e.bn_aggr(out=mv[:, :], in_=stats[:, :]), r=["stats"], w=["mv"])
                        P.op("act", lambda e: e.activation(out=mv[:, 1:2], in_=mv[:, 1:2], func=AF.Sqrt, scale=1.0, bias=eps_t[:, 0:1]), r=["mv"], w=["mv"])
                        P.op("dve", lambda e: e.reciprocal(out=mv[:, 1:2], in_=mv[:, 1:2]), r=["mv"], w=["mv"])
                        P.op("dve", lambda e: e.tensor_scalar(out=vn[:, :], in0=vg[:, :], scalar1=mv[:, 0:1], scalar2=mv[:, 1:2],
                                                              op0=ALU.subtract, op1=ALU.mult), r=["vg", "mv"], w=["vn"])
                        P.op("dve", lambda e: e.tensor_tensor(out=vln[:, :], in0=vn[:, :], in1=gvb[:, :], op=ALU.mult), r=["vn", "gvb"], w=["vln"])
                        if upto == "c_va":
                            P.flush(); return
                        for g in range(4):
                            P.op("pe", lambda e, g=g: e.matmul(g_ps[:, g, :], vln[:, g * 128:(g + 1) * 128], wsT[:, g, :], start=True, stop=True),
                                 r=["vln", "wsT"], w=["g_ps"])
                        P.op("dve", lambda e: e.tensor_tensor(out=tt[:, :, :], in0=g_ps[:, :, :], in1=bsb[:, :].rearrange("p (g i) -> p g i", g=4), op=ALU.add),
                             r=["g_ps", "bsb"], w=["tt"])
                        P.op("dve", lambda e, s_=s_: e.tensor_tensor(out=yaT[:, :, s_ * 128:(s_ + 1) * 128], in0=tt[:, :, :],
                                                                     in1=uT[:, :, s_ * 128:(s_ + 1) * 128], op=ALU.mult),
                             r=["tt"] + [("uT", c) for c in range(4)], w=[("yaT", s_)])
                        if upto == "c_gate":
                            P.flush(); return

                    def get_ps(dc, tile=tile, n=n, t0=t0):
                        nonlocal pi
                        ps = pp[pi % 2]; pk = ("pp", pi % 2); pi += 1
                        rk = [("yaT", s_) for s_ in range(n // 128)]
                        for kc in range(8):
                            rhs = yaT[:, kc, :n] if kc < 4 else ybT[:, kc - 4, t0:t0 + n]
                            rr = rk if kc < 4 else [("ybT", tile, kc - 4, 0), ("ybT", tile, kc - 4, 64)]
                            P.op("pe", lambda e, ps=ps, kc=kc, rhs=rhs, dc=dc: e.matmul(ps[:, :n], wo[:, kc, dc * 128:(dc + 1) * 128], rhs,
                                                                                         start=(kc == 0), stop=(kc == 7)),
                                 r=["wo"] + rr, w=[pk])
                        return ps[:, :n], pk
                    post_norm_residual(tile, l, 0, b, get_ps, ss_ps, "ss", yo)
                P.flush()

    def qk_process(ps, pkey, n, gcol, rope, t0, dst, dkey, bps, sw_ps, cs, qn, qg, bkey="bps", skey="sw_ps"):
        P.op("act", lambda e: e.activation(out=sq[0][:, :n], in_=ps, func=AF.Square), r=[pkey], w=[("sq", 0)])
        P.op("pe", lambda e: e.matmul(bps[:, :n], bones_bf[:, :], sq[0][:, :n], start=True, stop=True), r=[("sq", 0), "bones"], w=[bkey])
        P.op("act", lambda e: e.activation(out=rstd[:, :n], in_=bps[:, :n], func=AF.Sqrt, scale=1.0 / 64, bias=eps_t[:, 0:1]), r=[bkey], w=["rstd"])
        P.op("dve", lambda e: e.reciprocal(out=rstd[:, :n], in_=rstd[:, :n]), r=["rstd"], w=["rstd"])
        P.op("dve", lambda e: e.tensor_tensor(out=qn[:, :n], in0=ps, in1=rstd[:, :n], op=ALU.mult), r=[pkey, "rstd"], w=["qn"])
        if not rope:
            P.op("act", lambda e: e.activation(out=dst, in_=qn[:, :n], func=AF.Identity, scale=qkg[:, gcol:gcol + 1]), r=["qn"], w=[dkey])
            return
        P.op("act", lambda e: e.activation(out=qg[:, :n], in_=qn[:, :n], func=AF.Identity, scale=qkg[:, gcol:gcol + 1]), r=["qn"], w=["qg"])
        P.op("pe", lambda e: e.matmul(sw_ps[:, :n], perm_bf[:, :], qg[:, :n], start=True, stop=True), r=["qg", "perm"], w=[skey])
        P.op("dve", lambda e: e.tensor_tensor(out=tmp[0][:, :n], in0=qg[:, :n], in1=cs[:, 0, :n], op=ALU.mult), r=["qg", "cs"], w=[("tmp", 0)])
        P.op("dve", lambda e: e.tensor_tensor(out=tmp[1][:, :n], in0=sw_ps[:, :n], in1=cs[:, 1, :n], op=ALU.mult), r=[skey, "cs"], w=[("tmp", 1)])
        P.op("dve", lambda e: e.tensor_tensor(out=dst, in0=tmp[0][:, :n], in1=tmp[1][:, :n], op=ALU.add), r=[("tmp", 0), ("tmp", 1)], w=[dkey])

    def mixer_odd(b, l):
        with ExitStack() as es0:
            kT = sb(es0, "kT", (128, 2, NT), BF16)
            vaug = sb(es0, "vaug", (128, 18, 4, 128), BF16)
            cs = sb(es0, "cs", (128, 2, 512))
            qn = sb(es0, "qn", (128, 512))
            qg = sb(es0, "qg", (128, 512), BF16)
            with ExitStack() as es:
                wk = sb(es, "wk", (128, 8, 256), BF16)
                wv = sb(es, "wv", (128, 8, 256), BF16)
                ss_ps = pst(es, "ss_ps", (128, 512))
                pp = [pst(es, f"pp{i}", (128, 512)) for i in range(2)]
                bps = pst(es, "bps", (128, 512))
                sw_ps = pst(es, "sw_ps", (128, 512))
                P.dma("pool", wk[:, :, :], wk_d, w=["wk"])
                P.dma("pool", wv[:, :, :], wv_d, w=["wv"])
                P.op("dve", lambda e: e.memset(vaug[:, :, :, :], 1.0), w=[("vaug", t) for t in range(18)])
                pi = 0
                for tile in range(5):
                    n = tile_norm(tile, l, 0, b, ss_ps, "ss")
                    t0 = tile * 512
                    if tile < 4:
                        P.dma("sp", cs[:, 0, :], cos_d[:, t0:t0 + 512], w=["cs"], slot="cs0")
                        P.dma("sp", cs[:, 1, :], sin_d[:, t0:t0 + 512], w=["cs"], slot="cs1")
                    for c in range(2):
                        ps = pp[pi % 2]; pk = ("pp", pi % 2); pi += 1
                        fm_proj(ps, pk, lambda kc, c=c: wk[:, kc, c * 128:(c + 1) * 128], ["wk"],
                                lambda kc: xt[:, kc, :n], [("xt", kc) for kc in range(8)], n)
                        qk_process(ps[:, :n], pk, n, 1, tile < 4, t0, kT[:, c, t0:t0 + n], ("kT", tile, c), bps, sw_ps, cs, qn, qg)
                    for s_ in range(n // 128):
                        ps = pp[pi % 2]; pk = ("pp", pi % 2); pi += 1
                        ti = tile * 4 + s_
                        for kc in range(8):
                            P.op("pe", lambda e, ps=ps, kc=kc, s_=s_: e.matmul(ps[:, 0:256], xt[:, kc, s_ * 128:(s_ + 1) * 128], wv[:, kc, :],
                                                                                 start=(kc == 0), stop=(kc == 7)),
                                 r=[("xt", kc), "wv"], w=[pk])
                        for g in range(4):
                            P.op("dve", lambda e, ps=ps, ti=ti, g=g: e.tensor_copy(out=vaug[:, ti, g, 0:64], in_=ps[:, g * 64:(g + 1) * 64]),
                                 r=[pk], w=[("vaug", ti)])
                P.flush()
            with ExitStack() as es:
                wq = sb(es, "wq", (128, 8, D), BF16)
                wo = sb(es, "wo", (128, 8, D), BF16)
                qT = sb(es, "qT", (128, 8, 512), BF16)
                yT = sb(es, "yT", (128, 8, 512), BF16)
                yo = sb(es, "yo", (128, 8, 512))
                pT = [sb(es, f"pT{i}", (128, 512), BF16) for i in range(2)]
                rec = sb(es, "rec", (128, 512))
                ss_ps = pst(es, "ss_ps", (128, 512))
                pp = [pst(es, f"pp{i}", (128, 512)) for i in range(2)]
                xps = pst(es, "xps", (128, 512))
                s_ps = [pst(es, f"s_ps{i}", (128, 512)) for i in range(2)]
                o_ps = [pst(es, f"o_ps{i}", (128, 512)) for i in range(2)]
                P.dma("pool", wq[:, :, :], wq_d, w=["wq"])
                P.dma("pool", wo[:, :, :], woc_d, w=["wo"])
                acnt = [0, 0]
                pi = 0
                for tile in range(4):
                    n = tile_norm(tile, l, 0, b, ss_ps, "ss")
                    t0 = tile * 512
                    tagc = []
                    P.dma("sp", cs[:, 0, :], cos_d[:, t0:t0 + 512], w=["cs"], slot="cs0")
                    P.dma("sp", cs[:, 1, :], sin_d[:, t0:t0 + 512], w=["cs"], slot="cs1")
                    for c in range(8):
                        ps = pp[pi % 2]; pk = ("pp", pi % 2); pi += 1
                        fm_proj(ps, pk, lambda kc, c=c: wq[:, kc, c * 128:(c + 1) * 128], ["wq"],
                                lambda kc: xt[:, kc, :n], [("xt", kc) for kc in range(8)], n)
                        qk_process(ps[:, :n], pk, n, 0, True, t0, qT[:, c, :n], ("qT", c), xps, xps, cs, qn, qg, bkey="xps", skey="xps")
                    for c in range(8):
                        for s2 in range(2):
                            hp = s2 * 64
                            g = 2 * (c // 4) + s2
                            kc_, = [g // 2]
                            keys = []
                            for t in range(18):
                                keys.append((kT[hp:hp + 64, kc_, t * 128:(t + 1) * 128], ("kT", t // 4, kc_), vaug[:, t, g, :], ("vaug", t), None, None))
                            attention(qT[hp:hp + 64, c, :n], ("qT", c), n, keys, 0.125, s_ps, o_ps, pT, None, rec,
                                      yT[hp:hp + 64, c, :n], ("yT", c, hp), hp, tagc)
                    attn_flush(tagc, s_ps, o_ps, pT, None, rec, acnt)

                    def get_ps(dc, n=n):
                        nonlocal pi
                        ps = pp[pi % 2]; pk = ("pp", pi % 2); pi += 1
                        for kc in range(8):
                            P.op("pe", lambda e, ps=ps, kc=kc, dc=dc: e.matmul(ps[:, :n], wo[:, kc, dc * 128:(dc + 1) * 128], yT[:, kc, :n],
                                                                               start=(kc == 0), stop=(kc == 7)),
                                 r=["wo", ("yT", kc, 0), ("yT", kc, 64)], w=[pk])
                        return ps[:, :n], pk
                    post_norm_residual(tile, l, 0, b, get_ps, ss_ps, "ss", yo)
                P.flush()

    def ffn(b, l, with_ctx, upto=None):
        passes = [(0, 0, 1024), (0, 1024, 1024)] + ([(1, 0, 256)] if with_ctx else [])
        for (isctx, s0, T) in passes:
            with ExitStack() as es:
                xf = sb(es, "xf", (128, 8, 1026), BF16)
                actT = sb(es, "actT", (128, NCH, 1024), BF16)
                ca = sb(es, "ca", (128, 1024))
                cg = sb(es, "cg", (128, 1024))
                wub = [sb(es, f"wub{i}", (128, 2, 8, 128), BF16) for i in range(3)]
                wdb = [sb(es, f"wdb{i}", (128, NCH, 128), BF16) for i in range(2)]
                yo = sb(es, "yo", (128, 8, 512))
                ss_ps = pst(es, "ss_ps", (128, 512))
                a_ps = pst(es, "a_ps", (128, 1536))
                g_ps = pst(es, "g_ps", (128, 1536))
                pp = [pst(es, "ppd", (128, 512))]
                v = 2 if isctx else b
                A_, B_ = modA[:, l, 1, v, :], modB[:, l, 1, v, :]
                Sq = LC if isctx else S

                def hap(kc, a, n):
                    return hcT[:, kc, a:a + n] if isctx else hT[:, kc, a:a + n]

                def hk(a):
                    return 4 if isctx else a // 512
                segs = []
                if s0 > 0:
                    P.op("dve", lambda e: e.tensor_copy(out=xf[:, :, 0:1], in_=xhalo[:, :, 0:1]), r=["xhalo"], w=[("xf", "l")])
                else:
                    P.op("dve", lambda e: e.memset(xf[:, :, 0:1], 0.0), w=[("xf", "l")])
                for a in range(s0, s0 + T, 512):
                    nn = min(512, s0 + T - a)
                    segs.append((a, nn, 1 + a - s0))
                if s0 + T < Sq:
                    segs.append((s0 + T, 1, 1 + T))
                else:
                    P.op("dve", lambda e: e.memset(xf[:, :, 1 + T:2 + T], 0.0), w=[("xf", "r")])
                xkeys = []
                for (a, nn, col) in segs:
                    key = ("xf", a)
                    xkeys.append(key)
                    norm_mod(lambda kc, a=a, nn=nn: hap(kc, a, nn), [hkey(hk(a), kc) for kc in range(8)], nn, A_, B_,
                             lambda kc, col=col, nn=nn: xf[:, kc, col:col + nn], [key] * 8, ss_ps, "ss")
                xall = xkeys + [("xf", "l"), ("xf", "r")]
                if s0 + T < Sq:
                    P.op("dve", lambda e: e.tensor_copy(out=xhalo[:, :, 0:1], in_=xf[:, :, T:T + 1]), r=xkeys, w=["xhalo"])
                if upto == "f_norm":
                    P.flush(); return
                for j in range(NCH):
                    wb_ = wub[j % 3]
                    wkey = ("wub", j % 3)
                    P.dma("pool", wb_[:, 0, :, :], wup_d[l, j], w=[wkey], slot=("wub", j % 3, 0))
                    P.dma("pool", wb_[:, 1, :, :], wup_d[l, NCH + j], w=[wkey], slot=("wub", j % 3, 1))
                    for part, (ps_, pkey, cdst, ckey) in enumerate(((a_ps, "a_ps", ca, "ca"), (g_ps, "g_ps", cg, "cg"))):
                        ch = part * NCH + j
                        for c0 in range(0, T + 2, 512):
                            c1 = min(c0 + 512, T + 2)
                            for kc in range(8):
                                P.op("pe", lambda e, ps_=ps_, kc=kc, c0=c0, c1=c1, part=part, wb_=wb_: e.matmul(
                                    ps_[:, c0:c1], wb_[:, part, kc, :], xf[:, kc, c0:c1], start=(kc == 0), stop=(kc == 7)),
                                    r=[wkey] + xall, w=[pkey])
                        w0, w1, w2 = (cwT[:, l, ch, i:i + 1] for i in range(3))
                        P.op("act", lambda e, ps_=ps_, cdst=cdst, w1=w1, ch=ch: e.activation(out=cdst[:, :T], in_=ps_[:, 1:T + 1], func=AF.Identity,
                                                                                               scale=w1, bias=cbT[:, l, ch:ch + 1]),
                             r=[pkey], w=[ckey])
                        P.op("dve", lambda e, ps_=ps_, cdst=cdst, w0=w0: e.scalar_tensor_tensor(out=cdst[:, 0:T], in0=ps_[:, 0:T], scalar=w0,
                                                                                                  in1=cdst[:, 0:T], op0=ALU.mult, op1=ALU.add),
                             r=[pkey, ckey], w=[ckey])
                        P.op("dve", lambda e, ps_=ps_, cdst=cdst, w2=w2: e.scalar_tensor_tensor(out=cdst[:, 0:T], in0=ps_[:, 2:T + 2], scalar=w2,
                                                                                                  in1=cdst[:, 0:T], op0=ALU.mult, op1=ALU.add),
                             r=[pkey, ckey], w=[ckey])
                    P.op("act", lambda e: e.activation(out=cg[:, :T], in_=cg[:, :T], func=AF.Silu), r=["cg"], w=["cg"])
                    P.op("dve", lambda e, j=j: e.tensor_tensor(out=actT[:, j, :T], in0=cg[:, :T], in1=ca[:, :T], op=ALU.mult),
                         r=["cg", "ca"], w=[("actT", j)])
                    if upto == "f_up1":
                        P.flush(); return
                if upto == "f_up":
                    P.flush(); return
                pi = 0
                di = 0
                for a in range(0, T, 512):
                    nn = min(512, T - a)
                    tile = 4 if isctx else (s0 + a) // 512

                    def get_ps(dc, a=a, nn=nn):
                        nonlocal pi, di
                        wd_ = wdb[di % 2]; wdk = ("wdb", di % 2); di += 1
                        P.dma("pool", wd_[:, 0:11, :], wdn_d[l, dc][:, 0:11, :], w=[wdk], slot=("wdb", (di - 1) % 2, 0))
                        P.dma("pool", wd_[:, 11:22, :], wdn_d[l, dc][:, 11:22, :], w=[wdk], slot=("wdb", (di - 1) % 2, 1))
                        ps = pp[0]; pk = ("pp", 0); pi += 1
                        for kc in range(NCH):
                            P.op("pe", lambda e, ps=ps, kc=kc, wd_=wd_: e.matmul(ps[:, :nn], wd_[:, kc, :], actT[:, kc, a:a + nn],
                                                                                 start=(kc == 0), stop=(kc == NCH - 1)),
                                 r=[wdk, ("actT", kc)], w=[pk])
                        return ps[:, :nn], pk
                    post_norm_residual(tile, l, 1, b, get_ps, ss_ps, "ss", yo)
                P.flush()

    for b in range(nb):
        for kc in range(8):
            P.dma("sp", hT[:, kc, :], xT_d[b, :, kc, :], w=[hkey(t, kc) for t in range(4)], slot=("ld", kc))
        P.dma("sp", hcT[:, :, :], ctxT_d[b], w=[hkey(4, kc) for kc in range(8)], slot="ldc")
        P.flush()
        if "mix0a" in stages:
            mixer_even(b, 0, upto="a", halves=(0,))
        if "mix0b" in stages:
            mixer_even(b, 0, upto="b", halves=(0,))
        for st_ in ("c_u", "c_va", "c_gate"):
            if st_ in stages:
                mixer_even(b, 0, upto=st_, halves=())
        if "mix0ab" in stages:
            mixer_even(b, 0, upto="b")
        if "mix0c" in stages:
            mixer_even(b, 0, halves=())
        if "mix0" in stages:
            mixer_even(b, 0)
        for st_ in ("f_norm", "f_up1", "f_up"):
            if st_ in stages:
                ffn(b, 0, with_ctx=True, upto=st_)
        if "ffn0" in stages:
            ffn(b, 0, with_ctx=True)
        if "mix1" in stages:
            mixer_odd(b, 1)
        if "ffn1" in stages:
            ffn(b, 1, with_ctx=False)
        if dbg:
            P.dma("sp", hc_out[b], hcT[:, :, :], r=[hkey(4, kc) for kc in range(8)], slot="sthc")
        for kc in range(8):
            P.dma("sp", out_d[b, :, kc, :], hT[:, kc, :], r=[hkey(t, kc) for t in range(4)], slot=("st", kc))
        P.flush()
    ES.close()
    return nc


def _prep_shared(inp):
    f = lambda a: np.ascontiguousarray(np.asarray(a, dtype=np.float32))
    sh = {}
    w_ada = f(inp["w_ada"])
    sh["w_ada"] = np.ascontiguousarray(w_ada.reshape(2, 8, 128, 6 * D).transpose(0, 2, 1, 3))
    sh["b_adaT"] = np.ascontiguousarray(f(inp["b_ada"]).reshape(2, 48, 128).transpose(0, 2, 1))
    sh["gT"] = np.ascontiguousarray(f(inp["norm_g"]).reshape(2, 4, 8, 128).transpose(0, 1, 3, 2))
    sh["w_in"] = _lay_w(f(inp["w_in_ab"])[0])
    sh["w_out_ab"] = _lay_w(f(inp["w_out_ab"])[0])
    wqkv = f(inp["w_qkv_c"])[0]
    qcols = []
    for c in range(8):
        for s in range(2):
            h = HMAP[c][s]
            qcols.extend(range(h * 64, (h + 1) * 64))
    sh["w_q"] = _lay_w(wqkv[:, qcols])
    sh["w_k"] = _lay_w(wqkv[:, 1024:1280])
    sh["w_v"] = _lay_w(wqkv[:, 1280:1536])
    sh["w_out_c"] = _lay_w(f(inp["w_out_c"])[0][qcols, :])
    sh["w_sT"] = np.ascontiguousarray(f(inp["a_w_s"])[0].transpose(2, 0, 1))
    sh["b_s_bc"] = np.ascontiguousarray(np.broadcast_to(f(inp["a_b_s"])[0].reshape(1, 512), (128, 512)))
    sh["gv_bc"] = np.ascontiguousarray(np.broadcast_to(f(inp["a_v_g"])[0].reshape(1, 512), (128, 512)))
    bias, index, plan = _build_na_bias(f(inp["b_rpb"])[0])
    sh["bias_na"] = bias
    sh["qg2"] = np.ascontiguousarray(np.tile(f(inp["c_q_g"])[0], 2).reshape(128, 1))
    sh["kg2"] = np.ascontiguousarray(np.tile(f(inp["c_k_g"])[0], 2).reshape(128, 1))
    wup = f(inp["w_up"])
    sh["w_up"] = np.ascontiguousarray(wup.reshape(2, 8, 128, 2 * NCH, 128).transpose(0, 3, 2, 1, 4))
    sh["conv_wT"] = np.ascontiguousarray(f(inp["conv_w"]).reshape(2, 3, 2 * NCH, 128).transpose(0, 3, 2, 1))
    sh["conv_bT"] = np.ascontiguousarray(f(inp["conv_b"]).reshape(2, 2 * NCH, 128).transpose(0, 2, 1))
    wdn = f(inp["w_down"])
    sh["w_down"] = np.ascontiguousarray(wdn.reshape(2, NCH, 128, 8, 128).transpose(0, 3, 2, 1, 4))
    cosT, sinT, perm = _rope_tables()
    sh["cosT"], sh["sinT"], sh["perm"] = cosT, sinT, perm
    return sh, index, plan


def _lay_act(a):
    nb_, L, _ = a.shape
    return np.ascontiguousarray(a.transpose(0, 2, 1).reshape(nb_, 8, 128, L).transpose(0, 2, 1, 3))


def kernel(**inp):
    x = np.asarray(inp["x"], np.float32)
    c = np.asarray(inp["c"], np.float32)
    ctx = np.asarray(inp["ctx"], np.float32)
    c_ctx = np.asarray(inp["c_ctx"], np.float32)
    sh, index, plan = _prep_shared(inp)
    n_cores = 8
    nb = x.shape[0] // n_cores
    nc = build_program(sh["bias_na"].shape[0], index, plan, nb=nb)
    in_maps = []
    for i in range(n_cores):
        m = dict(sh)
        sl = slice(i * nb, (i + 1) * nb)
        m["xT"] = _lay_act(x[sl])
        m["ctxT"] = _lay_act(ctx[sl])
        cv = np.stack([c[i * nb], c[i * nb + 1], c_ctx], -1)
        m["cT"] = np.ascontiguousarray(cv.reshape(8, 128, 3).transpose(1, 0, 2))
        in_maps.append(m)
    res = run_bass_kernel_spmd(nc, in_maps, core_ids=list(range(n_cores)))
    out = np.empty_like(x)
    for i in range(n_cores):
        o = res.results[i]["outT"]
        out[i * nb:(i + 1) * nb] = o.transpose(0, 3, 2, 1).reshape(nb, S, D)
    return out
```

```python
import numpy as np
from contextlib import ExitStack
import concourse.bass as bass
import concourse.mybir as mybir
from concourse.bass_utils import run_bass_kernel_spmd

F32 = mybir.dt.float32
BF16 = mybir.dt.bfloat16
AF = mybir.ActivationFunctionType
ALU = mybir.AluOpType

D = 1024
S = 2048
LC = 256
NT = S + LC
DFF = 2816
EPS = 1e-6
NCH = 22


import types


def _freeze(fn, depth=0):
    if not isinstance(fn, types.FunctionType) or fn.__closure__ is None or depth > 4:
        return fn
    cells = []
    for c in fn.__closure__:
        try:
            v = c.cell_contents
        except ValueError:
            cells.append(c)
            continue
        if isinstance(v, types.FunctionType):
            v = _freeze(v, depth + 1)
        cells.append(types.CellType(v))
    g = types.FunctionType(fn.__code__, fn.__globals__, fn.__name__, fn.__defaults__, tuple(cells))
    g.__kwdefaults__ = fn.__kwdefaults__
    return g


class Prog:
    def __init__(self, nc):
        self.nc = nc
        self.eng = {"pe": nc.tensor, "act": nc.scalar, "dve": nc.vector, "pool": nc.gpsimd, "sp": nc.sync}
        self.sem = {e: nc.alloc_semaphore(name=f"sem_{e}") for e in self.eng}
        self.cnt = {e: 0 for e in self.eng}
        self.dsem = {}
        self.waited = {e: {} for e in self.eng}
        self.ops = []
        self.last_w = {}
        self.readers = {}
        self.last_of_eng = {}
        self.last_dma = {}

    def op(self, eng, fn, r=(), w=(), dma=None):
        idx = len(self.ops)
        deps = {}
        for k in r:
            d = self.last_w.get(k)
            if d is not None:
                deps[d] = "raw"
        for k in w:
            d = self.last_w.get(k)
            if d is not None:
                deps[d] = "raw"
            lastr = {}
            for rd in self.readers.get(k, ()):
                o_ = self.ops[rd]
                if o_["dma"] is not None:
                    lastr[("d", rd)] = rd
                else:
                    lastr[o_["eng"]] = rd
            for rd in lastr.values():
                if rd not in deps:
                    deps[rd] = "war"
        deps.pop(idx, None)
        self.ops.append(dict(eng=eng, fn=_freeze(fn), deps=deps, dma=dma, sig=None))
        for k in r:
            self.readers.setdefault(k, []).append(idx)
        for k in w:
            self.last_w[k] = idx
            self.readers[k] = []
        if dma is None:
            self.last_of_eng[eng] = idx
        else:
            self.last_dma[dma] = idx
        return idx

    def dma(self, eng, out, in_, r=(), w=(), slot=None):
        if slot is None:
            slot = w[0] if w else "out"
        return self.op(eng, lambda e: e.dma_start(out=out, in_=in_), r=r, w=w, dma=slot)

    def flush(self):
        alld = {}
        for e, i in self.last_of_eng.items():
            alld[i] = "raw"
        for s, i in self.last_dma.items():
            alld[i] = "raw"
        for e in self.eng:
            self.ops.append(dict(eng=e, fn=None, deps=dict(alld), dma=None, sig=None, barrier=True))
        ops = self.ops
        need = [False] * len(ops)
        for i, o in enumerate(ops):
            fd = []
            for d, kind in o["deps"].items():
                od = ops[d]
                if od["dma"] is None and od["eng"] == o["eng"] and not o.get("barrier"):
                    if o["eng"] == "pe" or kind == "war":
                        continue
                if od["dma"] is None and od["eng"] == o["eng"] and o.get("barrier"):
                    continue
                fd.append(d)
                if od["dma"] is None:
                    need[d] = True
            o["fd"] = fd
        for i, o in enumerate(ops):
            E = o["eng"]
            eo = self.eng[E]
            waits = {}
            for d in o["fd"]:
                key, val = ops[d]["sig"]
                if waits.get(key, 0) < val:
                    waits[key] = val
            for key, val in waits.items():
                if self.waited[E].get(key, 0) < val:
                    h = self.sem[key[1]] if key[0] == "e" else self.dsem[key[1]][0]
                    eo.wait_ge(h, val)
                    self.waited[E][key] = val
            if o["fn"] is None:
                continue
            ins = o["fn"](eo)
            if o["dma"] is not None:
                slot = o["dma"]
                if slot not in self.dsem:
                    self.dsem[slot] = [self.nc.alloc_semaphore(name=f"dma_{len(self.dsem)}"), 0]
                s = self.dsem[slot]
                s[1] += 16
                ins.then_inc(s[0], 16)
                o["sig"] = (("d", slot), s[1])
            elif need[i]:
                self.cnt[E] += 1
                ins.then_inc(self.sem[E], 1)
                o["sig"] = (("e", E), self.cnt[E])
        self.ops = []
        self.last_w = {}
        self.readers = {}
        self.last_of_eng = {}
        self.last_dma = {}


def _na_tile_plan():
    plan = []
    for A in range(4):
        rows = list(range(8 * A, 8 * A + 8))
        lo = min(max(r - 4, 0) if r - 4 <= 24 else 24 for r in rows)
        lo = min(min(max(r - 4, 0), 24) for r in rows)
        hi = max(min(max(r - 4, 0), 24) + 7 for r in rows)
        t0, t1 = lo // 2, hi // 2
        plan.append(list(range(t0, t1 + 1)))
    return plan


def _build_na_bias(rpb):
    plan = _na_tile_plan()
    NEG = np.float32(-30000.0)
    tiles = []
    index = {}
    qr_l = np.arange(8)[:, None]
    qc = np.arange(64)[None, :]
    for A in range(4):
        qr = (8 * A + qr_l) + 0 * qc
        qcc = 0 * qr_l + qc
        r0 = np.clip(qr - 4, 0, 24)
        c0 = np.clip(qcc - 8, 0, 48)
        qr_f, qc_f, r0_f, c0_f = [a.reshape(-1) for a in (qr, qcc, r0, c0)]
        for h in range(8):
            for j, t in enumerate(plan[A]):
                kr = np.repeat(np.arange(2 * t, 2 * t + 2), 64)
                kc = np.tile(np.arange(64), 2)
                valid = ((kr[:, None] >= r0_f[None, :]) & (kr[:, None] < r0_f[None, :] + 8)
                         & (kc[:, None] >= c0_f[None, :]) & (kc[:, None] < c0_f[None, :] + 16))
                dr = np.clip(kr[:, None] - qr_f[None, :] + 7, 0, 14)
                dc = np.clip(kc[:, None] - qc_f[None, :], -15, 15) + 15
                b = rpb[h][dr, dc]
                tiles.append(np.where(valid, b, NEG).astype(np.float32))
                index[(A, h, j)] = len(tiles) - 1
    return np.stack(tiles, 0), index, plan


def _rope_tables():
    t = np.arange(S)
    inv = (10000.0 ** (-np.arange(16, dtype=np.float32) / 16)).astype(np.float32)
    rows = (t // 64).astype(np.float32)[:, None] * inv
    cols = (t % 64).astype(np.float32)[:, None] * inv
    ang = np.concatenate([rows, cols], -1)
    cos = np.cos(ang).astype(np.float32).T
    sin = np.sin(ang).astype(np.float32).T
    cosT = np.tile(cos, (4, 1))
    sinT = np.concatenate([-sin, sin, -sin, sin], 0)
    perm = np.zeros((128, 128), np.float32)
    for m in range(128):
        k = m + 32 if (m % 64) < 32 else m - 32
        perm[k, m] = 1.0
    return np.ascontiguousarray(cosT), np.ascontiguousarray(sinT), perm


HMAP = [[(4 * (2 * (c // 4)) + (c % 4)), (4 * (2 * (c // 4) + 1) + (c % 4))] for c in range(8)]


def _lay_w(w):
    K, N = w.shape
    return np.ascontiguousarray(w.reshape(K // 128, 128, N).transpose(1, 0, 2))


def build_program(n_bias_tiles, bias_index, na_plan, nb=2, stages=("mix0", "ffn0", "mix1", "ffn1"), dbg=False):
    nc = bass.Bass("TRN2", target_bir_lowering=False)
    P = Prog(nc)

    def din(name, shape):
        return nc.dram_tensor(name, list(shape), F32, kind="ExternalInput").ap()

    xT_d = din("xT", (nb, 128, 8, S))
    ctxT_d = din("ctxT", (nb, 128, 8, LC))
    out_d = nc.dram_tensor("outT", [nb, 128, 8, S], F32, kind="ExternalOutput").ap()
    cT_d = din("cT", (128, 8, 3))
    wada_d = din("w_ada", (2, 128, 8, 6 * D))
    badaT_d = din("b_adaT", (2, 128, 48))
    gT_d = din("gT", (2, 4, 128, 8))
    win_d = din("w_in", (128, 8, 2560))
    woab_d = din("w_out_ab", (128, 8, D))
    wq_d = din("w_q", (128, 8, D))
    wk_d = din("w_k", (128, 8, 256))
    wv_d = din("w_v", (128, 8, 256))
    woc_d = din("w_out_c", (128, 8, D))
    wsT_d = din("w_sT", (128, 4, 128))
    bs_d = din("b_s_bc", (128, 512))
    gv_d = din("gv_bc", (128, 512))
    bias_d = din("bias_na", (n_bias_tiles, 128, 512))
    qg_d = din("qg2", (128, 1))
    kg_d = din("kg2", (128, 1))
    wup_d = din("w_up", (2, 2 * NCH, 128, 8, 128))
    cw_d = din("conv_wT", (2, 128, 2 * NCH, 3))
    cb_d = din("conv_bT", (2, 128, 2 * NCH))
    wdn_d = din("w_down", (2, 8, 128, NCH, 128))
    cos_d = din("cosT", (128, S))
    sin_d = din("sinT", (128, S))
    perm_d = din("perm", (128, 128))

    ES = ExitStack()
    hc_out = nc.dram_tensor("hcT_out", [nb, 128, 8, LC], F32, kind="ExternalOutput").ap() if dbg else None

    uid = [0]

    def sb(es, name, shape, dt=F32):
        uid[0] += 1
        return es.enter_context(nc.sbuf_tensor(f"{name}_{uid[0]}", list(shape), dt))

    def pst(es, name, shape):
        uid[0] += 1
        return es.enter_context(nc.psum_tensor(f"{name}_{uid[0]}", list(shape), F32))

    hT = sb(ES, "hT", (128, 8, S))
    hcT = sb(ES, "hcT", (128, 8, LC))
    ones_bf = sb(ES, "ones_bf", (128, 128), BF16)
    bones_bf = sb(ES, "bones_bf", (128, 128), BF16)
    perm_bf = sb(ES, "perm_bf", (128, 128), BF16)
    modA = sb(ES, "modA", (128, 2, 2, 3, 8))
    modB = sb(ES, "modB", (128, 2, 2, 3, 8))
    modG = sb(ES, "modG", (128, 2, 2, 3, 8))
    qkg = sb(ES, "qkg", (128, 2))
    cwT = sb(ES, "cwT", (128, 2, 2 * NCH, 3))
    cbT = sb(ES, "cbT", (128, 2, 2 * NCH))
    sq = [sb(ES, f"sq{i}", (128, 512), BF16) for i in range(2)]
    rstd = sb(ES, "rstd", (128, 512))
    tmp = [sb(ES, f"tmp{i}", (128, 512)) for i in range(2)]
    xt = sb(ES, "xt", (128, 8, 512), BF16)
    xhalo = sb(ES, "xhalo", (128, 8, 1), BF16)

    def hsrc(tile):
        if tile < 4:
            return (lambda kc, a=0, n=512: hT[:, kc, tile * 512 + a: tile * 512 + a + n]), 512
        return (lambda kc, a=0, n=256: hcT[:, kc, a:a + n]), 256

    def hkey(tile, kc):
        return ("h", tile, kc)

    def norm_mod(src, skeys, n, A, B, dst, dkeys, ss_ps, ss_key):
        for kc in range(8):
            s_ = sq[kc % 2]
            P.op("act", lambda e, kc=kc, s_=s_: e.activation(out=s_[:, :n], in_=src(kc), func=AF.Square),
                 r=[skeys[kc]], w=[("sq", kc % 2)])
            P.op("pe", lambda e, kc=kc, s_=s_: e.matmul(ss_ps[:, :n], ones_bf[:, :], s_[:, :n], start=(kc == 0), stop=(kc == 7)),
                 r=[("sq", kc % 2)], w=[ss_key])
        P.op("act", lambda e: e.activation(out=rstd[:, :n], in_=ss_ps[:, :n], func=AF.Sqrt, scale=1.0 / D, bias=eps_t[:, 0:1]),
             r=[ss_key], w=["rstd"])
        P.op("dve", lambda e: e.reciprocal(out=rstd[:, :n], in_=rstd[:, :n]), r=["rstd"], w=["rstd"])
        for kc in range(8):
            t_ = tmp[kc % 2]
            P.op("dve", lambda e, kc=kc, t_=t_: e.tensor_tensor(out=t_[:, :n], in0=src(kc), in1=rstd[:, :n], op=ALU.mult),
                 r=[skeys[kc], "rstd"], w=[("tmp", kc % 2)])
            P.op("act", lambda e, kc=kc, t_=t_: e.activation(out=dst(kc), in_=t_[:, :n], func=AF.Identity,
                                                              scale=A[:, kc:kc + 1], bias=B[:, kc:kc + 1]),
                 r=[("tmp", kc % 2)], w=[dkeys[kc]])

    def tile_norm(tile, l, mf, b, ss_ps, ss_key, dst=None, dkeys=None):
        src, n = hsrc(tile)
        v = b if tile < 4 else 2
        if dst is None:
            dst = lambda kc: xt[:, kc, :n]
            dkeys = [("xt", kc) for kc in range(8)]
        norm_mod(lambda kc: src(kc), [hkey(tile, kc) for kc in range(8)], n,
                 modA[:, l, mf, v, :], modB[:, l, mf, v, :], dst, dkeys, ss_ps, ss_key)
        return n

    def fm_proj(ps, pkey, w_ap, wkeys, rhs, rkeys, n, nk=8):
        for kc in range(nk):
            P.op("pe", lambda e, kc=kc: e.matmul(ps[:, :n], w_ap(kc), rhs(kc), start=(kc == 0), stop=(kc == nk - 1)),
                 r=list(wkeys) + [rkeys[kc]], w=[pkey])

    def post_norm_residual(tile, l, mf, b, get_ps, ss_ps, ss_key, yo):
        src, n = hsrc(tile)
        v = b if tile < 4 else 2
        G = modG[:, l, mf, v, :]
        for dc in range(8):
            ps, pkey = get_ps(dc)
            s_ = sq[dc % 2]
            P.op("act", lambda e, ps=ps, dc=dc: e.activation(out=yo[:, dc, :n], in_=ps, func=AF.Copy), r=[pkey], w=[("yo", dc)])
            P.op("act", lambda e, ps=ps, s_=s_: e.activation(out=s_[:, :n], in_=ps, func=AF.Square), r=[pkey], w=[("sq", dc % 2)])
            P.op("pe", lambda e, dc=dc, s_=s_: e.matmul(ss_ps[:, :n], ones_bf[:, :], s_[:, :n], start=(dc == 0), stop=(dc == 7)),
                 r=[("sq", dc % 2)], w=[ss_key])
        P.op("act", lambda e: e.activation(out=rstd[:, :n], in_=ss_ps[:, :n], func=AF.Sqrt, scale=1.0 / D, bias=eps_t[:, 0:1]),
             r=[ss_key], w=["rstd"])
        P.op("dve", lambda e: e.reciprocal(out=rstd[:, :n], in_=rstd[:, :n]), r=["rstd"], w=["rstd"])
        for dc in range(8):
            t_ = tmp[dc % 2]
            P.op("dve", lambda e, dc=dc, t_=t_: e.tensor_tensor(out=t_[:, :n], in0=yo[:, dc, :n], in1=rstd[:, :n], op=ALU.mult),
                 r=[("yo", dc), "rstd"], w=[("tmp", dc % 2)])
            P.op("dve", lambda e, dc=dc, t_=t_: e.scalar_tensor_tensor(out=src(dc), in0=t_[:, :n], scalar=G[:, dc:dc + 1], in1=src(dc),
                                                                        op0=ALU.mult, op1=ALU.add),
                 r=[("tmp", dc % 2), hkey(tile, dc)], w=[hkey(tile, dc)])

    def attention(q_ap, qkey, n_q, keys, scale, s_ps, o_ps, pT, sbt, rec, out_ap, okey, hp, tagc):
        tagc.append(dict(q_ap=q_ap, qkey=qkey, n_q=n_q, keys=keys, scale=scale, out_ap=out_ap, okey=okey))

    GRP = 2

    def attn_flush(jobs, s_ps, o_ps, pT, sbt, rec, cnt, bias_sb=None):
        units = []
        for ji, J in enumerate(jobs):
            J["ob"] = (cnt[0] + ji) % 2
            ks = J["keys"]
            i = 0
            while i < len(ks):
                g = [i]
                while len(g) < GRP and i + len(g) < len(ks) and (ks[i + len(g)][4] is None) == (ks[i][4] is None):
                    g.append(i + len(g))
                units.append((J, g))
                i += len(g)
        cnt[0] += len(jobs)
        groups = []
        for k, (J, g) in enumerate(units):
            bsp = J["keys"][g[0]][4]
            if bsp is not None and (not groups or groups[-1] != bsp[0]):
                groups.append(bsp[0])
        gpos = {g: n_ for n_, g in enumerate(groups)}
        issued = [0]

        def issue_groups(upto):
            while issued[0] < min(upto, len(groups)):
                (bi_, i0, ln) = groups[issued[0]]
                bb = bias_sb[bi_ % 3]
                P.dma("sp", bb[:, 0:ln, :], bias_d[i0:i0 + ln].rearrange("t p q -> p t q"), w=[("bias", bi_ % 3)])
                issued[0] += 1

        def emit_S(k):
            J, g = units[k]
            sbk = (cnt[1] + k) % 2
            sp = s_ps[sbk]
            n_q, q_ap = J["n_q"], J["q_ap"]
            for gi_, i in enumerate(g):
                k_ap, kkey = J["keys"][i][0], J["keys"][i][1]
                P.op("pe", lambda e: e.matmul(sp[:, gi_, :n_q], k_ap, q_ap, start=True, stop=True), r=[kkey, J["qkey"]], w=[("sps", sbk)])

        def emit_post_pv(k):
            J, g = units[k]
            ng = len(g)
            sbk = (cnt[1] + k) % 2
            sp, p_ = s_ps[sbk], pT[sbk]
            n_q, scale, nk, ob = J["n_q"], J["scale"], len(J["keys"]), J["ob"]
            ops_ = o_ps[ob]
            b0 = J["keys"][g[0]][4]
            if b0 is not None:
                (gid, jj) = b0
                issue_groups(gpos[gid] + 3)
                b_ap = bias_sb[gid[0] % 3][:, jj:jj + ng, :n_q]
                bkey = ("bias", gid[0] % 3)
                sb_ = sbt[sbk]
                P.op("dve", lambda e: e.scalar_tensor_tensor(out=sb_[:, 0:ng, :n_q], in0=sp[:, 0:ng, :n_q], scalar=float(scale), in1=b_ap,
                                                             op0=ALU.mult, op1=ALU.add),
                     r=[("sps", sbk), bkey], w=[("sbt", sbk)])
                P.op("act", lambda e: e.activation(out=p_[:, 0:ng, :n_q], in_=sb_[:, 0:ng, :n_q], func=AF.Exp), r=[("sbt", sbk)], w=[("pT", sbk)])
            else:
                P.op("act", lambda e: e.activation(out=p_[:, 0:ng, :n_q], in_=sp[:, 0:ng, :n_q], func=AF.Exp, scale=float(scale)),
                     r=[("sps", sbk)], w=[("pT", sbk)])
            for gi_, i in enumerate(g):
                v_ap, vkey = J["keys"][i][2], J["keys"][i][3]
                P.op("pe", lambda e: e.matmul(ops_[:, :n_q], v_ap, p_[:, gi_, :n_q], start=(i == 0), stop=(i == nk - 1)),
                     r=[vkey, ("pT", sbk)], w=[("ops", ob)])
            if g[-1] == nk - 1:
                out_ap = J["out_ap"]
                P.op("dve", lambda e: e.reciprocal(out=rec[64:128, :n_q], in_=ops_[64:128, :n_q]), r=[("ops", ob)], w=["rec"])
                P.op("dve", lambda e: e.tensor_tensor(out=out_ap, in0=ops_[0:64, :n_q], in1=rec[64:128, :n_q], op=ALU.mult),
                     r=[("ops", ob), "rec"], w=[J["okey"]])

        if bias_sb is not None:
            issue_groups(2)
        emit_S(0)
        for k in range(len(units)):
            if k + 1 < len(units):
                emit_S(k + 1)
            emit_post_pv(k)
        cnt[1] += len(units)

    eps_t = sb(ES, "eps_t", (128, 1))
    with ExitStack() as es:
        wa = [sb(es, f"wa{i}", (128, 8, 1024)) for i in range(2)]
        scT = sb(es, "scT", (128, 8, 3))
        sgm = sb(es, "sgm", (128, 8, 3))
        mod = sb(es, "mod", (128, 2, 48, 3))
        bT = sb(es, "bT", (128, 2, 48))
        gTt = sb(es, "gTt", (128, 2, 4, 8))
        permf = sb(es, "permf", (128, 128))
        aps = pst(es, "aps", (128, 8, 3))
        P.op("dve", lambda e: e.memset(ones_bf[:, :], 1.0), w=["ones"])
        P.op("dve", lambda e: e.memset(bones_bf[:, :], 0.0), w=["bones"])
        P.op("dve", lambda e: e.memset(bones_bf[0:64, 0:64], 1.0), r=[], w=["bones"])
        P.op("dve", lambda e: e.memset(bones_bf[64:128, 64:128], 1.0), r=[], w=["bones"])
        P.op("dve", lambda e: e.memset(eps_t[:, :], EPS), w=["eps"])
        P.dma("sp", permf[:, :], perm_d, w=["permf"])
        P.op("dve", lambda e: e.tensor_copy(out=perm_bf[:, :], in_=permf[:, :]), r=["permf"], w=["perm"])
        P.dma("sp", scT[:, :, :], cT_d, w=["scT"])
        P.dma("sp", qkg[:, 0:1], qg_d, w=["qg"])
        P.dma("sp", qkg[:, 1:2], kg_d, w=["kg"])
        P.dma("sp", cwT[:, 0], cw_d[0], w=["cw0"])
        P.dma("sp", cwT[:, 1], cw_d[1], w=["cw1"])
        P.dma("sp", cbT[:, 0], cb_d[0], w=["cb0"])
        P.dma("sp", cbT[:, 1], cb_d[1], w=["cb1"])
        for l in range(2):
            P.dma("sp", bT[:, l, :], badaT_d[l], w=[("bT", l)])
            for i in range(4):
                P.dma("sp", gTt[:, l, i, :], gT_d[l, i], w=[("gT", l, i)])
        P.op("act", lambda e: e.activation(out=sgm[:, :, :], in_=scT[:, :, :], func=AF.Sigmoid), r=["scT"], w=["sgm"])
        P.op("dve", lambda e: e.tensor_tensor(out=scT[:, :, :], in0=scT[:, :, :], in1=sgm[:, :, :], op=ALU.mult), r=["scT", "sgm"], w=["scT"])
        it = 0
        for l in range(2):
            for grp in range(6):
                wbuf = wa[it % 2]
                P.dma("sp", wbuf[:, :, :], wada_d[l][:, :, grp * 1024:(grp + 1) * 1024], w=[("wa", it % 2)])
                for j in range(8):
                    for kc in range(8):
                        P.op("pe", lambda e, wbuf=wbuf, j=j, kc=kc: e.matmul(aps[:, j, :], wbuf[:, kc, j * 128:(j + 1) * 128], scT[:, kc, :],
                                                                               start=(kc == 0), stop=(kc == 7)),
                             r=[("wa", it % 2), "scT"], w=[("aps", j)])
                for v in range(3):
                    P.op("dve", lambda e, l=l, grp=grp, v=v: e.tensor_tensor(out=mod[:, l, grp * 8:(grp + 1) * 8, v], in0=aps[:, :, v],
                                                                              in1=bT[:, l, grp * 8:(grp + 1) * 8], op=ALU.add),
                         r=[("aps", j) for j in range(8)] + [("bT", l)], w=[("mod", l, grp, v)])
                it += 1
        for l in range(2):
            for mf in range(2):
                for v in range(3):
                    ish, isc, igt = 3 * mf, 3 * mf + 1, 3 * mf + 2
                    P.op("dve", lambda e, l=l, mf=mf, v=v, isc=isc: e.scalar_tensor_tensor(
                        out=modA[:, l, mf, v, :], in0=mod[:, l, isc * 8:(isc + 1) * 8, v], scalar=1.0, in1=gTt[:, l, 2 * mf, :],
                        op0=ALU.add, op1=ALU.mult), r=[("mod", l, isc, v), ("gT", l, 2 * mf)], w=[("modA", l, mf, v)])
                    P.op("dve", lambda e, l=l, mf=mf, v=v, ish=ish: e.tensor_copy(out=modB[:, l, mf, v, :], in_=mod[:, l, ish * 8:(ish + 1) * 8, v]),
                         r=[("mod", l, ish, v)], w=[("modB", l, mf, v)])
                    P.op("dve", lambda e, l=l, mf=mf, v=v, igt=igt: e.tensor_tensor(
                        out=modG[:, l, mf, v, :], in0=mod[:, l, igt * 8:(igt + 1) * 8, v], in1=gTt[:, l, 2 * mf + 1, :], op=ALU.mult),
                        r=[("mod", l, igt, v), ("gT", l, 2 * mf + 1)], w=[("modG", l, mf, v)])
        P.flush()

    def mixer_even(b, l, upto="c", halves=(0, 1)):
        with ExitStack() as es0:
            ybT = sb(es0, "ybT", (128, 4, NT), BF16)
            for hh in halves:
                with ExitStack() as es1:
                    kT = sb(es1, "kT", (128, 2, NT), BF16)
                    vaug = sb(es1, "vaug", (128, 18, 4, 128), BF16)
                    with ExitStack() as es:
                        wk = sb(es, "wk", (128, 8, 256), BF16)
                        wv = sb(es, "wv", (128, 8, 256), BF16)
                        ss_ps = pst(es, "ss_ps", (128, 512))
                        pp = [pst(es, f"pp{i}", (128, 512)) for i in range(2)]
                        P.dma("pool", wk[:, :, :], win_d[:, :, 1536 + hh * 256:1536 + (hh + 1) * 256], w=["wk"])
                        P.dma("pool", wv[:, :, :], win_d[:, :, 2048 + hh * 256:2048 + (hh + 1) * 256], w=["wv"])
                        P.op("dve", lambda e: e.memset(vaug[:, :, :, :], 1.0), w=[("vaug", t) for t in range(18)])
                        pi = 0
                        for tile in range(5):
                            n = tile_norm(tile, l, 0, b, ss_ps, "ss")
                            t0 = tile * 512
                            for c in range(2):
                                ps = pp[pi % 2]; pk = ("pp", pi % 2); pi += 1
                                fm_proj(ps, pk, lambda kc, c=c: wk[:, kc, c * 128:(c + 1) * 128], ["wk"],
                                        lambda kc: xt[:, kc, :n], [("xt", kc) for kc in range(8)], n)
                                P.op("act", lambda e, ps=ps, c=c: e.activation(out=kT[:, c, t0:t0 + n], in_=ps[:, :n], func=AF.Copy),
                                     r=[pk], w=[("kT", tile, c)])
                            for s_ in range(n // 128):
                                ps = pp[pi % 2]; pk = ("pp", pi % 2); pi += 1
                                ti = tile * 4 + s_
                                for kc in range(8):
                                    P.op("pe", lambda e, ps=ps, kc=kc, s_=s_: e.matmul(ps[:, 0:256], xt[:, kc, s_ * 128:(s_ + 1) * 128], wv[:, kc, :],
                                                                                         start=(kc == 0), stop=(kc == 7)),
                                         r=[("xt", kc), "wv"], w=[pk])
                                for hq in range(4):
                                    P.op("dve", lambda e, ps=ps, ti=ti, hq=hq: e.tensor_copy(out=vaug[:, ti, hq, 0:64], in_=ps[:, hq * 64:(hq + 1) * 64]),
                                         r=[pk], w=[("vaug", ti)])
                        P.flush()
                    if upto == "a":
                        continue
                    with ExitStack() as es:
                        wq = sb(es, "wq", (128, 8, 256), BF16)
                        qT = sb(es, "qT", (128, 2, 512), BF16)
                        bias_sb = [sb(es, f"bias{i}", (128, 4, 512)) for i in range(3)]
                        pT = [sb(es, f"pT{i}", (128, GRP, 512), BF16) for i in range(2)]
                        sbt = [sb(es, f"sbt{i}", (128, GRP, 512)) for i in range(2)]
                        rec = sb(es, "rec", (128, 512))
                        ss_ps = pst(es, "ss_ps", (128, 512))
                        pp = [pst(es, "pp0", (128, 512))] * 2
                        s_ps = [pst(es, f"s_ps{i}", (128, GRP, 512)) for i in range(2)]
                        o_ps = [pst(es, f"o_ps{i}", (128, 512)) for i in range(2)]
                        P.dma("pool", wq[:, :, :], win_d[:, :, 1024 + hh * 256:1024 + (hh + 1) * 256], w=["wq"])
                        acnt = [0, 0]
                        pi = 0
                        bi = 0
                        for tile in range(5):
                            n = tile_norm(tile, l, 0, b, ss_ps, "ss")
                            t0 = tile * 512
                            for c in range(2):
                                ps = pp[0]; pk = ("pp", 0); pi += 1
                                fm_proj(ps, pk, lambda kc, c=c: wq[:, kc, c * 128:(c + 1) * 128], ["wq"],
                                        lambda kc: xt[:, kc, :n], [("xt", kc) for kc in range(8)], n)
                                P.op("act", lambda e, ps=ps, c=c: e.activation(out=qT[:, c, :n], in_=ps[:, :n], func=AF.Copy), r=[pk], w=[("qT", c)])
                            tagc = []
                            for hq in range(4):
                                h = 4 * hh + hq
                                c, hp = hq // 2, (hq % 2) * 64
                                keys = []
                                if tile < 4:
                                    kts = na_plan[tile]
                                    for j0 in range(0, len(kts), 4):
                                        grp = kts[j0:j0 + 4]
                                        i0 = bias_index[(tile, h, j0)]
                                        gid = (bi, i0, len(grp)); bi += 1
                                        for jj, t in enumerate(grp):
                                            keys.append((kT[hp:hp + 64, c, t * 128:(t + 1) * 128], ("kT", t // 4, c), vaug[:, t, hq, :], ("vaug", t),
                                                         (gid, jj), None))
                                for t in (16, 17):
                                    keys.append((kT[hp:hp + 64, c, t * 128:(t + 1) * 128], ("kT", 4, c), vaug[:, t, hq, :], ("vaug", t), None, None))
                                attention(qT[hp:hp + 64, c, :n], ("qT", c), n, keys, 0.125, s_ps, o_ps, pT, sbt, rec,
                                          ybT[hp:hp + 64, 2 * hh + c, t0:t0 + n], ("ybT", tile, 2 * hh + c, hp), hp, tagc)
                            attn_flush(tagc, s_ps, o_ps, pT, sbt, rec, acnt, bias_sb)
                        P.flush()
            if upto in ("a", "b"):
                return
            with ExitStack() as es:
                wu = sb(es, "wu", (128, 8, 512), BF16)
                wva = sb(es, "wva", (128, 8, 512), BF16)
                wo = sb(es, "wo", (128, 8, D), BF16)
                wsT = sb(es, "wsT", (128, 4, 128), BF16)
                bsb = sb(es, "bsb", (128, 512))
                gvb = sb(es, "gvb", (128, 512))
                uT = sb(es, "uT", (128, 4, 512), BF16)
                vg2 = [sb(es, f"vg{i}", (128, 512)) for i in range(2)]
                vn2 = [sb(es, f"vn{i}", (128, 512)) for i in range(2)]
                vln2 = [sb(es, f"vln{i}", (128, 512), BF16) for i in range(2)]
                tt2 = [sb(es, f"tt{i}", (128, 4, 128)) for i in range(2)]
                yaT = sb(es, "yaT", (128, 4, 512), BF16)
                yo = sb(es, "yo", (128, 8, 512))
                stats2 = [sb(es, f"stats{i}", (128, 6)) for i in range(2)]
                mv2 = [sb(es, f"mv{i}", (128, 2)) for i in range(2)]
                ss_ps = pst(es, "ss_ps", (128, 512))
                pp = [pst(es, f"pp{i}", (128, 512)) for i in range(2)]
                g_ps2 = [pst(es, f"g_ps{i}", (128, 4, 128)) for i in range(2)]
                gi = 0
                P.dma("pool", wu[:, :, :], win_d[:, :, 0:512], w=["wu"])
                P.dma("pool", wva[:, :, :], win_d[:, :, 512:1024], w=["wva"])
                P.dma("pool", wo[:, :, :], woab_d, w=["wo"])
                P.dma("pool", wsT[:, :, :], wsT_d, w=["wsT"])
                P.dma("sp", bsb[:, :], bs_d, w=["bsb"])
                P.dma("sp", gvb[:, :], gv_d, w=["gvb"])
                pi = 0
                for tile in range(5):
                    n = tile_norm(tile, l, 0, b, ss_ps, "ss")
                    t0 = tile * 512
                    for c in range(4):
                        ps = pp[pi % 2]; pk = ("pp", pi % 2); pi += 1
                        fm_proj(ps, pk, lambda kc, c=c: wu[:, kc, c * 128:(c + 1) * 128], ["wu"],
                                lambda kc: xt[:, kc, :n], [("xt", kc) for kc in range(8)], n)
                        P.op("act", lambda e, ps=ps, c=c: e.activation(out=uT[:, c, :n], in_=ps[:, :n], func=AF.Gelu), r=[pk], w=[("uT", c)])
                    if upto == "c_u":
                        P.flush(); return
                    for s_ in range(n // 128):
                        ps = pp[pi % 2]; pk = ("pp", pi % 2); pi += 1
                        for kc in range(8):
                            P.op("pe", lambda e, ps=ps, kc=kc, s_=s_: e.matmul(ps[:, :], xt[:, kc, s_ * 128:(s_ + 1) * 128], wva[:, kc, :],
                                                                                 start=(kc == 0), stop=(kc == 7)),
                                 r=[("xt", kc), "wva"], w=[pk])
                        gp = gi % 2; gi += 1
                        vg, vn, vln, tt, stats, mv, g_ps = vg2[gp], vn2[gp], vln2[gp], tt2[gp], stats2[gp], mv2[gp], g_ps2[gp]
                        kv, kn, kl, kt, kst, kmv, kg = ("vg", gp), ("vn", gp), ("vln", gp), ("tt", gp), ("stats", gp), ("mv", gp), ("g_ps", gp)
                        P.op("act", lambda e, ps=ps: e.activation(out=vg[:, :], in_=ps[:, :], func=AF.Gelu), r=[pk], w=[kv])
                        P.op("dve", lambda e: e.bn_stats(out=stats[:, :], in_=vg[:, :]), r=[kv], w=[kst])
                        P.op("dve", lambda e: e.bn_aggr(out=mv[:, :], in_=stats[:, :]), r=[kst], w=[kmv])
                        P.op("act", lambda e: e.activation(out=mv[:, 1:2], in_=mv[:, 1:2], func=AF.Sqrt, scale=1.0, bias=eps_t[:, 0:1]), r=[kmv], w=[kmv])
                        P.op("dve", lambda e: e.reciprocal(out=mv[:, 1:2], in_=mv[:, 1:2]), r=[kmv], w=[kmv])
                        P.op("dve", lambda e: e.tensor_scalar(out=vn[:, :], in0=vg[:, :], scalar1=mv[:, 0:1], scalar2=mv[:, 1:2],
                                                              op0=ALU.subtract, op1=ALU.mult), r=[kv, kmv], w=[kn])
                        P.op("dve", lambda e: e.tensor_tensor(out=vln[:, :], in0=vn[:, :], in1=gvb[:, :], op=ALU.mult), r=[kn, "gvb"], w=[kl])
                        if upto == "c_va":
                            P.flush(); return
                        for g in range(4):
                            P.op("pe", lambda e, g=g: e.matmul(g_ps[:, g, :], vln[:, g * 128:(g + 1) * 128], wsT[:, g, :], start=True, stop=True),
                                 r=[kl, "wsT"], w=[kg])
                        P.op("dve", lambda e: e.tensor_tensor(out=tt[:, :, :], in0=g_ps[:, :, :], in1=bsb[:, :].rearrange("p (g i) -> p g i", g=4), op=ALU.add),
                             r=[kg, "bsb"], w=[kt])
                        P.op("dve", lambda e, s_=s_: e.tensor_tensor(out=yaT[:, :, s_ * 128:(s_ + 1) * 128], in0=tt[:, :, :],
                                                                     in1=uT[:, :, s_ * 128:(s_ + 1) * 128], op=ALU.mult),
                             r=[kt] + [("uT", c) for c in range(4)], w=[("yaT", s_)])
                        if upto == "c_gate":
                            P.flush(); return

                    def get_ps(dc, tile=tile, n=n, t0=t0):
                        nonlocal pi
                        ps = pp[pi % 2]; pk = ("pp", pi % 2); pi += 1
                        rk = [("yaT", s_) for s_ in range(n // 128)]
                        for kc in range(8):
                            rhs = yaT[:, kc, :n] if kc < 4 else ybT[:, kc - 4, t0:t0 + n]
                            rr = rk if kc < 4 else [("ybT", tile, kc - 4, 0), ("ybT", tile, kc - 4, 64)]
                            P.op("pe", lambda e, ps=ps, kc=kc, rhs=rhs, dc=dc: e.matmul(ps[:, :n], wo[:, kc, dc * 128:(dc + 1) * 128], rhs,
                                                                                         start=(kc == 0), stop=(kc == 7)),
                                 r=["wo"] + rr, w=[pk])
                        return ps[:, :n], pk
                    post_norm_residual(tile, l, 0, b, get_ps, ss_ps, "ss", yo)
                P.flush()

    def qk_process(ps, pkey, n, gcol, rope, dst, dkey, cs, R):
        (sq_, sqk), (bps, bk), (sw, swk), (rs, rsk) = R["sq"], R["bps"], R["sw"], R["rstd"]
        (qn, qnk), (qg, qgk), (t1, t1k), (t2, t2k) = R["qn"], R["qg"], R["t1"], R["t2"]
        P.op("act", lambda e: e.activation(out=sq_[:, :n], in_=ps, func=AF.Square), r=[pkey], w=[sqk])
        P.op("pe", lambda e: e.matmul(bps[:, :n], bones_bf[:, :], sq_[:, :n], start=True, stop=True), r=[sqk, "bones"], w=[bk])
        P.op("act", lambda e: e.activation(out=rs[:, :n], in_=bps[:, :n], func=AF.Sqrt, scale=1.0 / 64, bias=eps_t[:, 0:1]), r=[bk], w=[rsk])
        P.op("dve", lambda e: e.reciprocal(out=rs[:, :n], in_=rs[:, :n]), r=[rsk], w=[rsk])
        P.op("dve", lambda e: e.tensor_tensor(out=qn[:, :n], in0=ps, in1=rs[:, :n], op=ALU.mult), r=[pkey, rsk], w=[qnk])
        if not rope:
            P.op("act", lambda e: e.activation(out=dst, in_=qn[:, :n], func=AF.Identity, scale=qkg[:, gcol:gcol + 1]), r=[qnk], w=[dkey])
            return
        P.op("act", lambda e: e.activation(out=qg[:, :n], in_=qn[:, :n], func=AF.Identity, scale=qkg[:, gcol:gcol + 1]), r=[qnk], w=[qgk])
        P.op("pe", lambda e: e.matmul(sw[:, :n], perm_bf[:, :], qg[:, :n], start=True, stop=True), r=[qgk, "perm"], w=[swk])
        P.op("dve", lambda e: e.tensor_tensor(out=t1[:, :n], in0=qg[:, :n], in1=cs[:, 0, :n], op=ALU.mult), r=[qgk, "cs"], w=[t1k])
        P.op("dve", lambda e: e.tensor_tensor(out=t2[:, :n], in0=sw[:, :n], in1=cs[:, 1, :n], op=ALU.mult), r=[swk, "cs"], w=[t2k])
        P.op("dve", lambda e: e.tensor_tensor(out=dst, in0=t1[:, :n], in1=t2[:, :n], op=ALU.add), r=[t1k, t2k], w=[dkey])

    def mixer_odd(b, l):
        with ExitStack() as es0:
            kT = sb(es0, "kT", (128, 2, NT), BF16)
            vaug = sb(es0, "vaug", (128, 18, 4, 128), BF16)
            cs = sb(es0, "cs", (128, 2, 512))
            qg2 = [sb(es0, f"qg{i}", (128, 512), BF16) for i in range(2)]
            with ExitStack() as es:
                wk = sb(es, "wk", (128, 8, 256), BF16)
                wv = sb(es, "wv", (128, 8, 256), BF16)
                ss_ps = pst(es, "ss_ps", (128, 512))
                pp = [pst(es, f"pp{i}", (128, 512)) for i in range(2)]
                bps2 = [pst(es, f"bps{i}", (128, 512)) for i in range(2)]
                sw2 = [pst(es, f"sw{i}", (128, 512)) for i in range(2)]
                scr = sb(es, "scr", (128, 8, 512))
                RR = [dict(sq=(sq[i], ("sq", i)), bps=(bps2[i], ("bps", i)), sw=(sw2[i], ("sw", i)), rstd=(scr[:, 6 + i, :], ("scr", 6 + i)),
                           qn=(scr[:, i, :], ("scr", i)), qg=(qg2[i], ("qg", i)), t1=(scr[:, 2 + i, :], ("scr", 2 + i)),
                           t2=(scr[:, 4 + i, :], ("scr", 4 + i))) for i in range(2)]
                qi = 0
                P.dma("pool", wk[:, :, :], wk_d, w=["wk"])
                P.dma("pool", wv[:, :, :], wv_d, w=["wv"])
                P.op("dve", lambda e: e.memset(vaug[:, :, :, :], 1.0), w=[("vaug", t) for t in range(18)])
                pi = 0
                for tile in range(5):
                    n = tile_norm(tile, l, 0, b, ss_ps, "ss")
                    t0 = tile * 512
                    if tile < 4:
                        P.dma("sp", cs[:, 0, :], cos_d[:, t0:t0 + 512], w=["cs"], slot="cs0")
                        P.dma("sp", cs[:, 1, :], sin_d[:, t0:t0 + 512], w=["cs"], slot="cs1")
                    for c in range(2):
                        ps = pp[pi % 2]; pk = ("pp", pi % 2); pi += 1
                        fm_proj(ps, pk, lambda kc, c=c: wk[:, kc, c * 128:(c + 1) * 128], ["wk"],
                                lambda kc: xt[:, kc, :n], [("xt", kc) for kc in range(8)], n)
                        qk_process(ps[:, :n], pk, n, 1, tile < 4, kT[:, c, t0:t0 + n], ("kT", tile, c), cs, RR[qi % 2]); qi += 1
                    for s_ in range(n // 128):
                        ps = pp[pi % 2]; pk = ("pp", pi % 2); pi += 1
                        ti = tile * 4 + s_
                        for kc in range(8):
                            P.op("pe", lambda e, ps=ps, kc=kc, s_=s_: e.matmul(ps[:, 0:256], xt[:, kc, s_ * 128:(s_ + 1) * 128], wv[:, kc, :],
                                                                                 start=(kc == 0), stop=(kc == 7)),
                                 r=[("xt", kc), "wv"], w=[pk])
                        for g in range(4):
                            P.op("dve", lambda e, ps=ps, ti=ti, g=g: e.tensor_copy(out=vaug[:, ti, g, 0:64], in_=ps[:, g * 64:(g + 1) * 64]),
                                 r=[pk], w=[("vaug", ti)])
                P.flush()
            with ExitStack() as es:
                wq = sb(es, "wq", (128, 8, D), BF16)
                wo = sb(es, "wo", (128, 8, D), BF16)
                qT = sb(es, "qT", (128, 8, 512), BF16)
                yT = sb(es, "yT", (128, 8, 512), BF16)
                yo = sb(es, "yo", (128, 8, 512))
                pT = [sb(es, f"pT{i}", (128, GRP, 512), BF16) for i in range(2)]
                rec = sb(es, "rec", (128, 512))
                ss_ps = pst(es, "ss_ps", (128, 512))
                pp = [pst(es, "pp0", (128, 512))]
                s_ps = [pst(es, f"s_ps{i}", (128, GRP, 512)) for i in range(2)]
                o_ps = [pst(es, f"o_ps{i}", (128, 512)) for i in range(2)]
                RR = [dict(sq=(sq[i], ("sq", i)), bps=(s_ps[i][:, 0, :], ("sps", i)), sw=(o_ps[i], ("ops", i)), rstd=(yo[:, 6 + i, :], ("yo", 6 + i)),
                           qn=(yo[:, i, :], ("yo", i)), qg=(qg2[i], ("qg", i)), t1=(yo[:, 2 + i, :], ("yo", 2 + i)),
                           t2=(yo[:, 4 + i, :], ("yo", 4 + i))) for i in range(2)]
                qi = 0
                P.dma("pool", wq[:, :, :], wq_d, w=["wq"])
                P.dma("pool", wo[:, :, :], woc_d, w=["wo"])
                acnt = [0, 0]
                pi = 0
                for tile in range(4):
                    n = tile_norm(tile, l, 0, b, ss_ps, "ss")
                    t0 = tile * 512
                    tagc = []
                    P.dma("sp", cs[:, 0, :], cos_d[:, t0:t0 + 512], w=["cs"], slot="cs0")
                    P.dma("sp", cs[:, 1, :], sin_d[:, t0:t0 + 512], w=["cs"], slot="cs1")
                    for c in range(8):
                        ps = pp[0]; pk = ("pp", 0); pi += 1
                        fm_proj(ps, pk, lambda kc, c=c: wq[:, kc, c * 128:(c + 1) * 128], ["wq"],
                                lambda kc: xt[:, kc, :n], [("xt", kc) for kc in range(8)], n)
                        qk_process(ps[:, :n], pk, n, 0, True, qT[:, c, :n], ("qT", c), cs, RR[qi % 2]); qi += 1
                    for c in range(8):
                        for s2 in range(2):
                            hp = s2 * 64
                            g = 2 * (c // 4) + s2
                            kc_, = [g // 2]
                            keys = []
                            for t in range(18):
                                keys.append((kT[hp:hp + 64, kc_, t * 128:(t + 1) * 128], ("kT", t // 4, kc_), vaug[:, t, g, :], ("vaug", t), None, None))
                            attention(qT[hp:hp + 64, c, :n], ("qT", c), n, keys, 0.125, s_ps, o_ps, pT, None, rec,
                                      yT[hp:hp + 64, c, :n], ("yT", c, hp), hp, tagc)
                    attn_flush(tagc, s_ps, o_ps, pT, None, rec, acnt)

                    def get_ps(dc, n=n):
                        nonlocal pi
                        ps = pp[0]; pk = ("pp", 0); pi += 1
                        for kc in range(8):
                            P.op("pe", lambda e, ps=ps, kc=kc, dc=dc: e.matmul(ps[:, :n], wo[:, kc, dc * 128:(dc + 1) * 128], yT[:, kc, :n],
                                                                               start=(kc == 0), stop=(kc == 7)),
                                 r=["wo", ("yT", kc, 0), ("yT", kc, 64)], w=[pk])
                        return ps[:, :n], pk
                    post_norm_residual(tile, l, 0, b, get_ps, ss_ps, "ss", yo)
                P.flush()

    def ffn(b, l, with_ctx, upto=None):
        passes = [(0, 0, 1024), (0, 1024, 1024)] + ([(1, 0, 256)] if with_ctx else [])
        for (isctx, s0, T) in passes:
            with ExitStack() as es:
                xf = sb(es, "xf", (128, 8, 1026), BF16)
                actT = sb(es, "actT", (128, NCH, 1024), BF16)
                ca = sb(es, "ca", (128, 1024))
                cg = sb(es, "cg", (128, 1024))
                wub = [sb(es, f"wub{i}", (128, 2, 8, 128), BF16) for i in range(3)]
                wdb = [sb(es, f"wdb{i}", (128, NCH, 128), BF16) for i in range(2)]
                yo = sb(es, "yo", (128, 8, 512))
                ss_ps = pst(es, "ss_ps", (128, 512))
                a_ps = pst(es, "a_ps", (128, 1536))
                g_ps = pst(es, "g_ps", (128, 1536))
                pp = [pst(es, "ppd", (128, 512))]
                v = 2 if isctx else b
                A_, B_ = modA[:, l, 1, v, :], modB[:, l, 1, v, :]
                Sq = LC if isctx else S

                def hap(kc, a, n):
                    return hcT[:, kc, a:a + n] if isctx else hT[:, kc, a:a + n]

                def hk(a):
                    return 4 if isctx else a // 512
                segs = []
                if s0 > 0:
                    P.op("dve", lambda e: e.tensor_copy(out=xf[:, :, 0:1], in_=xhalo[:, :, 0:1]), r=["xhalo"], w=[("xf", "l")])
                else:
                    P.op("dve", lambda e: e.memset(xf[:, :, 0:1], 0.0), w=[("xf", "l")])
                for a in range(s0, s0 + T, 512):
                    nn = min(512, s0 + T - a)
                    segs.append((a, nn, 1 + a - s0))
                if s0 + T < Sq:
                    segs.append((s0 + T, 1, 1 + T))
                else:
                    P.op("dve", lambda e: e.memset(xf[:, :, 1 + T:2 + T], 0.0), w=[("xf", "r")])
                xkeys = []
                for (a, nn, col) in segs:
                    key = ("xf", a)
                    xkeys.append(key)
                    norm_mod(lambda kc, a=a, nn=nn: hap(kc, a, nn), [hkey(hk(a), kc) for kc in range(8)], nn, A_, B_,
                             lambda kc, col=col, nn=nn: xf[:, kc, col:col + nn], [key] * 8, ss_ps, "ss")
                xall = xkeys + [("xf", "l"), ("xf", "r")]
                if s0 + T < Sq:
                    P.op("dve", lambda e: e.tensor_copy(out=xhalo[:, :, 0:1], in_=xf[:, :, T:T + 1]), r=xkeys, w=["xhalo"])
                if upto == "f_norm":
                    P.flush(); return
                for j in range(NCH):
                    wb_ = wub[j % 3]
                    wkey = ("wub", j % 3)
                    P.dma("pool", wb_[:, 0, :, :], wup_d[l, j], w=[wkey], slot=("wub", j % 3, 0))
                    P.dma("pool", wb_[:, 1, :, :], wup_d[l, NCH + j], w=[wkey], slot=("wub", j % 3, 1))
                    for part, (ps_, pkey, cdst, ckey) in enumerate(((a_ps, "a_ps", ca, "ca"), (g_ps, "g_ps", cg, "cg"))):
                        ch = part * NCH + j
                        for c0 in range(0, T + 2, 512):
                            c1 = min(c0 + 512, T + 2)
                            for kc in range(8):
                                P.op("pe", lambda e, ps_=ps_, kc=kc, c0=c0, c1=c1, part=part, wb_=wb_: e.matmul(
                                    ps_[:, c0:c1], wb_[:, part, kc, :], xf[:, kc, c0:c1], start=(kc == 0), stop=(kc == 7)),
                                    r=[wkey] + xall, w=[pkey])
                        w0, w1, w2 = (cwT[:, l, ch, i:i + 1] for i in range(3))
                        P.op("act", lambda e, ps_=ps_, cdst=cdst, w1=w1, ch=ch: e.activation(out=cdst[:, :T], in_=ps_[:, 1:T + 1], func=AF.Identity,
                                                                                               scale=w1, bias=cbT[:, l, ch:ch + 1]),
                             r=[pkey], w=[ckey])
                        P.op("dve", lambda e, ps_=ps_, cdst=cdst, w0=w0: e.scalar_tensor_tensor(out=cdst[:, 0:T], in0=ps_[:, 0:T], scalar=w0,
                                                                                                  in1=cdst[:, 0:T], op0=ALU.mult, op1=ALU.add),
                             r=[pkey, ckey], w=[ckey])
                        P.op("dve", lambda e, ps_=ps_, cdst=cdst, w2=w2: e.scalar_tensor_tensor(out=cdst[:, 0:T], in0=ps_[:, 2:T + 2], scalar=w2,
                                                                                                  in1=cdst[:, 0:T], op0=ALU.mult, op1=ALU.add),
                             r=[pkey, ckey], w=[ckey])
                    P.op("act", lambda e: e.activation(out=cg[:, :T], in_=cg[:, :T], func=AF.Silu), r=["cg"], w=["cg"])
                    P.op("dve", lambda e, j=j: e.tensor_tensor(out=actT[:, j, :T], in0=cg[:, :T], in1=ca[:, :T], op=ALU.mult),
                         r=["cg", "ca"], w=[("actT", j)])
                    if upto == "f_up1":
                        P.flush(); return
                if upto == "f_up":
                    P.flush(); return
                pi = 0
                di = 0
                for a in range(0, T, 512):
                    nn = min(512, T - a)
                    tile = 4 if isctx else (s0 + a) // 512

                    def get_ps(dc, a=a, nn=nn):
                        nonlocal pi, di
                        wd_ = wdb[di % 2]; wdk = ("wdb", di % 2); di += 1
                        P.dma("pool", wd_[:, 0:11, :], wdn_d[l, dc][:, 0:11, :], w=[wdk], slot=("wdb", (di - 1) % 2, 0))
                        P.dma("pool", wd_[:, 11:22, :], wdn_d[l, dc][:, 11:22, :], w=[wdk], slot=("wdb", (di - 1) % 2, 1))
                        ps = pp[0]; pk = ("pp", 0); pi += 1
                        for kc in range(NCH):
                            P.op("pe", lambda e, ps=ps, kc=kc, wd_=wd_: e.matmul(ps[:, :nn], wd_[:, kc, :], actT[:, kc, a:a + nn],
                                                                                 start=(kc == 0), stop=(kc == NCH - 1)),
                                 r=[wdk, ("actT", kc)], w=[pk])
                        return ps[:, :nn], pk
                    post_norm_residual(tile, l, 1, b, get_ps, ss_ps, "ss", yo)
                P.flush()

    for b in range(nb):
        for kc in range(8):
            P.dma("sp", hT[:, kc, :], xT_d[b, :, kc, :], w=[hkey(t, kc) for t in range(4)], slot=("ld", kc))
        P.dma("sp", hcT[:, :, :], ctxT_d[b], w=[hkey(4, kc) for kc in range(8)], slot="ldc")
        P.flush()
        if "mix0a" in stages:
            mixer_even(b, 0, upto="a", halves=(0,))
        if "mix0b" in stages:
            mixer_even(b, 0, upto="b", halves=(0,))
        for st_ in ("c_u", "c_va", "c_gate"):
            if st_ in stages:
                mixer_even(b, 0, upto=st_, halves=())
        if "mix0ab" in stages:
            mixer_even(b, 0, upto="b")
        if "mix0c" in stages:
            mixer_even(b, 0, halves=())
        if "mix0" in stages:
            mixer_even(b, 0)
        for st_ in ("f_norm", "f_up1", "f_up"):
            if st_ in stages:
                ffn(b, 0, with_ctx=True, upto=st_)
        if "ffn0" in stages:
            ffn(b, 0, with_ctx=True)
        if "mix1" in stages:
            mixer_odd(b, 1)
        if "ffn1" in stages:
            ffn(b, 1, with_ctx=False)
        if dbg:
            P.dma("sp", hc_out[b], hcT[:, :, :], r=[hkey(4, kc) for kc in range(8)], slot="sthc")
        for kc in range(8):
            P.dma("sp", out_d[b, :, kc, :], hT[:, kc, :], r=[hkey(t, kc) for t in range(4)], slot=("st", kc))
        P.flush()
    ES.close()
    return nc


def _prep_shared(inp):
    f = lambda a: np.ascontiguousarray(np.asarray(a, dtype=np.float32))
    sh = {}
    w_ada = f(inp["w_ada"])
    sh["w_ada"] = np.ascontiguousarray(w_ada.reshape(2, 8, 128, 6 * D).transpose(0, 2, 1, 3))
    sh["b_adaT"] = np.ascontiguousarray(f(inp["b_ada"]).reshape(2, 48, 128).transpose(0, 2, 1))
    sh["gT"] = np.ascontiguousarray(f(inp["norm_g"]).reshape(2, 4, 8, 128).transpose(0, 1, 3, 2))
    sh["w_in"] = _lay_w(f(inp["w_in_ab"])[0])
    sh["w_out_ab"] = _lay_w(f(inp["w_out_ab"])[0])
    wqkv = f(inp["w_qkv_c"])[0]
    qcols = []
    for c in range(8):
        for s in range(2):
            h = HMAP[c][s]
            qcols.extend(range(h * 64, (h + 1) * 64))
    sh["w_q"] = _lay_w(wqkv[:, qcols])
    sh["w_k"] = _lay_w(wqkv[:, 1024:1280])
    sh["w_v"] = _lay_w(wqkv[:, 1280:1536])
    sh["w_out_c"] = _lay_w(f(inp["w_out_c"])[0][qcols, :])
    sh["w_sT"] = np.ascontiguousarray(f(inp["a_w_s"])[0].transpose(2, 0, 1))
    sh["b_s_bc"] = np.ascontiguousarray(np.broadcast_to(f(inp["a_b_s"])[0].reshape(1, 512), (128, 512)))
    sh["gv_bc"] = np.ascontiguousarray(np.broadcast_to(f(inp["a_v_g"])[0].reshape(1, 512), (128, 512)))
    bias, index, plan = _build_na_bias(f(inp["b_rpb"])[0])
    sh["bias_na"] = bias
    sh["qg2"] = np.ascontiguousarray(np.tile(f(inp["c_q_g"])[0], 2).reshape(128, 1))
    sh["kg2"] = np.ascontiguousarray(np.tile(f(inp["c_k_g"])[0], 2).reshape(128, 1))
    wup = f(inp["w_up"])
    sh["w_up"] = np.ascontiguousarray(wup.reshape(2, 8, 128, 2 * NCH, 128).transpose(0, 3, 2, 1, 4))
    sh["conv_wT"] = np.ascontiguousarray(f(inp["conv_w"]).reshape(2, 3, 2 * NCH, 128).transpose(0, 3, 2, 1))
    sh["conv_bT"] = np.ascontiguousarray(f(inp["conv_b"]).reshape(2, 2 * NCH, 128).transpose(0, 2, 1))
    wdn = f(inp["w_down"])
    sh["w_down"] = np.ascontiguousarray(wdn.reshape(2, NCH, 128, 8, 128).transpose(0, 3, 2, 1, 4))
    cosT, sinT, perm = _rope_tables()
    sh["cosT"], sh["sinT"], sh["perm"] = cosT, sinT, perm
    return sh, index, plan


def _lay_act(a):
    nb_, L, _ = a.shape
    return np.ascontiguousarray(a.transpose(0, 2, 1).reshape(nb_, 8, 128, L).transpose(0, 2, 1, 3))


def kernel(**inp):
    x = np.asarray(inp["x"], np.float32)
    c = np.asarray(inp["c"], np.float32)
    ctx = np.asarray(inp["ctx"], np.float32)
    c_ctx = np.asarray(inp["c_ctx"], np.float32)
    sh, index, plan = _prep_shared(inp)
    n_cores = 8
    nb = x.shape[0] // n_cores
    nc = build_program(sh["bias_na"].shape[0], index, plan, nb=nb)
    in_maps = []
    for i in range(n_cores):
        m = dict(sh)
        sl = slice(i * nb, (i + 1) * nb)
        m["xT"] = _lay_act(x[sl])
        m["ctxT"] = _lay_act(ctx[sl])
        cv = np.stack([c[i * nb], c[i * nb + 1], c_ctx], -1)
        m["cT"] = np.ascontiguousarray(cv.reshape(8, 128, 3).transpose(1, 0, 2))
        in_maps.append(m)
    res = run_bass_kernel_spmd(nc, in_maps, core_ids=list(range(n_cores)))
    out = np.empty_like(x)
    for i in range(n_cores):
        o = res.results[i]["outT"]
        out[i * nb:(i + 1) * nb] = o.transpose(0, 3, 2, 1).reshape(nb, S, D)
    return out
```

```python
import numpy as np
from contextlib import ExitStack
import concourse.bass as bass
import concourse.mybir as mybir
from concourse.bass_utils import run_bass_kernel_spmd

F32 = mybir.dt.float32
BF16 = mybir.dt.bfloat16
AF = mybir.ActivationFunctionType
ALU = mybir.AluOpType

D = 1024
S = 2048
LC = 256
NT = S + LC
DFF = 2816
EPS = 1e-6
NCH = 22


import types


def _freeze(fn, depth=0):
    if not isinstance(fn, types.FunctionType) or fn.__closure__ is None or depth > 4:
        return fn
    cells = []
    for c in fn.__closure__:
        try:
            v = c.cell_contents
        except ValueError:
            cells.append(c)
            continue
        if isinstance(v, types.FunctionType):
            v = _freeze(v, depth + 1)
        cells.append(types.CellType(v))
    g = types.FunctionType(fn.__code__, fn.__globals__, fn.__name__, fn.__defaults__, tuple(cells))
    g.__kwdefaults__ = fn.__kwdefaults__
    return g


class Prog:
    def __init__(self, nc):
        self.nc = nc
        self.eng = {"pe": nc.tensor, "act": nc.scalar, "dve": nc.vector, "pool": nc.gpsimd, "sp": nc.sync}
        self.sem = {e: nc.alloc_semaphore(name=f"sem_{e}") for e in self.eng}
        self.cnt = {e: 0 for e in self.eng}
        self.dsem = {}
        self.waited = {e: {} for e in self.eng}
        self.ops = []
        self.last_w = {}
        self.readers = {}
        self.last_of_eng = {}
        self.last_dma = {}

    def op(self, eng, fn, r=(), w=(), dma=None):
        idx = len(self.ops)
        deps = {}
        for k in r:
            d = self.last_w.get(k)
            if d is not None:
                deps[d] = "raw"
        for k in w:
            d = self.last_w.get(k)
            if d is not None:
                deps[d] = "raw"
            lastr = {}
            for rd in self.readers.get(k, ()):
                o_ = self.ops[rd]
                if o_["dma"] is not None:
                    lastr[("d", rd)] = rd
                else:
                    lastr[o_["eng"]] = rd
            for rd in lastr.values():
                if rd not in deps:
                    deps[rd] = "war"
        deps.pop(idx, None)
        self.ops.append(dict(eng=eng, fn=_freeze(fn), deps=deps, dma=dma, sig=None))
        for k in r:
            self.readers.setdefault(k, []).append(idx)
        for k in w:
            self.last_w[k] = idx
            self.readers[k] = []
        if dma is None:
            self.last_of_eng[eng] = idx
        else:
            self.last_dma[dma] = idx
        return idx

    def dma(self, eng, out, in_, r=(), w=(), slot=None):
        if slot is None:
            slot = w[0] if w else "out"
        return self.op(eng, lambda e: e.dma_start(out=out, in_=in_), r=r, w=w, dma=slot)

    def flush(self):
        alld = {}
        for e, i in self.last_of_eng.items():
            alld[i] = "raw"
        for s, i in self.last_dma.items():
            alld[i] = "raw"
        for e in self.eng:
            self.ops.append(dict(eng=e, fn=None, deps=dict(alld), dma=None, sig=None, barrier=True))
        ops = self.ops
        need = [False] * len(ops)
        for i, o in enumerate(ops):
            fd = []
            for d, kind in o["deps"].items():
                od = ops[d]
                if od["dma"] is None and od["eng"] == o["eng"] and not o.get("barrier"):
                    if o["eng"] == "pe" or kind == "war":
                        continue
                if od["dma"] is None and od["eng"] == o["eng"] and o.get("barrier"):
                    continue
                fd.append(d)
            latest = {}
            fd2 = []
            for d in fd:
                od = ops[d]
                if od["dma"] is None:
                    if d > latest.get(od["eng"], -1):
                        latest[od["eng"]] = d
                else:
                    fd2.append(d)
            fd2.extend(latest.values())
            for d in latest.values():
                need[d] = True
            o["fd"] = fd2
        for i, o in enumerate(ops):
            E = o["eng"]
            eo = self.eng[E]
            waits = {}
            for d in o["fd"]:
                key, val = ops[d]["sig"]
                if waits.get(key, 0) < val:
                    waits[key] = val
            for key, val in waits.items():
                if self.waited[E].get(key, 0) < val:
                    h = self.sem[key[1]] if key[0] == "e" else self.dsem[key[1]][0]
                    eo.wait_ge(h, val)
                    self.waited[E][key] = val
            if o["fn"] is None:
                continue
            ins = o["fn"](eo)
            if o["dma"] is not None:
                slot = o["dma"]
                if slot not in self.dsem:
                    self.dsem[slot] = [self.nc.alloc_semaphore(name=f"dma_{len(self.dsem)}"), 0]
                s = self.dsem[slot]
                s[1] += 16
                ins.then_inc(s[0], 16)
                o["sig"] = (("d", slot), s[1])
            elif need[i]:
                self.cnt[E] += 1
                ins.then_inc(self.sem[E], 1)
                o["sig"] = (("e", E), self.cnt[E])
        self.ops = []
        self.last_w = {}
        self.readers = {}
        self.last_of_eng = {}
        self.last_dma = {}


def _na_tile_plan():
    plan = []
    for A in range(4):
        rows = list(range(8 * A, 8 * A + 8))
        lo = min(max(r - 4, 0) if r - 4 <= 24 else 24 for r in rows)
        lo = min(min(max(r - 4, 0), 24) for r in rows)
        hi = max(min(max(r - 4, 0), 24) + 7 for r in rows)
        t0, t1 = lo // 2, hi // 2
        plan.append(list(range(t0, t1 + 1)))
    return plan


def _build_na_bias(rpb):
    plan = _na_tile_plan()
    NEG = np.float32(-30000.0)
    tiles = []
    index = {}
    qr_l = np.arange(8)[:, None]
    qc = np.arange(64)[None, :]
    for A in range(4):
        qr = (8 * A + qr_l) + 0 * qc
        qcc = 0 * qr_l + qc
        r0 = np.clip(qr - 4, 0, 24)
        c0 = np.clip(qcc - 8, 0, 48)
        qr_f, qc_f, r0_f, c0_f = [a.reshape(-1) for a in (qr, qcc, r0, c0)]
        for h in range(8):
            for j, t in enumerate(plan[A]):
                kr = np.repeat(np.arange(2 * t, 2 * t + 2), 64)
                kc = np.tile(np.arange(64), 2)
                valid = ((kr[:, None] >= r0_f[None, :]) & (kr[:, None] < r0_f[None, :] + 8)
                         & (kc[:, None] >= c0_f[None, :]) & (kc[:, None] < c0_f[None, :] + 16))
                dr = np.clip(kr[:, None] - qr_f[None, :] + 7, 0, 14)
                dc = np.clip(kc[:, None] - qc_f[None, :], -15, 15) + 15
                b = rpb[h][dr, dc]
                tiles.append(np.where(valid, b, NEG).astype(np.float32))
                index[(A, h, j)] = len(tiles) - 1
    return np.stack(tiles, 0), index, plan


def _rope_tables():
    t = np.arange(S)
    inv = (10000.0 ** (-np.arange(16, dtype=np.float32) / 16)).astype(np.float32)
    rows = (t // 64).astype(np.float32)[:, None] * inv
    cols = (t % 64).astype(np.float32)[:, None] * inv
    ang = np.concatenate([rows, cols], -1)
    cos = np.cos(ang).astype(np.float32).T
    sin = np.sin(ang).astype(np.float32).T
    cosT = np.tile(cos, (4, 1))
    sinT = np.concatenate([-sin, sin, -sin, sin], 0)
    perm = np.zeros((128, 128), np.float32)
    for m in range(128):
        k = m + 32 if (m % 64) < 32 else m - 32
        perm[k, m] = 1.0
    return np.ascontiguousarray(cosT), np.ascontiguousarray(sinT), perm


HMAP = [[(4 * (2 * (c // 4)) + (c % 4)), (4 * (2 * (c // 4) + 1) + (c % 4))] for c in range(8)]


def _lay_w(w):
    K, N = w.shape
    return np.ascontiguousarray(w.reshape(K // 128, 128, N).transpose(1, 0, 2))


def build_program(n_bias_tiles, bias_index, na_plan, nb=2, stages=("mix0", "ffn0", "mix1", "ffn1"), dbg=False):
    nc = bass.Bass("TRN2", target_bir_lowering=False)
    P = Prog(nc)

    def din(name, shape):
        return nc.dram_tensor(name, list(shape), F32, kind="ExternalInput").ap()

    xT_d = din("xT", (nb, 128, 8, S))
    ctxT_d = din("ctxT", (nb, 128, 8, LC))
    out_d = nc.dram_tensor("outT", [nb, 128, 8, S], F32, kind="ExternalOutput").ap()
    cT_d = din("cT", (128, 8, 3))
    wada_d = din("w_ada", (2, 128, 8, 6 * D))
    badaT_d = din("b_adaT", (2, 128, 48))
    gT_d = din("gT", (2, 4, 128, 8))
    win_d = din("w_in", (128, 8, 2560))
    woab_d = din("w_out_ab", (128, 8, D))
    wq_d = din("w_q", (128, 8, D))
    wk_d = din("w_k", (128, 8, 256))
    wv_d = din("w_v", (128, 8, 256))
    woc_d = din("w_out_c", (128, 8, D))
    wsT_d = din("w_sT", (128, 4, 128))
    bs_d = din("b_s_bc", (128, 512))
    gv_d = din("gv_bc", (128, 512))
    bias_d = din("bias_na", (n_bias_tiles, 128, 512))
    qg_d = din("qg2", (128, 1))
    kg_d = din("kg2", (128, 1))
    wup_d = din("w_up", (2, 2 * NCH, 128, 8, 128))
    cw_d = din("conv_wT", (2, 128, 2 * NCH, 3))
    cb_d = din("conv_bT", (2, 128, 2 * NCH))
    wdn_d = din("w_down", (2, 8, 128, NCH, 128))
    cos_d = din("cosT", (128, S))
    sin_d = din("sinT", (128, S))
    perm_d = din("perm", (128, 128))

    ES = ExitStack()
    hc_out = nc.dram_tensor("hcT_out", [nb, 128, 8, LC], F32, kind="ExternalOutput").ap() if dbg else None

    uid = [0]

    def sb(es, name, shape, dt=F32):
        uid[0] += 1
        return es.enter_context(nc.sbuf_tensor(f"{name}_{uid[0]}", list(shape), dt))

    def pst(es, name, shape):
        uid[0] += 1
        return es.enter_context(nc.psum_tensor(f"{name}_{uid[0]}", list(shape), F32))

    hT = sb(ES, "hT", (128, 8, S))
    hcT = sb(ES, "hcT", (128, 8, LC))
    ones_bf = sb(ES, "ones_bf", (128, 128), BF16)
    bones_bf = sb(ES, "bones_bf", (128, 128), BF16)
    perm_bf = sb(ES, "perm_bf", (128, 128), BF16)
    modA = sb(ES, "modA", (128, 2, 2, 3, 8))
    modB = sb(ES, "modB", (128, 2, 2, 3, 8))
    modG = sb(ES, "modG", (128, 2, 2, 3, 8))
    qkg = sb(ES, "qkg", (128, 2))
    cwT = sb(ES, "cwT", (128, 2, 2 * NCH, 3))
    cbT = sb(ES, "cbT", (128, 2, 2 * NCH))
    sq = [sb(ES, f"sq{i}", (128, 512), BF16) for i in range(2)]
    rstd = sb(ES, "rstd", (128, 512))
    tmp = [sb(ES, f"tmp{i}", (128, 512)) for i in range(2)]
    xhalo = sb(ES, "xhalo", (128, 8, 1), BF16)

    def hsrc(tile):
        if tile < 4:
            return (lambda kc, a=0, n=512: hT[:, kc, tile * 512 + a: tile * 512 + a + n]), 512
        return (lambda kc, a=0, n=256: hcT[:, kc, a:a + n]), 256

    def hkey(tile, kc):
        return ("h", tile, kc)

    def norm_mod(src, skeys, n, A, B, dst, dkeys, ss_ps, ss_key):
        for kc in range(8):
            s_ = sq[kc % 2]
            P.op("act", lambda e, kc=kc, s_=s_: e.activation(out=s_[:, :n], in_=src(kc), func=AF.Square),
                 r=[skeys[kc]], w=[("sq", kc % 2)])
            P.op("pe", lambda e, kc=kc, s_=s_: e.matmul(ss_ps[:, :n], ones_bf[:, :], s_[:, :n], start=(kc == 0), stop=(kc == 7)),
                 r=[("sq", kc % 2)], w=[ss_key])
        P.op("act", lambda e: e.activation(out=rstd[:, :n], in_=ss_ps[:, :n], func=AF.Sqrt, scale=1.0 / D, bias=eps_t[:, 0:1]),
             r=[ss_key], w=["rstd"])
        P.op("dve", lambda e: e.reciprocal(out=rstd[:, :n], in_=rstd[:, :n]), r=["rstd"], w=["rstd"])
        for kc in range(8):
            t_ = tmp[kc % 2]
            P.op("dve", lambda e, kc=kc, t_=t_: e.tensor_tensor(out=t_[:, :n], in0=src(kc), in1=rstd[:, :n], op=ALU.mult),
                 r=[skeys[kc], "rstd"], w=[("tmp", kc % 2)])
            P.op("act", lambda e, kc=kc, t_=t_: e.activation(out=dst(kc), in_=t_[:, :n], func=AF.Identity,
                                                              scale=A[:, kc:kc + 1], bias=B[:, kc:kc + 1]),
                 r=[("tmp", kc % 2)], w=[dkeys[kc]])

    def tile_norm(tile, l, mf, b, ss_ps, ss_key, dst=None, dkeys=None):
        src, n = hsrc(tile)
        v = b if tile < 4 else 2
        if dkeys is None:
            xt_ = dst
            dst = lambda kc: xt_[:, kc, :n]
            dkeys = [("xt", kc) for kc in range(8)]
        norm_mod(lambda kc: src(kc), [hkey(tile, kc) for kc in range(8)], n,
                 modA[:, l, mf, v, :], modB[:, l, mf, v, :], dst, dkeys, ss_ps, ss_key)
        return n

    def fm_proj(ps, pkey, w_ap, wkeys, rhs, rkeys, n, nk=8):
        for kc in range(nk):
            P.op("pe", lambda e, kc=kc: e.matmul(ps[:, :n], w_ap(kc), rhs(kc), start=(kc == 0), stop=(kc == nk - 1)),
                 r=list(wkeys) + [rkeys[kc]], w=[pkey])

    def post_norm_residual(tile, l, mf, b, get_ps, ss_ps, ss_key, yo):
        src, n = hsrc(tile)
        v = b if tile < 4 else 2
        G = modG[:, l, mf, v, :]
        for dc in range(8):
            ps, pkey = get_ps(dc)
            s_ = sq[dc % 2]
            P.op("act", lambda e, ps=ps, dc=dc: e.activation(out=yo[:, dc, :n], in_=ps, func=AF.Copy), r=[pkey], w=[("yo", dc)])
            P.op("act", lambda e, ps=ps, s_=s_: e.activation(out=s_[:, :n], in_=ps, func=AF.Square), r=[pkey], w=[("sq", dc % 2)])
            P.op("pe", lambda e, dc=dc, s_=s_: e.matmul(ss_ps[:, :n], ones_bf[:, :], s_[:, :n], start=(dc == 0), stop=(dc == 7)),
                 r=[("sq", dc % 2)], w=[ss_key])
        P.op("act", lambda e: e.activation(out=rstd[:, :n], in_=ss_ps[:, :n], func=AF.Sqrt, scale=1.0 / D, bias=eps_t[:, 0:1]),
             r=[ss_key], w=["rstd"])
        P.op("dve", lambda e: e.reciprocal(out=rstd[:, :n], in_=rstd[:, :n]), r=["rstd"], w=["rstd"])
        for dc in range(8):
            t_ = tmp[dc % 2]
            P.op("dve", lambda e, dc=dc, t_=t_: e.tensor_tensor(out=t_[:, :n], in0=yo[:, dc, :n], in1=rstd[:, :n], op=ALU.mult),
                 r=[("yo", dc), "rstd"], w=[("tmp", dc % 2)])
            P.op("dve", lambda e, dc=dc, t_=t_: e.scalar_tensor_tensor(out=src(dc), in0=t_[:, :n], scalar=G[:, dc:dc + 1], in1=src(dc),
                                                                        op0=ALU.mult, op1=ALU.add),
                 r=[("tmp", dc % 2), hkey(tile, dc)], w=[hkey(tile, dc)])

    def attention(q_ap, qkey, n_q, keys, scale, s_ps, o_ps, pT, sbt, rec, out_ap, okey, hp, tagc):
        tagc.append(dict(q_ap=q_ap, qkey=qkey, n_q=n_q, keys=keys, scale=scale, out_ap=out_ap, okey=okey))

    GRP = 2

    def attn_flush(jobs, s_ps, o_ps, pT, sbt, rec, cnt, bias_sb=None):
        units = []
        gsz = s_ps[0].shape[1]
        for ji, J in enumerate(jobs):
            J["ob"] = (cnt[0] + ji) % 2
            ks = J["keys"]
            i = 0
            while i < len(ks):
                g = [i]
                while len(g) < gsz and i + len(g) < len(ks) and (ks[i + len(g)][4] is None) == (ks[i][4] is None):
                    g.append(i + len(g))
                units.append((J, g))
                i += len(g)
        cnt[0] += len(jobs)
        groups = []
        for k, (J, g) in enumerate(units):
            bsp = J["keys"][g[0]][4]
            if bsp is not None and (not groups or groups[-1] != bsp[0]):
                groups.append(bsp[0])
        gpos = {g: n_ for n_, g in enumerate(groups)}
        issued = [0]

        def issue_groups(upto):
            while issued[0] < min(upto, len(groups)):
                (bi_, i0, ln) = groups[issued[0]]
                bb = bias_sb[bi_ % 3]
                P.dma("sp", bb[:, 0:ln, :], bias_d[i0:i0 + ln].rearrange("t p q -> p t q"), w=[("bias", bi_ % 3)])
                issued[0] += 1

        def emit_S(k):
            J, g = units[k]
            sbk = (cnt[1] + k) % len(s_ps)
            sp = s_ps[sbk]
            n_q, q_ap = J["n_q"], J["q_ap"]
            for gi_, i in enumerate(g):
                k_ap, kkey = J["keys"][i][0], J["keys"][i][1]
                P.op("pe", lambda e: e.matmul(sp[:, gi_, :n_q], k_ap, q_ap, start=True, stop=True), r=[kkey, J["qkey"]], w=[("sps", sbk)])

        def emit_post_pv(k):
            J, g = units[k]
            ng = len(g)
            sbk = (cnt[1] + k) % len(s_ps)
            sp, p_ = s_ps[sbk], pT[sbk]
            n_q, scale, nk, ob = J["n_q"], J["scale"], len(J["keys"]), J["ob"]
            ops_ = o_ps[ob]
            b0 = J["keys"][g[0]][4]
            if b0 is not None:
                (gid, jj) = b0
                issue_groups(gpos[gid] + 3)
                b_ap = bias_sb[gid[0] % 3][:, jj:jj + ng, :n_q]
                bkey = ("bias", gid[0] % 3)
                sb_ = sbt[sbk]
                P.op("dve", lambda e: e.scalar_tensor_tensor(out=sb_[:, 0:ng, :n_q], in0=sp[:, 0:ng, :n_q], scalar=float(scale), in1=b_ap,
                                                             op0=ALU.mult, op1=ALU.add),
                     r=[("sps", sbk), bkey], w=[("sbt", sbk)])
                P.op("act", lambda e: e.activation(out=p_[:, 0:ng, :n_q], in_=sb_[:, 0:ng, :n_q], func=AF.Exp), r=[("sbt", sbk)], w=[("pT", sbk)])
            else:
                P.op("act", lambda e: e.activation(out=p_[:, 0:ng, :n_q], in_=sp[:, 0:ng, :n_q], func=AF.Exp, scale=float(scale)),
                     r=[("sps", sbk)], w=[("pT", sbk)])
            for gi_, i in enumerate(g):
                v_ap, vkey = J["keys"][i][2], J["keys"][i][3]
                P.op("pe", lambda e: e.matmul(ops_[:, :n_q], v_ap, p_[:, gi_, :n_q], start=(i == 0), stop=(i == nk - 1)),
                     r=[vkey, ("pT", sbk)], w=[("ops", ob)])
            if g[-1] == nk - 1:
                out_ap = J["out_ap"]
                P.op("dve", lambda e: e.reciprocal(out=rec[64:128, :n_q], in_=ops_[64:128, :n_q]), r=[("ops", ob)], w=["rec"])
                P.op("dve", lambda e: e.tensor_tensor(out=out_ap, in0=ops_[0:64, :n_q], in1=rec[64:128, :n_q], op=ALU.mult),
                     r=[("ops", ob), "rec"], w=[J["okey"]])

        if bias_sb is not None:
            issue_groups(2)
        look = len(s_ps) - 1
        for k in range(min(look, len(units))):
            emit_S(k)
        for k in range(len(units)):
            if k + look < len(units):
                emit_S(k + look)
            emit_post_pv(k)
        cnt[1] += len(units)

    eps_t = sb(ES, "eps_t", (128, 1))
    with ExitStack() as es:
        wa = [sb(es, f"wa{i}", (128, 8, 1024)) for i in range(2)]
        scT = sb(es, "scT", (128, 8, 3))
        sgm = sb(es, "sgm", (128, 8, 3))
        mod = sb(es, "mod", (128, 2, 48, 3))
        bT = sb(es, "bT", (128, 2, 48))
        gTt = sb(es, "gTt", (128, 2, 4, 8))
        permf = sb(es, "permf", (128, 128))
        aps = pst(es, "aps", (128, 8, 3))
        P.op("dve", lambda e: e.memset(ones_bf[:, :], 1.0), w=["ones"])
        P.op("dve", lambda e: e.memset(bones_bf[:, :], 0.0), w=["bones"])
        P.op("dve", lambda e: e.memset(bones_bf[0:64, 0:64], 1.0), r=[], w=["bones"])
        P.op("dve", lambda e: e.memset(bones_bf[64:128, 64:128], 1.0), r=[], w=["bones"])
        P.op("dve", lambda e: e.memset(eps_t[:, :], EPS), w=["eps"])
        P.dma("sp", permf[:, :], perm_d, w=["permf"])
        P.op("dve", lambda e: e.tensor_copy(out=perm_bf[:, :], in_=permf[:, :]), r=["permf"], w=["perm"])
        P.dma("sp", scT[:, :, :], cT_d, w=["scT"])
        P.dma("sp", qkg[:, 0:1], qg_d, w=["qg"])
        P.dma("sp", qkg[:, 1:2], kg_d, w=["kg"])
        P.dma("sp", cwT[:, 0], cw_d[0], w=["cw0"])
        P.dma("sp", cwT[:, 1], cw_d[1], w=["cw1"])
        P.dma("sp", cbT[:, 0], cb_d[0], w=["cb0"])
        P.dma("sp", cbT[:, 1], cb_d[1], w=["cb1"])
        for l in range(2):
            P.dma("sp", bT[:, l, :], badaT_d[l], w=[("bT", l)])
            for i in range(4):
                P.dma("sp", gTt[:, l, i, :], gT_d[l, i], w=[("gT", l, i)])
        P.op("act", lambda e: e.activation(out=sgm[:, :, :], in_=scT[:, :, :], func=AF.Sigmoid), r=["scT"], w=["sgm"])
        P.op("dve", lambda e: e.tensor_tensor(out=scT[:, :, :], in0=scT[:, :, :], in1=sgm[:, :, :], op=ALU.mult), r=["scT", "sgm"], w=["scT"])
        it = 0
        for l in range(2):
            for grp in range(6):
                wbuf = wa[it % 2]
                P.dma("sp", wbuf[:, :, :], wada_d[l][:, :, grp * 1024:(grp + 1) * 1024], w=[("wa", it % 2)])
                for j in range(8):
                    for kc in range(8):
                        P.op("pe", lambda e, wbuf=wbuf, j=j, kc=kc: e.matmul(aps[:, j, :], wbuf[:, kc, j * 128:(j + 1) * 128], scT[:, kc, :],
                                                                               start=(kc == 0), stop=(kc == 7)),
                             r=[("wa", it % 2), "scT"], w=[("aps", j)])
                for v in range(3):
                    P.op("dve", lambda e, l=l, grp=grp, v=v: e.tensor_tensor(out=mod[:, l, grp * 8:(grp + 1) * 8, v], in0=aps[:, :, v],
                                                                              in1=bT[:, l, grp * 8:(grp + 1) * 8], op=ALU.add),
                         r=[("aps", j) for j in range(8)] + [("bT", l)], w=[("mod", l, grp, v)])
                it += 1
        for l in range(2):
            for mf in range(2):
                for v in range(3):
                    ish, isc, igt = 3 * mf, 3 * mf + 1, 3 * mf + 2
                    P.op("dve", lambda e, l=l, mf=mf, v=v, isc=isc: e.scalar_tensor_tensor(
                        out=modA[:, l, mf, v, :], in0=mod[:, l, isc * 8:(isc + 1) * 8, v], scalar=1.0, in1=gTt[:, l, 2 * mf, :],
                        op0=ALU.add, op1=ALU.mult), r=[("mod", l, isc, v), ("gT", l, 2 * mf)], w=[("modA", l, mf, v)])
                    P.op("dve", lambda e, l=l, mf=mf, v=v, ish=ish: e.tensor_copy(out=modB[:, l, mf, v, :], in_=mod[:, l, ish * 8:(ish + 1) * 8, v]),
                         r=[("mod", l, ish, v)], w=[("modB", l, mf, v)])
                    P.op("dve", lambda e, l=l, mf=mf, v=v, igt=igt: e.tensor_tensor(
                        out=modG[:, l, mf, v, :], in0=mod[:, l, igt * 8:(igt + 1) * 8, v], in1=gTt[:, l, 2 * mf + 1, :], op=ALU.mult),
                        r=[("mod", l, igt, v), ("gT", l, 2 * mf + 1)], w=[("modG", l, mf, v)])
        P.flush()

    def mixer_even(b, l, upto="c", halves=(0, 1)):
        with ExitStack() as es0:
            ybT = sb(es0, "ybT", (128, 4, NT), BF16)
            esx = ExitStack()
            xta = sb(esx, "xta", (128, 8, NT), BF16)
            if len(halves):
                with ExitStack() as es:
                    ss_ps = pst(es, "ss_ps", (128, 512))
                    for tile in range(5):
                        t0 = tile * 512
                        tile_norm(tile, l, 0, b, ss_ps, "ss", dst=(lambda kc, t0=t0, tile=tile: xta[:, kc, t0:t0 + (512 if tile < 4 else 256)]),
                                  dkeys=[("xta", tile, kc) for kc in range(8)])
                    P.flush()
            for hh in halves:
                with ExitStack() as es1:
                    kT = sb(es1, "kT", (128, 2, NT), BF16)
                    vaug = sb(es1, "vaug", (128, 18, 4, 128), BF16)
                    with ExitStack() as es:
                        wk = sb(es, "wk", (128, 8, 256), BF16)
                        wv = sb(es, "wv", (128, 8, 256), BF16)
                        ss_ps = pst(es, "ss_ps", (128, 512))
                        pp = [pst(es, f"pp{i}", (128, 512)) for i in range(2)]
                        P.dma("pool", wk[:, :, :], win_d[:, :, 1536 + hh * 256:1536 + (hh + 1) * 256], w=["wk"])
                        P.dma("pool", wv[:, :, :], win_d[:, :, 2048 + hh * 256:2048 + (hh + 1) * 256], w=["wv"])
                        P.op("dve", lambda e: e.memset(vaug[:, :, :, :], 1.0), w=[("vaug", t) for t in range(18)])
                        pi = 0
                        for tile in range(5):
                            n = 512 if tile < 4 else 256
                            t0 = tile * 512
                            for c in range(2):
                                ps = pp[pi % 2]; pk = ("pp", pi % 2); pi += 1
                                fm_proj(ps, pk, lambda kc, c=c: wk[:, kc, c * 128:(c + 1) * 128], ["wk"],
                                        lambda kc: xta[:, kc, t0:t0 + n], [("xta", tile, kc) for kc in range(8)], n)
                                P.op("act", lambda e, ps=ps, c=c: e.activation(out=kT[:, c, t0:t0 + n], in_=ps[:, :n], func=AF.Copy),
                                     r=[pk], w=[("kT", tile, c)])
                            for s_ in range(n // 128):
                                ps = pp[pi % 2]; pk = ("pp", pi % 2); pi += 1
                                ti = tile * 4 + s_
                                for kc in range(8):
                                    P.op("pe", lambda e, ps=ps, kc=kc, s_=s_: e.matmul(ps[:, 0:256], xta[:, kc, t0 + s_ * 128:t0 + (s_ + 1) * 128], wv[:, kc, :],
                                                                                         start=(kc == 0), stop=(kc == 7)),
                                         r=[("xta", tile, kc), "wv"], w=[pk])
                                for hq in range(4):
                                    P.op("dve", lambda e, ps=ps, ti=ti, hq=hq: e.tensor_copy(out=vaug[:, ti, hq, 0:64], in_=ps[:, hq * 64:(hq + 1) * 64]),
                                         r=[pk], w=[("vaug", ti)])
                        P.flush()
                    if upto == "a":
                        continue
                    with ExitStack() as es:
                        wq = sb(es, "wq", (128, 8, 256), BF16)
                        qT = sb(es, "qT", (128, 2, 512), BF16)
                        bias_sb = [sb(es, f"bias{i}", (128, 2, 512)) for i in range(3)]
                        pT = [sb(es, f"pT{i}", (128, GRP, 512), BF16) for i in range(3)]
                        sbt = [sb(es, f"sbt{i}", (128, GRP, 512)) for i in range(3)]
                        rec = sb(es, "rec", (128, 512))
                        s_ps = [pst(es, f"s_ps{i}", (128, GRP, 512)) for i in range(3)]
                        pp = [s_ps[2][:, 0, :]] * 2
                        o_ps = [pst(es, f"o_ps{i}", (128, 512)) for i in range(2)]
                        P.dma("pool", wq[:, :, :], win_d[:, :, 1024 + hh * 256:1024 + (hh + 1) * 256], w=["wq"])
                        acnt = [0, 0]
                        pi = 0
                        bi = 0
                        for tile in range(5):
                            n = 512 if tile < 4 else 256
                            t0 = tile * 512
                            for c in range(2):
                                ps = pp[0]; pk = ("sps", 2); pi += 1
                                fm_proj(ps, pk, lambda kc, c=c: wq[:, kc, c * 128:(c + 1) * 128], ["wq"],
                                        lambda kc: xta[:, kc, t0:t0 + n], [("xta", tile, kc) for kc in range(8)], n)
                                P.op("act", lambda e, ps=ps, c=c: e.activation(out=qT[:, c, :n], in_=ps[:, :n], func=AF.Copy), r=[pk], w=[("qT", c)])
                            tagc = []
                            for hq in range(4):
                                h = 4 * hh + hq
                                c, hp = hq // 2, (hq % 2) * 64
                                keys = []
                                if tile < 4:
                                    kts = na_plan[tile]
                                    for j0 in range(0, len(kts), 2):
                                        grp = kts[j0:j0 + 2]
                                        i0 = bias_index[(tile, h, j0)]
                                        gid = (bi, i0, len(grp)); bi += 1
                                        for jj, t in enumerate(grp):
                                            keys.append((kT[hp:hp + 64, c, t * 128:(t + 1) * 128], ("kT", t // 4, c), vaug[:, t, hq, :], ("vaug", t),
                                                         (gid, jj), None))
                                for t in (16, 17):
                                    keys.append((kT[hp:hp + 64, c, t * 128:(t + 1) * 128], ("kT", 4, c), vaug[:, t, hq, :], ("vaug", t), None, None))
                                attention(qT[hp:hp + 64, c, :n], ("qT", c), n, keys, 0.125, s_ps, o_ps, pT, sbt, rec,
                                          ybT[hp:hp + 64, 2 * hh + c, t0:t0 + n], ("ybT", tile, 2 * hh + c, hp), hp, tagc)
                            attn_flush(tagc, s_ps, o_ps, pT, sbt, rec, acnt, bias_sb)
                        P.flush()
            esx.close()
            if upto in ("a", "b"):
                return
            with ExitStack() as es:
                xt = sb(es, "xt", (128, 8, 512), BF16)
                wu = sb(es, "wu", (128, 8, 512), BF16)
                wva = sb(es, "wva", (128, 8, 512), BF16)
                wo = sb(es, "wo", (128, 8, D), BF16)
                wsT = sb(es, "wsT", (128, 4, 128), BF16)
                bsb = sb(es, "bsb", (128, 512))
                gvb = sb(es, "gvb", (128, 512))
                uT = sb(es, "uT", (128, 4, 512), BF16)
                vg2 = [sb(es, f"vg{i}", (128, 512)) for i in range(2)]
                vn2 = [sb(es, f"vn{i}", (128, 512)) for i in range(2)]
                vln2 = [sb(es, f"vln{i}", (128, 512), BF16) for i in range(2)]
                tt2 = [sb(es, f"tt{i}", (128, 4, 128)) for i in range(2)]
                yaT = sb(es, "yaT", (128, 4, 512), BF16)
                yo = sb(es, "yo", (128, 8, 512))
                stats2 = [sb(es, f"stats{i}", (128, 6)) for i in range(2)]
                mv2 = [sb(es, f"mv{i}", (128, 2)) for i in range(2)]
                ss_ps = pst(es, "ss_ps", (128, 512))
                pp = [pst(es, f"pp{i}", (128, 512)) for i in range(2)]
                g_ps2 = [pst(es, f"g_ps{i}", (128, 4, 128)) for i in range(2)]
                gi = 0
                P.dma("pool", wu[:, :, :], win_d[:, :, 0:512], w=["wu"])
                P.dma("pool", wva[:, :, :], win_d[:, :, 512:1024], w=["wva"])
                P.dma("pool", wo[:, :, :], woab_d, w=["wo"])
                P.dma("pool", wsT[:, :, :], wsT_d, w=["wsT"])
                P.dma("sp", bsb[:, :], bs_d, w=["bsb"])
                P.dma("sp", gvb[:, :], gv_d, w=["gvb"])
                pi = 0
                for tile in range(5):
                    n = tile_norm(tile, l, 0, b, ss_ps, "ss", dst=xt)
                    t0 = tile * 512
                    for c in range(4):
                        ps = pp[pi % 2]; pk = ("pp", pi % 2); pi += 1
                        fm_proj(ps, pk, lambda kc, c=c: wu[:, kc, c * 128:(c + 1) * 128], ["wu"],
                                lambda kc: xt[:, kc, :n], [("xt", kc) for kc in range(8)], n)
                        P.op("act", lambda e, ps=ps, c=c: e.activation(out=uT[:, c, :n], in_=ps[:, :n], func=AF.Gelu), r=[pk], w=[("uT", c)])
                    if upto == "c_u":
                        P.flush(); return
                    for s_ in range(n // 128):
                        ps = pp[pi % 2]; pk = ("pp", pi % 2); pi += 1
                        for kc in range(8):
                            P.op("pe", lambda e, ps=ps, kc=kc, s_=s_: e.matmul(ps[:, :], xt[:, kc, s_ * 128:(s_ + 1) * 128], wva[:, kc, :],
                                                                                 start=(kc == 0), stop=(kc == 7)),
                                 r=[("xt", kc), "wva"], w=[pk])
                        gp = gi % 2; gi += 1
                        vg, vn, vln, tt, stats, mv, g_ps = vg2[gp], vn2[gp], vln2[gp], tt2[gp], stats2[gp], mv2[gp], g_ps2[gp]
                        kv, kn, kl, kt, kst, kmv, kg = ("vg", gp), ("vn", gp), ("vln", gp), ("tt", gp), ("stats", gp), ("mv", gp), ("g_ps", gp)
                        P.op("act", lambda e, ps=ps: e.activation(out=vg[:, :], in_=ps[:, :], func=AF.Gelu), r=[pk], w=[kv])
                        P.op("dve", lambda e: e.bn_stats(out=stats[:, :], in_=vg[:, :]), r=[kv], w=[kst])
                        P.op("dve", lambda e: e.bn_aggr(out=mv[:, :], in_=stats[:, :]), r=[kst], w=[kmv])
                        P.op("act", lambda e: e.activation(out=mv[:, 1:2], in_=mv[:, 1:2], func=AF.Sqrt, scale=1.0, bias=eps_t[:, 0:1]), r=[kmv], w=[kmv])
                        P.op("dve", lambda e: e.reciprocal(out=mv[:, 1:2], in_=mv[:, 1:2]), r=[kmv], w=[kmv])
                        P.op("dve", lambda e: e.tensor_scalar(out=vn[:, :], in0=vg[:, :], scalar1=mv[:, 0:1], scalar2=mv[:, 1:2],
                                                              op0=ALU.subtract, op1=ALU.mult), r=[kv, kmv], w=[kn])
                        P.op("dve", lambda e: e.tensor_tensor(out=vln[:, :], in0=vn[:, :], in1=gvb[:, :], op=ALU.mult), r=[kn, "gvb"], w=[kl])
                        if upto == "c_va":
                            P.flush(); return
                        for g in range(4):
                            P.op("pe", lambda e, g=g: e.matmul(g_ps[:, g, :], vln[:, g * 128:(g + 1) * 128], wsT[:, g, :], start=True, stop=True),
                                 r=[kl, "wsT"], w=[kg])
                        P.op("dve", lambda e: e.tensor_tensor(out=tt[:, :, :], in0=g_ps[:, :, :], in1=bsb[:, :].rearrange("p (g i) -> p g i", g=4), op=ALU.add),
                             r=[kg, "bsb"], w=[kt])
                        P.op("dve", lambda e, s_=s_: e.tensor_tensor(out=yaT[:, :, s_ * 128:(s_ + 1) * 128], in0=tt[:, :, :],
                                                                     in1=uT[:, :, s_ * 128:(s_ + 1) * 128], op=ALU.mult),
                             r=[kt] + [("uT", c) for c in range(4)], w=[("yaT", s_)])
                        if upto == "c_gate":
                            P.flush(); return

                    def get_ps(dc, tile=tile, n=n, t0=t0):
                        nonlocal pi
                        ps = pp[pi % 2]; pk = ("pp", pi % 2); pi += 1
                        rk = [("yaT", s_) for s_ in range(n // 128)]
                        for kc in range(8):
                            rhs = yaT[:, kc, :n] if kc < 4 else ybT[:, kc - 4, t0:t0 + n]
                            rr = rk if kc < 4 else [("ybT", tile, kc - 4, 0), ("ybT", tile, kc - 4, 64)]
                            P.op("pe", lambda e, ps=ps, kc=kc, rhs=rhs, dc=dc: e.matmul(ps[:, :n], wo[:, kc, dc * 128:(dc + 1) * 128], rhs,
                                                                                         start=(kc == 0), stop=(kc == 7)),
                                 r=["wo"] + rr, w=[pk])
                        return ps[:, :n], pk
                    post_norm_residual(tile, l, 0, b, get_ps, ss_ps, "ss", yo)
                P.flush()

    def qk_process(ps, pkey, n, gcol, rope, dst, dkey, cs, R):
        (sq_, sqk), (bps, bk), (sw, swk), (rs, rsk) = R["sq"], R["bps"], R["sw"], R["rstd"]
        (qn, qnk), (qg, qgk), (t1, t1k), (t2, t2k) = R["qn"], R["qg"], R["t1"], R["t2"]
        P.op("act", lambda e: e.activation(out=sq_[:, :n], in_=ps, func=AF.Square), r=[pkey], w=[sqk])
        P.op("pe", lambda e: e.matmul(bps[:, :n], bones_bf[:, :], sq_[:, :n], start=True, stop=True), r=[sqk, "bones"], w=[bk])
        P.op("act", lambda e: e.activation(out=rs[:, :n], in_=bps[:, :n], func=AF.Sqrt, scale=1.0 / 64, bias=eps_t[:, 0:1]), r=[bk], w=[rsk])
        P.op("dve", lambda e: e.reciprocal(out=rs[:, :n], in_=rs[:, :n]), r=[rsk], w=[rsk])
        P.op("dve", lambda e: e.tensor_tensor(out=qn[:, :n], in0=ps, in1=rs[:, :n], op=ALU.mult), r=[pkey, rsk], w=[qnk])
        if not rope:
            P.op("act", lambda e: e.activation(out=dst, in_=qn[:, :n], func=AF.Identity, scale=qkg[:, gcol:gcol + 1]), r=[qnk], w=[dkey])
            return
        P.op("act", lambda e: e.activation(out=qg[:, :n], in_=qn[:, :n], func=AF.Identity, scale=qkg[:, gcol:gcol + 1]), r=[qnk], w=[qgk])
        P.op("pe", lambda e: e.matmul(sw[:, :n], perm_bf[:, :], qg[:, :n], start=True, stop=True), r=[qgk, "perm"], w=[swk])
        P.op("dve", lambda e: e.tensor_tensor(out=t1[:, :n], in0=qg[:, :n], in1=cs[:, 0, :n], op=ALU.mult), r=[qgk, "cs"], w=[t1k])
        P.op("dve", lambda e: e.tensor_tensor(out=t2[:, :n], in0=sw[:, :n], in1=cs[:, 1, :n], op=ALU.mult), r=[swk, "cs"], w=[t2k])
        P.op("dve", lambda e: e.tensor_tensor(out=dst, in0=t1[:, :n], in1=t2[:, :n], op=ALU.add), r=[t1k, t2k], w=[dkey])

    def mixer_odd(b, l):
        with ExitStack() as es0:
            kT = sb(es0, "kT", (128, 2, NT), BF16)
            vaug = sb(es0, "vaug", (128, 18, 4, 128), BF16)
            cs = sb(es0, "cs", (128, 2, 512))
            xt = sb(es0, "xt", (128, 8, 512), BF16)
            qg2 = [sb(es0, f"qg{i}", (128, 512), BF16) for i in range(2)]
            with ExitStack() as es:
                wk = sb(es, "wk", (128, 8, 256), BF16)
                wv = sb(es, "wv", (128, 8, 256), BF16)
                ss_ps = pst(es, "ss_ps", (128, 512))
                pp = [pst(es, f"pp{i}", (128, 512)) for i in range(2)]
                bps2 = [pst(es, f"bps{i}", (128, 512)) for i in range(2)]
                sw2 = [pst(es, f"sw{i}", (128, 512)) for i in range(2)]
                scr = sb(es, "scr", (128, 8, 512))
                RR = [dict(sq=(sq[i], ("sq", i)), bps=(bps2[i], ("bps", i)), sw=(sw2[i], ("sw", i)), rstd=(scr[:, 6 + i, :], ("scr", 6 + i)),
                           qn=(scr[:, i, :], ("scr", i)), qg=(qg2[i], ("qg", i)), t1=(scr[:, 2 + i, :], ("scr", 2 + i)),
                           t2=(scr[:, 4 + i, :], ("scr", 4 + i))) for i in range(2)]
                qi = 0
                P.dma("pool", wk[:, :, :], wk_d, w=["wk"])
                P.dma("pool", wv[:, :, :], wv_d, w=["wv"])
                P.op("dve", lambda e: e.memset(vaug[:, :, :, :], 1.0), w=[("vaug", t) for t in range(18)])
                pi = 0
                for tile in range(5):
                    n = tile_norm(tile, l, 0, b, ss_ps, "ss", dst=xt)
                    t0 = tile * 512
                    if tile < 4:
                        P.dma("sp", cs[:, 0, :], cos_d[:, t0:t0 + 512], w=["cs"], slot="cs0")
                        P.dma("sp", cs[:, 1, :], sin_d[:, t0:t0 + 512], w=["cs"], slot="cs1")
                    for c in range(2):
                        ps = pp[pi % 2]; pk = ("pp", pi % 2); pi += 1
                        fm_proj(ps, pk, lambda kc, c=c: wk[:, kc, c * 128:(c + 1) * 128], ["wk"],
                                lambda kc: xt[:, kc, :n], [("xt", kc) for kc in range(8)], n)
                        qk_process(ps[:, :n], pk, n, 1, tile < 4, kT[:, c, t0:t0 + n], ("kT", tile, c), cs, RR[qi % 2]); qi += 1
                    for s_ in range(n // 128):
                        ps = pp[pi % 2]; pk = ("pp", pi % 2); pi += 1
                        ti = tile * 4 + s_
                        for kc in range(8):
                            P.op("pe", lambda e, ps=ps, kc=kc, s_=s_: e.matmul(ps[:, 0:256], xt[:, kc, s_ * 128:(s_ + 1) * 128], wv[:, kc, :],
                                                                                 start=(kc == 0), stop=(kc == 7)),
                                 r=[("xt", kc), "wv"], w=[pk])
                        for g in range(4):
                            P.op("dve", lambda e, ps=ps, ti=ti, g=g: e.tensor_copy(out=vaug[:, ti, g, 0:64], in_=ps[:, g * 64:(g + 1) * 64]),
                                 r=[pk], w=[("vaug", ti)])
                P.flush()
            with ExitStack() as es:
                wq = sb(es, "wq", (128, 8, D), BF16)
                wo = sb(es, "wo", (128, 8, D), BF16)
                qT = sb(es, "qT", (128, 8, 512), BF16)
                yT = sb(es, "yT", (128, 8, 512), BF16)
                yo = sb(es, "yo", (128, 8, 512))
                pT = [sb(es, f"pT{i}", (128, GRP, 512), BF16) for i in range(2)]
                rec = sb(es, "rec", (128, 512))
                s_ps = [pst(es, f"s_ps{i}", (128, GRP, 512)) for i in range(2)]
                ss_ps = pst(es, "ss_ps", (128, 512))
                pp = [pst(es, "pp0", (128, 512))]
                o_ps = [pst(es, f"o_ps{i}", (128, 512)) for i in range(2)]
                RR = [dict(sq=(sq[i], ("sq", i)), bps=(s_ps[i][:, 0, :], ("sps", i)), sw=(o_ps[i], ("ops", i)), rstd=(yo[:, 6 + i, :], ("yo", 6 + i)),
                           qn=(yo[:, i, :], ("yo", i)), qg=(qg2[i], ("qg", i)), t1=(yo[:, 2 + i, :], ("yo", 2 + i)),
                           t2=(yo[:, 4 + i, :], ("yo", 4 + i))) for i in range(2)]
                qi = 0
                P.dma("pool", wq[:, :, :], wq_d, w=["wq"])
                P.dma("pool", wo[:, :, :], woc_d, w=["wo"])
                acnt = [0, 0]
                pi = 0
                for tile in range(4):
                    n = tile_norm(tile, l, 0, b, ss_ps, "ss", dst=xt)
                    t0 = tile * 512
                    tagc = []
                    P.dma("sp", cs[:, 0, :], cos_d[:, t0:t0 + 512], w=["cs"], slot="cs0")
                    P.dma("sp", cs[:, 1, :], sin_d[:, t0:t0 + 512], w=["cs"], slot="cs1")
                    for c in range(8):
                        ps = pp[0]; pk = ("pp", 0); pi += 1
                        fm_proj(ps, pk, lambda kc, c=c: wq[:, kc, c * 128:(c + 1) * 128], ["wq"],
                                lambda kc: xt[:, kc, :n], [("xt", kc) for kc in range(8)], n)
                        qk_process(ps[:, :n], pk, n, 0, True, qT[:, c, :n], ("qT", c), cs, RR[qi % 2]); qi += 1
                    for c in range(8):
                        for s2 in range(2):
                            hp = s2 * 64
                            g = 2 * (c // 4) + s2
                            kc_, = [g // 2]
                            keys = []
                            for t in range(18):
                                keys.append((kT[hp:hp + 64, kc_, t * 128:(t + 1) * 128], ("kT", t // 4, kc_), vaug[:, t, g, :], ("vaug", t), None, None))
                            attention(qT[hp:hp + 64, c, :n], ("qT", c), n, keys, 0.125, s_ps, o_ps, pT, None, rec,
                                      yT[hp:hp + 64, c, :n], ("yT", c, hp), hp, tagc)
                    attn_flush(tagc, s_ps, o_ps, pT, None, rec, acnt)

                    def get_ps(dc, n=n):
                        nonlocal pi
                        ps = pp[0]; pk = ("pp", 0); pi += 1
                        for kc in range(8):
                            P.op("pe", lambda e, ps=ps, kc=kc, dc=dc: e.matmul(ps[:, :n], wo[:, kc, dc * 128:(dc + 1) * 128], yT[:, kc, :n],
                                                                               start=(kc == 0), stop=(kc == 7)),
                                 r=["wo", ("yT", kc, 0), ("yT", kc, 64)], w=[pk])
                        return ps[:, :n], pk
                    post_norm_residual(tile, l, 0, b, get_ps, ss_ps, "ss", yo)
                P.flush()

    def ffn(b, l, with_ctx, upto=None):
        passes = [(0, 0, 1024), (0, 1024, 1024)] + ([(1, 0, 256)] if with_ctx else [])
        for (isctx, s0, T) in passes:
            with ExitStack() as es:
                xf = sb(es, "xf", (128, 8, 1026), BF16)
                actT = sb(es, "actT", (128, NCH, 1024), BF16)
                ca = sb(es, "ca", (128, 1024))
                cg = sb(es, "cg", (128, 1024))
                wub = [sb(es, f"wub{i}", (128, 2, 8, 128), BF16) for i in range(3)]
                wdb = [sb(es, f"wdb{i}", (128, NCH, 128), BF16) for i in range(2)]
                yo = sb(es, "yo", (128, 8, 512))
                ss_ps = pst(es, "ss_ps", (128, 512))
                a_ps = pst(es, "a_ps", (128, 1536))
                g_ps = pst(es, "g_ps", (128, 1536))
                pp = [pst(es, "ppd", (128, 512))]
                v = 2 if isctx else b
                A_, B_ = modA[:, l, 1, v, :], modB[:, l, 1, v, :]
                Sq = LC if isctx else S

                def hap(kc, a, n):
                    return hcT[:, kc, a:a + n] if isctx else hT[:, kc, a:a + n]

                def hk(a):
                    return 4 if isctx else a // 512
                segs = []
                if s0 > 0:
                    P.op("dve", lambda e: e.tensor_copy(out=xf[:, :, 0:1], in_=xhalo[:, :, 0:1]), r=["xhalo"], w=[("xf", "l")])
                else:
                    P.op("dve", lambda e: e.memset(xf[:, :, 0:1], 0.0), w=[("xf", "l")])
                for a in range(s0, s0 + T, 512):
                    nn = min(512, s0 + T - a)
                    segs.append((a, nn, 1 + a - s0))
                if s0 + T < Sq:
                    segs.append((s0 + T, 1, 1 + T))
                else:
                    P.op("dve", lambda e: e.memset(xf[:, :, 1 + T:2 + T], 0.0), w=[("xf", "r")])
                xkeys = []
                for (a, nn, col) in segs:
                    key = ("xf", a)
                    xkeys.append(key)
                    norm_mod(lambda kc, a=a, nn=nn: hap(kc, a, nn), [hkey(hk(a), kc) for kc in range(8)], nn, A_, B_,
                             lambda kc, col=col, nn=nn: xf[:, kc, col:col + nn], [key] * 8, ss_ps, "ss")
                xall = xkeys + [("xf", "l"), ("xf", "r")]
                if s0 + T < Sq:
                    P.op("dve", lambda e: e.tensor_copy(out=xhalo[:, :, 0:1], in_=xf[:, :, T:T + 1]), r=xkeys, w=["xhalo"])
                if upto == "f_norm":
                    P.flush(); return
                for j in range(NCH):
                    wb_ = wub[j % 3]
                    wkey = ("wub", j % 3)
                    P.dma("pool", wb_[:, 0, :, :], wup_d[l, j], w=[wkey], slot=("wub", j % 3, 0))
                    P.dma("pool", wb_[:, 1, :, :], wup_d[l, NCH + j], w=[wkey], slot=("wub", j % 3, 1))
                    for part, (ps_, pkey, cdst, ckey) in enumerate(((a_ps, "a_ps", ca, "ca"), (g_ps, "g_ps", cg, "cg"))):
                        ch = part * NCH + j
                        for c0 in range(0, T + 2, 512):
                            c1 = min(c0 + 512, T + 2)
                            for kc in range(8):
                                P.op("pe", lambda e, ps_=ps_, kc=kc, c0=c0, c1=c1, part=part, wb_=wb_: e.matmul(
                                    ps_[:, c0:c1], wb_[:, part, kc, :], xf[:, kc, c0:c1], start=(kc == 0), stop=(kc == 7)),
                                    r=[wkey] + xall, w=[pkey])
                        w0, w1, w2 = (cwT[:, l, ch, i:i + 1] for i in range(3))
                        P.op("act", lambda e, ps_=ps_, cdst=cdst, w1=w1, ch=ch: e.activation(out=cdst[:, :T], in_=ps_[:, 1:T + 1], func=AF.Identity,
                                                                                               scale=w1, bias=cbT[:, l, ch:ch + 1]),
                             r=[pkey], w=[ckey])
                        P.op("dve", lambda e, ps_=ps_, cdst=cdst, w0=w0: e.scalar_tensor_tensor(out=cdst[:, 0:T], in0=ps_[:, 0:T], scalar=w0,
                                                                                                  in1=cdst[:, 0:T], op0=ALU.mult, op1=ALU.add),
                             r=[pkey, ckey], w=[ckey])
                        P.op("dve", lambda e, ps_=ps_, cdst=cdst, w2=w2: e.scalar_tensor_tensor(out=cdst[:, 0:T], in0=ps_[:, 2:T + 2], scalar=w2,
                                                                                                  in1=cdst[:, 0:T], op0=ALU.mult, op1=ALU.add),
                             r=[pkey, ckey], w=[ckey])
                    P.op("act", lambda e: e.activation(out=cg[:, :T], in_=cg[:, :T], func=AF.Silu), r=["cg"], w=["cg"])
                    P.op("dve", lambda e, j=j: e.tensor_tensor(out=actT[:, j, :T], in0=cg[:, :T], in1=ca[:, :T], op=ALU.mult),
                         r=["cg", "ca"], w=[("actT", j)])
                    if upto == "f_up1":
                        P.flush(); return
                if upto == "f_up":
                    P.flush(); return
                pi = 0
                di = 0
                for a in range(0, T, 512):
                    nn = min(512, T - a)
                    tile = 4 if isctx else (s0 + a) // 512

                    def get_ps(dc, a=a, nn=nn):
                        nonlocal pi, di
                        wd_ = wdb[di % 2]; wdk = ("wdb", di % 2); di += 1
                        P.dma("pool", wd_[:, 0:11, :], wdn_d[l, dc][:, 0:11, :], w=[wdk], slot=("wdb", (di - 1) % 2, 0))
                        P.dma("pool", wd_[:, 11:22, :], wdn_d[l, dc][:, 11:22, :], w=[wdk], slot=("wdb", (di - 1) % 2, 1))
                        ps = pp[0]; pk = ("pp", 0); pi += 1
                        for kc in range(NCH):
                            P.op("pe", lambda e, ps=ps, kc=kc, wd_=wd_: e.matmul(ps[:, :nn], wd_[:, kc, :], actT[:, kc, a:a + nn],
                                                                                 start=(kc == 0), stop=(kc == NCH - 1)),
                                 r=[wdk, ("actT", kc)], w=[pk])
                        return ps[:, :nn], pk
                    post_norm_residual(tile, l, 1, b, get_ps, ss_ps, "ss", yo)
                P.flush()

    for b in range(nb):
        for kc in range(8):
            P.dma("sp", hT[:, kc, :], xT_d[b, :, kc, :], w=[hkey(t, kc) for t in range(4)], slot=("ld", kc))
        P.dma("sp", hcT[:, :, :], ctxT_d[b], w=[hkey(4, kc) for kc in range(8)], slot="ldc")
        P.flush()
        if "mix0a" in stages:
            mixer_even(b, 0, upto="a", halves=(0,))
        if "mix0b" in stages:
            mixer_even(b, 0, upto="b", halves=(0,))
        for st_ in ("c_u", "c_va", "c_gate"):
            if st_ in stages:
                mixer_even(b, 0, upto=st_, halves=())
        if "mix0ab" in stages:
            mixer_even(b, 0, upto="b")
        if "mix0c" in stages:
            mixer_even(b, 0, halves=())
        if "mix0" in stages:
            mixer_even(b, 0)
        for st_ in ("f_norm", "f_up1", "f_up"):
            if st_ in stages:
                ffn(b, 0, with_ctx=True, upto=st_)
        if "ffn0" in stages:
            ffn(b, 0, with_ctx=True)
        if "mix1" in stages:
            mixer_odd(b, 1)
        if "ffn1" in stages:
            ffn(b, 1, with_ctx=False)
        if dbg:
            P.dma("sp", hc_out[b], hcT[:, :, :], r=[hkey(4, kc) for kc in range(8)], slot="sthc")
        for kc in range(8):
            P.dma("sp", out_d[b, :, kc, :], hT[:, kc, :], r=[hkey(t, kc) for t in range(4)], slot=("st", kc))
        P.flush()
    ES.close()
    return nc


def _prep_shared(inp):
    f = lambda a: np.ascontiguousarray(np.asarray(a, dtype=np.float32))
    sh = {}
    w_ada = f(inp["w_ada"])
    sh["w_ada"] = np.ascontiguousarray(w_ada.reshape(2, 8, 128, 6 * D).transpose(0, 2, 1, 3))
    sh["b_adaT"] = np.ascontiguousarray(f(inp["b_ada"]).reshape(2, 48, 128).transpose(0, 2, 1))
    sh["gT"] = np.ascontiguousarray(f(inp["norm_g"]).reshape(2, 4, 8, 128).transpose(0, 1, 3, 2))
    sh["w_in"] = _lay_w(f(inp["w_in_ab"])[0])
    sh["w_out_ab"] = _lay_w(f(inp["w_out_ab"])[0])
    wqkv = f(inp["w_qkv_c"])[0]
    qcols = []
    for c in range(8):
        for s in range(2):
            h = HMAP[c][s]
            qcols.extend(range(h * 64, (h + 1) * 64))
    sh["w_q"] = _lay_w(wqkv[:, qcols])
    sh["w_k"] = _lay_w(wqkv[:, 1024:1280])
    sh["w_v"] = _lay_w(wqkv[:, 1280:1536])
    sh["w_out_c"] = _lay_w(f(inp["w_out_c"])[0][qcols, :])
    sh["w_sT"] = np.ascontiguousarray(f(inp["a_w_s"])[0].transpose(2, 0, 1))
    sh["b_s_bc"] = np.ascontiguousarray(np.broadcast_to(f(inp["a_b_s"])[0].reshape(1, 512), (128, 512)))
    sh["gv_bc"] = np.ascontiguousarray(np.broadcast_to(f(inp["a_v_g"])[0].reshape(1, 512), (128, 512)))
    bias, index, plan = _build_na_bias(f(inp["b_rpb"])[0])
    sh["bias_na"] = bias
    sh["qg2"] = np.ascontiguousarray(np.tile(f(inp["c_q_g"])[0], 2).reshape(128, 1))
    sh["kg2"] = np.ascontiguousarray(np.tile(f(inp["c_k_g"])[0], 2).reshape(128, 1))
    wup = f(inp["w_up"])
    sh["w_up"] = np.ascontiguousarray(wup.reshape(2, 8, 128, 2 * NCH, 128).transpose(0, 3, 2, 1, 4))
    sh["conv_wT"] = np.ascontiguousarray(f(inp["conv_w"]).reshape(2, 3, 2 * NCH, 128).transpose(0, 3, 2, 1))
    sh["conv_bT"] = np.ascontiguousarray(f(inp["conv_b"]).reshape(2, 2 * NCH, 128).transpose(0, 2, 1))
    wdn = f(inp["w_down"])
    sh["w_down"] = np.ascontiguousarray(wdn.reshape(2, NCH, 128, 8, 128).transpose(0, 3, 2, 1, 4))
    cosT, sinT, perm = _rope_tables()
    sh["cosT"], sh["sinT"], sh["perm"] = cosT, sinT, perm
    return sh, index, plan


def _lay_act(a):
    nb_, L, _ = a.shape
    return np.ascontiguousarray(a.transpose(0, 2, 1).reshape(nb_, 8, 128, L).transpose(0, 2, 1, 3))


def kernel(**inp):
    x = np.asarray(inp["x"], np.float32)
    c = np.asarray(inp["c"], np.float32)
    ctx = np.asarray(inp["ctx"], np.float32)
    c_ctx = np.asarray(inp["c_ctx"], np.float32)
    sh, index, plan = _prep_shared(inp)
    n_cores = 8
    nb = x.shape[0] // n_cores
    nc = build_program(sh["bias_na"].shape[0], index, plan, nb=nb)
    in_maps = []
    for i in range(n_cores):
        m = dict(sh)
        sl = slice(i * nb, (i + 1) * nb)
        m["xT"] = _lay_act(x[sl])
        m["ctxT"] = _lay_act(ctx[sl])
        cv = np.stack([c[i * nb], c[i * nb + 1], c_ctx], -1)
        m["cT"] = np.ascontiguousarray(cv.reshape(8, 128, 3).transpose(1, 0, 2))
        in_maps.append(m)
    res = run_bass_kernel_spmd(nc, in_maps, core_ids=list(range(n_cores)))
    out = np.empty_like(x)
    for i in range(n_cores):
        o = res.results[i]["outT"]
        out[i * nb:(i + 1) * nb] = o.transpose(0, 3, 2, 1).reshape(nb, S, D)
    return out
```

```python
import numpy as np
from contextlib import ExitStack
import concourse.bass as bass
import concourse.mybir as mybir
from concourse.bass_utils import run_bass_kernel_spmd

F32 = mybir.dt.float32
BF16 = mybir.dt.bfloat16
AF = mybir.ActivationFunctionType
ALU = mybir.AluOpType

D = 1024
S = 2048
LC = 256
NT = S + LC
DFF = 2816
EPS = 1e-6
NCH = 22


import types


def _freeze(fn, depth=0):
    if not isinstance(fn, types.FunctionType) or fn.__closure__ is None or depth > 4:
        return fn
    cells = []
    for c in fn.__closure__:
        try:
            v = c.cell_contents
        except ValueError:
            cells.append(c)
            continue
        if isinstance(v, types.FunctionType):
            v = _freeze(v, depth + 1)
        cells.append(types.CellType(v))
    g = types.FunctionType(fn.__code__, fn.__globals__, fn.__name__, fn.__defaults__, tuple(cells))
    g.__kwdefaults__ = fn.__kwdefaults__
    return g


class Prog:
    def __init__(self, nc):
        self.nc = nc
        self.eng = {"pe": nc.tensor, "act": nc.scalar, "dve": nc.vector, "pool": nc.gpsimd, "sp": nc.sync}
        self.sem = {e: nc.alloc_semaphore(name=f"sem_{e}") for e in self.eng}
        self.cnt = {e: 0 for e in self.eng}
        self.dsem = {}
        self.waited = {e: {} for e in self.eng}
        self.ops = []
        self.last_w = {}
        self.readers = {}
        self.last_of_eng = {}
        self.last_dma = {}

    def op(self, eng, fn, r=(), w=(), dma=None):
        idx = len(self.ops)
        deps = {}
        for k in r:
            d = self.last_w.get(k)
            if d is not None:
                deps[d] = "raw"
        for k in w:
            d = self.last_w.get(k)
            if d is not None:
                deps[d] = "raw"
            lastr = {}
            for rd in self.readers.get(k, ()):
                o_ = self.ops[rd]
                if o_["dma"] is not None:
                    lastr[("d", rd)] = rd
                else:
                    lastr[o_["eng"]] = rd
            for rd in lastr.values():
                if rd not in deps:
                    deps[rd] = "war"
        deps.pop(idx, None)
        self.ops.append(dict(eng=eng, fn=_freeze(fn), deps=deps, dma=dma, sig=None))
        for k in r:
            self.readers.setdefault(k, []).append(idx)
        for k in w:
            self.last_w[k] = idx
            self.readers[k] = []
        if dma is None:
            self.last_of_eng[eng] = idx
        else:
            self.last_dma[dma] = idx
        return idx

    def dma(self, eng, out, in_, r=(), w=(), slot=None):
        if slot is None:
            slot = w[0] if w else "out"
        return self.op(eng, lambda e: e.dma_start(out=out, in_=in_), r=r, w=w, dma=slot)

    def flush(self):
        alld = {}
        for e, i in self.last_of_eng.items():
            alld[i] = "raw"
        for s, i in self.last_dma.items():
            alld[i] = "raw"
        for e in self.eng:
            self.ops.append(dict(eng=e, fn=None, deps=dict(alld), dma=None, sig=None, barrier=True))
        ops = self.ops
        need = [False] * len(ops)
        for i, o in enumerate(ops):
            fd = []
            for d, kind in o["deps"].items():
                od = ops[d]
                if od["dma"] is None and od["eng"] == o["eng"] and not o.get("barrier"):
                    if o["eng"] == "pe" or kind == "war":
                        continue
                if od["dma"] is None and od["eng"] == o["eng"] and o.get("barrier"):
                    continue
                fd.append(d)
            latest = {}
            fd2 = []
            for d in fd:
                od = ops[d]
                if od["dma"] is None:
                    if d > latest.get(od["eng"], -1):
                        latest[od["eng"]] = d
                else:
                    fd2.append(d)
            fd2.extend(latest.values())
            for d in latest.values():
                need[d] = True
            o["fd"] = fd2
        for i, o in enumerate(ops):
            E = o["eng"]
            eo = self.eng[E]
            waits = {}
            for d in o["fd"]:
                key, val = ops[d]["sig"]
                if waits.get(key, 0) < val:
                    waits[key] = val
            for key, val in waits.items():
                if self.waited[E].get(key, 0) < val:
                    h = self.sem[key[1]] if key[0] == "e" else self.dsem[key[1]][0]
                    eo.wait_ge(h, val)
                    self.waited[E][key] = val
            if o["fn"] is None:
                continue
            ins = o["fn"](eo)
            if o["dma"] is not None:
                slot = o["dma"]
                if slot not in self.dsem:
                    self.dsem[slot] = [self.nc.alloc_semaphore(name=f"dma_{len(self.dsem)}"), 0]
                s = self.dsem[slot]
                s[1] += 16
                ins.then_inc(s[0], 16)
                o["sig"] = (("d", slot), s[1])
            elif need[i]:
                self.cnt[E] += 1
                ins.then_inc(self.sem[E], 1)
                o["sig"] = (("e", E), self.cnt[E])
        self.ops = []
        self.last_w = {}
        self.readers = {}
        self.last_of_eng = {}
        self.last_dma = {}


def _na_tile_plan():
    plan = []
    for A in range(4):
        rows = list(range(8 * A, 8 * A + 8))
        lo = min(max(r - 4, 0) if r - 4 <= 24 else 24 for r in rows)
        lo = min(min(max(r - 4, 0), 24) for r in rows)
        hi = max(min(max(r - 4, 0), 24) + 7 for r in rows)
        t0, t1 = lo // 2, hi // 2
        plan.append(list(range(t0, t1 + 1)))
    return plan


def _build_na_bias(rpb):
    plan = _na_tile_plan()
    NEG = np.float32(-30000.0)
    tiles = []
    index = {}
    qr_l = np.arange(8)[:, None]
    qc = np.arange(64)[None, :]
    for A in range(4):
        qr = (8 * A + qr_l) + 0 * qc
        qcc = 0 * qr_l + qc
        r0 = np.clip(qr - 4, 0, 24)
        c0 = np.clip(qcc - 8, 0, 48)
        qr_f, qc_f, r0_f, c0_f = [a.reshape(-1) for a in (qr, qcc, r0, c0)]
        for h in range(8):
            for j, t in enumerate(plan[A]):
                kr = np.repeat(np.arange(2 * t, 2 * t + 2), 64)
                kc = np.tile(np.arange(64), 2)
                valid = ((kr[:, None] >= r0_f[None, :]) & (kr[:, None] < r0_f[None, :] + 8)
                         & (kc[:, None] >= c0_f[None, :]) & (kc[:, None] < c0_f[None, :] + 16))
                dr = np.clip(kr[:, None] - qr_f[None, :] + 7, 0, 14)
                dc = np.clip(kc[:, None] - qc_f[None, :], -15, 15) + 15
                b = rpb[h][dr, dc]
                tiles.append(np.where(valid, b, NEG).astype(np.float32))
                index[(A, h, j)] = len(tiles) - 1
    return np.stack(tiles, 0), index, plan


def _rope_tables():
    t = np.arange(S)
    inv = (10000.0 ** (-np.arange(16, dtype=np.float32) / 16)).astype(np.float32)
    rows = (t // 64).astype(np.float32)[:, None] * inv
    cols = (t % 64).astype(np.float32)[:, None] * inv
    ang = np.concatenate([rows, cols], -1)
    cos = np.cos(ang).astype(np.float32).T
    sin = np.sin(ang).astype(np.float32).T
    cosT = np.tile(cos, (4, 1))
    sinT = np.concatenate([-sin, sin, -sin, sin], 0)
    perm = np.zeros((128, 128), np.float32)
    for m in range(128):
        k = m + 32 if (m % 64) < 32 else m - 32
        perm[k, m] = 1.0
    return np.ascontiguousarray(cosT), np.ascontiguousarray(sinT), perm


HMAP = [[(4 * (2 * (c // 4)) + (c % 4)), (4 * (2 * (c // 4) + 1) + (c % 4))] for c in range(8)]


def _lay_w(w):
    K, N = w.shape
    return np.ascontiguousarray(w.reshape(K // 128, 128, N).transpose(1, 0, 2))


def build_program(n_bias_tiles, bias_index, na_plan, nb=2, stages=("mix0", "ffn0", "mix1", "ffn1"), dbg=False):
    nc = bass.Bass("TRN2", target_bir_lowering=False)
    P = Prog(nc)

    def din(name, shape):
        return nc.dram_tensor(name, list(shape), F32, kind="ExternalInput").ap()

    xT_d = din("xT", (nb, 128, 8, S))
    ctxT_d = din("ctxT", (nb, 128, 8, LC))
    out_d = nc.dram_tensor("outT", [nb, 128, 8, S], F32, kind="ExternalOutput").ap()
    cT_d = din("cT", (128, 8, 3))
    wada_d = din("w_ada", (2, 128, 8, 6 * D))
    badaT_d = din("b_adaT", (2, 128, 48))
    gT_d = din("gT", (2, 4, 128, 8))
    win_d = din("w_in", (128, 8, 2560))
    woab_d = din("w_out_ab", (128, 8, D))
    wq_d = din("w_q", (128, 8, D))
    wk_d = din("w_k", (128, 8, 256))
    wv_d = din("w_v", (128, 8, 256))
    woc_d = din("w_out_c", (128, 8, D))
    wsT_d = din("w_sT", (128, 4, 128))
    bs_d = din("b_s_bc", (128, 512))
    gv_d = din("gv_bc", (128, 512))
    bias_d = din("bias_na", (n_bias_tiles, 128, 512))
    qg_d = din("qg2", (128, 1))
    kg_d = din("kg2", (128, 1))
    wup_d = din("w_up", (2, 2 * NCH, 128, 8, 128))
    cw_d = din("conv_wT", (2, 128, 2 * NCH, 3))
    cb_d = din("conv_bT", (2, 128, 2 * NCH))
    wdn_d = din("w_down", (2, 8, 128, NCH, 128))
    cos_d = din("cosT", (128, S))
    sin_d = din("sinT", (128, S))
    perm_d = din("perm", (128, 128))

    ES = ExitStack()
    hc_out = nc.dram_tensor("hcT_out", [nb, 128, 8, LC], F32, kind="ExternalOutput").ap() if dbg else None

    uid = [0]

    def sb(es, name, shape, dt=F32):
        uid[0] += 1
        return es.enter_context(nc.sbuf_tensor(f"{name}_{uid[0]}", list(shape), dt))

    def pst(es, name, shape):
        uid[0] += 1
        return es.enter_context(nc.psum_tensor(f"{name}_{uid[0]}", list(shape), F32))

    hT = sb(ES, "hT", (128, 8, S))
    hcT = sb(ES, "hcT", (128, 8, LC))
    ones_bf = sb(ES, "ones_bf", (128, 128), BF16)
    bones_bf = sb(ES, "bones_bf", (128, 128), BF16)
    perm_bf = sb(ES, "perm_bf", (128, 128), BF16)
    modA = sb(ES, "modA", (128, 2, 2, 3, 8))
    modB = sb(ES, "modB", (128, 2, 2, 3, 8))
    modG = sb(ES, "modG", (128, 2, 2, 3, 8))
    qkg = sb(ES, "qkg", (128, 2))
    cwT = sb(ES, "cwT", (128, 2, 2 * NCH, 3))
    cbT = sb(ES, "cbT", (128, 2, 2 * NCH))
    sq = [sb(ES, f"sq{i}", (128, 512), BF16) for i in range(2)]
    rstd = sb(ES, "rstd", (128, 512))
    tmp = [sb(ES, f"tmp{i}", (128, 512)) for i in range(2)]
    xhalo = sb(ES, "xhalo", (128, 8, 1), BF16)

    def hsrc(tile):
        if tile < 4:
            return (lambda kc, a=0, n=512: hT[:, kc, tile * 512 + a: tile * 512 + a + n]), 512
        return (lambda kc, a=0, n=256: hcT[:, kc, a:a + n]), 256

    def hkey(tile, kc):
        return ("h", tile, kc)

    def norm_mod(src, skeys, n, A, B, dst, dkeys, ss_ps, ss_key):
        for kc in range(8):
            s_ = sq[kc % 2]
            P.op("act", lambda e, kc=kc, s_=s_: e.activation(out=s_[:, :n], in_=src(kc), func=AF.Square),
                 r=[skeys[kc]], w=[("sq", kc % 2)])
            P.op("pe", lambda e, kc=kc, s_=s_: e.matmul(ss_ps[:, :n], ones_bf[:, :], s_[:, :n], start=(kc == 0), stop=(kc == 7)),
                 r=[("sq", kc % 2)], w=[ss_key])
        P.op("act", lambda e: e.activation(out=rstd[:, :n], in_=ss_ps[:, :n], func=AF.Ln, scale=1.0 / D, bias=eps_t[:, 0:1]),
             r=[ss_key], w=["rstd"])
        P.op("act", lambda e: e.activation(out=rstd[:, :n], in_=rstd[:, :n], func=AF.Exp, scale=-0.5), r=["rstd"], w=["rstd"])
        for kc in range(8):
            t_ = tmp[kc % 2]
            P.op("dve", lambda e, kc=kc, t_=t_: e.tensor_tensor(out=t_[:, :n], in0=src(kc), in1=rstd[:, :n], op=ALU.mult),
                 r=[skeys[kc], "rstd"], w=[("tmp", kc % 2)])
            P.op("act", lambda e, kc=kc, t_=t_: e.activation(out=dst(kc), in_=t_[:, :n], func=AF.Identity,
                                                              scale=A[:, kc:kc + 1], bias=B[:, kc:kc + 1]),
                 r=[("tmp", kc % 2)], w=[dkeys[kc]])

    def tile_norm(tile, l, mf, b, ss_ps, ss_key, dst=None, dkeys=None):
        src, n = hsrc(tile)
        v = b if tile < 4 else 2
        if dkeys is None:
            xt_ = dst
            dst = lambda kc: xt_[:, kc, :n]
            dkeys = [("xt", kc) for kc in range(8)]
        norm_mod(lambda kc: src(kc), [hkey(tile, kc) for kc in range(8)], n,
                 modA[:, l, mf, v, :], modB[:, l, mf, v, :], dst, dkeys, ss_ps, ss_key)
        return n

    def fm_proj(ps, pkey, w_ap, wkeys, rhs, rkeys, n, nk=8):
        for kc in range(nk):
            P.op("pe", lambda e, kc=kc: e.matmul(ps[:, :n], w_ap(kc), rhs(kc), start=(kc == 0), stop=(kc == nk - 1)),
                 r=list(wkeys) + [rkeys[kc]], w=[pkey])

    def post_norm_residual(tile, l, mf, b, get_ps, ss_ps, ss_key, yo):
        src, n = hsrc(tile)
        v = b if tile < 4 else 2
        G = modG[:, l, mf, v, :]
        for dc in range(8):
            ps, pkey = get_ps(dc)
            s_ = sq[dc % 2]
            P.op("act", lambda e, ps=ps, dc=dc: e.activation(out=yo[:, dc, :n], in_=ps, func=AF.Copy), r=[pkey], w=[("yo", dc)])
            P.op("act", lambda e, ps=ps, s_=s_: e.activation(out=s_[:, :n], in_=ps, func=AF.Square), r=[pkey], w=[("sq", dc % 2)])
            P.op("pe", lambda e, dc=dc, s_=s_: e.matmul(ss_ps[:, :n], ones_bf[:, :], s_[:, :n], start=(dc == 0), stop=(dc == 7)),
                 r=[("sq", dc % 2)], w=[ss_key])
        P.op("act", lambda e: e.activation(out=rstd[:, :n], in_=ss_ps[:, :n], func=AF.Ln, scale=1.0 / D, bias=eps_t[:, 0:1]),
             r=[ss_key], w=["rstd"])
        P.op("act", lambda e: e.activation(out=rstd[:, :n], in_=rstd[:, :n], func=AF.Exp, scale=-0.5), r=["rstd"], w=["rstd"])
        for dc in range(8):
            t_ = tmp[dc % 2]
            P.op("dve", lambda e, dc=dc, t_=t_: e.tensor_tensor(out=t_[:, :n], in0=yo[:, dc, :n], in1=rstd[:, :n], op=ALU.mult),
                 r=[("yo", dc), "rstd"], w=[("tmp", dc % 2)])
            P.op("dve", lambda e, dc=dc, t_=t_: e.scalar_tensor_tensor(out=src(dc), in0=t_[:, :n], scalar=G[:, dc:dc + 1], in1=src(dc),
                                                                        op0=ALU.mult, op1=ALU.add),
                 r=[("tmp", dc % 2), hkey(tile, dc)], w=[hkey(tile, dc)])

    def attention(q_ap, qkey, n_q, keys, scale, s_ps, o_ps, pT, sbt, rec, out_ap, okey, hp, tagc):
        tagc.append(dict(q_ap=q_ap, qkey=qkey, n_q=n_q, keys=keys, scale=scale, out_ap=out_ap, okey=okey))

    GRP = 2

    def attn_flush(jobs, s_ps, o_ps, pT, sbt, rec, cnt, bias_sb=None):
        units = []
        gsz = s_ps[0].shape[1]
        for ji, J in enumerate(jobs):
            J["ob"] = (cnt[0] + ji) % 2
            ks = J["keys"]
            i = 0
            while i < len(ks):
                g = [i]
                while len(g) < gsz and i + len(g) < len(ks) and (ks[i + len(g)][4] is None) == (ks[i][4] is None):
                    g.append(i + len(g))
                units.append((J, g))
                i += len(g)
        cnt[0] += len(jobs)
        groups = []
        for k, (J, g) in enumerate(units):
            bsp = J["keys"][g[0]][4]
            if bsp is not None and (not groups or groups[-1] != bsp[0]):
                groups.append(bsp[0])
        gpos = {g: n_ for n_, g in enumerate(groups)}
        issued = [0]

        def issue_groups(upto):
            while issued[0] < min(upto, len(groups)):
                (bi_, i0, ln) = groups[issued[0]]
                bb = bias_sb[bi_ % 3]
                P.dma("sp", bb[:, 0:ln, :], bias_d[i0:i0 + ln].rearrange("t p q -> p t q"), w=[("bias", bi_ % 3)])
                issued[0] += 1

        def emit_S(k):
            J, g = units[k]
            sbk = (cnt[1] + k) % len(s_ps)
            sp = s_ps[sbk]
            n_q, q_ap = J["n_q"], J["q_ap"]
            for gi_, i in enumerate(g):
                k_ap, kkey = J["keys"][i][0], J["keys"][i][1]
                P.op("pe", lambda e: e.matmul(sp[:, gi_, :n_q], k_ap, q_ap, start=True, stop=True), r=[kkey, J["qkey"]], w=[("sps", sbk)])

        def emit_post_pv(k):
            J, g = units[k]
            ng = len(g)
            sbk = (cnt[1] + k) % len(s_ps)
            sp, p_ = s_ps[sbk], pT[sbk]
            n_q, scale, nk, ob = J["n_q"], J["scale"], len(J["keys"]), J["ob"]
            ops_ = o_ps[ob]
            b0 = J["keys"][g[0]][4]
            if b0 is not None:
                (gid, jj) = b0
                issue_groups(gpos[gid] + 3)
                b_ap = bias_sb[gid[0] % 3][:, jj:jj + ng, :n_q]
                bkey = ("bias", gid[0] % 3)
                sb_ = sbt[sbk]
                P.op("dve", lambda e: e.scalar_tensor_tensor(out=sb_[:, 0:ng, :n_q], in0=sp[:, 0:ng, :n_q], scalar=float(scale), in1=b_ap,
                                                             op0=ALU.mult, op1=ALU.add),
                     r=[("sps", sbk), bkey], w=[("sbt", sbk)])
                P.op("act", lambda e: e.activation(out=p_[:, 0:ng, :n_q], in_=sb_[:, 0:ng, :n_q], func=AF.Exp), r=[("sbt", sbk)], w=[("pT", sbk)])
            else:
                P.op("act", lambda e: e.activation(out=p_[:, 0:ng, :n_q], in_=sp[:, 0:ng, :n_q], func=AF.Exp, scale=float(scale)),
                     r=[("sps", sbk)], w=[("pT", sbk)])
            for gi_, i in enumerate(g):
                v_ap, vkey = J["keys"][i][2], J["keys"][i][3]
                P.op("pe", lambda e: e.matmul(ops_[:, :n_q], v_ap, p_[:, gi_, :n_q], start=(i == 0), stop=(i == nk - 1)),
                     r=[vkey, ("pT", sbk)], w=[("ops", ob)])
            if g[-1] == nk - 1:
                out_ap = J["out_ap"]
                P.op("act", lambda e: e.activation(out=rec[64:128, :n_q], in_=ops_[64:128, :n_q], func=AF.Ln), r=[("ops", ob)], w=["rec"])
                P.op("act", lambda e: e.activation(out=rec[64:128, :n_q], in_=rec[64:128, :n_q], func=AF.Exp, scale=-1.0), r=["rec"], w=["rec"])
                P.op("dve", lambda e: e.tensor_tensor(out=out_ap, in0=ops_[0:64, :n_q], in1=rec[64:128, :n_q], op=ALU.mult),
                     r=[("ops", ob), "rec"], w=[J["okey"]])

        if bias_sb is not None:
            issue_groups(2)
        look = len(s_ps) - 1
        for k in range(min(look, len(units))):
            emit_S(k)
        for k in range(len(units)):
            if k + look < len(units):
                emit_S(k + look)
            emit_post_pv(k)
        cnt[1] += len(units)

    eps_t = sb(ES, "eps_t", (128, 1))
    with ExitStack() as es:
        wa = [sb(es, f"wa{i}", (128, 8, 1024)) for i in range(2)]
        scT = sb(es, "scT", (128, 8, 3))
        sgm = sb(es, "sgm", (128, 8, 3))
        mod = sb(es, "mod", (128, 2, 48, 3))
        bT = sb(es, "bT", (128, 2, 48))
        gTt = sb(es, "gTt", (128, 2, 4, 8))
        permf = sb(es, "permf", (128, 128))
        aps = pst(es, "aps", (128, 8, 3))
        P.op("dve", lambda e: e.memset(ones_bf[:, :], 1.0), w=["ones"])
        P.op("dve", lambda e: e.memset(bones_bf[:, :], 0.0), w=["bones"])
        P.op("dve", lambda e: e.memset(bones_bf[0:64, 0:64], 1.0), r=[], w=["bones"])
        P.op("dve", lambda e: e.memset(bones_bf[64:128, 64:128], 1.0), r=[], w=["bones"])
        P.op("dve", lambda e: e.memset(eps_t[:, :], EPS), w=["eps"])
        P.dma("sp", permf[:, :], perm_d, w=["permf"])
        P.op("dve", lambda e: e.tensor_copy(out=perm_bf[:, :], in_=permf[:, :]), r=["permf"], w=["perm"])
        P.dma("sp", scT[:, :, :], cT_d, w=["scT"])
        P.dma("sp", qkg[:, 0:1], qg_d, w=["qg"])
        P.dma("sp", qkg[:, 1:2], kg_d, w=["kg"])
        P.dma("sp", cwT[:, 0], cw_d[0], w=["cw0"])
        P.dma("sp", cwT[:, 1], cw_d[1], w=["cw1"])
        P.dma("sp", cbT[:, 0], cb_d[0], w=["cb0"])
        P.dma("sp", cbT[:, 1], cb_d[1], w=["cb1"])
        for l in range(2):
            P.dma("sp", bT[:, l, :], badaT_d[l], w=[("bT", l)])
            for i in range(4):
                P.dma("sp", gTt[:, l, i, :], gT_d[l, i], w=[("gT", l, i)])
        P.op("act", lambda e: e.activation(out=sgm[:, :, :], in_=scT[:, :, :], func=AF.Sigmoid), r=["scT"], w=["sgm"])
        P.op("dve", lambda e: e.tensor_tensor(out=scT[:, :, :], in0=scT[:, :, :], in1=sgm[:, :, :], op=ALU.mult), r=["scT", "sgm"], w=["scT"])
        it = 0
        for l in range(2):
            for grp in range(6):
                wbuf = wa[it % 2]
                P.dma("sp", wbuf[:, :, :], wada_d[l][:, :, grp * 1024:(grp + 1) * 1024], w=[("wa", it % 2)])
                for j in range(8):
                    for kc in range(8):
                        P.op("pe", lambda e, wbuf=wbuf, j=j, kc=kc: e.matmul(aps[:, j, :], wbuf[:, kc, j * 128:(j + 1) * 128], scT[:, kc, :],
                                                                               start=(kc == 0), stop=(kc == 7)),
                             r=[("wa", it % 2), "scT"], w=[("aps", j)])
                for v in range(3):
                    P.op("dve", lambda e, l=l, grp=grp, v=v: e.tensor_tensor(out=mod[:, l, grp * 8:(grp + 1) * 8, v], in0=aps[:, :, v],
                                                                              in1=bT[:, l, grp * 8:(grp + 1) * 8], op=ALU.add),
                         r=[("aps", j) for j in range(8)] + [("bT", l)], w=[("mod", l, grp, v)])
                it += 1
        for l in range(2):
            for mf in range(2):
                for v in range(3):
                    ish, isc, igt = 3 * mf, 3 * mf + 1, 3 * mf + 2
                    P.op("dve", lambda e, l=l, mf=mf, v=v, isc=isc: e.scalar_tensor_tensor(
                        out=modA[:, l, mf, v, :], in0=mod[:, l, isc * 8:(isc + 1) * 8, v], scalar=1.0, in1=gTt[:, l, 2 * mf, :],
                        op0=ALU.add, op1=ALU.mult), r=[("mod", l, isc, v), ("gT", l, 2 * mf)], w=[("modA", l, mf, v)])
                    P.op("dve", lambda e, l=l, mf=mf, v=v, ish=ish: e.tensor_copy(out=modB[:, l, mf, v, :], in_=mod[:, l, ish * 8:(ish + 1) * 8, v]),
                         r=[("mod", l, ish, v)], w=[("modB", l, mf, v)])
                    P.op("dve", lambda e, l=l, mf=mf, v=v, igt=igt: e.tensor_tensor(
                        out=modG[:, l, mf, v, :], in0=mod[:, l, igt * 8:(igt + 1) * 8, v], in1=gTt[:, l, 2 * mf + 1, :], op=ALU.mult),
                        r=[("mod", l, igt, v), ("gT", l, 2 * mf + 1)], w=[("modG", l, mf, v)])
        P.flush()

    def mixer_even(b, l, upto="c", halves=(0, 1)):
        with ExitStack() as es0:
            ybT = sb(es0, "ybT", (128, 4, NT), BF16)
            esx = ExitStack()
            xta = sb(esx, "xta", (128, 8, NT), BF16)
            if len(halves):
                with ExitStack() as es:
                    ss_ps = pst(es, "ss_ps", (128, 512))
                    for tile in range(5):
                        t0 = tile * 512
                        tile_norm(tile, l, 0, b, ss_ps, "ss", dst=(lambda kc, t0=t0, tile=tile: xta[:, kc, t0:t0 + (512 if tile < 4 else 256)]),
                                  dkeys=[("xta", tile, kc) for kc in range(8)])
                    P.flush()
            for hh in halves:
                with ExitStack() as es1:
                    kT = sb(es1, "kT", (128, 2, NT), BF16)
                    vaug = sb(es1, "vaug", (128, 18, 4, 128), BF16)
                    with ExitStack() as es:
                        wk = sb(es, "wk", (128, 8, 256), BF16)
                        wv = sb(es, "wv", (128, 8, 256), BF16)
                        ss_ps = pst(es, "ss_ps", (128, 512))
                        pp = [pst(es, f"pp{i}", (128, 512)) for i in range(2)]
                        P.dma("pool", wk[:, :, :], win_d[:, :, 1536 + hh * 256:1536 + (hh + 1) * 256], w=["wk"])
                        P.dma("pool", wv[:, :, :], win_d[:, :, 2048 + hh * 256:2048 + (hh + 1) * 256], w=["wv"])
                        P.op("dve", lambda e: e.memset(vaug[:, :, :, :], 1.0), w=[("vaug", t) for t in range(18)])
                        pi = 0
                        for tile in range(5):
                            n = 512 if tile < 4 else 256
                            t0 = tile * 512
                            for c in range(2):
                                ps = pp[pi % 2]; pk = ("pp", pi % 2); pi += 1
                                fm_proj(ps, pk, lambda kc, c=c: wk[:, kc, c * 128:(c + 1) * 128], ["wk"],
                                        lambda kc: xta[:, kc, t0:t0 + n], [("xta", tile, kc) for kc in range(8)], n)
                                P.op("act", lambda e, ps=ps, c=c: e.activation(out=kT[:, c, t0:t0 + n], in_=ps[:, :n], func=AF.Copy),
                                     r=[pk], w=[("kT", tile, c)])
                            for s_ in range(n // 128):
                                ps = pp[pi % 2]; pk = ("pp", pi % 2); pi += 1
                                ti = tile * 4 + s_
                                for kc in range(8):
                                    P.op("pe", lambda e, ps=ps, kc=kc, s_=s_: e.matmul(ps[:, 0:256], xta[:, kc, t0 + s_ * 128:t0 + (s_ + 1) * 128], wv[:, kc, :],
                                                                                         start=(kc == 0), stop=(kc == 7)),
                                         r=[("xta", tile, kc), "wv"], w=[pk])
                                for hq in range(4):
                                    P.op("dve", lambda e, ps=ps, ti=ti, hq=hq: e.tensor_copy(out=vaug[:, ti, hq, 0:64], in_=ps[:, hq * 64:(hq + 1) * 64]),
                                         r=[pk], w=[("vaug", ti)])
                        P.flush()
                    if upto == "a":
                        continue
                    with ExitStack() as es:
                        wq = sb(es, "wq", (128, 8, 256), BF16)
                        qT = sb(es, "qT", (128, 2, 512), BF16)
                        bias_sb = [sb(es, f"bias{i}", (128, 2, 512)) for i in range(3)]
                        pT = [sb(es, f"pT{i}", (128, GRP, 512), BF16) for i in range(3)]
                        sbt = [sb(es, f"sbt{i}", (128, GRP, 512)) for i in range(3)]
                        rec = sb(es, "rec", (128, 512))
                        s_ps = [pst(es, f"s_ps{i}", (128, GRP, 512)) for i in range(3)]
                        pp = [s_ps[2][:, 0, :]] * 2
                        o_ps = [pst(es, f"o_ps{i}", (128, 512)) for i in range(2)]
                        P.dma("pool", wq[:, :, :], win_d[:, :, 1024 + hh * 256:1024 + (hh + 1) * 256], w=["wq"])
                        acnt = [0, 0]
                        pi = 0
                        bi = 0
                        for tile in range(5):
                            n = 512 if tile < 4 else 256
                            t0 = tile * 512
                            for c in range(2):
                                ps = pp[0]; pk = ("sps", 2); pi += 1
                                fm_proj(ps, pk, lambda kc, c=c: wq[:, kc, c * 128:(c + 1) * 128], ["wq"],
                                        lambda kc: xta[:, kc, t0:t0 + n], [("xta", tile, kc) for kc in range(8)], n)
                                P.op("act", lambda e, ps=ps, c=c: e.activation(out=qT[:, c, :n], in_=ps[:, :n], func=AF.Copy), r=[pk], w=[("qT", c)])
                            tagc = []
                            for hq in range(4):
                                h = 4 * hh + hq
                                c, hp = hq // 2, (hq % 2) * 64
                                keys = []
                                if tile < 4:
                                    kts = na_plan[tile]
                                    for j0 in range(0, len(kts), 2):
                                        grp = kts[j0:j0 + 2]
                                        i0 = bias_index[(tile, h, j0)]
                                        gid = (bi, i0, len(grp)); bi += 1
                                        for jj, t in enumerate(grp):
                                            keys.append((kT[hp:hp + 64, c, t * 128:(t + 1) * 128], ("kT", t // 4, c), vaug[:, t, hq, :], ("vaug", t),
                                                         (gid, jj), None))
                                for t in (16, 17):
                                    keys.append((kT[hp:hp + 64, c, t * 128:(t + 1) * 128], ("kT", 4, c), vaug[:, t, hq, :], ("vaug", t), None, None))
                                attention(qT[hp:hp + 64, c, :n], ("qT", c), n, keys, 0.125, s_ps, o_ps, pT, sbt, rec,
                                          ybT[hp:hp + 64, 2 * hh + c, t0:t0 + n], ("ybT", tile, 2 * hh + c, hp), hp, tagc)
                            attn_flush(tagc, s_ps, o_ps, pT, sbt, rec, acnt, bias_sb)
                        P.flush()
            esx.close()
            if upto in ("a", "b"):
                return
            with ExitStack() as es:
                xt = sb(es, "xt", (128, 8, 512), BF16)
                wu = sb(es, "wu", (128, 8, 512), BF16)
                wva = sb(es, "wva", (128, 8, 512), BF16)
                wo = sb(es, "wo", (128, 8, D), BF16)
                wsT = sb(es, "wsT", (128, 4, 128), BF16)
                bsb = sb(es, "bsb", (128, 512))
                gvb = sb(es, "gvb", (128, 512))
                uT = sb(es, "uT", (128, 4, 512), BF16)
                vg2 = [sb(es, f"vg{i}", (128, 512)) for i in range(2)]
                vn2 = [sb(es, f"vn{i}", (128, 512)) for i in range(2)]
                vln2 = [sb(es, f"vln{i}", (128, 512), BF16) for i in range(2)]
                tt2 = [sb(es, f"tt{i}", (128, 4, 128)) for i in range(2)]
                yaT = sb(es, "yaT", (128, 4, 512), BF16)
                yo = sb(es, "yo", (128, 8, 512))
                stats2 = [sb(es, f"stats{i}", (128, 6)) for i in range(2)]
                mv2 = [sb(es, f"mv{i}", (128, 2)) for i in range(2)]
                ss_ps = pst(es, "ss_ps", (128, 512))
                pp = [pst(es, f"pp{i}", (128, 512)) for i in range(2)]
                g_ps2 = [pst(es, f"g_ps{i}", (128, 4, 128)) for i in range(2)]
                gi = 0
                P.dma("pool", wu[:, :, :], win_d[:, :, 0:512], w=["wu"])
                P.dma("pool", wva[:, :, :], win_d[:, :, 512:1024], w=["wva"])
                P.dma("pool", wo[:, :, :], woab_d, w=["wo"])
                P.dma("pool", wsT[:, :, :], wsT_d, w=["wsT"])
                P.dma("sp", bsb[:, :], bs_d, w=["bsb"])
                P.dma("sp", gvb[:, :], gv_d, w=["gvb"])
                pi = 0
                for tile in range(5):
                    n = tile_norm(tile, l, 0, b, ss_ps, "ss", dst=xt)
                    t0 = tile * 512
                    for c in range(4):
                        ps = pp[pi % 2]; pk = ("pp", pi % 2); pi += 1
                        fm_proj(ps, pk, lambda kc, c=c: wu[:, kc, c * 128:(c + 1) * 128], ["wu"],
                                lambda kc: xt[:, kc, :n], [("xt", kc) for kc in range(8)], n)
                        P.op("act", lambda e, ps=ps, c=c: e.activation(out=uT[:, c, :n], in_=ps[:, :n], func=AF.Gelu), r=[pk], w=[("uT", c)])
                    if upto == "c_u":
                        P.flush(); return
                    for s_ in range(n // 128):
                        ps = pp[pi % 2]; pk = ("pp", pi % 2); pi += 1
                        for kc in range(8):
                            P.op("pe", lambda e, ps=ps, kc=kc, s_=s_: e.matmul(ps[:, :], xt[:, kc, s_ * 128:(s_ + 1) * 128], wva[:, kc, :],
                                                                                 start=(kc == 0), stop=(kc == 7)),
                                 r=[("xt", kc), "wva"], w=[pk])
                        gp = gi % 2; gi += 1
                        vg, vn, vln, tt, stats, mv, g_ps = vg2[gp], vn2[gp], vln2[gp], tt2[gp], stats2[gp], mv2[gp], g_ps2[gp]
                        kv, kn, kl, kt, kst, kmv, kg = ("vg", gp), ("vn", gp), ("vln", gp), ("tt", gp), ("stats", gp), ("mv", gp), ("g_ps", gp)
                        P.op("act", lambda e, ps=ps: e.activation(out=vg[:, :], in_=ps[:, :], func=AF.Gelu), r=[pk], w=[kv])
                        P.op("dve", lambda e: e.bn_stats(out=stats[:, :], in_=vg[:, :]), r=[kv], w=[kst])
                        P.op("dve", lambda e: e.bn_aggr(out=mv[:, :], in_=stats[:, :]), r=[kst], w=[kmv])
                        P.op("act", lambda e: e.activation(out=mv[:, 1:2], in_=mv[:, 1:2], func=AF.Sqrt, scale=1.0, bias=eps_t[:, 0:1]), r=[kmv], w=[kmv])
                        P.op("dve", lambda e: e.reciprocal(out=mv[:, 1:2], in_=mv[:, 1:2]), r=[kmv], w=[kmv])
                        P.op("dve", lambda e: e.tensor_scalar(out=vn[:, :], in0=vg[:, :], scalar1=mv[:, 0:1], scalar2=mv[:, 1:2],
                                                              op0=ALU.subtract, op1=ALU.mult), r=[kv, kmv], w=[kn])
                        P.op("dve", lambda e: e.tensor_tensor(out=vln[:, :], in0=vn[:, :], in1=gvb[:, :], op=ALU.mult), r=[kn, "gvb"], w=[kl])
                        if upto == "c_va":
                            P.flush(); return
                        for g in range(4):
                            P.op("pe", lambda e, g=g: e.matmul(g_ps[:, g, :], vln[:, g * 128:(g + 1) * 128], wsT[:, g, :], start=True, stop=True),
                                 r=[kl, "wsT"], w=[kg])
                        P.op("dve", lambda e: e.tensor_tensor(out=tt[:, :, :], in0=g_ps[:, :, :], in1=bsb[:, :].rearrange("p (g i) -> p g i", g=4), op=ALU.add),
                             r=[kg, "bsb"], w=[kt])
                        P.op("dve", lambda e, s_=s_: e.tensor_tensor(out=yaT[:, :, s_ * 128:(s_ + 1) * 128], in0=tt[:, :, :],
                                                                     in1=uT[:, :, s_ * 128:(s_ + 1) * 128], op=ALU.mult),
                             r=[kt] + [("uT", c) for c in range(4)], w=[("yaT", s_)])
                        if upto == "c_gate":
                            P.flush(); return

                    def get_ps(dc, tile=tile, n=n, t0=t0):
                        nonlocal pi
                        ps = pp[pi % 2]; pk = ("pp", pi % 2); pi += 1
                        rk = [("yaT", s_) for s_ in range(n // 128)]
                        for kc in range(8):
                            rhs = yaT[:, kc, :n] if kc < 4 else ybT[:, kc - 4, t0:t0 + n]
                            rr = rk if kc < 4 else [("ybT", tile, kc - 4, 0), ("ybT", tile, kc - 4, 64)]
                            P.op("pe", lambda e, ps=ps, kc=kc, rhs=rhs, dc=dc: e.matmul(ps[:, :n], wo[:, kc, dc * 128:(dc + 1) * 128], rhs,
                                                                                         start=(kc == 0), stop=(kc == 7)),
                                 r=["wo"] + rr, w=[pk])
                        return ps[:, :n], pk
                    post_norm_residual(tile, l, 0, b, get_ps, ss_ps, "ss", yo)
                P.flush()

    def qk_process(ps, pkey, n, gcol, rope, dst, dkey, cs, R):
        (sq_, sqk), (bps, bk), (sw, swk), (rs, rsk) = R["sq"], R["bps"], R["sw"], R["rstd"]
        (qn, qnk), (qg, qgk), (t1, t1k), (t2, t2k) = R["qn"], R["qg"], R["t1"], R["t2"]
        P.op("act", lambda e: e.activation(out=sq_[:, :n], in_=ps, func=AF.Square), r=[pkey], w=[sqk])
        P.op("pe", lambda e: e.matmul(bps[:, :n], bones_bf[:, :], sq_[:, :n], start=True, stop=True), r=[sqk, "bones"], w=[bk])
        P.op("act", lambda e: e.activation(out=rs[:, :n], in_=bps[:, :n], func=AF.Ln, scale=1.0 / 64, bias=eps_t[:, 0:1]), r=[bk], w=[rsk])
        P.op("act", lambda e: e.activation(out=rs[:, :n], in_=rs[:, :n], func=AF.Exp, scale=-0.5), r=[rsk], w=[rsk])
        P.op("dve", lambda e: e.tensor_tensor(out=qn[:, :n], in0=ps, in1=rs[:, :n], op=ALU.mult), r=[pkey, rsk], w=[qnk])
        if not rope:
            P.op("act", lambda e: e.activation(out=dst, in_=qn[:, :n], func=AF.Identity, scale=qkg[:, gcol:gcol + 1]), r=[qnk], w=[dkey])
            return
        P.op("act", lambda e: e.activation(out=qg[:, :n], in_=qn[:, :n], func=AF.Identity, scale=qkg[:, gcol:gcol + 1]), r=[qnk], w=[qgk])
        P.op("pe", lambda e: e.matmul(sw[:, :n], perm_bf[:, :], qg[:, :n], start=True, stop=True), r=[qgk, "perm"], w=[swk])
        P.op("dve", lambda e: e.tensor_tensor(out=t1[:, :n], in0=qg[:, :n], in1=cs[:, 0, :n], op=ALU.mult), r=[qgk, "cs"], w=[t1k])
        P.op("dve", lambda e: e.tensor_tensor(out=t2[:, :n], in0=sw[:, :n], in1=cs[:, 1, :n], op=ALU.mult), r=[swk, "cs"], w=[t2k])
        P.op("dve", lambda e: e.tensor_tensor(out=dst, in0=t1[:, :n], in1=t2[:, :n], op=ALU.add), r=[t1k, t2k], w=[dkey])

    def mixer_odd(b, l):
        with ExitStack() as es0:
            kT = sb(es0, "kT", (128, 2, NT), BF16)
            vaug = sb(es0, "vaug", (128, 18, 4, 128), BF16)
            cs = sb(es0, "cs", (128, 2, 512))
            xt = sb(es0, "xt", (128, 8, 512), BF16)
            qg2 = [sb(es0, f"qg{i}", (128, 512), BF16) for i in range(2)]
            with ExitStack() as es:
                wk = sb(es, "wk", (128, 8, 256), BF16)
                wv = sb(es, "wv", (128, 8, 256), BF16)
                ss_ps = pst(es, "ss_ps", (128, 512))
                pp = [pst(es, f"pp{i}", (128, 512)) for i in range(2)]
                bps2 = [pst(es, f"bps{i}", (128, 512)) for i in range(2)]
                sw2 = [pst(es, f"sw{i}", (128, 512)) for i in range(2)]
                scr = sb(es, "scr", (128, 8, 512))
                RR = [dict(sq=(sq[i], ("sq", i)), bps=(bps2[i], ("bps", i)), sw=(sw2[i], ("sw", i)), rstd=(scr[:, 6 + i, :], ("scr", 6 + i)),
                           qn=(scr[:, i, :], ("scr", i)), qg=(qg2[i], ("qg", i)), t1=(scr[:, 2 + i, :], ("scr", 2 + i)),
                           t2=(scr[:, 4 + i, :], ("scr", 4 + i))) for i in range(2)]
                qi = 0
                P.dma("pool", wk[:, :, :], wk_d, w=["wk"])
                P.dma("pool", wv[:, :, :], wv_d, w=["wv"])
                P.op("dve", lambda e: e.memset(vaug[:, :, :, :], 1.0), w=[("vaug", t) for t in range(18)])
                pi = 0
                for tile in range(5):
                    n = tile_norm(tile, l, 0, b, ss_ps, "ss", dst=xt)
                    t0 = tile * 512
                    if tile < 4:
                        P.dma("sp", cs[:, 0, :], cos_d[:, t0:t0 + 512], w=["cs"], slot="cs0")
                        P.dma("sp", cs[:, 1, :], sin_d[:, t0:t0 + 512], w=["cs"], slot="cs1")
                    for c in range(2):
                        ps = pp[pi % 2]; pk = ("pp", pi % 2); pi += 1
                        fm_proj(ps, pk, lambda kc, c=c: wk[:, kc, c * 128:(c + 1) * 128], ["wk"],
                                lambda kc: xt[:, kc, :n], [("xt", kc) for kc in range(8)], n)
                        qk_process(ps[:, :n], pk, n, 1, tile < 4, kT[:, c, t0:t0 + n], ("kT", tile, c), cs, RR[qi % 2]); qi += 1
                    for s_ in range(n // 128):
                        ps = pp[pi % 2]; pk = ("pp", pi % 2); pi += 1
                        ti = tile * 4 + s_
                        for kc in range(8):
                            P.op("pe", lambda e, ps=ps, kc=kc, s_=s_: e.matmul(ps[:, 0:256], xt[:, kc, s_ * 128:(s_ + 1) * 128], wv[:, kc, :],
                                                                                 start=(kc == 0), stop=(kc == 7)),
                                 r=[("xt", kc), "wv"], w=[pk])
                        for g in range(4):
                            P.op("dve", lambda e, ps=ps, ti=ti, g=g: e.tensor_copy(out=vaug[:, ti, g, 0:64], in_=ps[:, g * 64:(g + 1) * 64]),
                                 r=[pk], w=[("vaug", ti)])
                P.flush()
            with ExitStack() as es:
                wq = sb(es, "wq", (128, 8, D), BF16)
                wo = sb(es, "wo", (128, 8, D), BF16)
                qT = sb(es, "qT", (128, 8, 512), BF16)
                yT = sb(es, "yT", (128, 8, 512), BF16)
                yo = sb(es, "yo", (128, 8, 512))
                pT = [sb(es, f"pT{i}", (128, GRP, 512), BF16) for i in range(2)]
                rec = sb(es, "rec", (128, 512))
                s_ps = [pst(es, f"s_ps{i}", (128, GRP, 512)) for i in range(2)]
                ss_ps = pst(es, "ss_ps", (128, 512))
                pp = [pst(es, "pp0", (128, 512))]
                o_ps = [pst(es, f"o_ps{i}", (128, 512)) for i in range(2)]
                RR = [dict(sq=(sq[i], ("sq", i)), bps=(s_ps[i][:, 0, :], ("sps", i)), sw=(o_ps[i], ("ops", i)), rstd=(yo[:, 6 + i, :], ("yo", 6 + i)),
                           qn=(yo[:, i, :], ("yo", i)), qg=(qg2[i], ("qg", i)), t1=(yo[:, 2 + i, :], ("yo", 2 + i)),
                           t2=(yo[:, 4 + i, :], ("yo", 4 + i))) for i in range(2)]
                qi = 0
                P.dma("pool", wq[:, :, :], wq_d, w=["wq"])
                P.dma("pool", wo[:, :, :], woc_d, w=["wo"])
                acnt = [0, 0]
                pi = 0
                for tile in range(4):
                    n = tile_norm(tile, l, 0, b, ss_ps, "ss", dst=xt)
                    t0 = tile * 512
                    tagc = []
                    P.dma("sp", cs[:, 0, :], cos_d[:, t0:t0 + 512], w=["cs"], slot="cs0")
                    P.dma("sp", cs[:, 1, :], sin_d[:, t0:t0 + 512], w=["cs"], slot="cs1")
                    for c in range(8):
                        ps = pp[0]; pk = ("pp", 0); pi += 1
                        fm_proj(ps, pk, lambda kc, c=c: wq[:, kc, c * 128:(c + 1) * 128], ["wq"],
                                lambda kc: xt[:, kc, :n], [("xt", kc) for kc in range(8)], n)
                        qk_process(ps[:, :n], pk, n, 0, True, qT[:, c, :n], ("qT", c), cs, RR[qi % 2]); qi += 1
                    for c in range(8):
                        for s2 in range(2):
                            hp = s2 * 64
                            g = 2 * (c // 4) + s2
                            kc_, = [g // 2]
                            keys = []
                            for t in range(18):
                                keys.append((kT[hp:hp + 64, kc_, t * 128:(t + 1) * 128], ("kT", t // 4, kc_), vaug[:, t, g, :], ("vaug", t), None, None))
                            attention(qT[hp:hp + 64, c, :n], ("qT", c), n, keys, 0.125, s_ps, o_ps, pT, None, rec,
                                      yT[hp:hp + 64, c, :n], ("yT", c, hp), hp, tagc)
                    attn_flush(tagc, s_ps, o_ps, pT, None, rec, acnt)

                    def get_ps(dc, n=n):
                        nonlocal pi
                        ps = pp[0]; pk = ("pp", 0); pi += 1
                        for kc in range(8):
                            P.op("pe", lambda e, ps=ps, kc=kc, dc=dc: e.matmul(ps[:, :n], wo[:, kc, dc * 128:(dc + 1) * 128], yT[:, kc, :n],
                                                                               start=(kc == 0), stop=(kc == 7)),
                                 r=["wo", ("yT", kc, 0), ("yT", kc, 64)], w=[pk])
                        return ps[:, :n], pk
                    post_norm_residual(tile, l, 0, b, get_ps, ss_ps, "ss", yo)
                P.flush()

    def ffn(b, l, with_ctx, upto=None):
        passes = [(0, 0, 1024), (0, 1024, 1024)] + ([(1, 0, 256)] if with_ctx else [])
        for (isctx, s0, T) in passes:
            with ExitStack() as es:
                xf = sb(es, "xf", (128, 8, 1026), BF16)
                actT = sb(es, "actT", (128, NCH, 1024), BF16)
                ca = sb(es, "ca", (128, 1024))
                cg = sb(es, "cg", (128, 1024))
                wub = [sb(es, f"wub{i}", (128, 2, 8, 128), BF16) for i in range(3)]
                wdb = [sb(es, f"wdb{i}", (128, NCH, 128), BF16) for i in range(2)]
                yo = sb(es, "yo", (128, 8, 512))
                ss_ps = pst(es, "ss_ps", (128, 512))
                a_ps = pst(es, "a_ps", (128, 1536))
                g_ps = pst(es, "g_ps", (128, 1536))
                pp = [pst(es, "ppd", (128, 512))]
                v = 2 if isctx else b
                A_, B_ = modA[:, l, 1, v, :], modB[:, l, 1, v, :]
                Sq = LC if isctx else S

                def hap(kc, a, n):
                    return hcT[:, kc, a:a + n] if isctx else hT[:, kc, a:a + n]

                def hk(a):
                    return 4 if isctx else a // 512
                segs = []
                if s0 > 0:
                    P.op("dve", lambda e: e.tensor_copy(out=xf[:, :, 0:1], in_=xhalo[:, :, 0:1]), r=["xhalo"], w=[("xf", "l")])
                else:
                    P.op("dve", lambda e: e.memset(xf[:, :, 0:1], 0.0), w=[("xf", "l")])
                for a in range(s0, s0 + T, 512):
                    nn = min(512, s0 + T - a)
                    segs.append((a, nn, 1 + a - s0))
                if s0 + T < Sq:
                    segs.append((s0 + T, 1, 1 + T))
                else:
                    P.op("dve", lambda e: e.memset(xf[:, :, 1 + T:2 + T], 0.0), w=[("xf", "r")])
                xkeys = []
                for (a, nn, col) in segs:
                    key = ("xf", a)
                    xkeys.append(key)
                    norm_mod(lambda kc, a=a, nn=nn: hap(kc, a, nn), [hkey(hk(a), kc) for kc in range(8)], nn, A_, B_,
                             lambda kc, col=col, nn=nn: xf[:, kc, col:col + nn], [key] * 8, ss_ps, "ss")
                xall = xkeys + [("xf", "l"), ("xf", "r")]
                if s0 + T < Sq:
                    P.op("dve", lambda e: e.tensor_copy(out=xhalo[:, :, 0:1], in_=xf[:, :, T:T + 1]), r=xkeys, w=["xhalo"])
                if upto == "f_norm":
                    P.flush(); return
                for j in range(NCH):
                    wb_ = wub[j % 3]
                    wkey = ("wub", j % 3)
                    P.dma("pool", wb_[:, 0, :, :], wup_d[l, j], w=[wkey], slot=("wub", j % 3, 0))
                    P.dma("pool", wb_[:, 1, :, :], wup_d[l, NCH + j], w=[wkey], slot=("wub", j % 3, 1))
                    for part, (ps_, pkey, cdst, ckey) in enumerate(((a_ps, "a_ps", ca, "ca"), (g_ps, "g_ps", cg, "cg"))):
                        ch = part * NCH + j
                        for c0 in range(0, T + 2, 512):
                            c1 = min(c0 + 512, T + 2)
                            for kc in range(8):
                                P.op("pe", lambda e, ps_=ps_, kc=kc, c0=c0, c1=c1, part=part, wb_=wb_: e.matmul(
                                    ps_[:, c0:c1], wb_[:, part, kc, :], xf[:, kc, c0:c1], start=(kc == 0), stop=(kc == 7)),
                                    r=[wkey] + xall, w=[pkey])
                        w0, w1, w2 = (cwT[:, l, ch, i:i + 1] for i in range(3))
                        P.op("act", lambda e, ps_=ps_, cdst=cdst, w1=w1, ch=ch: e.activation(out=cdst[:, :T], in_=ps_[:, 1:T + 1], func=AF.Identity,
                                                                                               scale=w1, bias=cbT[:, l, ch:ch + 1]),
                             r=[pkey], w=[ckey])
                        P.op("dve", lambda e, ps_=ps_, cdst=cdst, w0=w0: e.scalar_tensor_tensor(out=cdst[:, 0:T], in0=ps_[:, 0:T], scalar=w0,
                                                                                                  in1=cdst[:, 0:T], op0=ALU.mult, op1=ALU.add),
                             r=[pkey, ckey], w=[ckey])
                        P.op("dve", lambda e, ps_=ps_, cdst=cdst, w2=w2: e.scalar_tensor_tensor(out=cdst[:, 0:T], in0=ps_[:, 2:T + 2], scalar=w2,
                                                                                                  in1=cdst[:, 0:T], op0=ALU.mult, op1=ALU.add),
                             r=[pkey, ckey], w=[ckey])
                    P.op("act", lambda e: e.activation(out=cg[:, :T], in_=cg[:, :T], func=AF.Silu), r=["cg"], w=["cg"])
                    P.op("dve", lambda e, j=j: e.tensor_tensor(out=actT[:, j, :T], in0=cg[:, :T], in1=ca[:, :T], op=ALU.mult),
                         r=["cg", "ca"], w=[("actT", j)])
                    if upto == "f_up1":
                        P.flush(); return
                if upto == "f_up":
                    P.flush(); return
                pi = 0
                di = 0
                for a in range(0, T, 512):
                    nn = min(512, T - a)
                    tile = 4 if isctx else (s0 + a) // 512

                    def get_ps(dc, a=a, nn=nn):
                        nonlocal pi, di
                        wd_ = wdb[di % 2]; wdk = ("wdb", di % 2); di += 1
                        P.dma("pool", wd_[:, 0:11, :], wdn_d[l, dc][:, 0:11, :], w=[wdk], slot=("wdb", (di - 1) % 2, 0))
                        P.dma("pool", wd_[:, 11:22, :], wdn_d[l, dc][:, 11:22, :], w=[wdk], slot=("wdb", (di - 1) % 2, 1))
                        ps = pp[0]; pk = ("pp", 0); pi += 1
                        for kc in range(NCH):
                            P.op("pe", lambda e, ps=ps, kc=kc, wd_=wd_: e.matmul(ps[:, :nn], wd_[:, kc, :], actT[:, kc, a:a + nn],
                                                                                 start=(kc == 0), stop=(kc == NCH - 1)),
                                 r=[wdk, ("actT", kc)], w=[pk])
                        return ps[:, :nn], pk
                    post_norm_residual(tile, l, 1, b, get_ps, ss_ps, "ss", yo)
                P.flush()

    for b in range(nb):
        for kc in range(8):
            P.dma("sp", hT[:, kc, :], xT_d[b, :, kc, :], w=[hkey(t, kc) for t in range(4)], slot=("ld", kc))
        P.dma("sp", hcT[:, :, :], ctxT_d[b], w=[hkey(4, kc) for kc in range(8)], slot="ldc")
        P.flush()
        if "mix0a" in stages:
            mixer_even(b, 0, upto="a", halves=(0,))
        if "mix0b" in stages:
            mixer_even(b, 0, upto="b", halves=(0,))
        for st_ in ("c_u", "c_va", "c_gate"):
            if st_ in stages:
                mixer_even(b, 0, upto=st_, halves=())
        if "mix0ab" in stages:
            mixer_even(b, 0, upto="b")
        if "mix0c" in stages:
            mixer_even(b, 0, halves=())
        if "mix0" in stages:
            mixer_even(b, 0)
        for st_ in ("f_norm", "f_up1", "f_up"):
            if st_ in stages:
                ffn(b, 0, with_ctx=True, upto=st_)
        if "ffn0" in stages:
            ffn(b, 0, with_ctx=True)
        if "mix1" in stages:
            mixer_odd(b, 1)
        if "ffn1" in stages:
            ffn(b, 1, with_ctx=False)
        if dbg:
            P.dma("sp", hc_out[b], hcT[:, :, :], r=[hkey(4, kc) for kc in range(8)], slot="sthc")
        for kc in range(8):
            P.dma("sp", out_d[b, :, kc, :], hT[:, kc, :], r=[hkey(t, kc) for t in range(4)], slot=("st", kc))
        P.flush()
    ES.close()
    return nc


def _prep_shared(inp):
    f = lambda a: np.ascontiguousarray(np.asarray(a, dtype=np.float32))
    sh = {}
    w_ada = f(inp["w_ada"])
    sh["w_ada"] = np.ascontiguousarray(w_ada.reshape(2, 8, 128, 6 * D).transpose(0, 2, 1, 3))
    sh["b_adaT"] = np.ascontiguousarray(f(inp["b_ada"]).reshape(2, 48, 128).transpose(0, 2, 1))
    sh["gT"] = np.ascontiguousarray(f(inp["norm_g"]).reshape(2, 4, 8, 128).transpose(0, 1, 3, 2))
    sh["w_in"] = _lay_w(f(inp["w_in_ab"])[0])
    sh["w_out_ab"] = _lay_w(f(inp["w_out_ab"])[0])
    wqkv = f(inp["w_qkv_c"])[0]
    qcols = []
    for c in range(8):
        for s in range(2):
            h = HMAP[c][s]
            qcols.extend(range(h * 64, (h + 1) * 64))
    sh["w_q"] = _lay_w(wqkv[:, qcols])
    sh["w_k"] = _lay_w(wqkv[:, 1024:1280])
    sh["w_v"] = _lay_w(wqkv[:, 1280:1536])
    sh["w_out_c"] = _lay_w(f(inp["w_out_c"])[0][qcols, :])
    sh["w_sT"] = np.ascontiguousarray(f(inp["a_w_s"])[0].transpose(2, 0, 1))
    sh["b_s_bc"] = np.ascontiguousarray(np.broadcast_to(f(inp["a_b_s"])[0].reshape(1, 512), (128, 512)))
    sh["gv_bc"] = np.ascontiguousarray(np.broadcast_to(f(inp["a_v_g"])[0].reshape(1, 512), (128, 512)))
    bias, index, plan = _build_na_bias(f(inp["b_rpb"])[0])
    sh["bias_na"] = bias
    sh["qg2"] = np.ascontiguousarray(np.tile(f(inp["c_q_g"])[0], 2).reshape(128, 1))
    sh["kg2"] = np.ascontiguousarray(np.tile(f(inp["c_k_g"])[0], 2).reshape(128, 1))
    wup = f(inp["w_up"])
    sh["w_up"] = np.ascontiguousarray(wup.reshape(2, 8, 128, 2 * NCH, 128).transpose(0, 3, 2, 1, 4))
    sh["conv_wT"] = np.ascontiguousarray(f(inp["conv_w"]).reshape(2, 3, 2 * NCH, 128).transpose(0, 3, 2, 1))
    sh["conv_bT"] = np.ascontiguousarray(f(inp["conv_b"]).reshape(2, 2 * NCH, 128).transpose(0, 2, 1))
    wdn = f(inp["w_down"])
    sh["w_down"] = np.ascontiguousarray(wdn.reshape(2, NCH, 128, 8, 128).transpose(0, 3, 2, 1, 4))
    cosT, sinT, perm = _rope_tables()
    sh["cosT"], sh["sinT"], sh["perm"] = cosT, sinT, perm
    return sh, index, plan


def _lay_act(a):
    nb_, L, _ = a.shape
    return np.ascontiguousarray(a.transpose(0, 2, 1).reshape(nb_, 8, 128, L).transpose(0, 2, 1, 3))


def kernel(**inp):
    x = np.asarray(inp["x"], np.float32)
    c = np.asarray(inp["c"], np.float32)
    ctx = np.asarray(inp["ctx"], np.float32)
    c_ctx = np.asarray(inp["c_ctx"], np.float32)
    sh, index, plan = _prep_shared(inp)
    n_cores = 8
    nb = x.shape[0] // n_cores
    nc = build_program(sh["bias_na"].shape[0], index, plan, nb=nb)
    in_maps = []
    for i in range(n_cores):
        m = dict(sh)
        sl = slice(i * nb, (i + 1) * nb)
        m["xT"] = _lay_act(x[sl])
        m["ctxT"] = _lay_act(ctx[sl])
        cv = np.stack([c[i * nb], c[i * nb + 1], c_ctx], -1)
        m["cT"] = np.ascontiguousarray(cv.reshape(8, 128, 3).transpose(1, 0, 2))
        in_maps.append(m)
    res = run_bass_kernel_spmd(nc, in_maps, core_ids=list(range(n_cores)))
    out = np.empty_like(x)
    for i in range(n_cores):
        o = res.results[i]["outT"]
        out[i * nb:(i + 1) * nb] = o.transpose(0, 3, 2, 1).reshape(nb, S, D)
    return out
```

```python
import numpy as np
from contextlib import ExitStack
import concourse.bass as bass
import concourse.mybir as mybir
from concourse.bass_utils import run_bass_kernel_spmd

F32 = mybir.dt.float32
BF16 = mybir.dt.bfloat16
AF = mybir.ActivationFunctionType
ALU = mybir.AluOpType

D = 1024
S = 2048
LC = 256
NT = S + LC
DFF = 2816
EPS = 1e-6
NCH = 22


import types


def _freeze(fn, depth=0):
    if not isinstance(fn, types.FunctionType) or fn.__closure__ is None or depth > 4:
        return fn
    cells = []
    for c in fn.__closure__:
        try:
            v = c.cell_contents
        except ValueError:
            cells.append(c)
            continue
        if isinstance(v, types.FunctionType):
            v = _freeze(v, depth + 1)
        cells.append(types.CellType(v))
    g = types.FunctionType(fn.__code__, fn.__globals__, fn.__name__, fn.__defaults__, tuple(cells))
    g.__kwdefaults__ = fn.__kwdefaults__
    return g


class Prog:
    def __init__(self, nc):
        self.nc = nc
        self.eng = {"pe": nc.tensor, "act": nc.scalar, "dve": nc.vector, "pool": nc.gpsimd, "sp": nc.sync}
        self.sem = {e: nc.alloc_semaphore(name=f"sem_{e}") for e in self.eng}
        self.cnt = {e: 0 for e in self.eng}
        self.dsem = {}
        self.waited = {e: {} for e in self.eng}
        self.ops = []
        self.last_w = {}
        self.readers = {}
        self.last_of_eng = {}
        self.last_dma = {}

    def op(self, eng, fn, r=(), w=(), dma=None):
        idx = len(self.ops)
        deps = {}
        for k in r:
            d = self.last_w.get(k)
            if d is not None:
                deps[d] = "raw"
        for k in w:
            d = self.last_w.get(k)
            if d is not None:
                deps[d] = "raw"
            lastr = {}
            for rd in self.readers.get(k, ()):
                o_ = self.ops[rd]
                if o_["dma"] is not None:
                    lastr[("d", rd)] = rd
                else:
                    lastr[o_["eng"]] = rd
            for rd in lastr.values():
                if rd not in deps:
                    deps[rd] = "war"
        deps.pop(idx, None)
        self.ops.append(dict(eng=eng, fn=_freeze(fn), deps=deps, dma=dma, sig=None))
        for k in r:
            self.readers.setdefault(k, []).append(idx)
        for k in w:
            self.last_w[k] = idx
            self.readers[k] = []
        if dma is None:
            self.last_of_eng[eng] = idx
        else:
            self.last_dma[dma] = idx
        return idx

    def dma(self, eng, out, in_, r=(), w=(), slot=None):
        if slot is None:
            slot = w[0] if w else "out"
        return self.op(eng, lambda e: e.dma_start(out=out, in_=in_), r=r, w=w, dma=slot)

    def flush(self):
        alld = {}
        for e, i in self.last_of_eng.items():
            alld[i] = "raw"
        for s, i in self.last_dma.items():
            alld[i] = "raw"
        for e in self.eng:
            self.ops.append(dict(eng=e, fn=None, deps=dict(alld), dma=None, sig=None, barrier=True))
        ops = self.ops
        need = [False] * len(ops)
        for i, o in enumerate(ops):
            fd = []
            for d, kind in o["deps"].items():
                od = ops[d]
                if od["dma"] is None and od["eng"] == o["eng"] and not o.get("barrier"):
                    if o["eng"] == "pe" or kind == "war":
                        continue
                if od["dma"] is None and od["eng"] == o["eng"] and o.get("barrier"):
                    continue
                fd.append(d)
            latest = {}
            fd2 = []
            for d in fd:
                od = ops[d]
                if od["dma"] is None:
                    if d > latest.get(od["eng"], -1):
                        latest[od["eng"]] = d
                else:
                    fd2.append(d)
            fd2.extend(latest.values())
            for d in latest.values():
                need[d] = True
            o["fd"] = fd2
        for i, o in enumerate(ops):
            E = o["eng"]
            eo = self.eng[E]
            waits = {}
            for d in o["fd"]:
                key, val = ops[d]["sig"]
                if waits.get(key, 0) < val:
                    waits[key] = val
            for key, val in waits.items():
                if self.waited[E].get(key, 0) < val:
                    h = self.sem[key[1]] if key[0] == "e" else self.dsem[key[1]][0]
                    eo.wait_ge(h, val)
                    self.waited[E][key] = val
            if o["fn"] is None:
                continue
            ins = o["fn"](eo)
            if o["dma"] is not None:
                slot = o["dma"]
                if slot not in self.dsem:
                    self.dsem[slot] = [self.nc.alloc_semaphore(name=f"dma_{len(self.dsem)}"), 0]
                s = self.dsem[slot]
                s[1] += 16
                ins.then_inc(s[0], 16)
                o["sig"] = (("d", slot), s[1])
            elif need[i]:
                self.cnt[E] += 1
                ins.then_inc(self.sem[E], 1)
                o["sig"] = (("e", E), self.cnt[E])
        self.ops = []
        self.last_w = {}
        self.readers = {}
        self.last_of_eng = {}
        self.last_dma = {}


def _na_tile_plan():
    plan = []
    for A in range(4):
        rows = list(range(8 * A, 8 * A + 8))
        lo = min(max(r - 4, 0) if r - 4 <= 24 else 24 for r in rows)
        lo = min(min(max(r - 4, 0), 24) for r in rows)
        hi = max(min(max(r - 4, 0), 24) + 7 for r in rows)
        t0, t1 = lo // 2, hi // 2
        plan.append(list(range(t0, t1 + 1)))
    return plan


def _build_na_bias(rpb):
    plan = _na_tile_plan()
    NEG = np.float32(-30000.0)
    tiles = []
    index = {}
    qr_l = np.arange(8)[:, None]
    qc = np.arange(64)[None, :]
    for A in range(4):
        qr = (8 * A + qr_l) + 0 * qc
        qcc = 0 * qr_l + qc
        r0 = np.clip(qr - 4, 0, 24)
        c0 = np.clip(qcc - 8, 0, 48)
        qr_f, qc_f, r0_f, c0_f = [a.reshape(-1) for a in (qr, qcc, r0, c0)]
        for h in range(8):
            for j, t in enumerate(plan[A]):
                kr = np.repeat(np.arange(2 * t, 2 * t + 2), 64)
                kc = np.tile(np.arange(64), 2)
                valid = ((kr[:, None] >= r0_f[None, :]) & (kr[:, None] < r0_f[None, :] + 8)
                         & (kc[:, None] >= c0_f[None, :]) & (kc[:, None] < c0_f[None, :] + 16))
                dr = np.clip(kr[:, None] - qr_f[None, :] + 7, 0, 14)
                dc = np.clip(kc[:, None] - qc_f[None, :], -15, 15) + 15
                b = rpb[h][dr, dc]
                tiles.append(np.where(valid, b, NEG).astype(np.float32))
                index[(A, h, j)] = len(tiles) - 1
    return np.stack(tiles, 0), index, plan


def _rope_tables():
    t = np.arange(S)
    inv = (10000.0 ** (-np.arange(16, dtype=np.float32) / 16)).astype(np.float32)
    rows = (t // 64).astype(np.float32)[:, None] * inv
    cols = (t % 64).astype(np.float32)[:, None] * inv
    ang = np.concatenate([rows, cols], -1)
    cos = np.cos(ang).astype(np.float32).T
    sin = np.sin(ang).astype(np.float32).T
    cosT = np.tile(cos, (4, 1))
    sinT = np.concatenate([-sin, sin, -sin, sin], 0)
    perm = np.zeros((128, 128), np.float32)
    for m in range(128):
        k = m + 32 if (m % 64) < 32 else m - 32
        perm[k, m] = 1.0
    return np.ascontiguousarray(cosT), np.ascontiguousarray(sinT), perm


HMAP = [[(4 * (2 * (c // 4)) + (c % 4)), (4 * (2 * (c // 4) + 1) + (c % 4))] for c in range(8)]


def _lay_w(w):
    K, N = w.shape
    return np.ascontiguousarray(w.reshape(K // 128, 128, N).transpose(1, 0, 2))


def build_program(n_bias_tiles, bias_index, na_plan, nb=2, stages=("mix0", "ffn0", "mix1", "ffn1"), dbg=False):
    nc = bass.Bass("TRN2", target_bir_lowering=False)
    P = Prog(nc)

    def din(name, shape):
        return nc.dram_tensor(name, list(shape), F32, kind="ExternalInput").ap()

    xT_d = din("xT", (nb, 128, 8, S))
    ctxT_d = din("ctxT", (nb, 128, 8, LC))
    out_d = nc.dram_tensor("outT", [nb, 128, 8, S], F32, kind="ExternalOutput").ap()
    cT_d = din("cT", (128, 8, 3))
    wada_d = din("w_ada", (2, 128, 8, 6 * D))
    badaT_d = din("b_adaT", (2, 128, 48))
    gT_d = din("gT", (2, 4, 128, 8))
    win_d = din("w_in", (128, 8, 2560))
    woab_d = din("w_out_ab", (128, 8, D))
    wq_d = din("w_q", (128, 8, D))
    wk_d = din("w_k", (128, 8, 256))
    wv_d = din("w_v", (128, 8, 256))
    woc_d = din("w_out_c", (128, 8, D))
    wsT_d = din("w_sT", (128, 4, 128))
    bs_d = din("b_s_bc", (128, 512))
    gv_d = din("gv_bc", (128, 512))
    bias_d = din("bias_na", (n_bias_tiles, 128, 512))
    qg_d = din("qg2", (128, 1))
    kg_d = din("kg2", (128, 1))
    wup_d = din("w_up", (2, 2 * NCH, 128, 8, 128))
    cw_d = din("conv_wT", (2, 128, 2 * NCH, 3))
    cb_d = din("conv_bT", (2, 128, 2 * NCH))
    wdn_d = din("w_down", (2, 8, 128, NCH, 128))
    cos_d = din("cosT", (128, S))
    sin_d = din("sinT", (128, S))
    perm_d = din("perm", (128, 128))

    ES = ExitStack()
    hc_out = nc.dram_tensor("hcT_out", [nb, 128, 8, LC], F32, kind="ExternalOutput").ap() if dbg else None

    uid = [0]

    def sb(es, name, shape, dt=F32):
        uid[0] += 1
        return es.enter_context(nc.sbuf_tensor(f"{name}_{uid[0]}", list(shape), dt))

    def pst(es, name, shape):
        uid[0] += 1
        return es.enter_context(nc.psum_tensor(f"{name}_{uid[0]}", list(shape), F32))

    hT = sb(ES, "hT", (128, 8, S))
    hcT = sb(ES, "hcT", (128, 8, LC))
    ones_bf = sb(ES, "ones_bf", (128, 128), BF16)
    bones_bf = sb(ES, "bones_bf", (128, 128), BF16)
    perm_bf = sb(ES, "perm_bf", (128, 128), BF16)
    modA = sb(ES, "modA", (128, 2, 2, 3, 8))
    modB = sb(ES, "modB", (128, 2, 2, 3, 8))
    modG = sb(ES, "modG", (128, 2, 2, 3, 8))
    qkg = sb(ES, "qkg", (128, 2))
    cwT = sb(ES, "cwT", (128, 2, 2 * NCH, 3))
    cbT = sb(ES, "cbT", (128, 2, 2 * NCH))
    sq = [sb(ES, f"sq{i}", (128, 512), BF16) for i in range(2)]
    rstd = sb(ES, "rstd", (128, 512))
    tmp = [sb(ES, f"tmp{i}", (128, 512)) for i in range(2)]
    xhalo = sb(ES, "xhalo", (128, 8, 1), BF16)

    def hsrc(tile):
        if tile < 4:
            return (lambda kc, a=0, n=512: hT[:, kc, tile * 512 + a: tile * 512 + a + n]), 512
        return (lambda kc, a=0, n=256: hcT[:, kc, a:a + n]), 256

    def hkey(tile, kc):
        return ("h", tile, kc)

    def norm_mod(src, skeys, n, A, B, dst, dkeys, ss_ps, ss_key):
        for kc in range(8):
            s_ = sq[kc % 2]
            P.op("act", lambda e, kc=kc, s_=s_: e.activation(out=s_[:, :n], in_=src(kc), func=AF.Square),
                 r=[skeys[kc]], w=[("sq", kc % 2)])
            P.op("pe", lambda e, kc=kc, s_=s_: e.matmul(ss_ps[:, :n], ones_bf[:, :], s_[:, :n], start=(kc == 0), stop=(kc == 7)),
                 r=[("sq", kc % 2)], w=[ss_key])
        P.op("act", lambda e: e.activation(out=rstd[:, :n], in_=ss_ps[:, :n], func=AF.Ln, scale=1.0 / D, bias=eps_t[:, 0:1]),
             r=[ss_key], w=["rstd"])
        P.op("act", lambda e: e.activation(out=rstd[:, :n], in_=rstd[:, :n], func=AF.Exp, scale=-0.5), r=["rstd"], w=["rstd"])
        for kc in range(8):
            t_ = tmp[kc % 2]
            P.op("dve", lambda e, kc=kc, t_=t_: e.tensor_tensor(out=t_[:, :n], in0=src(kc), in1=rstd[:, :n], op=ALU.mult),
                 r=[skeys[kc], "rstd"], w=[("tmp", kc % 2)])
            P.op("act", lambda e, kc=kc, t_=t_: e.activation(out=dst(kc), in_=t_[:, :n], func=AF.Identity,
                                                              scale=A[:, kc:kc + 1], bias=B[:, kc:kc + 1]),
                 r=[("tmp", kc % 2)], w=[dkeys[kc]])

    def tile_norm(tile, l, mf, b, ss_ps, ss_key, dst=None, dkeys=None):
        src, n = hsrc(tile)
        v = b if tile < 4 else 2
        if dkeys is None:
            xt_ = dst
            dst = lambda kc: xt_[:, kc, :n]
            dkeys = [("xt", kc) for kc in range(8)]
        norm_mod(lambda kc: src(kc), [hkey(tile, kc) for kc in range(8)], n,
                 modA[:, l, mf, v, :], modB[:, l, mf, v, :], dst, dkeys, ss_ps, ss_key)
        return n

    def fm_proj(ps, pkey, w_ap, wkeys, rhs, rkeys, n, nk=8):
        for kc in range(nk):
            P.op("pe", lambda e, kc=kc: e.matmul(ps[:, :n], w_ap(kc), rhs(kc), start=(kc == 0), stop=(kc == nk - 1)),
                 r=list(wkeys) + [rkeys[kc]], w=[pkey])

    def post_norm_residual(tile, l, mf, b, get_ps, ss_ps, ss_key, yo):
        src, n = hsrc(tile)
        v = b if tile < 4 else 2
        G = modG[:, l, mf, v, :]
        for dc in range(8):
            ps, pkey = get_ps(dc)
            s_ = sq[dc % 2]
            P.op("act", lambda e, ps=ps, dc=dc: e.activation(out=yo[:, dc, :n], in_=ps, func=AF.Copy), r=[pkey], w=[("yo", dc)])
            P.op("act", lambda e, ps=ps, s_=s_: e.activation(out=s_[:, :n], in_=ps, func=AF.Square), r=[pkey], w=[("sq", dc % 2)])
            P.op("pe", lambda e, dc=dc, s_=s_: e.matmul(ss_ps[:, :n], ones_bf[:, :], s_[:, :n], start=(dc == 0), stop=(dc == 7)),
                 r=[("sq", dc % 2)], w=[ss_key])
        P.op("act", lambda e: e.activation(out=rstd[:, :n], in_=ss_ps[:, :n], func=AF.Ln, scale=1.0 / D, bias=eps_t[:, 0:1]),
             r=[ss_key], w=["rstd"])
        P.op("act", lambda e: e.activation(out=rstd[:, :n], in_=rstd[:, :n], func=AF.Exp, scale=-0.5), r=["rstd"], w=["rstd"])
        for dc in range(8):
            t_ = tmp[dc % 2]
            P.op("dve", lambda e, dc=dc, t_=t_: e.tensor_tensor(out=t_[:, :n], in0=yo[:, dc, :n], in1=rstd[:, :n], op=ALU.mult),
                 r=[("yo", dc), "rstd"], w=[("tmp", dc % 2)])
            P.op("dve", lambda e, dc=dc, t_=t_: e.scalar_tensor_tensor(out=src(dc), in0=t_[:, :n], scalar=G[:, dc:dc + 1], in1=src(dc),
                                                                        op0=ALU.mult, op1=ALU.add),
                 r=[("tmp", dc % 2), hkey(tile, dc)], w=[hkey(tile, dc)])

    def attention(q_ap, qkey, n_q, keys, scale, s_ps, o_ps, pT, sbt, rec, out_ap, okey, hp, tagc):
        tagc.append(dict(q_ap=q_ap, qkey=qkey, n_q=n_q, keys=keys, scale=scale, out_ap=out_ap, okey=okey))

    GRP = 2

    def attn_flush(jobs, s_ps, o_ps, pT, sbt, rec, cnt, bias_sb=None):
        units = []
        gsz = s_ps[0].shape[1]
        for ji, J in enumerate(jobs):
            J["ob"] = (cnt[0] + ji) % 2
            ks = J["keys"]
            i = 0
            while i < len(ks):
                g = [i]
                while len(g) < gsz and i + len(g) < len(ks) and (ks[i + len(g)][4] is None) == (ks[i][4] is None):
                    g.append(i + len(g))
                units.append((J, g))
                i += len(g)
        cnt[0] += len(jobs)
        groups = []
        for k, (J, g) in enumerate(units):
            bsp = J["keys"][g[0]][4]
            if bsp is not None and (not groups or groups[-1] != bsp[0]):
                groups.append(bsp[0])
        gpos = {g: n_ for n_, g in enumerate(groups)}
        issued = [0]

        def issue_groups(upto):
            while issued[0] < min(upto, len(groups)):
                (bi_, i0, ln) = groups[issued[0]]
                bb = bias_sb[bi_ % 3]
                P.dma("sp", bb[:, 0:ln, :], bias_d[i0:i0 + ln].rearrange("t p q -> p t q"), w=[("bias", bi_ % 3)])
                issued[0] += 1

        def emit_S(k):
            J, g = units[k]
            sbk = (cnt[1] + k) % len(s_ps)
            sp = s_ps[sbk]
            n_q, q_ap = J["n_q"], J["q_ap"]
            for gi_, i in enumerate(g):
                k_ap, kkey = J["keys"][i][0], J["keys"][i][1]
                P.op("pe", lambda e: e.matmul(sp[:, gi_, :n_q], k_ap, q_ap, start=True, stop=True), r=[kkey, J["qkey"]], w=[("sps", sbk)])

        def emit_post_pv(k):
            J, g = units[k]
            ng = len(g)
            sbk = (cnt[1] + k) % len(s_ps)
            sp, p_ = s_ps[sbk], pT[sbk]
            n_q, scale, nk, ob = J["n_q"], J["scale"], len(J["keys"]), J["ob"]
            ops_ = o_ps[ob]
            b0 = J["keys"][g[0]][4]
            if b0 is not None:
                (gid, jj) = b0
                issue_groups(gpos[gid] + 3)
                b_ap = bias_sb[gid[0] % 3][:, jj:jj + ng, :n_q]
                bkey = ("bias", gid[0] % 3)
                sb_ = sbt[sbk]
                P.op("dve", lambda e: e.scalar_tensor_tensor(out=sb_[:, 0:ng, :n_q], in0=sp[:, 0:ng, :n_q], scalar=float(scale), in1=b_ap,
                                                             op0=ALU.mult, op1=ALU.add),
                     r=[("sps", sbk), bkey], w=[("sbt", sbk)])
                P.op("act", lambda e: e.activation(out=p_[:, 0:ng, :n_q], in_=sb_[:, 0:ng, :n_q], func=AF.Exp), r=[("sbt", sbk)], w=[("pT", sbk)])
            else:
                P.op("act", lambda e: e.activation(out=p_[:, 0:ng, :n_q], in_=sp[:, 0:ng, :n_q], func=AF.Exp, scale=float(scale)),
                     r=[("sps", sbk)], w=[("pT", sbk)])
            for gi_, i in enumerate(g):
                v_ap, vkey = J["keys"][i][2], J["keys"][i][3]
                P.op("pe", lambda e: e.matmul(ops_[:, :n_q], v_ap, p_[:, gi_, :n_q], start=(i == 0), stop=(i == nk - 1)),
                     r=[vkey, ("pT", sbk)], w=[("ops", ob)])
            if g[-1] == nk - 1:
                out_ap = J["out_ap"]
                P.op("act", lambda e: e.activation(out=rec[64:128, :n_q], in_=ops_[64:128, :n_q], func=AF.Ln), r=[("ops", ob)], w=["rec"])
                P.op("act", lambda e: e.activation(out=rec[64:128, :n_q], in_=rec[64:128, :n_q], func=AF.Exp, scale=-1.0), r=["rec"], w=["rec"])
                P.op("dve", lambda e: e.tensor_tensor(out=out_ap, in0=ops_[0:64, :n_q], in1=rec[64:128, :n_q], op=ALU.mult),
                     r=[("ops", ob), "rec"], w=[J["okey"]])

        if bias_sb is not None:
            issue_groups(2)
        look = len(s_ps) - 1
        for k in range(min(look, len(units))):
            emit_S(k)
        for k in range(len(units)):
            if k + look < len(units):
                emit_S(k + look)
            emit_post_pv(k)
        cnt[1] += len(units)

    eps_t = sb(ES, "eps_t", (128, 1))
    with ExitStack() as es:
        wa = [sb(es, f"wa{i}", (128, 8, 1024), BF16) for i in range(2)]
        scb = sb(es, "scb", (128, 8, 3), BF16)
        scT = sb(es, "scT", (128, 8, 3))
        sgm = sb(es, "sgm", (128, 8, 3))
        mod = sb(es, "mod", (128, 2, 48, 3))
        bT = sb(es, "bT", (128, 2, 48))
        gTt = sb(es, "gTt", (128, 2, 4, 8))
        permf = sb(es, "permf", (128, 128))
        aps = pst(es, "aps", (128, 8, 3))
        P.op("dve", lambda e: e.memset(ones_bf[:, :], 1.0), w=["ones"])
        P.op("dve", lambda e: e.memset(bones_bf[:, :], 0.0), w=["bones"])
        P.op("dve", lambda e: e.memset(bones_bf[0:64, 0:64], 1.0), r=[], w=["bones"])
        P.op("dve", lambda e: e.memset(bones_bf[64:128, 64:128], 1.0), r=[], w=["bones"])
        P.op("dve", lambda e: e.memset(eps_t[:, :], EPS), w=["eps"])
        P.dma("sp", permf[:, :], perm_d, w=["permf"])
        P.op("dve", lambda e: e.tensor_copy(out=perm_bf[:, :], in_=permf[:, :]), r=["permf"], w=["perm"])
        P.dma("sp", scT[:, :, :], cT_d, w=["scT"])
        P.dma("sp", qkg[:, 0:1], qg_d, w=["qg"])
        P.dma("sp", qkg[:, 1:2], kg_d, w=["kg"])
        P.dma("sp", cwT[:, 0], cw_d[0], w=["cw0"])
        P.dma("sp", cwT[:, 1], cw_d[1], w=["cw1"])
        P.dma("sp", cbT[:, 0], cb_d[0], w=["cb0"])
        P.dma("sp", cbT[:, 1], cb_d[1], w=["cb1"])
        for l in range(2):
            P.dma("sp", bT[:, l, :], badaT_d[l], w=[("bT", l)])
            for i in range(4):
                P.dma("sp", gTt[:, l, i, :], gT_d[l, i], w=[("gT", l, i)])
        P.op("act", lambda e: e.activation(out=sgm[:, :, :], in_=scT[:, :, :], func=AF.Sigmoid), r=["scT"], w=["sgm"])
        P.op("dve", lambda e: e.tensor_tensor(out=scb[:, :, :], in0=scT[:, :, :], in1=sgm[:, :, :], op=ALU.mult), r=["scT", "sgm"], w=["scb"])
        it = 0
        for l in range(2):
            for grp in range(6):
                wbuf = wa[it % 2]
                P.dma("pool", wbuf[:, :, :], wada_d[l][:, :, grp * 1024:(grp + 1) * 1024], w=[("wa", it % 2)])
                for j in range(8):
                    for kc in range(8):
                        P.op("pe", lambda e, wbuf=wbuf, j=j, kc=kc: e.matmul(aps[:, j, :], wbuf[:, kc, j * 128:(j + 1) * 128], scb[:, kc, :],
                                                                               start=(kc == 0), stop=(kc == 7)),
                             r=[("wa", it % 2), "scb"], w=[("aps", j)])
                for v in range(3):
                    P.op("dve", lambda e, l=l, grp=grp, v=v: e.tensor_tensor(out=mod[:, l, grp * 8:(grp + 1) * 8, v], in0=aps[:, :, v],
                                                                              in1=bT[:, l, grp * 8:(grp + 1) * 8], op=ALU.add),
                         r=[("aps", j) for j in range(8)] + [("bT", l)], w=[("mod", l, grp, v)])
                it += 1
        for l in range(2):
            for mf in range(2):
                for v in range(3):
                    ish, isc, igt = 3 * mf, 3 * mf + 1, 3 * mf + 2
                    P.op("dve", lambda e, l=l, mf=mf, v=v, isc=isc: e.scalar_tensor_tensor(
                        out=modA[:, l, mf, v, :], in0=mod[:, l, isc * 8:(isc + 1) * 8, v], scalar=1.0, in1=gTt[:, l, 2 * mf, :],
                        op0=ALU.add, op1=ALU.mult), r=[("mod", l, isc, v), ("gT", l, 2 * mf)], w=[("modA", l, mf, v)])
                    P.op("dve", lambda e, l=l, mf=mf, v=v, ish=ish: e.tensor_copy(out=modB[:, l, mf, v, :], in_=mod[:, l, ish * 8:(ish + 1) * 8, v]),
                         r=[("mod", l, ish, v)], w=[("modB", l, mf, v)])
                    P.op("dve", lambda e, l=l, mf=mf, v=v, igt=igt: e.tensor_tensor(
                        out=modG[:, l, mf, v, :], in0=mod[:, l, igt * 8:(igt + 1) * 8, v], in1=gTt[:, l, 2 * mf + 1, :], op=ALU.mult),
                        r=[("mod", l, igt, v), ("gT", l, 2 * mf + 1)], w=[("modG", l, mf, v)])
        P.flush()

    def mixer_even(b, l, upto="c", halves=(0, 1)):
        with ExitStack() as es0:
            ybT = sb(es0, "ybT", (128, 4, NT), BF16)
            esx = ExitStack()
            xta = sb(esx, "xta", (128, 8, NT), BF16)
            if len(halves):
                with ExitStack() as es:
                    ss_ps = pst(es, "ss_ps", (128, 512))
                    for tile in range(5):
                        t0 = tile * 512
                        tile_norm(tile, l, 0, b, ss_ps, "ss", dst=(lambda kc, t0=t0, tile=tile: xta[:, kc, t0:t0 + (512 if tile < 4 else 256)]),
                                  dkeys=[("xta", tile, kc) for kc in range(8)])
                    P.flush()
            for hh in halves:
                with ExitStack() as es1:
                    kT = sb(es1, "kT", (128, 2, NT), BF16)
                    vaug = sb(es1, "vaug", (128, 18, 4, 128), BF16)
                    with ExitStack() as es:
                        wk = sb(es, "wk", (128, 8, 256), BF16)
                        wv = sb(es, "wv", (128, 8, 256), BF16)
                        ss_ps = pst(es, "ss_ps", (128, 512))
                        pp = [pst(es, f"pp{i}", (128, 512)) for i in range(2)]
                        P.dma("pool", wk[:, :, :], win_d[:, :, 1536 + hh * 256:1536 + (hh + 1) * 256], w=["wk"])
                        P.dma("pool", wv[:, :, :], win_d[:, :, 2048 + hh * 256:2048 + (hh + 1) * 256], w=["wv"])
                        P.op("dve", lambda e: e.memset(vaug[:, :, :, :], 1.0), w=[("vaug", t) for t in range(18)])
                        pi = 0
                        for tile in range(5):
                            n = 512 if tile < 4 else 256
                            t0 = tile * 512
                            for c in range(2):
                                ps = pp[pi % 2]; pk = ("pp", pi % 2); pi += 1
                                fm_proj(ps, pk, lambda kc, c=c: wk[:, kc, c * 128:(c + 1) * 128], ["wk"],
                                        lambda kc: xta[:, kc, t0:t0 + n], [("xta", tile, kc) for kc in range(8)], n)
                                P.op("act", lambda e, ps=ps, c=c: e.activation(out=kT[:, c, t0:t0 + n], in_=ps[:, :n], func=AF.Copy),
                                     r=[pk], w=[("kT", tile, c)])
                            for s_ in range(n // 128):
                                ps = pp[pi % 2]; pk = ("pp", pi % 2); pi += 1
                                ti = tile * 4 + s_
                                for kc in range(8):
                                    P.op("pe", lambda e, ps=ps, kc=kc, s_=s_: e.matmul(ps[:, 0:256], xta[:, kc, t0 + s_ * 128:t0 + (s_ + 1) * 128], wv[:, kc, :],
                                                                                         start=(kc == 0), stop=(kc == 7)),
                                         r=[("xta", tile, kc), "wv"], w=[pk])
                                for hq in range(4):
                                    P.op("dve", lambda e, ps=ps, ti=ti, hq=hq: e.tensor_copy(out=vaug[:, ti, hq, 0:64], in_=ps[:, hq * 64:(hq + 1) * 64]),
                                         r=[pk], w=[("vaug", ti)])
                        P.flush()
                    if upto == "a":
                        continue
                    with ExitStack() as es:
                        wq = sb(es, "wq", (128, 8, 256), BF16)
                        qT = sb(es, "qT", (128, 2, 512), BF16)
                        bias_sb = [sb(es, f"bias{i}", (128, 2, 512)) for i in range(3)]
                        pT = [sb(es, f"pT{i}", (128, GRP, 512), BF16) for i in range(3)]
                        sbt = [sb(es, f"sbt{i}", (128, GRP, 512)) for i in range(3)]
                        rec = sb(es, "rec", (128, 512))
                        s_ps = [pst(es, f"s_ps{i}", (128, GRP, 512)) for i in range(3)]
                        pp = [s_ps[2][:, 0, :]] * 2
                        o_ps = [pst(es, f"o_ps{i}", (128, 512)) for i in range(2)]
                        P.dma("pool", wq[:, :, :], win_d[:, :, 1024 + hh * 256:1024 + (hh + 1) * 256], w=["wq"])
                        acnt = [0, 0]
                        pi = 0
                        bi = 0
                        for tile in range(5):
                            n = 512 if tile < 4 else 256
                            t0 = tile * 512
                            for c in range(2):
                                ps = pp[0]; pk = ("sps", 2); pi += 1
                                fm_proj(ps, pk, lambda kc, c=c: wq[:, kc, c * 128:(c + 1) * 128], ["wq"],
                                        lambda kc: xta[:, kc, t0:t0 + n], [("xta", tile, kc) for kc in range(8)], n)
                                P.op("act", lambda e, ps=ps, c=c: e.activation(out=qT[:, c, :n], in_=ps[:, :n], func=AF.Copy), r=[pk], w=[("qT", c)])
                            tagc = []
                            for hq in range(4):
                                h = 4 * hh + hq
                                c, hp = hq // 2, (hq % 2) * 64
                                keys = []
                                if tile < 4:
                                    kts = na_plan[tile]
                                    for j0 in range(0, len(kts), 2):
                                        grp = kts[j0:j0 + 2]
                                        i0 = bias_index[(tile, h, j0)]
                                        gid = (bi, i0, len(grp)); bi += 1
                                        for jj, t in enumerate(grp):
                                            keys.append((kT[hp:hp + 64, c, t * 128:(t + 1) * 128], ("kT", t // 4, c), vaug[:, t, hq, :], ("vaug", t),
                                                         (gid, jj), None))
                                for t in (16, 17):
                                    keys.append((kT[hp:hp + 64, c, t * 128:(t + 1) * 128], ("kT", 4, c), vaug[:, t, hq, :], ("vaug", t), None, None))
                                attention(qT[hp:hp + 64, c, :n], ("qT", c), n, keys, 0.125, s_ps, o_ps, pT, sbt, rec,
                                          ybT[hp:hp + 64, 2 * hh + c, t0:t0 + n], ("ybT", tile, 2 * hh + c, hp), hp, tagc)
                            attn_flush(tagc, s_ps, o_ps, pT, sbt, rec, acnt, bias_sb)
                        P.flush()
            esx.close()
            if upto in ("a", "b"):
                return
            with ExitStack() as es:
                xt = sb(es, "xt", (128, 8, 512), BF16)
                wu = sb(es, "wu", (128, 8, 512), BF16)
                wva = sb(es, "wva", (128, 8, 512), BF16)
                wo = sb(es, "wo", (128, 8, D), BF16)
                wsT = sb(es, "wsT", (128, 4, 128), BF16)
                bsb = sb(es, "bsb", (128, 512))
                gvb = sb(es, "gvb", (128, 512))
                uT = sb(es, "uT", (128, 4, 512), BF16)
                vg2 = [sb(es, f"vg{i}", (128, 512)) for i in range(2)]
                vn2 = [sb(es, f"vn{i}", (128, 512)) for i in range(2)]
                vln2 = [sb(es, f"vln{i}", (128, 512), BF16) for i in range(2)]
                tt2 = [sb(es, f"tt{i}", (128, 4, 128)) for i in range(2)]
                yaT = sb(es, "yaT", (128, 4, 512), BF16)
                yo = sb(es, "yo", (128, 8, 512))
                stats2 = [sb(es, f"stats{i}", (128, 6)) for i in range(2)]
                mv2 = [sb(es, f"mv{i}", (128, 2)) for i in range(2)]
                ss_ps = pst(es, "ss_ps", (128, 512))
                pp = [pst(es, f"pp{i}", (128, 512)) for i in range(2)]
                g_ps2 = [pst(es, f"g_ps{i}", (128, 4, 128)) for i in range(2)]
                gi = 0
                P.dma("pool", wu[:, :, :], win_d[:, :, 0:512], w=["wu"])
                P.dma("pool", wva[:, :, :], win_d[:, :, 512:1024], w=["wva"])
                P.dma("pool", wo[:, :, :], woab_d, w=["wo"])
                P.dma("pool", wsT[:, :, :], wsT_d, w=["wsT"])
                P.dma("sp", bsb[:, :], bs_d, w=["bsb"])
                P.dma("sp", gvb[:, :], gv_d, w=["gvb"])
                pi = 0
                for tile in range(5):
                    n = tile_norm(tile, l, 0, b, ss_ps, "ss", dst=xt)
                    t0 = tile * 512
                    for c in range(4):
                        ps = pp[pi % 2]; pk = ("pp", pi % 2); pi += 1
                        fm_proj(ps, pk, lambda kc, c=c: wu[:, kc, c * 128:(c + 1) * 128], ["wu"],
                                lambda kc: xt[:, kc, :n], [("xt", kc) for kc in range(8)], n)
                        P.op("act", lambda e, ps=ps, c=c: e.activation(out=uT[:, c, :n], in_=ps[:, :n], func=AF.Gelu), r=[pk], w=[("uT", c)])
                    if upto == "c_u":
                        P.flush(); return
                    for s_ in range(n // 128):
                        ps = pp[pi % 2]; pk = ("pp", pi % 2); pi += 1
                        for kc in range(8):
                            P.op("pe", lambda e, ps=ps, kc=kc, s_=s_: e.matmul(ps[:, :], xt[:, kc, s_ * 128:(s_ + 1) * 128], wva[:, kc, :],
                                                                                 start=(kc == 0), stop=(kc == 7)),
                                 r=[("xt", kc), "wva"], w=[pk])
                        gp = gi % 2; gi += 1
                        vg, vn, vln, tt, stats, mv, g_ps = vg2[gp], vn2[gp], vln2[gp], tt2[gp], stats2[gp], mv2[gp], g_ps2[gp]
                        kv, kn, kl, kt, kst, kmv, kg = ("vg", gp), ("vn", gp), ("vln", gp), ("tt", gp), ("stats", gp), ("mv", gp), ("g_ps", gp)
                        P.op("act", lambda e, ps=ps: e.activation(out=vg[:, :], in_=ps[:, :], func=AF.Gelu), r=[pk], w=[kv])
                        P.op("dve", lambda e: e.bn_stats(out=stats[:, :], in_=vg[:, :]), r=[kv], w=[kst])
                        P.op("dve", lambda e: e.bn_aggr(out=mv[:, :], in_=stats[:, :]), r=[kst], w=[kmv])
                        P.op("act", lambda e: e.activation(out=mv[:, 1:2], in_=mv[:, 1:2], func=AF.Sqrt, scale=1.0, bias=eps_t[:, 0:1]), r=[kmv], w=[kmv])
                        P.op("dve", lambda e: e.reciprocal(out=mv[:, 1:2], in_=mv[:, 1:2]), r=[kmv], w=[kmv])
                        P.op("dve", lambda e: e.tensor_scalar(out=vn[:, :], in0=vg[:, :], scalar1=mv[:, 0:1], scalar2=mv[:, 1:2],
                                                              op0=ALU.subtract, op1=ALU.mult), r=[kv, kmv], w=[kn])
                        P.op("dve", lambda e: e.tensor_tensor(out=vln[:, :], in0=vn[:, :], in1=gvb[:, :], op=ALU.mult), r=[kn, "gvb"], w=[kl])
                        if upto == "c_va":
                            P.flush(); return
                        for g in range(4):
                            P.op("pe", lambda e, g=g: e.matmul(g_ps[:, g, :], vln[:, g * 128:(g + 1) * 128], wsT[:, g, :], start=True, stop=True),
                                 r=[kl, "wsT"], w=[kg])
                        P.op("dve", lambda e: e.tensor_tensor(out=tt[:, :, :], in0=g_ps[:, :, :], in1=bsb[:, :].rearrange("p (g i) -> p g i", g=4), op=ALU.add),
                             r=[kg, "bsb"], w=[kt])
                        P.op("dve", lambda e, s_=s_: e.tensor_tensor(out=yaT[:, :, s_ * 128:(s_ + 1) * 128], in0=tt[:, :, :],
                                                                     in1=uT[:, :, s_ * 128:(s_ + 1) * 128], op=ALU.mult),
                             r=[kt] + [("uT", c) for c in range(4)], w=[("yaT", s_)])
                        if upto == "c_gate":
                            P.flush(); return

                    def get_ps(dc, tile=tile, n=n, t0=t0):
                        nonlocal pi
                        ps = pp[pi % 2]; pk = ("pp", pi % 2); pi += 1
                        rk = [("yaT", s_) for s_ in range(n // 128)]
                        for kc in range(8):
                            rhs = yaT[:, kc, :n] if kc < 4 else ybT[:, kc - 4, t0:t0 + n]
                            rr = rk if kc < 4 else [("ybT", tile, kc - 4, 0), ("ybT", tile, kc - 4, 64)]
                            P.op("pe", lambda e, ps=ps, kc=kc, rhs=rhs, dc=dc: e.matmul(ps[:, :n], wo[:, kc, dc * 128:(dc + 1) * 128], rhs,
                                                                                         start=(kc == 0), stop=(kc == 7)),
                                 r=["wo"] + rr, w=[pk])
                        return ps[:, :n], pk
                    post_norm_residual(tile, l, 0, b, get_ps, ss_ps, "ss", yo)
                P.flush()

    def qk_process(ps, pkey, n, gcol, rope, dst, dkey, cs, R):
        (sq_, sqk), (bps, bk), (sw, swk), (rs, rsk) = R["sq"], R["bps"], R["sw"], R["rstd"]
        (qn, qnk), (qg, qgk), (t1, t1k), (t2, t2k) = R["qn"], R["qg"], R["t1"], R["t2"]
        P.op("act", lambda e: e.activation(out=sq_[:, :n], in_=ps, func=AF.Square), r=[pkey], w=[sqk])
        P.op("pe", lambda e: e.matmul(bps[:, :n], bones_bf[:, :], sq_[:, :n], start=True, stop=True), r=[sqk, "bones"], w=[bk])
        P.op("act", lambda e: e.activation(out=rs[:, :n], in_=bps[:, :n], func=AF.Ln, scale=1.0 / 64, bias=eps_t[:, 0:1]), r=[bk], w=[rsk])
        P.op("act", lambda e: e.activation(out=rs[:, :n], in_=rs[:, :n], func=AF.Exp, scale=-0.5), r=[rsk], w=[rsk])
        P.op("dve", lambda e: e.tensor_tensor(out=qn[:, :n], in0=ps, in1=rs[:, :n], op=ALU.mult), r=[pkey, rsk], w=[qnk])
        if not rope:
            P.op("act", lambda e: e.activation(out=dst, in_=qn[:, :n], func=AF.Identity, scale=qkg[:, gcol:gcol + 1]), r=[qnk], w=[dkey])
            return
        P.op("act", lambda e: e.activation(out=qg[:, :n], in_=qn[:, :n], func=AF.Identity, scale=qkg[:, gcol:gcol + 1]), r=[qnk], w=[qgk])
        P.op("pe", lambda e: e.matmul(sw[:, :n], perm_bf[:, :], qg[:, :n], start=True, stop=True), r=[qgk, "perm"], w=[swk])
        P.op("dve", lambda e: e.tensor_tensor(out=t1[:, :n], in0=qg[:, :n], in1=cs[:, 0, :n], op=ALU.mult), r=[qgk, "cs"], w=[t1k])
        P.op("dve", lambda e: e.tensor_tensor(out=t2[:, :n], in0=sw[:, :n], in1=cs[:, 1, :n], op=ALU.mult), r=[swk, "cs"], w=[t2k])
        P.op("dve", lambda e: e.tensor_tensor(out=dst, in0=t1[:, :n], in1=t2[:, :n], op=ALU.add), r=[t1k, t2k], w=[dkey])

    def mixer_odd(b, l):
        with ExitStack() as es0:
            kT = sb(es0, "kT", (128, 2, NT), BF16)
            vaug = sb(es0, "vaug", (128, 18, 4, 128), BF16)
            cs = sb(es0, "cs", (128, 2, 512))
            xt = sb(es0, "xt", (128, 8, 512), BF16)
            qg2 = [sb(es0, f"qg{i}", (128, 512), BF16) for i in range(2)]
            with ExitStack() as es:
                wk = sb(es, "wk", (128, 8, 256), BF16)
                wv = sb(es, "wv", (128, 8, 256), BF16)
                ss_ps = pst(es, "ss_ps", (128, 512))
                pp = [pst(es, f"pp{i}", (128, 512)) for i in range(2)]
                bps2 = [pst(es, f"bps{i}", (128, 512)) for i in range(2)]
                sw2 = [pst(es, f"sw{i}", (128, 512)) for i in range(2)]
                scr = sb(es, "scr", (128, 8, 512))
                RR = [dict(sq=(sq[i], ("sq", i)), bps=(bps2[i], ("bps", i)), sw=(sw2[i], ("sw", i)), rstd=(scr[:, 6 + i, :], ("scr", 6 + i)),
                           qn=(scr[:, i, :], ("scr", i)), qg=(qg2[i], ("qg", i)), t1=(scr[:, 2 + i, :], ("scr", 2 + i)),
                           t2=(scr[:, 4 + i, :], ("scr", 4 + i))) for i in range(2)]
                qi = 0
                P.dma("pool", wk[:, :, :], wk_d, w=["wk"])
                P.dma("pool", wv[:, :, :], wv_d, w=["wv"])
                P.op("dve", lambda e: e.memset(vaug[:, :, :, :], 1.0), w=[("vaug", t) for t in range(18)])
                pi = 0
                for tile in range(5):
                    n = tile_norm(tile, l, 0, b, ss_ps, "ss", dst=xt)
                    t0 = tile * 512
                    if tile < 4:
                        P.dma("sp", cs[:, 0, :], cos_d[:, t0:t0 + 512], w=["cs"], slot="cs0")
                        P.dma("sp", cs[:, 1, :], sin_d[:, t0:t0 + 512], w=["cs"], slot="cs1")
                    for c in range(2):
                        ps = pp[pi % 2]; pk = ("pp", pi % 2); pi += 1
                        fm_proj(ps, pk, lambda kc, c=c: wk[:, kc, c * 128:(c + 1) * 128], ["wk"],
                                lambda kc: xt[:, kc, :n], [("xt", kc) for kc in range(8)], n)
                        qk_process(ps[:, :n], pk, n, 1, tile < 4, kT[:, c, t0:t0 + n], ("kT", tile, c), cs, RR[qi % 2]); qi += 1
                    for s_ in range(n // 128):
                        ps = pp[pi % 2]; pk = ("pp", pi % 2); pi += 1
                        ti = tile * 4 + s_
                        for kc in range(8):
                            P.op("pe", lambda e, ps=ps, kc=kc, s_=s_: e.matmul(ps[:, 0:256], xt[:, kc, s_ * 128:(s_ + 1) * 128], wv[:, kc, :],
                                                                                 start=(kc == 0), stop=(kc == 7)),
                                 r=[("xt", kc), "wv"], w=[pk])
                        for g in range(4):
                            P.op("dve", lambda e, ps=ps, ti=ti, g=g: e.tensor_copy(out=vaug[:, ti, g, 0:64], in_=ps[:, g * 64:(g + 1) * 64]),
                                 r=[pk], w=[("vaug", ti)])
                P.flush()
            with ExitStack() as es:
                wq = sb(es, "wq", (128, 8, D), BF16)
                wo = sb(es, "wo", (128, 8, D), BF16)
                qT = sb(es, "qT", (128, 8, 512), BF16)
                yT = sb(es, "yT", (128, 8, 512), BF16)
                yo = sb(es, "yo", (128, 8, 512))
                pT = [sb(es, f"pT{i}", (128, GRP, 512), BF16) for i in range(2)]
                rec = sb(es, "rec", (128, 512))
                s_ps = [pst(es, f"s_ps{i}", (128, GRP, 512)) for i in range(2)]
                ss_ps = pst(es, "ss_ps", (128, 512))
                pp = [pst(es, "pp0", (128, 512))]
                o_ps = [pst(es, f"o_ps{i}", (128, 512)) for i in range(2)]
                RR = [dict(sq=(sq[i], ("sq", i)), bps=(s_ps[i][:, 0, :], ("sps", i)), sw=(o_ps[i], ("ops", i)), rstd=(yo[:, 6 + i, :], ("yo", 6 + i)),
                           qn=(yo[:, i, :], ("yo", i)), qg=(qg2[i], ("qg", i)), t1=(yo[:, 2 + i, :], ("yo", 2 + i)),
                           t2=(yo[:, 4 + i, :], ("yo", 4 + i))) for i in range(2)]
                qi = 0
                P.dma("pool", wq[:, :, :], wq_d, w=["wq"])
                P.dma("pool", wo[:, :, :], woc_d, w=["wo"])
                acnt = [0, 0]
                pi = 0
                for tile in range(4):
                    n = tile_norm(tile, l, 0, b, ss_ps, "ss", dst=xt)
                    t0 = tile * 512
                    tagc = []
                    P.dma("sp", cs[:, 0, :], cos_d[:, t0:t0 + 512], w=["cs"], slot="cs0")
                    P.dma("sp", cs[:, 1, :], sin_d[:, t0:t0 + 512], w=["cs"], slot="cs1")
                    for c in range(8):
                        ps = pp[0]; pk = ("pp", 0); pi += 1
                        fm_proj(ps, pk, lambda kc, c=c: wq[:, kc, c * 128:(c + 1) * 128], ["wq"],
                                lambda kc: xt[:, kc, :n], [("xt", kc) for kc in range(8)], n)
                        qk_process(ps[:, :n], pk, n, 0, True, qT[:, c, :n], ("qT", c), cs, RR[qi % 2]); qi += 1
                    for c in range(8):
                        for s2 in range(2):
                            hp = s2 * 64
                            g = 2 * (c // 4) + s2
                            kc_, = [g // 2]
                            keys = []
                            for t in range(18):
                                keys.append((kT[hp:hp + 64, kc_, t * 128:(t + 1) * 128], ("kT", t // 4, kc_), vaug[:, t, g, :], ("vaug", t), None, None))
                            attention(qT[hp:hp + 64, c, :n], ("qT", c), n, keys, 0.125, s_ps, o_ps, pT, None, rec,
                                      yT[hp:hp + 64, c, :n], ("yT", c, hp), hp, tagc)
                    attn_flush(tagc, s_ps, o_ps, pT, None, rec, acnt)

                    def get_ps(dc, n=n):
                        nonlocal pi
                        ps = pp[0]; pk = ("pp", 0); pi += 1
                        for kc in range(8):
                            P.op("pe", lambda e, ps=ps, kc=kc, dc=dc: e.matmul(ps[:, :n], wo[:, kc, dc * 128:(dc + 1) * 128], yT[:, kc, :n],
                                                                               start=(kc == 0), stop=(kc == 7)),
                                 r=["wo", ("yT", kc, 0), ("yT", kc, 64)], w=[pk])
                        return ps[:, :n], pk
                    post_norm_residual(tile, l, 0, b, get_ps, ss_ps, "ss", yo)
                P.flush()

    def ffn(b, l, with_ctx, upto=None, store=False):
        passes = [(0, 0, 1024), (0, 1024, 1024)] + ([(1, 0, 256)] if with_ctx else [])
        for (isctx, s0, T) in passes:
            with ExitStack() as es:
                xf = sb(es, "xf", (128, 8, 1026), BF16)
                actT = sb(es, "actT", (128, NCH, 1024), BF16)
                ca = sb(es, "ca", (128, 1024))
                cg = sb(es, "cg", (128, 1024))
                wub = [sb(es, f"wub{i}", (128, 2, 8, 128), BF16) for i in range(3)]
                wdb = [sb(es, f"wdb{i}", (128, NCH, 128), BF16) for i in range(2)]
                yo = sb(es, "yo", (128, 8, 512))
                ss_ps = pst(es, "ss_ps", (128, 512))
                a_ps = pst(es, "a_ps", (128, 1536))
                g_ps = pst(es, "g_ps", (128, 1536))
                pp = [pst(es, "ppd", (128, 512))]
                v = 2 if isctx else b
                A_, B_ = modA[:, l, 1, v, :], modB[:, l, 1, v, :]
                Sq = LC if isctx else S

                def hap(kc, a, n):
                    return hcT[:, kc, a:a + n] if isctx else hT[:, kc, a:a + n]

                def hk(a):
                    return 4 if isctx else a // 512
                segs = []
                if s0 > 0:
                    P.op("dve", lambda e: e.tensor_copy(out=xf[:, :, 0:1], in_=xhalo[:, :, 0:1]), r=["xhalo"], w=[("xf", "l")])
                else:
                    P.op("dve", lambda e: e.memset(xf[:, :, 0:1], 0.0), w=[("xf", "l")])
                for a in range(s0, s0 + T, 512):
                    nn = min(512, s0 + T - a)
                    segs.append((a, nn, 1 + a - s0))
                if s0 + T < Sq:
                    segs.append((s0 + T, 1, 1 + T))
                else:
                    P.op("dve", lambda e: e.memset(xf[:, :, 1 + T:2 + T], 0.0), w=[("xf", "r")])
                xkeys = []
                for (a, nn, col) in segs:
                    key = ("xf", a)
                    xkeys.append(key)
                    norm_mod(lambda kc, a=a, nn=nn: hap(kc, a, nn), [hkey(hk(a), kc) for kc in range(8)], nn, A_, B_,
                             lambda kc, col=col, nn=nn: xf[:, kc, col:col + nn], [key] * 8, ss_ps, "ss")
                xall = xkeys + [("xf", "l"), ("xf", "r")]
                if s0 + T < Sq:
                    P.op("dve", lambda e: e.tensor_copy(out=xhalo[:, :, 0:1], in_=xf[:, :, T:T + 1]), r=xkeys, w=["xhalo"])
                if upto == "f_norm":
                    P.flush(); return
                for j in range(NCH):
                    wb_ = wub[j % 3]
                    wkey = ("wub", j % 3)
                    P.dma("pool", wb_[:, 0, :, :], wup_d[l, j], w=[wkey], slot=("wub", j % 3, 0))
                    P.dma("pool", wb_[:, 1, :, :], wup_d[l, NCH + j], w=[wkey], slot=("wub", j % 3, 1))
                    for part, (ps_, pkey, cdst, ckey) in enumerate(((a_ps, "a_ps", ca, "ca"), (g_ps, "g_ps", cg, "cg"))):
                        ch = part * NCH + j
                        for c0 in range(0, T + 2, 512):
                            c1 = min(c0 + 512, T + 2)
                            for kc in range(8):
                                P.op("pe", lambda e, ps_=ps_, kc=kc, c0=c0, c1=c1, part=part, wb_=wb_: e.matmul(
                                    ps_[:, c0:c1], wb_[:, part, kc, :], xf[:, kc, c0:c1], start=(kc == 0), stop=(kc == 7)),
                                    r=[wkey] + xall, w=[pkey])
                        w0, w1, w2 = (cwT[:, l, ch, i:i + 1] for i in range(3))
                        P.op("act", lambda e, ps_=ps_, cdst=cdst, w1=w1, ch=ch: e.activation(out=cdst[:, :T], in_=ps_[:, 1:T + 1], func=AF.Identity,
                                                                                               scale=w1, bias=cbT[:, l, ch:ch + 1]),
                             r=[pkey], w=[ckey])
                        P.op("dve", lambda e, ps_=ps_, cdst=cdst, w0=w0: e.scalar_tensor_tensor(out=cdst[:, 0:T], in0=ps_[:, 0:T], scalar=w0,
                                                                                                  in1=cdst[:, 0:T], op0=ALU.mult, op1=ALU.add),
                             r=[pkey, ckey], w=[ckey])
                        P.op("dve", lambda e, ps_=ps_, cdst=cdst, w2=w2: e.scalar_tensor_tensor(out=cdst[:, 0:T], in0=ps_[:, 2:T + 2], scalar=w2,
                                                                                                  in1=cdst[:, 0:T], op0=ALU.mult, op1=ALU.add),
                             r=[pkey, ckey], w=[ckey])
                    P.op("act", lambda e: e.activation(out=cg[:, :T], in_=cg[:, :T], func=AF.Silu), r=["cg"], w=["cg"])
                    P.op("dve", lambda e, j=j: e.tensor_tensor(out=actT[:, j, :T], in0=cg[:, :T], in1=ca[:, :T], op=ALU.mult),
                         r=["cg", "ca"], w=[("actT", j)])
                    if upto == "f_up1":
                        P.flush(); return
                if upto == "f_up":
                    P.flush(); return
                pi = 0
                di = 0
                for a in range(0, T, 512):
                    nn = min(512, T - a)
                    tile = 4 if isctx else (s0 + a) // 512

                    def get_ps(dc, a=a, nn=nn):
                        nonlocal pi, di
                        wd_ = wdb[di % 2]; wdk = ("wdb", di % 2); di += 1
                        P.dma("pool", wd_[:, 0:11, :], wdn_d[l, dc][:, 0:11, :], w=[wdk], slot=("wdb", (di - 1) % 2, 0))
                        P.dma("pool", wd_[:, 11:22, :], wdn_d[l, dc][:, 11:22, :], w=[wdk], slot=("wdb", (di - 1) % 2, 1))
                        ps = pp[0]; pk = ("pp", 0); pi += 1
                        for kc in range(NCH):
                            P.op("pe", lambda e, ps=ps, kc=kc, wd_=wd_: e.matmul(ps[:, :nn], wd_[:, kc, :], actT[:, kc, a:a + nn],
                                                                                 start=(kc == 0), stop=(kc == NCH - 1)),
                                 r=[wdk, ("actT", kc)], w=[pk])
                        return ps[:, :nn], pk
                    post_norm_residual(tile, l, 1, b, get_ps, ss_ps, "ss", yo)
                    if store and not isctx:
                        t0_ = tile * 512
                        P.dma("sp", out_d[b, :, :, t0_:t0_ + 512], hT[:, :, t0_:t0_ + 512], r=[hkey(tile, kc) for kc in range(8)], slot=("st", tile))
                P.flush()

    for b in range(nb):
        for kc in range(8):
            P.dma("sp", hT[:, kc, :], xT_d[b, :, kc, :], w=[hkey(t, kc) for t in range(4)], slot=("ld", kc))
        P.dma("sp", hcT[:, :, :], ctxT_d[b], w=[hkey(4, kc) for kc in range(8)], slot="ldc")
        P.flush()
        if "mix0a" in stages:
            mixer_even(b, 0, upto="a", halves=(0,))
        if "mix0b" in stages:
            mixer_even(b, 0, upto="b", halves=(0,))
        for st_ in ("c_u", "c_va", "c_gate"):
            if st_ in stages:
                mixer_even(b, 0, upto=st_, halves=())
        if "mix0ab" in stages:
            mixer_even(b, 0, upto="b")
        if "mix0c" in stages:
            mixer_even(b, 0, halves=())
        if "mix0" in stages:
            mixer_even(b, 0)
        for st_ in ("f_norm", "f_up1", "f_up"):
            if st_ in stages:
                ffn(b, 0, with_ctx=True, upto=st_)
        if "ffn0" in stages:
            ffn(b, 0, with_ctx=True)
        if "mix1" in stages:
            mixer_odd(b, 1)
        if "ffn1" in stages:
            ffn(b, 1, with_ctx=False, store=True)
        if dbg:
            P.dma("sp", hc_out[b], hcT[:, :, :], r=[hkey(4, kc) for kc in range(8)], slot="sthc")
        if "ffn1" not in stages:
            for kc in range(8):
                P.dma("sp", out_d[b, :, kc, :], hT[:, kc, :], r=[hkey(t, kc) for t in range(4)], slot=("st", kc))
        P.flush()
    ES.close()
    return nc


def _prep_shared(inp):
    f = lambda a: np.ascontiguousarray(np.asarray(a, dtype=np.float32))
    sh = {}
    w_ada = f(inp["w_ada"])
    sh["w_ada"] = np.ascontiguousarray(w_ada.reshape(2, 8, 128, 6 * D).transpose(0, 2, 1, 3))
    sh["b_adaT"] = np.ascontiguousarray(f(inp["b_ada"]).reshape(2, 48, 128).transpose(0, 2, 1))
    sh["gT"] = np.ascontiguousarray(f(inp["norm_g"]).reshape(2, 4, 8, 128).transpose(0, 1, 3, 2))
    sh["w_in"] = _lay_w(f(inp["w_in_ab"])[0])
    sh["w_out_ab"] = _lay_w(f(inp["w_out_ab"])[0])
    wqkv = f(inp["w_qkv_c"])[0]
    qcols = []
    for c in range(8):
        for s in range(2):
            h = HMAP[c][s]
            qcols.extend(range(h * 64, (h + 1) * 64))
    sh["w_q"] = _lay_w(wqkv[:, qcols])
    sh["w_k"] = _lay_w(wqkv[:, 1024:1280])
    sh["w_v"] = _lay_w(wqkv[:, 1280:1536])
    sh["w_out_c"] = _lay_w(f(inp["w_out_c"])[0][qcols, :])
    sh["w_sT"] = np.ascontiguousarray(f(inp["a_w_s"])[0].transpose(2, 0, 1))
    sh["b_s_bc"] = np.ascontiguousarray(np.broadcast_to(f(inp["a_b_s"])[0].reshape(1, 512), (128, 512)))
    sh["gv_bc"] = np.ascontiguousarray(np.broadcast_to(f(inp["a_v_g"])[0].reshape(1, 512), (128, 512)))
    bias, index, plan = _build_na_bias(f(inp["b_rpb"])[0])
    sh["bias_na"] = bias
    sh["qg2"] = np.ascontiguousarray(np.tile(f(inp["c_q_g"])[0], 2).reshape(128, 1))
    sh["kg2"] = np.ascontiguousarray(np.tile(f(inp["c_k_g"])[0], 2).reshape(128, 1))
    wup = f(inp["w_up"])
    sh["w_up"] = np.ascontiguousarray(wup.reshape(2, 8, 128, 2 * NCH, 128).transpose(0, 3, 2, 1, 4))
    sh["conv_wT"] = np.ascontiguousarray(f(inp["conv_w"]).reshape(2, 3, 2 * NCH, 128).transpose(0, 3, 2, 1))
    sh["conv_bT"] = np.ascontiguousarray(f(inp["conv_b"]).reshape(2, 2 * NCH, 128).transpose(0, 2, 1))
    wdn = f(inp["w_down"])
    sh["w_down"] = np.ascontiguousarray(wdn.reshape(2, NCH, 128, 8, 128).transpose(0, 3, 2, 1, 4))
    cosT, sinT, perm = _rope_tables()
    sh["cosT"], sh["sinT"], sh["perm"] = cosT, sinT, perm
    return sh, index, plan


def _lay_act(a):
    nb_, L, _ = a.shape
    return np.ascontiguousarray(a.transpose(0, 2, 1).reshape(nb_, 8, 128, L).transpose(0, 2, 1, 3))


def kernel(**inp):
    x = np.asarray(inp["x"], np.float32)
    c = np.asarray(inp["c"], np.float32)
    ctx = np.asarray(inp["ctx"], np.float32)
    c_ctx = np.asarray(inp["c_ctx"], np.float32)
    sh, index, plan = _prep_shared(inp)
    n_cores = 8
    nb = x.shape[0] // n_cores
    nc = build_program(sh["bias_na"].shape[0], index, plan, nb=nb)
    in_maps = []
    for i in range(n_cores):
        m = dict(sh)
        sl = slice(i * nb, (i + 1) * nb)
        m["xT"] = _lay_act(x[sl])
        m["ctxT"] = _lay_act(ctx[sl])
        cv = np.stack([c[i * nb], c[i * nb + 1], c_ctx], -1)
        m["cT"] = np.ascontiguousarray(cv.reshape(8, 128, 3).transpose(1, 0, 2))
        in_maps.append(m)
    res = run_bass_kernel_spmd(nc, in_maps, core_ids=list(range(n_cores)))
    out = np.empty_like(x)
    for i in range(n_cores):
        o = res.results[i]["outT"]
        out[i * nb:(i + 1) * nb] = o.transpose(0, 3, 2, 1).reshape(nb, S, D)
    return out
```

```python
import numpy as np
from contextlib import ExitStack
import concourse.bass as bass
import concourse.mybir as mybir
from concourse.bass_utils import run_bass_kernel_spmd

F32 = mybir.dt.float32
BF16 = mybir.dt.bfloat16
AF = mybir.ActivationFunctionType
ALU = mybir.AluOpType

D = 1024
S = 2048
LC = 256
NT = S + LC
DFF = 2816
EPS = 1e-6
NCH = 22


import types


def _freeze(fn, depth=0):
    if not isinstance(fn, types.FunctionType) or fn.__closure__ is None or depth > 4:
        return fn
    cells = []
    for c in fn.__closure__:
        try:
            v = c.cell_contents
        except ValueError:
            cells.append(c)
            continue
        if isinstance(v, types.FunctionType):
            v = _freeze(v, depth + 1)
        cells.append(types.CellType(v))
    g = types.FunctionType(fn.__code__, fn.__globals__, fn.__name__, fn.__defaults__, tuple(cells))
    g.__kwdefaults__ = fn.__kwdefaults__
    return g


class Prog:
    def __init__(self, nc):
        self.nc = nc
        self.eng = {"pe": nc.tensor, "act": nc.scalar, "dve": nc.vector, "pool": nc.gpsimd, "sp": nc.sync}
        self.sem = {e: nc.alloc_semaphore(name=f"sem_{e}") for e in self.eng}
        self.cnt = {e: 0 for e in self.eng}
        self.dsem = {}
        self.waited = {e: {} for e in self.eng}
        self.ops = []
        self.last_w = {}
        self.readers = {}
        self.last_of_eng = {}
        self.last_dma = {}

    def op(self, eng, fn, r=(), w=(), dma=None):
        idx = len(self.ops)
        deps = {}
        for k in r:
            d = self.last_w.get(k)
            if d is not None:
                deps[d] = "raw"
        for k in w:
            d = self.last_w.get(k)
            if d is not None:
                deps[d] = "raw"
            lastr = {}
            for rd in self.readers.get(k, ()):
                o_ = self.ops[rd]
                if o_["dma"] is not None:
                    lastr[("d", rd)] = rd
                else:
                    lastr[o_["eng"]] = rd
            for rd in lastr.values():
                if rd not in deps:
                    deps[rd] = "war"
        deps.pop(idx, None)
        self.ops.append(dict(eng=eng, fn=_freeze(fn), deps=deps, dma=dma, sig=None))
        for k in r:
            self.readers.setdefault(k, []).append(idx)
        for k in w:
            self.last_w[k] = idx
            self.readers[k] = []
        if dma is None:
            self.last_of_eng[eng] = idx
        else:
            self.last_dma[dma] = idx
        return idx

    def dma(self, eng, out, in_, r=(), w=(), slot=None):
        if slot is None:
            slot = w[0] if w else "out"
        return self.op(eng, lambda e: e.dma_start(out=out, in_=in_), r=r, w=w, dma=slot)

    def flush(self):
        alld = {}
        for e, i in self.last_of_eng.items():
            alld[i] = "raw"
        for s, i in self.last_dma.items():
            alld[i] = "raw"
        for e in self.eng:
            self.ops.append(dict(eng=e, fn=None, deps=dict(alld), dma=None, sig=None, barrier=True))
        ops = self.ops
        need = [False] * len(ops)
        for i, o in enumerate(ops):
            fd = []
            for d, kind in o["deps"].items():
                od = ops[d]
                if od["dma"] is None and od["eng"] == o["eng"] and not o.get("barrier"):
                    if o["eng"] == "pe" or kind == "war":
                        continue
                if od["dma"] is None and od["eng"] == o["eng"] and o.get("barrier"):
                    continue
                fd.append(d)
            latest = {}
            fd2 = []
            for d in fd:
                od = ops[d]
                if od["dma"] is None:
                    if d > latest.get(od["eng"], -1):
                        latest[od["eng"]] = d
                else:
                    fd2.append(d)
            fd2.extend(latest.values())
            for d in latest.values():
                need[d] = True
            o["fd"] = fd2
        for i, o in enumerate(ops):
            E = o["eng"]
            eo = self.eng[E]
            waits = {}
            for d in o["fd"]:
                key, val = ops[d]["sig"]
                if waits.get(key, 0) < val:
                    waits[key] = val
            for key, val in waits.items():
                if self.waited[E].get(key, 0) < val:
                    h = self.sem[key[1]] if key[0] == "e" else self.dsem[key[1]][0]
                    eo.wait_ge(h, val)
                    self.waited[E][key] = val
            if o["fn"] is None:
                continue
            ins = o["fn"](eo)
            if o["dma"] is not None:
                slot = o["dma"]
                if slot not in self.dsem:
                    self.dsem[slot] = [self.nc.alloc_semaphore(name=f"dma_{len(self.dsem)}"), 0]
                s = self.dsem[slot]
                s[1] += 16
                ins.then_inc(s[0], 16)
                o["sig"] = (("d", slot), s[1])
            elif need[i]:
                self.cnt[E] += 1
                ins.then_inc(self.sem[E], 1)
                o["sig"] = (("e", E), self.cnt[E])
        self.ops = []
        self.last_w = {}
        self.readers = {}
        self.last_of_eng = {}
        self.last_dma = {}


def _na_tile_plan():
    plan = []
    for A in range(4):
        rows = list(range(8 * A, 8 * A + 8))
        lo = min(max(r - 4, 0) if r - 4 <= 24 else 24 for r in rows)
        lo = min(min(max(r - 4, 0), 24) for r in rows)
        hi = max(min(max(r - 4, 0), 24) + 7 for r in rows)
        t0, t1 = lo // 2, hi // 2
        plan.append(list(range(t0, t1 + 1)))
    return plan


def _build_na_bias(rpb):
    plan = _na_tile_plan()
    NEG = np.float32(-30000.0)
    tiles = []
    index = {}
    qr_l = np.arange(8)[:, None]
    qc = np.arange(64)[None, :]
    for A in range(4):
        qr = (8 * A + qr_l) + 0 * qc
        qcc = 0 * qr_l + qc
        r0 = np.clip(qr - 4, 0, 24)
        c0 = np.clip(qcc - 8, 0, 48)
        qr_f, qc_f, r0_f, c0_f = [a.reshape(-1) for a in (qr, qcc, r0, c0)]
        for h in range(8):
            for j, t in enumerate(plan[A]):
                kr = np.repeat(np.arange(2 * t, 2 * t + 2), 64)
                kc = np.tile(np.arange(64), 2)
                valid = ((kr[:, None] >= r0_f[None, :]) & (kr[:, None] < r0_f[None, :] + 8)
                         & (kc[:, None] >= c0_f[None, :]) & (kc[:, None] < c0_f[None, :] + 16))
                dr = np.clip(kr[:, None] - qr_f[None, :] + 7, 0, 14)
                dc = np.clip(kc[:, None] - qc_f[None, :], -15, 15) + 15
                b = rpb[h][dr, dc]
                tiles.append(np.where(valid, b, NEG).astype(np.float32))
                index[(A, h, j)] = len(tiles) - 1
    return np.stack(tiles, 0), index, plan


def _rope_tables():
    t = np.arange(S)
    inv = (10000.0 ** (-np.arange(16, dtype=np.float32) / 16)).astype(np.float32)
    rows = (t // 64).astype(np.float32)[:, None] * inv
    cols = (t % 64).astype(np.float32)[:, None] * inv
    ang = np.concatenate([rows, cols], -1)
    cos = np.cos(ang).astype(np.float32).T
    sin = np.sin(ang).astype(np.float32).T
    cosT = np.tile(cos, (4, 1))
    sinT = np.concatenate([-sin, sin, -sin, sin], 0)
    perm = np.zeros((128, 128), np.float32)
    for m in range(128):
        k = m + 32 if (m % 64) < 32 else m - 32
        perm[k, m] = 1.0
    return np.ascontiguousarray(cosT), np.ascontiguousarray(sinT), perm


HMAP = [[(4 * (2 * (c // 4)) + (c % 4)), (4 * (2 * (c // 4) + 1) + (c % 4))] for c in range(8)]


def _lay_w(w):
    K, N = w.shape
    return np.ascontiguousarray(w.reshape(K // 128, 128, N).transpose(1, 0, 2))


def build_program(n_bias_tiles, bias_index, na_plan, nb=2, stages=("mix0", "ffn0", "mix1", "ffn1"), dbg=False):
    nc = bass.Bass("TRN2", target_bir_lowering=False)
    P = Prog(nc)

    def din(name, shape):
        return nc.dram_tensor(name, list(shape), F32, kind="ExternalInput").ap()

    xT_d = din("xT", (nb, 128, 8, S))
    ctxT_d = din("ctxT", (nb, 128, 8, LC))
    out_d = nc.dram_tensor("outT", [nb, 128, 8, S], F32, kind="ExternalOutput").ap()
    cT_d = din("cT", (128, 8, 3))
    wada_d = din("w_ada", (2, 128, 8, 6 * D))
    badaT_d = din("b_adaT", (2, 128, 48))
    gT_d = din("gT", (2, 4, 128, 8))
    win_d = din("w_in", (128, 8, 2560))
    woab_d = din("w_out_ab", (128, 8, D))
    wq_d = din("w_q", (128, 8, D))
    wk_d = din("w_k", (128, 8, 256))
    wv_d = din("w_v", (128, 8, 256))
    woc_d = din("w_out_c", (128, 8, D))
    wsT_d = din("w_sT", (128, 4, 128))
    bs_d = din("b_s_bc", (128, 512))
    gv_d = din("gv_bc", (128, 512))
    bias_d = din("bias_na", (n_bias_tiles, 128, 512))
    qg_d = din("qg2", (128, 1))
    kg_d = din("kg2", (128, 1))
    wup_d = din("w_up", (2, 2 * NCH, 128, 8, 128))
    cw_d = din("conv_wT", (2, 128, 2 * NCH, 3))
    cb_d = din("conv_bT", (2, 128, 2 * NCH))
    wdn_d = din("w_down", (2, 8, 128, NCH, 128))
    cos_d = din("cosT", (128, S))
    sin_d = din("sinT", (128, S))
    perm_d = din("perm", (128, 128))

    ES = ExitStack()
    hc_out = nc.dram_tensor("hcT_out", [nb, 128, 8, LC], F32, kind="ExternalOutput").ap() if dbg else None

    uid = [0]

    def sb(es, name, shape, dt=F32):
        uid[0] += 1
        return es.enter_context(nc.sbuf_tensor(f"{name}_{uid[0]}", list(shape), dt))

    def pst(es, name, shape):
        uid[0] += 1
        return es.enter_context(nc.psum_tensor(f"{name}_{uid[0]}", list(shape), F32))

    hT = sb(ES, "hT", (128, 8, S))
    hcT = sb(ES, "hcT", (128, 8, LC))
    ones_bf = sb(ES, "ones_bf", (128, 128), BF16)
    bones_bf = sb(ES, "bones_bf", (128, 128), BF16)
    perm_bf = sb(ES, "perm_bf", (128, 128), BF16)
    modA = sb(ES, "modA", (128, 2, 2, 3, 8))
    modB = sb(ES, "modB", (128, 2, 2, 3, 8))
    modG = sb(ES, "modG", (128, 2, 2, 3, 8))
    qkg = sb(ES, "qkg", (128, 2))
    cwT = sb(ES, "cwT", (128, 2, 2 * NCH, 3))
    cbT = sb(ES, "cbT", (128, 2, 2 * NCH))
    sq = [sb(ES, f"sq{i}", (128, 512), BF16) for i in range(2)]
    rstd = sb(ES, "rstd", (128, 512))
    tmp = [sb(ES, f"tmp{i}", (128, 512)) for i in range(2)]
    xhalo = sb(ES, "xhalo", (128, 8, 1), BF16)

    def hsrc(tile):
        if tile < 4:
            return (lambda kc, a=0, n=512: hT[:, kc, tile * 512 + a: tile * 512 + a + n]), 512
        return (lambda kc, a=0, n=256: hcT[:, kc, a:a + n]), 256

    def hkey(tile, kc):
        return ("h", tile, kc)

    def norm_mod(src, skeys, n, A, B, dst, dkeys, ss_ps, ss_key):
        for kc in range(8):
            s_ = sq[kc % 2]
            P.op("act", lambda e, kc=kc, s_=s_: e.activation(out=s_[:, :n], in_=src(kc), func=AF.Square),
                 r=[skeys[kc]], w=[("sq", kc % 2)])
            P.op("pe", lambda e, kc=kc, s_=s_: e.matmul(ss_ps[:, :n], ones_bf[:, :], s_[:, :n], start=(kc == 0), stop=(kc == 7)),
                 r=[("sq", kc % 2)], w=[ss_key])
        P.op("act", lambda e: e.activation(out=rstd[:, :n], in_=ss_ps[:, :n], func=AF.Ln, scale=1.0 / D, bias=eps_t[:, 0:1]),
             r=[ss_key], w=["rstd"])
        P.op("act", lambda e: e.activation(out=rstd[:, :n], in_=rstd[:, :n], func=AF.Exp, scale=-0.5), r=["rstd"], w=["rstd"])
        for kc in range(8):
            t_ = tmp[kc % 2]
            P.op("dve", lambda e, kc=kc, t_=t_: e.tensor_tensor(out=t_[:, :n], in0=src(kc), in1=rstd[:, :n], op=ALU.mult),
                 r=[skeys[kc], "rstd"], w=[("tmp", kc % 2)])
            P.op("act", lambda e, kc=kc, t_=t_: e.activation(out=dst(kc), in_=t_[:, :n], func=AF.Identity,
                                                              scale=A[:, kc:kc + 1], bias=B[:, kc:kc + 1]),
                 r=[("tmp", kc % 2)], w=[dkeys[kc]])

    def tile_norm(tile, l, mf, b, ss_ps, ss_key, dst=None, dkeys=None):
        src, n = hsrc(tile)
        v = b if tile < 4 else 2
        if dkeys is None:
            xt_ = dst
            dst = lambda kc: xt_[:, kc, :n]
            dkeys = [("xt", kc) for kc in range(8)]
        norm_mod(lambda kc: src(kc), [hkey(tile, kc) for kc in range(8)], n,
                 modA[:, l, mf, v, :], modB[:, l, mf, v, :], dst, dkeys, ss_ps, ss_key)
        return n

    def fm_proj(ps, pkey, w_ap, wkeys, rhs, rkeys, n, nk=8):
        for kc in range(nk):
            P.op("pe", lambda e, kc=kc: e.matmul(ps[:, :n], w_ap(kc), rhs(kc), start=(kc == 0), stop=(kc == nk - 1)),
                 r=list(wkeys) + [rkeys[kc]], w=[pkey])

    def post_norm_residual(tile, l, mf, b, get_ps, ss_ps, ss_key, yo):
        src, n = hsrc(tile)
        v = b if tile < 4 else 2
        G = modG[:, l, mf, v, :]
        for dc in range(8):
            ps, pkey = get_ps(dc)
            s_ = sq[dc % 2]
            P.op("act", lambda e, ps=ps, dc=dc: e.activation(out=yo[:, dc, :n], in_=ps, func=AF.Copy), r=[pkey], w=[("yo", dc)])
            P.op("act", lambda e, ps=ps, s_=s_: e.activation(out=s_[:, :n], in_=ps, func=AF.Square), r=[pkey], w=[("sq", dc % 2)])
            P.op("pe", lambda e, dc=dc, s_=s_: e.matmul(ss_ps[:, :n], ones_bf[:, :], s_[:, :n], start=(dc == 0), stop=(dc == 7)),
                 r=[("sq", dc % 2)], w=[ss_key])
        P.op("act", lambda e: e.activation(out=rstd[:, :n], in_=ss_ps[:, :n], func=AF.Ln, scale=1.0 / D, bias=eps_t[:, 0:1]),
             r=[ss_key], w=["rstd"])
        P.op("act", lambda e: e.activation(out=rstd[:, :n], in_=rstd[:, :n], func=AF.Exp, scale=-0.5), r=["rstd"], w=["rstd"])
        for dc in range(8):
            t_ = tmp[dc % 2]
            P.op("dve", lambda e, dc=dc, t_=t_: e.tensor_tensor(out=t_[:, :n], in0=yo[:, dc, :n], in1=rstd[:, :n], op=ALU.mult),
                 r=[("yo", dc), "rstd"], w=[("tmp", dc % 2)])
            P.op("dve", lambda e, dc=dc, t_=t_: e.scalar_tensor_tensor(out=src(dc), in0=t_[:, :n], scalar=G[:, dc:dc + 1], in1=src(dc),
                                                                        op0=ALU.mult, op1=ALU.add),
                 r=[("tmp", dc % 2), hkey(tile, dc)], w=[hkey(tile, dc)])

    def attention(q_ap, qkey, n_q, keys, scale, s_ps, o_ps, pT, sbt, rec, out_ap, okey, hp, tagc):
        tagc.append(dict(q_ap=q_ap, qkey=qkey, n_q=n_q, keys=keys, scale=scale, out_ap=out_ap, okey=okey))

    GRP = 2

    def attn_flush(jobs, s_ps, o_ps, pT, sbt, rec, cnt, bias_sb=None):
        units = []
        gsz = s_ps[0].shape[1]
        for ji, J in enumerate(jobs):
            J["ob"] = (cnt[0] + ji) % 2
            ks = J["keys"]
            i = 0
            while i < len(ks):
                g = [i]
                while len(g) < gsz and i + len(g) < len(ks) and (ks[i + len(g)][4] is None) == (ks[i][4] is None):
                    g.append(i + len(g))
                units.append((J, g))
                i += len(g)
        cnt[0] += len(jobs)
        groups = []
        for k, (J, g) in enumerate(units):
            bsp = J["keys"][g[0]][4]
            if bsp is not None and (not groups or groups[-1] != bsp[0]):
                groups.append(bsp[0])
        gpos = {g: n_ for n_, g in enumerate(groups)}
        issued = [0]

        def issue_groups(upto):
            while issued[0] < min(upto, len(groups)):
                (bi_, i0, ln) = groups[issued[0]]
                bb = bias_sb[bi_ % 3]
                P.dma("sp", bb[:, 0:ln, :], bias_d[i0:i0 + ln].rearrange("t p q -> p t q"), w=[("bias", bi_ % 3)])
                issued[0] += 1

        def emit_S(k):
            J, g = units[k]
            sbk = (cnt[1] + k) % len(s_ps)
            sp = s_ps[sbk]
            n_q, q_ap = J["n_q"], J["q_ap"]
            for gi_, i in enumerate(g):
                k_ap, kkey = J["keys"][i][0], J["keys"][i][1]
                P.op("pe", lambda e: e.matmul(sp[:, gi_, :n_q], k_ap, q_ap, start=True, stop=True), r=[kkey, J["qkey"]], w=[("sps", sbk)])

        def emit_post_pv(k):
            J, g = units[k]
            ng = len(g)
            sbk = (cnt[1] + k) % len(s_ps)
            sp, p_ = s_ps[sbk], pT[sbk]
            n_q, scale, nk, ob = J["n_q"], J["scale"], len(J["keys"]), J["ob"]
            ops_ = o_ps[ob]
            b0 = J["keys"][g[0]][4]
            if b0 is not None:
                (gid, jj) = b0
                issue_groups(gpos[gid] + 3)
                b_ap = bias_sb[gid[0] % 3][:, jj:jj + ng, :n_q]
                bkey = ("bias", gid[0] % 3)
                sb_ = sbt[sbk]
                P.op("dve", lambda e: e.scalar_tensor_tensor(out=sb_[:, 0:ng, :n_q], in0=sp[:, 0:ng, :n_q], scalar=float(scale), in1=b_ap,
                                                             op0=ALU.mult, op1=ALU.add),
                     r=[("sps", sbk), bkey], w=[("sbt", sbk)])
                P.op("act", lambda e: e.activation(out=p_[:, 0:ng, :n_q], in_=sb_[:, 0:ng, :n_q], func=AF.Exp), r=[("sbt", sbk)], w=[("pT", sbk)])
            else:
                P.op("act", lambda e: e.activation(out=p_[:, 0:ng, :n_q], in_=sp[:, 0:ng, :n_q], func=AF.Exp, scale=float(scale)),
                     r=[("sps", sbk)], w=[("pT", sbk)])
            for gi_, i in enumerate(g):
                v_ap, vkey = J["keys"][i][2], J["keys"][i][3]
                P.op("pe", lambda e: e.matmul(ops_[:, :n_q], v_ap, p_[:, gi_, :n_q], start=(i == 0), stop=(i == nk - 1)),
                     r=[vkey, ("pT", sbk)], w=[("ops", ob)])
            if g[-1] == nk - 1:
                out_ap = J["out_ap"]
                P.op("act", lambda e: e.activation(out=rec[64:128, :n_q], in_=ops_[64:128, :n_q], func=AF.Ln), r=[("ops", ob)], w=["rec"])
                P.op("act", lambda e: e.activation(out=rec[64:128, :n_q], in_=rec[64:128, :n_q], func=AF.Exp, scale=-1.0), r=["rec"], w=["rec"])
                P.op("dve", lambda e: e.tensor_tensor(out=out_ap, in0=ops_[0:64, :n_q], in1=rec[64:128, :n_q], op=ALU.mult),
                     r=[("ops", ob), "rec"], w=[J["okey"]])

        if bias_sb is not None:
            issue_groups(2)
        look = len(s_ps) - 1
        for k in range(min(look, len(units))):
            emit_S(k)
        for k in range(len(units)):
            if k + look < len(units):
                emit_S(k + look)
            emit_post_pv(k)
        cnt[1] += len(units)

    eps_t = sb(ES, "eps_t", (128, 1))
    with ExitStack() as es:
        wa = [sb(es, f"wa{i}", (128, 8, 1024), BF16) for i in range(2)]
        scb = sb(es, "scb", (128, 8, 3), BF16)
        scT = sb(es, "scT", (128, 8, 3))
        sgm = sb(es, "sgm", (128, 8, 3))
        mod = sb(es, "mod", (128, 2, 48, 3))
        bT = sb(es, "bT", (128, 2, 48))
        gTt = sb(es, "gTt", (128, 2, 4, 8))
        permf = sb(es, "permf", (128, 128))
        aps = pst(es, "aps", (128, 8, 3))
        P.op("dve", lambda e: e.memset(ones_bf[:, :], 1.0), w=["ones"])
        P.op("dve", lambda e: e.memset(bones_bf[:, :], 0.0), w=["bones"])
        P.op("dve", lambda e: e.memset(bones_bf[0:64, 0:64], 1.0), r=[], w=["bones"])
        P.op("dve", lambda e: e.memset(bones_bf[64:128, 64:128], 1.0), r=[], w=["bones"])
        P.op("dve", lambda e: e.memset(eps_t[:, :], EPS), w=["eps"])
        P.dma("sp", permf[:, :], perm_d, w=["permf"])
        P.op("dve", lambda e: e.tensor_copy(out=perm_bf[:, :], in_=permf[:, :]), r=["permf"], w=["perm"])
        P.dma("sp", scT[:, :, :], cT_d, w=["scT"])
        P.dma("sp", qkg[:, 0:1], qg_d, w=["qg"])
        P.dma("sp", qkg[:, 1:2], kg_d, w=["kg"])
        P.dma("sp", cwT[:, 0], cw_d[0], w=["cw0"])
        P.dma("sp", cwT[:, 1], cw_d[1], w=["cw1"])
        P.dma("sp", cbT[:, 0], cb_d[0], w=["cb0"])
        P.dma("sp", cbT[:, 1], cb_d[1], w=["cb1"])
        for l in range(2):
            P.dma("sp", bT[:, l, :], badaT_d[l], w=[("bT", l)])
            for i in range(4):
                P.dma("sp", gTt[:, l, i, :], gT_d[l, i], w=[("gT", l, i)])
        P.op("act", lambda e: e.activation(out=sgm[:, :, :], in_=scT[:, :, :], func=AF.Sigmoid), r=["scT"], w=["sgm"])
        P.op("dve", lambda e: e.tensor_tensor(out=scb[:, :, :], in0=scT[:, :, :], in1=sgm[:, :, :], op=ALU.mult), r=["scT", "sgm"], w=["scb"])
        it = 0
        for l in range(2):
            for grp in range(6):
                wbuf = wa[it % 2]
                P.dma("pool", wbuf[:, :, :], wada_d[l][:, :, grp * 1024:(grp + 1) * 1024], w=[("wa", it % 2)])
                for j in range(8):
                    for kc in range(8):
                        P.op("pe", lambda e, wbuf=wbuf, j=j, kc=kc: e.matmul(aps[:, j, :], wbuf[:, kc, j * 128:(j + 1) * 128], scb[:, kc, :],
                                                                               start=(kc == 0), stop=(kc == 7)),
                             r=[("wa", it % 2), "scb"], w=[("aps", j)])
                for v in range(3):
                    P.op("dve", lambda e, l=l, grp=grp, v=v: e.tensor_tensor(out=mod[:, l, grp * 8:(grp + 1) * 8, v], in0=aps[:, :, v],
                                                                              in1=bT[:, l, grp * 8:(grp + 1) * 8], op=ALU.add),
                         r=[("aps", j) for j in range(8)] + [("bT", l)], w=[("mod", l, grp, v)])
                it += 1
        for l in range(2):
            for mf in range(2):
                for v in range(3):
                    ish, isc, igt = 3 * mf, 3 * mf + 1, 3 * mf + 2
                    P.op("dve", lambda e, l=l, mf=mf, v=v, isc=isc: e.scalar_tensor_tensor(
                        out=modA[:, l, mf, v, :], in0=mod[:, l, isc * 8:(isc + 1) * 8, v], scalar=1.0, in1=gTt[:, l, 2 * mf, :],
                        op0=ALU.add, op1=ALU.mult), r=[("mod", l, isc, v), ("gT", l, 2 * mf)], w=[("modA", l, mf, v)])
                    P.op("dve", lambda e, l=l, mf=mf, v=v, ish=ish: e.tensor_copy(out=modB[:, l, mf, v, :], in_=mod[:, l, ish * 8:(ish + 1) * 8, v]),
                         r=[("mod", l, ish, v)], w=[("modB", l, mf, v)])
                    P.op("dve", lambda e, l=l, mf=mf, v=v, igt=igt: e.tensor_tensor(
                        out=modG[:, l, mf, v, :], in0=mod[:, l, igt * 8:(igt + 1) * 8, v], in1=gTt[:, l, 2 * mf + 1, :], op=ALU.mult),
                        r=[("mod", l, igt, v), ("gT", l, 2 * mf + 1)], w=[("modG", l, mf, v)])
        P.flush()

    def mixer_even(b, l, upto="c", halves=(0, 1)):
        with ExitStack() as es0:
            ybT = sb(es0, "ybT", (128, 4, NT), BF16)
            esx = ExitStack()
            xta = sb(esx, "xta", (128, 8, NT), BF16)
            if len(halves):
                with ExitStack() as es:
                    ss_ps = pst(es, "ss_ps", (128, 512))
                    for tile in range(5):
                        t0 = tile * 512
                        tile_norm(tile, l, 0, b, ss_ps, "ss", dst=(lambda kc, t0=t0, tile=tile: xta[:, kc, t0:t0 + (512 if tile < 4 else 256)]),
                                  dkeys=[("xta", tile, kc) for kc in range(8)])
                    P.flush()
            for hh in halves:
                with ExitStack() as es1:
                    kT = sb(es1, "kT", (128, 2, NT), BF16)
                    vaug = sb(es1, "vaug", (128, 18, 4, 128), BF16)
                    with ExitStack() as es:
                        wk = sb(es, "wk", (128, 8, 256), BF16)
                        wv = sb(es, "wv", (128, 8, 256), BF16)
                        ss_ps = pst(es, "ss_ps", (128, 512))
                        pp = [pst(es, f"pp{i}", (128, 512)) for i in range(2)]
                        P.dma("pool", wk[:, :, :], win_d[:, :, 1536 + hh * 256:1536 + (hh + 1) * 256], w=["wk"])
                        P.dma("pool", wv[:, :, :], win_d[:, :, 2048 + hh * 256:2048 + (hh + 1) * 256], w=["wv"])
                        P.op("dve", lambda e: e.memset(vaug[:, :, :, :], 1.0), w=[("vaug", t) for t in range(18)])
                        pi = 0
                        for tile in range(5):
                            n = 512 if tile < 4 else 256
                            t0 = tile * 512
                            for c in range(2):
                                ps = pp[pi % 2]; pk = ("pp", pi % 2); pi += 1
                                fm_proj(ps, pk, lambda kc, c=c: wk[:, kc, c * 128:(c + 1) * 128], ["wk"],
                                        lambda kc: xta[:, kc, t0:t0 + n], [("xta", tile, kc) for kc in range(8)], n)
                                P.op("act", lambda e, ps=ps, c=c: e.activation(out=kT[:, c, t0:t0 + n], in_=ps[:, :n], func=AF.Copy),
                                     r=[pk], w=[("kT", tile, c)])
                            for s_ in range(n // 128):
                                ps = pp[pi % 2]; pk = ("pp", pi % 2); pi += 1
                                ti = tile * 4 + s_
                                for kc in range(8):
                                    P.op("pe", lambda e, ps=ps, kc=kc, s_=s_: e.matmul(ps[:, 0:256], xta[:, kc, t0 + s_ * 128:t0 + (s_ + 1) * 128], wv[:, kc, :],
                                                                                         start=(kc == 0), stop=(kc == 7)),
                                         r=[("xta", tile, kc), "wv"], w=[pk])
                                for hq in range(4):
                                    P.op("dve", lambda e, ps=ps, ti=ti, hq=hq: e.tensor_copy(out=vaug[:, ti, hq, 0:64], in_=ps[:, hq * 64:(hq + 1) * 64]),
                                         r=[pk], w=[("vaug", ti)])
                        P.flush()
                    if upto == "a":
                        continue
                    with ExitStack() as es:
                        wq = sb(es, "wq", (128, 8, 256), BF16)
                        qT = sb(es, "qT", (128, 2, 512), BF16)
                        bias_sb = [sb(es, f"bias{i}", (128, 2, 512)) for i in range(3)]
                        pT = [sb(es, f"pT{i}", (128, GRP, 512), BF16) for i in range(3)]
                        sbt = [sb(es, f"sbt{i}", (128, GRP, 512)) for i in range(3)]
                        rec = sb(es, "rec", (128, 512))
                        s_ps = [pst(es, f"s_ps{i}", (128, GRP, 512)) for i in range(3)]
                        pp = [s_ps[2][:, 0, :]] * 2
                        o_ps = [pst(es, f"o_ps{i}", (128, 512)) for i in range(2)]
                        P.dma("pool", wq[:, :, :], win_d[:, :, 1024 + hh * 256:1024 + (hh + 1) * 256], w=["wq"])
                        acnt = [0, 0]
                        pi = 0
                        bi = 0
                        for tile in range(5):
                            n = 512 if tile < 4 else 256
                            t0 = tile * 512
                            for c in range(2):
                                ps = pp[0]; pk = ("sps", 2); pi += 1
                                fm_proj(ps, pk, lambda kc, c=c: wq[:, kc, c * 128:(c + 1) * 128], ["wq"],
                                        lambda kc: xta[:, kc, t0:t0 + n], [("xta", tile, kc) for kc in range(8)], n)
                                P.op("act", lambda e, ps=ps, c=c: e.activation(out=qT[:, c, :n], in_=ps[:, :n], func=AF.Copy), r=[pk], w=[("qT", c)])
                            tagc = []
                            for hq in range(4):
                                h = 4 * hh + hq
                                c, hp = hq // 2, (hq % 2) * 64
                                keys = []
                                if tile < 4:
                                    kts = na_plan[tile]
                                    for j0 in range(0, len(kts), 2):
                                        grp = kts[j0:j0 + 2]
                                        i0 = bias_index[(tile, h, j0)]
                                        gid = (bi, i0, len(grp)); bi += 1
                                        for jj, t in enumerate(grp):
                                            keys.append((kT[hp:hp + 64, c, t * 128:(t + 1) * 128], ("kT", t // 4, c), vaug[:, t, hq, :], ("vaug", t),
                                                         (gid, jj), None))
                                for t in (16, 17):
                                    keys.append((kT[hp:hp + 64, c, t * 128:(t + 1) * 128], ("kT", 4, c), vaug[:, t, hq, :], ("vaug", t), None, None))
                                attention(qT[hp:hp + 64, c, :n], ("qT", c), n, keys, 0.125, s_ps, o_ps, pT, sbt, rec,
                                          ybT[hp:hp + 64, 2 * hh + c, t0:t0 + n], ("ybT", tile, 2 * hh + c, hp), hp, tagc)
                            attn_flush(tagc, s_ps, o_ps, pT, sbt, rec, acnt, bias_sb)
                        P.flush()
            esx.close()
            if upto in ("a", "b"):
                return
            with ExitStack() as es:
                xt = sb(es, "xt", (128, 8, 512), BF16)
                wu = sb(es, "wu", (128, 8, 512), BF16)
                wva = sb(es, "wva", (128, 8, 512), BF16)
                wo = sb(es, "wo", (128, 8, D), BF16)
                wsT = sb(es, "wsT", (128, 4, 128), BF16)
                bsb = sb(es, "bsb", (128, 512))
                gvb = sb(es, "gvb", (128, 512))
                uT = sb(es, "uT", (128, 4, 512), BF16)
                vg2 = [sb(es, f"vg{i}", (128, 512)) for i in range(2)]
                vn2 = [sb(es, f"vn{i}", (128, 512)) for i in range(2)]
                vln2 = [sb(es, f"vln{i}", (128, 512), BF16) for i in range(2)]
                tt2 = [sb(es, f"tt{i}", (128, 4, 128)) for i in range(2)]
                yaT = sb(es, "yaT", (128, 4, 512), BF16)
                yo = sb(es, "yo", (128, 8, 512))
                stats2 = [sb(es, f"stats{i}", (128, 6)) for i in range(2)]
                mv2 = [sb(es, f"mv{i}", (128, 2)) for i in range(2)]
                ss_ps = pst(es, "ss_ps", (128, 512))
                pp = [pst(es, f"pp{i}", (128, 512)) for i in range(2)]
                g_ps2 = [pst(es, f"g_ps{i}", (128, 4, 128)) for i in range(2)]
                gi = 0
                P.dma("pool", wu[:, :, :], win_d[:, :, 0:512], w=["wu"])
                P.dma("pool", wva[:, :, :], win_d[:, :, 512:1024], w=["wva"])
                P.dma("pool", wo[:, :, :], woab_d, w=["wo"])
                P.dma("pool", wsT[:, :, :], wsT_d, w=["wsT"])
                P.dma("sp", bsb[:, :], bs_d, w=["bsb"])
                P.dma("sp", gvb[:, :], gv_d, w=["gvb"])
                pi = 0
                for tile in range(5):
                    n = tile_norm(tile, l, 0, b, ss_ps, "ss", dst=xt)
                    t0 = tile * 512
                    for c in range(4):
                        ps = pp[pi % 2]; pk = ("pp", pi % 2); pi += 1
                        fm_proj(ps, pk, lambda kc, c=c: wu[:, kc, c * 128:(c + 1) * 128], ["wu"],
                                lambda kc: xt[:, kc, :n], [("xt", kc) for kc in range(8)], n)
                        P.op("act", lambda e, ps=ps, c=c: e.activation(out=uT[:, c, :n], in_=ps[:, :n], func=AF.Gelu), r=[pk], w=[("uT", c)])
                    if upto == "c_u":
                        P.flush(); return
                    for s_ in range(n // 128):
                        ps = pp[pi % 2]; pk = ("pp", pi % 2); pi += 1
                        for kc in range(8):
                            P.op("pe", lambda e, ps=ps, kc=kc, s_=s_: e.matmul(ps[:, :], xt[:, kc, s_ * 128:(s_ + 1) * 128], wva[:, kc, :],
                                                                                 start=(kc == 0), stop=(kc == 7)),
                                 r=[("xt", kc), "wva"], w=[pk])
                        gp = gi % 2; gi += 1
                        vg, vn, vln, tt, stats, mv, g_ps = vg2[gp], vn2[gp], vln2[gp], tt2[gp], stats2[gp], mv2[gp], g_ps2[gp]
                        kv, kn, kl, kt, kst, kmv, kg = ("vg", gp), ("vn", gp), ("vln", gp), ("tt", gp), ("stats", gp), ("mv", gp), ("g_ps", gp)
                        P.op("act", lambda e, ps=ps: e.activation(out=vg[:, :], in_=ps[:, :], func=AF.Gelu), r=[pk], w=[kv])
                        P.op("dve", lambda e: e.bn_stats(out=stats[:, :], in_=vg[:, :]), r=[kv], w=[kst])
                        P.op("dve", lambda e: e.bn_aggr(out=mv[:, :], in_=stats[:, :]), r=[kst], w=[kmv])
                        P.op("act", lambda e: e.activation(out=mv[:, 1:2], in_=mv[:, 1:2], func=AF.Sqrt, scale=1.0, bias=eps_t[:, 0:1]), r=[kmv], w=[kmv])
                        P.op("dve", lambda e: e.reciprocal(out=mv[:, 1:2], in_=mv[:, 1:2]), r=[kmv], w=[kmv])
                        P.op("dve", lambda e: e.tensor_scalar(out=vn[:, :], in0=vg[:, :], scalar1=mv[:, 0:1], scalar2=mv[:, 1:2],
                                                              op0=ALU.subtract, op1=ALU.mult), r=[kv, kmv], w=[kn])
                        P.op("dve", lambda e: e.tensor_tensor(out=vln[:, :], in0=vn[:, :], in1=gvb[:, :], op=ALU.mult), r=[kn, "gvb"], w=[kl])
                        if upto == "c_va":
                            P.flush(); return
                        for g in range(4):
                            P.op("pe", lambda e, g=g: e.matmul(g_ps[:, g, :], vln[:, g * 128:(g + 1) * 128], wsT[:, g, :], start=True, stop=True),
                                 r=[kl, "wsT"], w=[kg])
                        P.op("dve", lambda e: e.tensor_tensor(out=tt[:, :, :], in0=g_ps[:, :, :], in1=bsb[:, :].rearrange("p (g i) -> p g i", g=4), op=ALU.add),
                             r=[kg, "bsb"], w=[kt])
                        P.op("dve", lambda e, s_=s_: e.tensor_tensor(out=yaT[:, :, s_ * 128:(s_ + 1) * 128], in0=tt[:, :, :],
                                                                     in1=uT[:, :, s_ * 128:(s_ + 1) * 128], op=ALU.mult),
                             r=[kt] + [("uT", c) for c in range(4)], w=[("yaT", s_)])
                        if upto == "c_gate":
                            P.flush(); return

                    def get_ps(dc, tile=tile, n=n, t0=t0):
                        nonlocal pi
                        ps = pp[pi % 2]; pk = ("pp", pi % 2); pi += 1
                        rk = [("yaT", s_) for s_ in range(n // 128)]
                        for kc in range(8):
                            rhs = yaT[:, kc, :n] if kc < 4 else ybT[:, kc - 4, t0:t0 + n]
                            rr = rk if kc < 4 else [("ybT", tile, kc - 4, 0), ("ybT", tile, kc - 4, 64)]
                            P.op("pe", lambda e, ps=ps, kc=kc, rhs=rhs, dc=dc: e.matmul(ps[:, :n], wo[:, kc, dc * 128:(dc + 1) * 128], rhs,
                                                                                         start=(kc == 0), stop=(kc == 7)),
                                 r=["wo"] + rr, w=[pk])
                        return ps[:, :n], pk
                    post_norm_residual(tile, l, 0, b, get_ps, ss_ps, "ss", yo)
                P.flush()

    def qk_process(ps, pkey, n, gcol, rope, dst, dkey, cs, R):
        (sq_, sqk), (bps, bk), (sw, swk), (rs, rsk) = R["sq"], R["bps"], R["sw"], R["rstd"]
        (qn, qnk), (qg, qgk), (t1, t1k), (t2, t2k) = R["qn"], R["qg"], R["t1"], R["t2"]
        P.op("act", lambda e: e.activation(out=sq_[:, :n], in_=ps, func=AF.Square), r=[pkey], w=[sqk])
        P.op("pe", lambda e: e.matmul(bps[:, :n], bones_bf[:, :], sq_[:, :n], start=True, stop=True), r=[sqk, "bones"], w=[bk])
        P.op("act", lambda e: e.activation(out=rs[:, :n], in_=bps[:, :n], func=AF.Ln, scale=1.0 / 64, bias=eps_t[:, 0:1]), r=[bk], w=[rsk])
        P.op("act", lambda e: e.activation(out=rs[:, :n], in_=rs[:, :n], func=AF.Exp, scale=-0.5), r=[rsk], w=[rsk])
        P.op("dve", lambda e: e.tensor_tensor(out=qn[:, :n], in0=ps, in1=rs[:, :n], op=ALU.mult), r=[pkey, rsk], w=[qnk])
        if not rope:
            P.op("act", lambda e: e.activation(out=dst, in_=qn[:, :n], func=AF.Identity, scale=qkg[:, gcol:gcol + 1]), r=[qnk], w=[dkey])
            return
        P.op("act", lambda e: e.activation(out=qg[:, :n], in_=qn[:, :n], func=AF.Identity, scale=qkg[:, gcol:gcol + 1]), r=[qnk], w=[qgk])
        P.op("pe", lambda e: e.matmul(sw[:, :n], perm_bf[:, :], qg[:, :n], start=True, stop=True), r=[qgk, "perm"], w=[swk])
        P.op("dve", lambda e: e.tensor_tensor(out=t1[:, :n], in0=qg[:, :n], in1=cs[:, 0, :n], op=ALU.mult), r=[qgk, "cs"], w=[t1k])
        P.op("dve", lambda e: e.tensor_tensor(out=t2[:, :n], in0=sw[:, :n], in1=cs[:, 1, :n], op=ALU.mult), r=[swk, "cs"], w=[t2k])
        P.op("dve", lambda e: e.tensor_tensor(out=dst, in0=t1[:, :n], in1=t2[:, :n], op=ALU.add), r=[t1k, t2k], w=[dkey])

    def mixer_odd(b, l):
        with ExitStack() as es0:
            kT = sb(es0, "kT", (128, 2, NT), BF16)
            vaug = sb(es0, "vaug", (128, 18, 4, 128), BF16)
            cs = sb(es0, "cs", (128, 2, 512))
            xt = sb(es0, "xt", (128, 8, 512), BF16)
            qg2 = [sb(es0, f"qg{i}", (128, 512), BF16) for i in range(2)]
            with ExitStack() as es:
                wk = sb(es, "wk", (128, 8, 256), BF16)
                wv = sb(es, "wv", (128, 8, 256), BF16)
                ss_ps = pst(es, "ss_ps", (128, 512))
                pp = [pst(es, f"pp{i}", (128, 512)) for i in range(2)]
                bps2 = [pst(es, f"bps{i}", (128, 512)) for i in range(2)]
                sw2 = [pst(es, f"sw{i}", (128, 512)) for i in range(2)]
                scr = sb(es, "scr", (128, 8, 512))
                RR = [dict(sq=(sq[i], ("sq", i)), bps=(bps2[i], ("bps", i)), sw=(sw2[i], ("sw", i)), rstd=(scr[:, 6 + i, :], ("scr", 6 + i)),
                           qn=(scr[:, i, :], ("scr", i)), qg=(qg2[i], ("qg", i)), t1=(scr[:, 2 + i, :], ("scr", 2 + i)),
                           t2=(scr[:, 4 + i, :], ("scr", 4 + i))) for i in range(2)]
                qi = 0
                P.dma("pool", wk[:, :, :], wk_d, w=["wk"])
                P.dma("pool", wv[:, :, :], wv_d, w=["wv"])
                P.op("dve", lambda e: e.memset(vaug[:, :, :, :], 1.0), w=[("vaug", t) for t in range(18)])
                pi = 0
                for tile in range(5):
                    n = tile_norm(tile, l, 0, b, ss_ps, "ss", dst=xt)
                    t0 = tile * 512
                    if tile < 4:
                        P.dma("sp", cs[:, 0, :], cos_d[:, t0:t0 + 512], w=["cs"], slot="cs0")
                        P.dma("sp", cs[:, 1, :], sin_d[:, t0:t0 + 512], w=["cs"], slot="cs1")
                    for c in range(2):
                        ps = pp[pi % 2]; pk = ("pp", pi % 2); pi += 1
                        fm_proj(ps, pk, lambda kc, c=c: wk[:, kc, c * 128:(c + 1) * 128], ["wk"],
                                lambda kc: xt[:, kc, :n], [("xt", kc) for kc in range(8)], n)
                        qk_process(ps[:, :n], pk, n, 1, tile < 4, kT[:, c, t0:t0 + n], ("kT", tile, c), cs, RR[qi % 2]); qi += 1
                    for s_ in range(n // 128):
                        ps = pp[pi % 2]; pk = ("pp", pi % 2); pi += 1
                        ti = tile * 4 + s_
                        for kc in range(8):
                            P.op("pe", lambda e, ps=ps, kc=kc, s_=s_: e.matmul(ps[:, 0:256], xt[:, kc, s_ * 128:(s_ + 1) * 128], wv[:, kc, :],
                                                                                 start=(kc == 0), stop=(kc == 7)),
                                 r=[("xt", kc), "wv"], w=[pk])
                        for g in range(4):
                            P.op("dve", lambda e, ps=ps, ti=ti, g=g: e.tensor_copy(out=vaug[:, ti, g, 0:64], in_=ps[:, g * 64:(g + 1) * 64]),
                                 r=[pk], w=[("vaug", ti)])
                P.flush()
            with ExitStack() as es:
                wq = sb(es, "wq", (128, 8, D), BF16)
                wo = sb(es, "wo", (128, 8, D), BF16)
                qT = sb(es, "qT", (128, 8, 512), BF16)
                yT = sb(es, "yT", (128, 8, 512), BF16)
                yo = sb(es, "yo", (128, 8, 512))
                pT = [sb(es, f"pT{i}", (128, GRP, 512), BF16) for i in range(2)]
                rec = sb(es, "rec", (128, 512))
                s_ps = [pst(es, f"s_ps{i}", (128, GRP, 512)) for i in range(2)]
                ss_ps = pst(es, "ss_ps", (128, 512))
                pp = [pst(es, "pp0", (128, 512))]
                o_ps = [pst(es, f"o_ps{i}", (128, 512)) for i in range(2)]
                RR = [dict(sq=(sq[i], ("sq", i)), bps=(s_ps[i][:, 0, :], ("sps", i)), sw=(o_ps[i], ("ops", i)), rstd=(yo[:, 6 + i, :], ("yo", 6 + i)),
                           qn=(yo[:, i, :], ("yo", i)), qg=(qg2[i], ("qg", i)), t1=(yo[:, 2 + i, :], ("yo", 2 + i)),
                           t2=(yo[:, 4 + i, :], ("yo", 4 + i))) for i in range(2)]
                qi = 0
                P.dma("pool", wq[:, :, :], wq_d, w=["wq"])
                P.dma("pool", wo[:, :, :], woc_d, w=["wo"])
                acnt = [0, 0]
                pi = 0
                for tile in range(4):
                    n = tile_norm(tile, l, 0, b, ss_ps, "ss", dst=xt)
                    t0 = tile * 512
                    tagc = []
                    P.dma("sp", cs[:, 0, :], cos_d[:, t0:t0 + 512], w=["cs"], slot="cs0")
                    P.dma("sp", cs[:, 1, :], sin_d[:, t0:t0 + 512], w=["cs"], slot="cs1")
                    for c in range(8):
                        ps = pp[0]; pk = ("pp", 0); pi += 1
                        fm_proj(ps, pk, lambda kc, c=c: wq[:, kc, c * 128:(c + 1) * 128], ["wq"],
                                lambda kc: xt[:, kc, :n], [("xt", kc) for kc in range(8)], n)
                        qk_process(ps[:, :n], pk, n, 0, True, qT[:, c, :n], ("qT", c), cs, RR[qi % 2]); qi += 1
                    for c in range(8):
                        for s2 in range(2):
                            hp = s2 * 64
                            g = 2 * (c // 4) + s2
                            kc_, = [g // 2]
                            keys = []
                            for t in range(18):
                                keys.append((kT[hp:hp + 64, kc_, t * 128:(t + 1) * 128], ("kT", t // 4, kc_), vaug[:, t, g, :], ("vaug", t), None, None))
                            attention(qT[hp:hp + 64, c, :n], ("qT", c), n, keys, 0.125, s_ps, o_ps, pT, None, rec,
                                      yT[hp:hp + 64, c, :n], ("yT", c, hp), hp, tagc)
                    attn_flush(tagc, s_ps, o_ps, pT, None, rec, acnt)

                    def get_ps(dc, n=n):
                        nonlocal pi
                        ps = pp[0]; pk = ("pp", 0); pi += 1
                        for kc in range(8):
                            P.op("pe", lambda e, ps=ps, kc=kc, dc=dc: e.matmul(ps[:, :n], wo[:, kc, dc * 128:(dc + 1) * 128], yT[:, kc, :n],
                                                                               start=(kc == 0), stop=(kc == 7)),
                                 r=["wo", ("yT", kc, 0), ("yT", kc, 64)], w=[pk])
                        return ps[:, :n], pk
                    post_norm_residual(tile, l, 0, b, get_ps, ss_ps, "ss", yo)
                P.flush()

    def ffn(b, l, with_ctx, upto=None, store=False):
        passes = [[(0, 0, 1024, 0, 0)], [(0, 1024, 1024, 0, 0)] + ([(1, 0, 256, 1026, 1024)] if with_ctx else [])]
        for segs_ in passes:
            W = sum(sg[2] for sg in segs_)
            XW = sum(sg[2] + 2 for sg in segs_)
            with ExitStack() as es:
                xf = sb(es, "xf", (128, 8, XW), BF16)
                actT = sb(es, "actT", (128, NCH, W), BF16)
                ca = sb(es, "ca", (128, W))
                cg = sb(es, "cg", (128, W))
                wub = [sb(es, f"wub{i}", (128, 2, 8, 128), BF16) for i in range(3)]
                wdb = [sb(es, f"wdb{i}", (128, NCH, 128), BF16) for i in range(2)]
                yo = sb(es, "yo", (128, 8, 512))
                ss_ps = pst(es, "ss_ps", (128, 512))
                a_ps = pst(es, "a_ps", (128, 1536))
                g_ps = pst(es, "g_ps", (128, 1536))
                pp = [pst(es, "ppd", (128, 512))]
                xall = []
                mgroups = []
                for (isctx, s0, T, xo, ao) in segs_:
                    v = 2 if isctx else b
                    A_, B_ = modA[:, l, 1, v, :], modB[:, l, 1, v, :]
                    Sq = LC if isctx else S

                    def hap(kc, a, n, isctx=isctx):
                        return hcT[:, kc, a:a + n] if isctx else hT[:, kc, a:a + n]
                    lk, rk = ("xf", isctx, "l"), ("xf", isctx, "r")
                    nsegs = []
                    if s0 > 0:
                        P.op("dve", lambda e, xo=xo: e.tensor_copy(out=xf[:, :, xo:xo + 1], in_=xhalo[:, :, 0:1]), r=["xhalo"], w=[lk])
                    else:
                        P.op("dve", lambda e, xo=xo: e.memset(xf[:, :, xo:xo + 1], 0.0), w=[lk])
                    for a in range(s0, s0 + T, 512):
                        nn = min(512, s0 + T - a)
                        nsegs.append((a, nn, xo + 1 + a - s0))
                    if s0 + T < Sq:
                        nsegs.append((s0 + T, 1, xo + 1 + T))
                    else:
                        P.op("dve", lambda e, xo=xo, T=T: e.memset(xf[:, :, xo + 1 + T:xo + 2 + T], 0.0), w=[rk])
                    xkeys = []
                    for (a, nn, col) in nsegs:
                        key = ("xf", isctx, a)
                        xkeys.append(key)
                        tl = 4 if isctx else a // 512
                        norm_mod(lambda kc, a=a, nn=nn, hap=hap: hap(kc, a, nn), [hkey(tl, kc) for kc in range(8)], nn, A_, B_,
                                 lambda kc, col=col, nn=nn: xf[:, kc, col:col + nn], [key] * 8, ss_ps, "ss")
                    xall += xkeys + [lk, rk]
                    if s0 + T < Sq:
                        P.op("dve", lambda e, xo=xo, T=T: e.tensor_copy(out=xhalo[:, :, 0:1], in_=xf[:, :, xo + T:xo + T + 1]), r=xkeys, w=["xhalo"])
                    c0 = xo
                    while c0 < xo + T + 2:
                        c1 = min((c0 // 512 + 1) * 512, xo + T + 2)
                        mgroups.append((c0, c1))
                        c0 = c1
                if upto == "f_norm":
                    P.flush(); return
                for j in range(NCH):
                    wb_ = wub[j % 3]
                    wkey = ("wub", j % 3)
                    P.dma("pool", wb_[:, 0, :, :], wup_d[l, j], w=[wkey], slot=("wub", j % 3, 0))
                    P.dma("pool", wb_[:, 1, :, :], wup_d[l, NCH + j], w=[wkey], slot=("wub", j % 3, 1))
                    for part, (ps_, pkey, cdst, ckey) in enumerate(((a_ps, "a_ps", ca, "ca"), (g_ps, "g_ps", cg, "cg"))):
                        ch = part * NCH + j
                        for (c0, c1) in mgroups:
                            for kc in range(8):
                                P.op("pe", lambda e, ps_=ps_, kc=kc, c0=c0, c1=c1, part=part, wb_=wb_: e.matmul(
                                    ps_[:, c0:c1], wb_[:, part, kc, :], xf[:, kc, c0:c1], start=(kc == 0), stop=(kc == 7)),
                                    r=[wkey] + xall, w=[pkey])
                        w0, w1, w2 = (cwT[:, l, ch, i:i + 1] for i in range(3))
                        for (isctx, s0, T, xo, ao) in segs_:
                            P.op("act", lambda e, ps_=ps_, cdst=cdst, w1=w1, ch=ch, T=T, xo=xo, ao=ao: e.activation(
                                out=cdst[:, ao:ao + T], in_=ps_[:, xo + 1:xo + T + 1], func=AF.Identity, scale=w1, bias=cbT[:, l, ch:ch + 1]),
                                r=[pkey], w=[ckey])
                            P.op("dve", lambda e, ps_=ps_, cdst=cdst, w0=w0, T=T, xo=xo, ao=ao: e.scalar_tensor_tensor(
                                out=cdst[:, ao:ao + T], in0=ps_[:, xo:xo + T], scalar=w0, in1=cdst[:, ao:ao + T], op0=ALU.mult, op1=ALU.add),
                                r=[pkey, ckey], w=[ckey])
                            P.op("dve", lambda e, ps_=ps_, cdst=cdst, w2=w2, T=T, xo=xo, ao=ao: e.scalar_tensor_tensor(
                                out=cdst[:, ao:ao + T], in0=ps_[:, xo + 2:xo + T + 2], scalar=w2, in1=cdst[:, ao:ao + T], op0=ALU.mult, op1=ALU.add),
                                r=[pkey, ckey], w=[ckey])
                    P.op("act", lambda e: e.activation(out=cg[:, :W], in_=cg[:, :W], func=AF.Silu), r=["cg"], w=["cg"])
                    P.op("dve", lambda e, j=j: e.tensor_tensor(out=actT[:, j, :W], in0=cg[:, :W], in1=ca[:, :W], op=ALU.mult),
                         r=["cg", "ca"], w=[("actT", j)])
                    if upto == "f_up1":
                        P.flush(); return
                if upto == "f_up":
                    P.flush(); return
                pi = 0
                di = 0
                for (isctx, s0, T, xo, ao) in segs_:
                    for a in range(0, T, 512):
                        nn = min(512, T - a)
                        tile = 4 if isctx else (s0 + a) // 512

                        def get_ps(dc, a=a, nn=nn, ao=ao):
                            nonlocal pi, di
                            wd_ = wdb[di % 2]; wdk = ("wdb", di % 2); di += 1
                            P.dma("pool", wd_[:, 0:11, :], wdn_d[l, dc][:, 0:11, :], w=[wdk], slot=("wdb", (di - 1) % 2, 0))
                            P.dma("pool", wd_[:, 11:22, :], wdn_d[l, dc][:, 11:22, :], w=[wdk], slot=("wdb", (di - 1) % 2, 1))
                            ps = pp[0]; pk = ("pp", 0); pi += 1
                            for kc in range(NCH):
                                P.op("pe", lambda e, ps=ps, kc=kc, wd_=wd_: e.matmul(ps[:, :nn], wd_[:, kc, :], actT[:, kc, ao + a:ao + a + nn],
                                                                                     start=(kc == 0), stop=(kc == NCH - 1)),
                                     r=[wdk, ("actT", kc)], w=[pk])
                            return ps[:, :nn], pk
                        post_norm_residual(tile, l, 1, b, get_ps, ss_ps, "ss", yo)
                        if store and not isctx:
                            t0_ = tile * 512
                            P.dma("sp", out_d[b, :, :, t0_:t0_ + 512], hT[:, :, t0_:t0_ + 512], r=[hkey(tile, kc) for kc in range(8)], slot=("st", tile))
                P.flush()

    for b in range(nb):
        for kc in range(8):
            P.dma("sp", hT[:, kc, :], xT_d[b, :, kc, :], w=[hkey(t, kc) for t in range(4)], slot=("ld", kc))
        P.dma("sp", hcT[:, :, :], ctxT_d[b], w=[hkey(4, kc) for kc in range(8)], slot="ldc")
        P.flush()
        if "mix0a" in stages:
            mixer_even(b, 0, upto="a", halves=(0,))
        if "mix0b" in stages:
            mixer_even(b, 0, upto="b", halves=(0,))
        for st_ in ("c_u", "c_va", "c_gate"):
            if st_ in stages:
                mixer_even(b, 0, upto=st_, halves=())
        if "mix0ab" in stages:
            mixer_even(b, 0, upto="b")
        if "mix0c" in stages:
            mixer_even(b, 0, halves=())
        if "mix0" in stages:
            mixer_even(b, 0)
        for st_ in ("f_norm", "f_up1", "f_up"):
            if st_ in stages:
                ffn(b, 0, with_ctx=True, upto=st_)
        if "ffn0" in stages:
            ffn(b, 0, with_ctx=True)
        if "mix1" in stages:
            mixer_odd(b, 1)
        if "ffn1" in stages:
            ffn(b, 1, with_ctx=False, store=True)
        if dbg:
            P.dma("sp", hc_out[b], hcT[:, :, :], r=[hkey(4, kc) for kc in range(8)], slot="sthc")
        if "ffn1" not in stages:
            for kc in range(8):
                P.dma("sp", out_d[b, :, kc, :], hT[:, kc, :], r=[hkey(t, kc) for t in range(4)], slot=("st", kc))
        P.flush()
    ES.close()
    return nc


def _prep_shared(inp):
    f = lambda a: np.ascontiguousarray(np.asarray(a, dtype=np.float32))
    sh = {}
    w_ada = f(inp["w_ada"])
    sh["w_ada"] = np.ascontiguousarray(w_ada.reshape(2, 8, 128, 6 * D).transpose(0, 2, 1, 3))
    sh["b_adaT"] = np.ascontiguousarray(f(inp["b_ada"]).reshape(2, 48, 128).transpose(0, 2, 1))
    sh["gT"] = np.ascontiguousarray(f(inp["norm_g"]).reshape(2, 4, 8, 128).transpose(0, 1, 3, 2))
    sh["w_in"] = _lay_w(f(inp["w_in_ab"])[0])
    sh["w_out_ab"] = _lay_w(f(inp["w_out_ab"])[0])
    wqkv = f(inp["w_qkv_c"])[0]
    qcols = []
    for c in range(8):
        for s in range(2):
            h = HMAP[c][s]
            qcols.extend(range(h * 64, (h + 1) * 64))
    sh["w_q"] = _lay_w(wqkv[:, qcols])
    sh["w_k"] = _lay_w(wqkv[:, 1024:1280])
    sh["w_v"] = _lay_w(wqkv[:, 1280:1536])
    sh["w_out_c"] = _lay_w(f(inp["w_out_c"])[0][qcols, :])
    sh["w_sT"] = np.ascontiguousarray(f(inp["a_w_s"])[0].transpose(2, 0, 1))
    sh["b_s_bc"] = np.ascontiguousarray(np.broadcast_to(f(inp["a_b_s"])[0].reshape(1, 512), (128, 512)))
    sh["gv_bc"] = np.ascontiguousarray(np.broadcast_to(f(inp["a_v_g"])[0].reshape(1, 512), (128, 512)))
    bias, index, plan = _build_na_bias(f(inp["b_rpb"])[0])
    sh["bias_na"] = bias
    sh["qg2"] = np.ascontiguousarray(np.tile(f(inp["c_q_g"])[0], 2).reshape(128, 1))
    sh["kg2"] = np.ascontiguousarray(np.tile(f(inp["c_k_g"])[0], 2).reshape(128, 1))
    wup = f(inp["w_up"])
    sh["w_up"] = np.ascontiguousarray(wup.reshape(2, 8, 128, 2 * NCH, 128).transpose(0, 3, 2, 1, 4))
    sh["conv_wT"] = np.ascontiguousarray(f(inp["conv_w"]).reshape(2, 3, 2 * NCH, 128).transpose(0, 3, 2, 1))
    sh["conv_bT"] = np.ascontiguousarray(f(inp["conv_b"]).reshape(2, 2 * NCH, 128).transpose(0, 2, 1))
    wdn = f(inp["w_down"])
    sh["w_down"] = np.ascontiguousarray(wdn.reshape(2, NCH, 128, 8, 128).transpose(0, 3, 2, 1, 4))
    cosT, sinT, perm = _rope_tables()
    sh["cosT"], sh["sinT"], sh["perm"] = cosT, sinT, perm
    return sh, index, plan


def _lay_act(a):
    nb_, L, _ = a.shape
    return np.ascontiguousarray(a.transpose(0, 2, 1).reshape(nb_, 8, 128, L).transpose(0, 2, 1, 3))


def kernel(**inp):
    x = np.asarray(inp["x"], np.float32)
    c = np.asarray(inp["c"], np.float32)
    ctx = np.asarray(inp["ctx"], np.float32)
    c_ctx = np.asarray(inp["c_ctx"], np.float32)
    sh, index, plan = _prep_shared(inp)
    n_cores = 8
    nb = x.shape[0] // n_cores
    nc = build_program(sh["bias_na"].shape[0], index, plan, nb=nb)
    in_maps = []
    for i in range(n_cores):
        m = dict(sh)
        sl = slice(i * nb, (i + 1) * nb)
        m["xT"] = _lay_act(x[sl])
        m["ctxT"] = _lay_act(ctx[sl])
        cv = np.stack([c[i * nb], c[i * nb + 1], c_ctx], -1)
        m["cT"] = np.ascontiguousarray(cv.reshape(8, 128, 3).transpose(1, 0, 2))
        in_maps.append(m)
    res = run_bass_kernel_spmd(nc, in_maps, core_ids=list(range(n_cores)))
    out = np.empty_like(x)
    for i in range(n_cores):
        o = res.results[i]["outT"]
        out[i * nb:(i + 1) * nb] = o.transpose(0, 3, 2, 1).reshape(nb, S, D)
    return out
```

```python
import numpy as np
from contextlib import ExitStack
import concourse.bass as bass
import concourse.mybir as mybir
from concourse.bass_utils import run_bass_kernel_spmd

F32 = mybir.dt.float32
BF16 = mybir.dt.bfloat16
AF = mybir.ActivationFunctionType
ALU = mybir.AluOpType

D = 1024
S = 2048
LC = 256
NT = S + LC
DFF = 2816
EPS = 1e-6
NCH = 22


import types


def _freeze(fn, depth=0):
    if not isinstance(fn, types.FunctionType) or fn.__closure__ is None or depth > 4:
        return fn
    cells = []
    for c in fn.__closure__:
        try:
            v = c.cell_contents
        except ValueError:
            cells.append(c)
            continue
        if isinstance(v, types.FunctionType):
            v = _freeze(v, depth + 1)
        cells.append(types.CellType(v))
    g = types.FunctionType(fn.__code__, fn.__globals__, fn.__name__, fn.__defaults__, tuple(cells))
    g.__kwdefaults__ = fn.__kwdefaults__
    return g


class Prog:
    def __init__(self, nc):
        self.nc = nc
        self.eng = {"pe": nc.tensor, "act": nc.scalar, "dve": nc.vector, "pool": nc.gpsimd, "sp": nc.sync}
        self.sem = {e: nc.alloc_semaphore(name=f"sem_{e}") for e in self.eng}
        self.cnt = {e: 0 for e in self.eng}
        self.dsem = {}
        self.waited = {e: {} for e in self.eng}
        self.ops = []
        self.last_w = {}
        self.readers = {}
        self.last_of_eng = {}
        self.last_dma = {}

    def op(self, eng, fn, r=(), w=(), dma=None):
        idx = len(self.ops)
        deps = {}
        for k in r:
            d = self.last_w.get(k)
            if d is not None:
                deps[d] = "raw"
        for k in w:
            d = self.last_w.get(k)
            if d is not None:
                deps[d] = "raw"
            lastr = {}
            for rd in self.readers.get(k, ()):
                o_ = self.ops[rd]
                if o_["dma"] is not None:
                    lastr[("d", rd)] = rd
                else:
                    lastr[o_["eng"]] = rd
            for rd in lastr.values():
                if rd not in deps:
                    deps[rd] = "war"
        deps.pop(idx, None)
        self.ops.append(dict(eng=eng, fn=_freeze(fn), deps=deps, dma=dma, sig=None))
        for k in r:
            self.readers.setdefault(k, []).append(idx)
        for k in w:
            self.last_w[k] = idx
            self.readers[k] = []
        if dma is None:
            self.last_of_eng[eng] = idx
        else:
            self.last_dma[dma] = idx
        return idx

    def dma(self, eng, out, in_, r=(), w=(), slot=None):
        if slot is None:
            slot = w[0] if w else "out"
        return self.op(eng, lambda e: e.dma_start(out=out, in_=in_), r=r, w=w, dma=slot)

    def flush(self):
        alld = {}
        for e, i in self.last_of_eng.items():
            alld[i] = "raw"
        for s, i in self.last_dma.items():
            alld[i] = "raw"
        for e in self.eng:
            self.ops.append(dict(eng=e, fn=None, deps=dict(alld), dma=None, sig=None, barrier=True))
        ops = self.ops
        need = [False] * len(ops)
        for i, o in enumerate(ops):
            fd = []
            for d, kind in o["deps"].items():
                od = ops[d]
                if od["dma"] is None and od["eng"] == o["eng"] and not o.get("barrier"):
                    if o["eng"] == "pe" or kind == "war":
                        continue
                if od["dma"] is None and od["eng"] == o["eng"] and o.get("barrier"):
                    continue
                fd.append(d)
            latest = {}
            fd2 = []
            for d in fd:
                od = ops[d]
                if od["dma"] is None:
                    if d > latest.get(od["eng"], -1):
                        latest[od["eng"]] = d
                else:
                    fd2.append(d)
            fd2.extend(latest.values())
            for d in latest.values():
                need[d] = True
            o["fd"] = fd2
        for i, o in enumerate(ops):
            E = o["eng"]
            eo = self.eng[E]
            waits = {}
            for d in o["fd"]:
                key, val = ops[d]["sig"]
                if waits.get(key, 0) < val:
                    waits[key] = val
            for key, val in waits.items():
                if self.waited[E].get(key, 0) < val:
                    h = self.sem[key[1]] if key[0] == "e" else self.dsem[key[1]][0]
                    eo.wait_ge(h, val)
                    self.waited[E][key] = val
            if o["fn"] is None:
                continue
            ins = o["fn"](eo)
            if o["dma"] is not None:
                slot = o["dma"]
                if slot not in self.dsem:
                    self.dsem[slot] = [self.nc.alloc_semaphore(name=f"dma_{len(self.dsem)}"), 0]
                s = self.dsem[slot]
                s[1] += 16
                ins.then_inc(s[0], 16)
                o["sig"] = (("d", slot), s[1])
            elif need[i]:
                self.cnt[E] += 1
                ins.then_inc(self.sem[E], 1)
                o["sig"] = (("e", E), self.cnt[E])
        self.ops = []
        self.last_w = {}
        self.readers = {}
        self.last_of_eng = {}
        self.last_dma = {}


def _na_tile_plan():
    plan = []
    for A in range(4):
        rows = list(range(8 * A, 8 * A + 8))
        lo = min(max(r - 4, 0) if r - 4 <= 24 else 24 for r in rows)
        lo = min(min(max(r - 4, 0), 24) for r in rows)
        hi = max(min(max(r - 4, 0), 24) + 7 for r in rows)
        t0, t1 = lo // 2, hi // 2
        plan.append(list(range(t0, t1 + 1)))
    return plan


def _build_na_bias(rpb):
    plan = _na_tile_plan()
    NEG = np.float32(-30000.0)
    tiles = []
    index = {}
    qr_l = np.arange(8)[:, None]
    qc = np.arange(64)[None, :]
    for A in range(4):
        qr = (8 * A + qr_l) + 0 * qc
        qcc = 0 * qr_l + qc
        r0 = np.clip(qr - 4, 0, 24)
        c0 = np.clip(qcc - 8, 0, 48)
        qr_f, qc_f, r0_f, c0_f = [a.reshape(-1) for a in (qr, qcc, r0, c0)]
        for h in range(8):
            for j, t in enumerate(plan[A]):
                kr = np.repeat(np.arange(2 * t, 2 * t + 2), 64)
                kc = np.tile(np.arange(64), 2)
                valid = ((kr[:, None] >= r0_f[None, :]) & (kr[:, None] < r0_f[None, :] + 8)
                         & (kc[:, None] >= c0_f[None, :]) & (kc[:, None] < c0_f[None, :] + 16))
                dr = np.clip(kr[:, None] - qr_f[None, :] + 7, 0, 14)
                dc = np.clip(kc[:, None] - qc_f[None, :], -15, 15) + 15
                b = rpb[h][dr, dc]
                tiles.append(np.where(valid, b, NEG).astype(np.float32))
                index[(A, h, j)] = len(tiles) - 1
    return np.stack(tiles, 0), index, plan


def _rope_tables():
    t = np.arange(S)
    inv = (10000.0 ** (-np.arange(16, dtype=np.float32) / 16)).astype(np.float32)
    rows = (t // 64).astype(np.float32)[:, None] * inv
    cols = (t % 64).astype(np.float32)[:, None] * inv
    ang = np.concatenate([rows, cols], -1)
    cos = np.cos(ang).astype(np.float32).T
    sin = np.sin(ang).astype(np.float32).T
    cosT = np.tile(cos, (4, 1))
    sinT = np.concatenate([-sin, sin, -sin, sin], 0)
    perm = np.zeros((128, 128), np.float32)
    for m in range(128):
        k = m + 32 if (m % 64) < 32 else m - 32
        perm[k, m] = 1.0
    return np.ascontiguousarray(cosT), np.ascontiguousarray(sinT), perm


HMAP = [[(4 * (2 * (c // 4)) + (c % 4)), (4 * (2 * (c // 4) + 1) + (c % 4))] for c in range(8)]


def _lay_w(w):
    K, N = w.shape
    return np.ascontiguousarray(w.reshape(K // 128, 128, N).transpose(1, 0, 2))


def build_program(n_bias_tiles, bias_index, na_plan, nb=2, stages=("mix0", "ffn0", "mix1", "ffn1"), dbg=False):
    nc = bass.Bass("TRN2", target_bir_lowering=False)
    P = Prog(nc)

    def din(name, shape):
        return nc.dram_tensor(name, list(shape), F32, kind="ExternalInput").ap()

    xT_d = din("xT", (nb, 128, 8, S))
    ctxT_d = din("ctxT", (nb, 128, 8, LC))
    out_d = nc.dram_tensor("outT", [nb, 128, 8, S], F32, kind="ExternalOutput").ap()
    cT_d = din("cT", (128, 8, 3))
    wada_d = din("w_ada", (2, 128, 8, 6 * D))
    badaT_d = din("b_adaT", (2, 128, 48))
    gT_d = din("gT", (2, 4, 128, 8))
    win_d = din("w_in", (128, 8, 2560))
    woab_d = din("w_out_ab", (128, 8, D))
    wq_d = din("w_q", (128, 8, D))
    wk_d = din("w_k", (128, 8, 256))
    wv_d = din("w_v", (128, 8, 256))
    woc_d = din("w_out_c", (128, 8, D))
    wsT_d = din("w_sT", (128, 4, 128))
    bs_d = din("b_s_bc", (128, 512))
    gv_d = din("gv_bc", (128, 512))
    bias_d = din("bias_na", (n_bias_tiles, 128, 512))
    qg_d = din("qg2", (128, 1))
    kg_d = din("kg2", (128, 1))
    wup_d = din("w_up", (2, 2 * NCH, 128, 8, 128))
    cw_d = din("conv_wT", (2, 128, 2 * NCH, 3))
    cb_d = din("conv_bT", (2, 128, 2 * NCH))
    wdn_d = din("w_down", (2, 8, 128, NCH, 128))
    cos_d = din("cosT", (128, S))
    sin_d = din("sinT", (128, S))
    perm_d = din("perm", (128, 128))

    ES = ExitStack()
    hc_out = nc.dram_tensor("hcT_out", [nb, 128, 8, LC], F32, kind="ExternalOutput").ap() if dbg else None

    uid = [0]

    def sb(es, name, shape, dt=F32):
        uid[0] += 1
        return es.enter_context(nc.sbuf_tensor(f"{name}_{uid[0]}", list(shape), dt))

    def pst(es, name, shape):
        uid[0] += 1
        return es.enter_context(nc.psum_tensor(f"{name}_{uid[0]}", list(shape), F32))

    hT = sb(ES, "hT", (128, 8, S))
    hcT = sb(ES, "hcT", (128, 8, LC))
    ones_bf = sb(ES, "ones_bf", (128, 128), BF16)
    bones_bf = sb(ES, "bones_bf", (128, 128), BF16)
    perm_bf = sb(ES, "perm_bf", (128, 128), BF16)
    modA = sb(ES, "modA", (128, 2, 2, 3, 8))
    modB = sb(ES, "modB", (128, 2, 2, 3, 8))
    modG = sb(ES, "modG", (128, 2, 2, 3, 8))
    qkg = sb(ES, "qkg", (128, 2))
    cwT = sb(ES, "cwT", (128, 2, 2 * NCH, 3))
    cbT = sb(ES, "cbT", (128, 2, 2 * NCH))
    sq = [sb(ES, f"sq{i}", (128, 512), BF16) for i in range(2)]
    rstd = sb(ES, "rstd", (128, 512))
    tmp = [sb(ES, f"tmp{i}", (128, 512)) for i in range(2)]
    xhalo = sb(ES, "xhalo", (128, 8, 1), BF16)

    def hsrc(tile):
        if tile < 4:
            return (lambda kc, a=0, n=512: hT[:, kc, tile * 512 + a: tile * 512 + a + n]), 512
        return (lambda kc, a=0, n=256: hcT[:, kc, a:a + n]), 256

    def hkey(tile, kc):
        return ("h", tile, kc)

    def norm_mod(src, skeys, n, A, B, dst, dkeys, ss_ps, ss_key):
        for kc in range(8):
            s_ = sq[kc % 2]
            P.op("act", lambda e, kc=kc, s_=s_: e.activation(out=s_[:, :n], in_=src(kc), func=AF.Square),
                 r=[skeys[kc]], w=[("sq", kc % 2)])
            P.op("pe", lambda e, kc=kc, s_=s_: e.matmul(ss_ps[:, :n], ones_bf[:, :], s_[:, :n], start=(kc == 0), stop=(kc == 7)),
                 r=[("sq", kc % 2)], w=[ss_key])
        P.op("act", lambda e: e.activation(out=rstd[:, :n], in_=ss_ps[:, :n], func=AF.Ln, scale=1.0 / D, bias=eps_t[:, 0:1]),
             r=[ss_key], w=["rstd"])
        P.op("act", lambda e: e.activation(out=rstd[:, :n], in_=rstd[:, :n], func=AF.Exp, scale=-0.5), r=["rstd"], w=["rstd"])
        for kc in range(8):
            t_ = tmp[kc % 2]
            P.op("dve", lambda e, kc=kc, t_=t_: e.tensor_tensor(out=t_[:, :n], in0=src(kc), in1=rstd[:, :n], op=ALU.mult),
                 r=[skeys[kc], "rstd"], w=[("tmp", kc % 2)])
            P.op("act", lambda e, kc=kc, t_=t_: e.activation(out=dst(kc), in_=t_[:, :n], func=AF.Identity,
                                                              scale=A[:, kc:kc + 1], bias=B[:, kc:kc + 1]),
                 r=[("tmp", kc % 2)], w=[dkeys[kc]])

    def tile_norm(tile, l, mf, b, ss_ps, ss_key, dst=None, dkeys=None):
        src, n = hsrc(tile)
        v = b if tile < 4 else 2
        if dkeys is None:
            xt_ = dst
            dst = lambda kc: xt_[:, kc, :n]
            dkeys = [("xt", kc) for kc in range(8)]
        norm_mod(lambda kc: src(kc), [hkey(tile, kc) for kc in range(8)], n,
                 modA[:, l, mf, v, :], modB[:, l, mf, v, :], dst, dkeys, ss_ps, ss_key)
        return n

    def fm_proj(ps, pkey, w_ap, wkeys, rhs, rkeys, n, nk=8):
        for kc in range(nk):
            P.op("pe", lambda e, kc=kc: e.matmul(ps[:, :n], w_ap(kc), rhs(kc), start=(kc == 0), stop=(kc == nk - 1)),
                 r=list(wkeys) + [rkeys[kc]], w=[pkey])

    def post_norm_residual(tile, l, mf, b, get_ps, ss_ps, ss_key, yo):
        src, n = hsrc(tile)
        v = b if tile < 4 else 2
        G = modG[:, l, mf, v, :]
        for dc in range(8):
            ps, pkey = get_ps(dc)
            s_ = sq[dc % 2]
            P.op("act", lambda e, ps=ps, dc=dc: e.activation(out=yo[:, dc, :n], in_=ps, func=AF.Copy), r=[pkey], w=[("yo", dc)])
            P.op("act", lambda e, ps=ps, s_=s_: e.activation(out=s_[:, :n], in_=ps, func=AF.Square), r=[pkey], w=[("sq", dc % 2)])
            P.op("pe", lambda e, dc=dc, s_=s_: e.matmul(ss_ps[:, :n], ones_bf[:, :], s_[:, :n], start=(dc == 0), stop=(dc == 7)),
                 r=[("sq", dc % 2)], w=[ss_key])
        P.op("act", lambda e: e.activation(out=rstd[:, :n], in_=ss_ps[:, :n], func=AF.Ln, scale=1.0 / D, bias=eps_t[:, 0:1]),
             r=[ss_key], w=["rstd"])
        P.op("act", lambda e: e.activation(out=rstd[:, :n], in_=rstd[:, :n], func=AF.Exp, scale=-0.5), r=["rstd"], w=["rstd"])
        for dc in range(8):
            t_ = tmp[dc % 2]
            P.op("dve", lambda e, dc=dc, t_=t_: e.tensor_tensor(out=t_[:, :n], in0=yo[:, dc, :n], in1=rstd[:, :n], op=ALU.mult),
                 r=[("yo", dc), "rstd"], w=[("tmp", dc % 2)])
            P.op("dve", lambda e, dc=dc, t_=t_: e.scalar_tensor_tensor(out=src(dc), in0=t_[:, :n], scalar=G[:, dc:dc + 1], in1=src(dc),
                                                                        op0=ALU.mult, op1=ALU.add),
                 r=[("tmp", dc % 2), hkey(tile, dc)], w=[hkey(tile, dc)])

    def attention(q_ap, qkey, n_q, keys, scale, s_ps, o_ps, pT, sbt, rec, out_ap, okey, hp, tagc):
        tagc.append(dict(q_ap=q_ap, qkey=qkey, n_q=n_q, keys=keys, scale=scale, out_ap=out_ap, okey=okey))

    GRP = 2

    def attn_flush(jobs, s_ps, o_ps, pT, sbt, rec, cnt, bias_sb=None):
        units = []
        gsz = s_ps[0].shape[1]
        for ji, J in enumerate(jobs):
            J["ob"] = (cnt[0] + ji) % 2
            ks = J["keys"]
            i = 0
            while i < len(ks):
                g = [i]
                while len(g) < gsz and i + len(g) < len(ks) and (ks[i + len(g)][4] is None) == (ks[i][4] is None):
                    g.append(i + len(g))
                units.append((J, g))
                i += len(g)
        cnt[0] += len(jobs)
        groups = []
        for k, (J, g) in enumerate(units):
            bsp = J["keys"][g[0]][4]
            if bsp is not None and (not groups or groups[-1] != bsp[0]):
                groups.append(bsp[0])
        gpos = {g: n_ for n_, g in enumerate(groups)}
        issued = [0]

        def issue_groups(upto):
            while issued[0] < min(upto, len(groups)):
                (bi_, i0, ln) = groups[issued[0]]
                bb = bias_sb[bi_ % 3]
                P.dma("sp", bb[:, 0:ln, :], bias_d[i0:i0 + ln].rearrange("t p q -> p t q"), w=[("bias", bi_ % 3)])
                issued[0] += 1

        def emit_S(k):
            J, g = units[k]
            sbk = (cnt[1] + k) % len(s_ps)
            sp = s_ps[sbk]
            n_q, q_ap = J["n_q"], J["q_ap"]
            for gi_, i in enumerate(g):
                k_ap, kkey = J["keys"][i][0], J["keys"][i][1]
                P.op("pe", lambda e: e.matmul(sp[:, gi_, :n_q], k_ap, q_ap, start=True, stop=True), r=[kkey, J["qkey"]], w=[("sps", sbk)])

        def emit_post_pv(k):
            J, g = units[k]
            ng = len(g)
            sbk = (cnt[1] + k) % len(s_ps)
            sp, p_ = s_ps[sbk], pT[sbk]
            n_q, scale, nk, ob = J["n_q"], J["scale"], len(J["keys"]), J["ob"]
            ops_ = o_ps[ob]
            b0 = J["keys"][g[0]][4]
            if b0 is not None:
                (gid, jj) = b0
                issue_groups(gpos[gid] + 3)
                b_ap = bias_sb[gid[0] % 3][:, jj:jj + ng, :n_q]
                bkey = ("bias", gid[0] % 3)
                sb_ = sbt[sbk]
                P.op("dve", lambda e: e.scalar_tensor_tensor(out=sb_[:, 0:ng, :n_q], in0=sp[:, 0:ng, :n_q], scalar=float(scale), in1=b_ap,
                                                             op0=ALU.mult, op1=ALU.add),
                     r=[("sps", sbk), bkey], w=[("sbt", sbk)])
                P.op("act", lambda e: e.activation(out=p_[:, 0:ng, :n_q], in_=sb_[:, 0:ng, :n_q], func=AF.Exp), r=[("sbt", sbk)], w=[("pT", sbk)])
            else:
                P.op("act", lambda e: e.activation(out=p_[:, 0:ng, :n_q], in_=sp[:, 0:ng, :n_q], func=AF.Exp, scale=float(scale)),
                     r=[("sps", sbk)], w=[("pT", sbk)])
            for gi_, i in enumerate(g):
                v_ap, vkey = J["keys"][i][2], J["keys"][i][3]
                P.op("pe", lambda e: e.matmul(ops_[:, :n_q], v_ap, p_[:, gi_, :n_q], start=(i == 0), stop=(i == nk - 1)),
                     r=[vkey, ("pT", sbk)], w=[("ops", ob)])
            if g[-1] == nk - 1:
                out_ap = J["out_ap"]
                P.op("act", lambda e: e.activation(out=rec[64:128, :n_q], in_=ops_[64:128, :n_q], func=AF.Ln), r=[("ops", ob)], w=["rec"])
                P.op("act", lambda e: e.activation(out=rec[64:128, :n_q], in_=rec[64:128, :n_q], func=AF.Exp, scale=-1.0), r=["rec"], w=["rec"])
                P.op("dve", lambda e: e.tensor_tensor(out=out_ap, in0=ops_[0:64, :n_q], in1=rec[64:128, :n_q], op=ALU.mult),
                     r=[("ops", ob), "rec"], w=[J["okey"]])

        if bias_sb is not None:
            issue_groups(2)
        look = len(s_ps) - 1
        for k in range(min(look, len(units))):
            emit_S(k)
        for k in range(len(units)):
            if k + look < len(units):
                emit_S(k + look)
            emit_post_pv(k)
        cnt[1] += len(units)

    eps_t = sb(ES, "eps_t", (128, 1))
    with ExitStack() as es:
        wa = [sb(es, f"wa{i}", (128, 8, 1024), BF16) for i in range(2)]
        scb = sb(es, "scb", (128, 8, 3), BF16)
        scT = sb(es, "scT", (128, 8, 3))
        sgm = sb(es, "sgm", (128, 8, 3))
        mod = sb(es, "mod", (128, 2, 48, 3))
        bT = sb(es, "bT", (128, 2, 48))
        gTt = sb(es, "gTt", (128, 2, 4, 8))
        permf = sb(es, "permf", (128, 128))
        aps = pst(es, "aps", (128, 8, 3))
        P.op("dve", lambda e: e.memset(ones_bf[:, :], 1.0), w=["ones"])
        P.op("dve", lambda e: e.memset(bones_bf[:, :], 0.0), w=["bones"])
        P.op("dve", lambda e: e.memset(bones_bf[0:64, 0:64], 1.0), r=[], w=["bones"])
        P.op("dve", lambda e: e.memset(bones_bf[64:128, 64:128], 1.0), r=[], w=["bones"])
        P.op("dve", lambda e: e.memset(eps_t[:, :], EPS), w=["eps"])
        P.dma("sp", permf[:, :], perm_d, w=["permf"])
        P.op("dve", lambda e: e.tensor_copy(out=perm_bf[:, :], in_=permf[:, :]), r=["permf"], w=["perm"])
        P.dma("sp", scT[:, :, :], cT_d, w=["scT"])
        P.dma("sp", qkg[:, 0:1], qg_d, w=["qg"])
        P.dma("sp", qkg[:, 1:2], kg_d, w=["kg"])
        P.dma("sp", cwT[:, 0], cw_d[0], w=["cw0"])
        P.dma("sp", cwT[:, 1], cw_d[1], w=["cw1"])
        P.dma("sp", cbT[:, 0], cb_d[0], w=["cb0"])
        P.dma("sp", cbT[:, 1], cb_d[1], w=["cb1"])
        for l in range(2):
            P.dma("sp", bT[:, l, :], badaT_d[l], w=[("bT", l)])
            for i in range(4):
                P.dma("sp", gTt[:, l, i, :], gT_d[l, i], w=[("gT", l, i)])
        P.op("act", lambda e: e.activation(out=sgm[:, :, :], in_=scT[:, :, :], func=AF.Sigmoid), r=["scT"], w=["sgm"])
        P.op("dve", lambda e: e.tensor_tensor(out=scb[:, :, :], in0=scT[:, :, :], in1=sgm[:, :, :], op=ALU.mult), r=["scT", "sgm"], w=["scb"])
        it = 0
        for l in range(2):
            for grp in range(6):
                wbuf = wa[it % 2]
                P.dma("pool", wbuf[:, :, :], wada_d[l][:, :, grp * 1024:(grp + 1) * 1024], w=[("wa", it % 2)])
                for j in range(8):
                    for kc in range(8):
                        P.op("pe", lambda e, wbuf=wbuf, j=j, kc=kc: e.matmul(aps[:, j, :], wbuf[:, kc, j * 128:(j + 1) * 128], scb[:, kc, :],
                                                                               start=(kc == 0), stop=(kc == 7)),
                             r=[("wa", it % 2), "scb"], w=[("aps", j)])
                for v in range(3):
                    P.op("dve", lambda e, l=l, grp=grp, v=v: e.tensor_tensor(out=mod[:, l, grp * 8:(grp + 1) * 8, v], in0=aps[:, :, v],
                                                                              in1=bT[:, l, grp * 8:(grp + 1) * 8], op=ALU.add),
                         r=[("aps", j) for j in range(8)] + [("bT", l)], w=[("mod", l, grp, v)])
                it += 1
        for l in range(2):
            for mf in range(2):
                for v in range(3):
                    ish, isc, igt = 3 * mf, 3 * mf + 1, 3 * mf + 2
                    P.op("dve", lambda e, l=l, mf=mf, v=v, isc=isc: e.scalar_tensor_tensor(
                        out=modA[:, l, mf, v, :], in0=mod[:, l, isc * 8:(isc + 1) * 8, v], scalar=1.0, in1=gTt[:, l, 2 * mf, :],
                        op0=ALU.add, op1=ALU.mult), r=[("mod", l, isc, v), ("gT", l, 2 * mf)], w=[("modA", l, mf, v)])
                    P.op("dve", lambda e, l=l, mf=mf, v=v, ish=ish: e.tensor_copy(out=modB[:, l, mf, v, :], in_=mod[:, l, ish * 8:(ish + 1) * 8, v]),
                         r=[("mod", l, ish, v)], w=[("modB", l, mf, v)])
                    P.op("dve", lambda e, l=l, mf=mf, v=v, igt=igt: e.tensor_tensor(
                        out=modG[:, l, mf, v, :], in0=mod[:, l, igt * 8:(igt + 1) * 8, v], in1=gTt[:, l, 2 * mf + 1, :], op=ALU.mult),
                        r=[("mod", l, igt, v), ("gT", l, 2 * mf + 1)], w=[("modG", l, mf, v)])
        P.flush()

    def mixer_even(b, l, upto="c", halves=(0, 1)):
        with ExitStack() as es0:
            ybT = sb(es0, "ybT", (128, 4, NT), BF16)
            esx = ExitStack()
            xta = sb(esx, "xta", (128, 8, NT), BF16)
            if len(halves):
                with ExitStack() as es:
                    ss_ps = pst(es, "ss_ps", (128, 512))
                    for tile in range(5):
                        t0 = tile * 512
                        tile_norm(tile, l, 0, b, ss_ps, "ss", dst=(lambda kc, t0=t0, tile=tile: xta[:, kc, t0:t0 + (512 if tile < 4 else 256)]),
                                  dkeys=[("xta", tile, kc) for kc in range(8)])
                    P.flush()
            for hh in halves:
                with ExitStack() as es1:
                    kT = sb(es1, "kT", (128, 2, NT), BF16)
                    vaug = sb(es1, "vaug", (128, 18, 4, 128), BF16)
                    with ExitStack() as es:
                        wk = sb(es, "wk", (128, 8, 256), BF16)
                        wv = sb(es, "wv", (128, 8, 256), BF16)
                        ss_ps = pst(es, "ss_ps", (128, 512))
                        pp = [pst(es, f"pp{i}", (128, 512)) for i in range(2)]
                        P.dma("pool", wk[:, :, :], win_d[:, :, 1536 + hh * 256:1536 + (hh + 1) * 256], w=["wk"])
                        P.dma("pool", wv[:, :, :], win_d[:, :, 2048 + hh * 256:2048 + (hh + 1) * 256], w=["wv"])
                        P.op("dve", lambda e: e.memset(vaug[:, :, :, :], 1.0), w=[("vaug", t) for t in range(18)])
                        pi = 0
                        for tile in range(5):
                            n = 512 if tile < 4 else 256
                            t0 = tile * 512
                            for c in range(2):
                                ps = pp[pi % 2]; pk = ("pp", pi % 2); pi += 1
                                fm_proj(ps, pk, lambda kc, c=c: wk[:, kc, c * 128:(c + 1) * 128], ["wk"],
                                        lambda kc: xta[:, kc, t0:t0 + n], [("xta", tile, kc) for kc in range(8)], n)
                                P.op("act", lambda e, ps=ps, c=c: e.activation(out=kT[:, c, t0:t0 + n], in_=ps[:, :n], func=AF.Copy),
                                     r=[pk], w=[("kT", tile, c)])
                            for s_ in range(n // 128):
                                ps = pp[pi % 2]; pk = ("pp", pi % 2); pi += 1
                                ti = tile * 4 + s_
                                for kc in range(8):
                                    P.op("pe", lambda e, ps=ps, kc=kc, s_=s_: e.matmul(ps[:, 0:256], xta[:, kc, t0 + s_ * 128:t0 + (s_ + 1) * 128], wv[:, kc, :],
                                                                                         start=(kc == 0), stop=(kc == 7)),
                                         r=[("xta", tile, kc), "wv"], w=[pk])
                                for hq in range(4):
                                    P.op("dve", lambda e, ps=ps, ti=ti, hq=hq: e.tensor_copy(out=vaug[:, ti, hq, 0:64], in_=ps[:, hq * 64:(hq + 1) * 64]),
                                         r=[pk], w=[("vaug", ti)])
                        P.flush()
                    if upto == "a":
                        continue
                    with ExitStack() as es:
                        wq = sb(es, "wq", (128, 8, 256), BF16)
                        qT = sb(es, "qT", (128, 2, 512), BF16)
                        bias_sb = [sb(es, f"bias{i}", (128, 2, 512)) for i in range(3)]
                        pT = [sb(es, f"pT{i}", (128, GRP, 512), BF16) for i in range(3)]
                        sbt = [sb(es, f"sbt{i}", (128, GRP, 512)) for i in range(3)]
                        rec = sb(es, "rec", (128, 512))
                        s_ps = [pst(es, f"s_ps{i}", (128, GRP, 512)) for i in range(3)]
                        pp = [s_ps[2][:, 0, :]] * 2
                        o_ps = [pst(es, f"o_ps{i}", (128, 512)) for i in range(2)]
                        P.dma("pool", wq[:, :, :], win_d[:, :, 1024 + hh * 256:1024 + (hh + 1) * 256], w=["wq"])
                        acnt = [0, 0]
                        pi = 0
                        bi = 0
                        for tile in range(5):
                            n = 512 if tile < 4 else 256
                            t0 = tile * 512
                            for c in range(2):
                                ps = pp[0]; pk = ("sps", 2); pi += 1
                                fm_proj(ps, pk, lambda kc, c=c: wq[:, kc, c * 128:(c + 1) * 128], ["wq"],
                                        lambda kc: xta[:, kc, t0:t0 + n], [("xta", tile, kc) for kc in range(8)], n)
                                P.op("act", lambda e, ps=ps, c=c: e.activation(out=qT[:, c, :n], in_=ps[:, :n], func=AF.Copy), r=[pk], w=[("qT", c)])
                            tagc = []
                            for hq in range(4):
                                h = 4 * hh + hq
                                c, hp = hq // 2, (hq % 2) * 64
                                keys = []
                                if tile < 4:
                                    kts = na_plan[tile]
                                    for j0 in range(0, len(kts), 2):
                                        grp = kts[j0:j0 + 2]
                                        i0 = bias_index[(tile, h, j0)]
                                        gid = (bi, i0, len(grp)); bi += 1
                                        for jj, t in enumerate(grp):
                                            keys.append((kT[hp:hp + 64, c, t * 128:(t + 1) * 128], ("kT", t // 4, c), vaug[:, t, hq, :], ("vaug", t),
                                                         (gid, jj), None))
                                for t in (16, 17):
                                    keys.append((kT[hp:hp + 64, c, t * 128:(t + 1) * 128], ("kT", 4, c), vaug[:, t, hq, :], ("vaug", t), None, None))
                                attention(qT[hp:hp + 64, c, :n], ("qT", c), n, keys, 0.125, s_ps, o_ps, pT, sbt, rec,
                                          ybT[hp:hp + 64, 2 * hh + c, t0:t0 + n], ("ybT", tile, 2 * hh + c, hp), hp, tagc)
                            attn_flush(tagc, s_ps, o_ps, pT, sbt, rec, acnt, bias_sb)
                        P.flush()
            esx.close()
            if upto in ("a", "b"):
                return
            with ExitStack() as es:
                xt = sb(es, "xt", (128, 8, 512), BF16)
                wu = sb(es, "wu", (128, 8, 512), BF16)
                wva = sb(es, "wva", (128, 8, 512), BF16)
                wo = sb(es, "wo", (128, 8, D), BF16)
                wsT = sb(es, "wsT", (128, 4, 128), BF16)
                bsb = sb(es, "bsb", (128, 512))
                gvb = sb(es, "gvb", (128, 512))
                uT = sb(es, "uT", (128, 4, 512), BF16)
                vg2 = [sb(es, f"vg{i}", (128, 512)) for i in range(2)]
                vn2 = [sb(es, f"vn{i}", (128, 512)) for i in range(2)]
                vln2 = [sb(es, f"vln{i}", (128, 512), BF16) for i in range(2)]
                tt2 = [sb(es, f"tt{i}", (128, 4, 128)) for i in range(2)]
                yaT = sb(es, "yaT", (128, 4, 512), BF16)
                yo = sb(es, "yo", (128, 8, 512))
                stats2 = [sb(es, f"stats{i}", (128, 6)) for i in range(2)]
                mv2 = [sb(es, f"mv{i}", (128, 2)) for i in range(2)]
                ss_ps = pst(es, "ss_ps", (128, 512))
                pp = [pst(es, f"pp{i}", (128, 512)) for i in range(2)]
                g_ps2 = [pst(es, f"g_ps{i}", (128, 4, 128)) for i in range(2)]
                gi = 0
                P.dma("pool", wu[:, :, :], win_d[:, :, 0:512], w=["wu"])
                P.dma("pool", wva[:, :, :], win_d[:, :, 512:1024], w=["wva"])
                P.dma("pool", wo[:, :, :], woab_d, w=["wo"])
                P.dma("pool", wsT[:, :, :], wsT_d, w=["wsT"])
                P.dma("sp", bsb[:, :], bs_d, w=["bsb"])
                P.dma("sp", gvb[:, :], gv_d, w=["gvb"])
                pi = 0
                for tile in range(5):
                    n = tile_norm(tile, l, 0, b, ss_ps, "ss", dst=xt)
                    t0 = tile * 512
                    for c in range(4):
                        ps = pp[pi % 2]; pk = ("pp", pi % 2); pi += 1
                        fm_proj(ps, pk, lambda kc, c=c: wu[:, kc, c * 128:(c + 1) * 128], ["wu"],
                                lambda kc: xt[:, kc, :n], [("xt", kc) for kc in range(8)], n)
                        P.op("act", lambda e, ps=ps, c=c: e.activation(out=uT[:, c, :n], in_=ps[:, :n], func=AF.Gelu), r=[pk], w=[("uT", c)])
                    if upto == "c_u":
                        P.flush(); return
                    for s_ in range(n // 128):
                        ps = pp[pi % 2]; pk = ("pp", pi % 2); pi += 1
                        for kc in range(8):
                            P.op("pe", lambda e, ps=ps, kc=kc, s_=s_: e.matmul(ps[:, :], xt[:, kc, s_ * 128:(s_ + 1) * 128], wva[:, kc, :],
                                                                                 start=(kc == 0), stop=(kc == 7)),
                                 r=[("xt", kc), "wva"], w=[pk])
                        gp = gi % 2; gi += 1
                        vg, vn, vln, tt, stats, mv, g_ps = vg2[gp], vn2[gp], vln2[gp], tt2[gp], stats2[gp], mv2[gp], g_ps2[gp]
                        kv, kn, kl, kt, kst, kmv, kg = ("vg", gp), ("vn", gp), ("vln", gp), ("tt", gp), ("stats", gp), ("mv", gp), ("g_ps", gp)
                        P.op("act", lambda e, ps=ps: e.activation(out=vg[:, :], in_=ps[:, :], func=AF.Gelu), r=[pk], w=[kv])
                        P.op("dve", lambda e: e.bn_stats(out=stats[:, :], in_=vg[:, :]), r=[kv], w=[kst])
                        P.op("dve", lambda e: e.bn_aggr(out=mv[:, :], in_=stats[:, :]), r=[kst], w=[kmv])
                        P.op("act", lambda e: e.activation(out=mv[:, 1:2], in_=mv[:, 1:2], func=AF.Ln, scale=1.0, bias=eps_t[:, 0:1]), r=[kmv], w=[kmv])
                        P.op("act", lambda e: e.activation(out=mv[:, 1:2], in_=mv[:, 1:2], func=AF.Exp, scale=-0.5), r=[kmv], w=[kmv])
                        P.op("dve", lambda e: e.tensor_scalar(out=vn[:, :], in0=vg[:, :], scalar1=mv[:, 0:1], scalar2=mv[:, 1:2],
                                                              op0=ALU.subtract, op1=ALU.mult), r=[kv, kmv], w=[kn])
                        P.op("dve", lambda e: e.tensor_tensor(out=vln[:, :], in0=vn[:, :], in1=gvb[:, :], op=ALU.mult), r=[kn, "gvb"], w=[kl])
                        if upto == "c_va":
                            P.flush(); return
                        for g in range(4):
                            P.op("pe", lambda e, g=g: e.matmul(g_ps[:, g, :], vln[:, g * 128:(g + 1) * 128], wsT[:, g, :], start=True, stop=True),
                                 r=[kl, "wsT"], w=[kg])
                        P.op("dve", lambda e: e.tensor_tensor(out=tt[:, :, :], in0=g_ps[:, :, :], in1=bsb[:, :].rearrange("p (g i) -> p g i", g=4), op=ALU.add),
                             r=[kg, "bsb"], w=[kt])
                        P.op("dve", lambda e, s_=s_: e.tensor_tensor(out=yaT[:, :, s_ * 128:(s_ + 1) * 128], in0=tt[:, :, :],
                                                                     in1=uT[:, :, s_ * 128:(s_ + 1) * 128], op=ALU.mult),
                             r=[kt] + [("uT", c) for c in range(4)], w=[("yaT", s_)])
                        if upto == "c_gate":
                            P.flush(); return

                    def get_ps(dc, tile=tile, n=n, t0=t0):
                        nonlocal pi
                        ps = pp[pi % 2]; pk = ("pp", pi % 2); pi += 1
                        rk = [("yaT", s_) for s_ in range(n // 128)]
                        for kc in range(8):
                            rhs = yaT[:, kc, :n] if kc < 4 else ybT[:, kc - 4, t0:t0 + n]
                            rr = rk if kc < 4 else [("ybT", tile, kc - 4, 0), ("ybT", tile, kc - 4, 64)]
                            P.op("pe", lambda e, ps=ps, kc=kc, rhs=rhs, dc=dc: e.matmul(ps[:, :n], wo[:, kc, dc * 128:(dc + 1) * 128], rhs,
                                                                                         start=(kc == 0), stop=(kc == 7)),
                                 r=["wo"] + rr, w=[pk])
                        return ps[:, :n], pk
                    post_norm_residual(tile, l, 0, b, get_ps, ss_ps, "ss", yo)
                P.flush()

    def qk_process(ps, pkey, n, gcol, rope, dst, dkey, cs, R):
        (sq_, sqk), (bps, bk), (sw, swk), (rs, rsk) = R["sq"], R["bps"], R["sw"], R["rstd"]
        (qn, qnk), (qg, qgk), (t1, t1k), (t2, t2k) = R["qn"], R["qg"], R["t1"], R["t2"]
        P.op("act", lambda e: e.activation(out=sq_[:, :n], in_=ps, func=AF.Square), r=[pkey], w=[sqk])
        P.op("pe", lambda e: e.matmul(bps[:, :n], bones_bf[:, :], sq_[:, :n], start=True, stop=True), r=[sqk, "bones"], w=[bk])
        P.op("act", lambda e: e.activation(out=rs[:, :n], in_=bps[:, :n], func=AF.Ln, scale=1.0 / 64, bias=eps_t[:, 0:1]), r=[bk], w=[rsk])
        P.op("act", lambda e: e.activation(out=rs[:, :n], in_=rs[:, :n], func=AF.Exp, scale=-0.5), r=[rsk], w=[rsk])
        P.op("dve", lambda e: e.tensor_tensor(out=qn[:, :n], in0=ps, in1=rs[:, :n], op=ALU.mult), r=[pkey, rsk], w=[qnk])
        if not rope:
            P.op("act", lambda e: e.activation(out=dst, in_=qn[:, :n], func=AF.Identity, scale=qkg[:, gcol:gcol + 1]), r=[qnk], w=[dkey])
            return
        P.op("act", lambda e: e.activation(out=qg[:, :n], in_=qn[:, :n], func=AF.Identity, scale=qkg[:, gcol:gcol + 1]), r=[qnk], w=[qgk])
        P.op("pe", lambda e: e.matmul(sw[:, :n], perm_bf[:, :], qg[:, :n], start=True, stop=True), r=[qgk, "perm"], w=[swk])
        P.op("dve", lambda e: e.tensor_tensor(out=t1[:, :n], in0=qg[:, :n], in1=cs[:, 0, :n], op=ALU.mult), r=[qgk, "cs"], w=[t1k])
        P.op("dve", lambda e: e.tensor_tensor(out=t2[:, :n], in0=sw[:, :n], in1=cs[:, 1, :n], op=ALU.mult), r=[swk, "cs"], w=[t2k])
        P.op("dve", lambda e: e.tensor_tensor(out=dst, in0=t1[:, :n], in1=t2[:, :n], op=ALU.add), r=[t1k, t2k], w=[dkey])

    def mixer_odd(b, l):
        with ExitStack() as es0:
            kT = sb(es0, "kT", (128, 2, NT), BF16)
            vaug = sb(es0, "vaug", (128, 18, 4, 128), BF16)
            cs = sb(es0, "cs", (128, 2, 512))
            xt = sb(es0, "xt", (128, 8, 512), BF16)
            qg2 = [sb(es0, f"qg{i}", (128, 512), BF16) for i in range(2)]
            with ExitStack() as es:
                wk = sb(es, "wk", (128, 8, 256), BF16)
                wv = sb(es, "wv", (128, 8, 256), BF16)
                ss_ps = pst(es, "ss_ps", (128, 512))
                pp = [pst(es, f"pp{i}", (128, 512)) for i in range(2)]
                bps2 = [pst(es, f"bps{i}", (128, 512)) for i in range(2)]
                sw2 = [pst(es, f"sw{i}", (128, 512)) for i in range(2)]
                scr = sb(es, "scr", (128, 8, 512))
                RR = [dict(sq=(sq[i], ("sq", i)), bps=(bps2[i], ("bps", i)), sw=(sw2[i], ("sw", i)), rstd=(scr[:, 6 + i, :], ("scr", 6 + i)),
                           qn=(scr[:, i, :], ("scr", i)), qg=(qg2[i], ("qg", i)), t1=(scr[:, 2 + i, :], ("scr", 2 + i)),
                           t2=(scr[:, 4 + i, :], ("scr", 4 + i))) for i in range(2)]
                qi = 0
                P.dma("pool", wk[:, :, :], wk_d, w=["wk"])
                P.dma("pool", wv[:, :, :], wv_d, w=["wv"])
                P.op("dve", lambda e: e.memset(vaug[:, :, :, :], 1.0), w=[("vaug", t) for t in range(18)])
                pi = 0
                for tile in range(5):
                    n = tile_norm(tile, l, 0, b, ss_ps, "ss", dst=xt)
                    t0 = tile * 512
                    if tile < 4:
                        P.dma("sp", cs[:, 0, :], cos_d[:, t0:t0 + 512], w=["cs"], slot="cs0")
                        P.dma("sp", cs[:, 1, :], sin_d[:, t0:t0 + 512], w=["cs"], slot="cs1")
                    for c in range(2):
                        ps = pp[pi % 2]; pk = ("pp", pi % 2); pi += 1
                        fm_proj(ps, pk, lambda kc, c=c: wk[:, kc, c * 128:(c + 1) * 128], ["wk"],
                                lambda kc: xt[:, kc, :n], [("xt", kc) for kc in range(8)], n)
                        qk_process(ps[:, :n], pk, n, 1, tile < 4, kT[:, c, t0:t0 + n], ("kT", tile, c), cs, RR[qi % 2]); qi += 1
                    for s_ in range(n // 128):
                        ps = pp[pi % 2]; pk = ("pp", pi % 2); pi += 1
                        ti = tile * 4 + s_
                        for kc in range(8):
                            P.op("pe", lambda e, ps=ps, kc=kc, s_=s_: e.matmul(ps[:, 0:256], xt[:, kc, s_ * 128:(s_ + 1) * 128], wv[:, kc, :],
                                                                                 start=(kc == 0), stop=(kc == 7)),
                                 r=[("xt", kc), "wv"], w=[pk])
                        for g in range(4):
                            P.op("dve", lambda e, ps=ps, ti=ti, g=g: e.tensor_copy(out=vaug[:, ti, g, 0:64], in_=ps[:, g * 64:(g + 1) * 64]),
                                 r=[pk], w=[("vaug", ti)])
                P.flush()
            with ExitStack() as es:
                wq = sb(es, "wq", (128, 8, D), BF16)
                wo = sb(es, "wo", (128, 8, D), BF16)
                qT = sb(es, "qT", (128, 8, 512), BF16)
                yT = sb(es, "yT", (128, 8, 512), BF16)
                yo = sb(es, "yo", (128, 8, 512))
                pT = [sb(es, f"pT{i}", (128, GRP, 512), BF16) for i in range(2)]
                rec = sb(es, "rec", (128, 512))
                s_ps = [pst(es, f"s_ps{i}", (128, GRP, 512)) for i in range(2)]
                ss_ps = pst(es, "ss_ps", (128, 512))
                pp = [pst(es, "pp0", (128, 512))]
                o_ps = [pst(es, f"o_ps{i}", (128, 512)) for i in range(2)]
                RR = [dict(sq=(sq[i], ("sq", i)), bps=(s_ps[i][:, 0, :], ("sps", i)), sw=(o_ps[i], ("ops", i)), rstd=(yo[:, 6 + i, :], ("yo", 6 + i)),
                           qn=(yo[:, i, :], ("yo", i)), qg=(qg2[i], ("qg", i)), t1=(yo[:, 2 + i, :], ("yo", 2 + i)),
                           t2=(yo[:, 4 + i, :], ("yo", 4 + i))) for i in range(2)]
                qi = 0
                P.dma("pool", wq[:, :, :], wq_d, w=["wq"])
                P.dma("pool", wo[:, :, :], woc_d, w=["wo"])
                acnt = [0, 0]
                pi = 0
                for tile in range(4):
                    n = tile_norm(tile, l, 0, b, ss_ps, "ss", dst=xt)
                    t0 = tile * 512
                    tagc = []
                    P.dma("sp", cs[:, 0, :], cos_d[:, t0:t0 + 512], w=["cs"], slot="cs0")
                    P.dma("sp", cs[:, 1, :], sin_d[:, t0:t0 + 512], w=["cs"], slot="cs1")
                    for c in range(8):
                        ps = pp[0]; pk = ("pp", 0); pi += 1
                        fm_proj(ps, pk, lambda kc, c=c: wq[:, kc, c * 128:(c + 1) * 128], ["wq"],
                                lambda kc: xt[:, kc, :n], [("xt", kc) for kc in range(8)], n)
                        qk_process(ps[:, :n], pk, n, 0, True, qT[:, c, :n], ("qT", c), cs, RR[qi % 2]); qi += 1
                    for c in range(8):
                        for s2 in range(2):
                            hp = s2 * 64
                            g = 2 * (c // 4) + s2
                            kc_, = [g // 2]
                            keys = []
                            for t in range(18):
                                keys.append((kT[hp:hp + 64, kc_, t * 128:(t + 1) * 128], ("kT", t // 4, kc_), vaug[:, t, g, :], ("vaug", t), None, None))
                            attention(qT[hp:hp + 64, c, :n], ("qT", c), n, keys, 0.125, s_ps, o_ps, pT, None, rec,
                                      yT[hp:hp + 64, c, :n], ("yT", c, hp), hp, tagc)
                    attn_flush(tagc, s_ps, o_ps, pT, None, rec, acnt)

                    def get_ps(dc, n=n):
                        nonlocal pi
                        ps = pp[0]; pk = ("pp", 0); pi += 1
                        for kc in range(8):
                            P.op("pe", lambda e, ps=ps, kc=kc, dc=dc: e.matmul(ps[:, :n], wo[:, kc, dc * 128:(dc + 1) * 128], yT[:, kc, :n],
                                                                               start=(kc == 0), stop=(kc == 7)),
                                 r=["wo", ("yT", kc, 0), ("yT", kc, 64)], w=[pk])
                        return ps[:, :n], pk
                    post_norm_residual(tile, l, 0, b, get_ps, ss_ps, "ss", yo)
                P.flush()

    def ffn(b, l, with_ctx, upto=None, store=False):
        passes = [[(0, 0, 1024, 0, 0)], [(0, 1024, 1024, 0, 0)] + ([(1, 0, 256, 1026, 1024)] if with_ctx else [])]
        for segs_ in passes:
            W = sum(sg[2] for sg in segs_)
            XW = sum(sg[2] + 2 for sg in segs_)
            with ExitStack() as es:
                xf = sb(es, "xf", (128, 8, XW), BF16)
                actT = sb(es, "actT", (128, NCH, W), BF16)
                ca = sb(es, "ca", (128, W))
                cg = sb(es, "cg", (128, W))
                wub = [sb(es, f"wub{i}", (128, 2, 8, 128), BF16) for i in range(3)]
                wdb = [sb(es, f"wdb{i}", (128, NCH, 128), BF16) for i in range(2)]
                yo = sb(es, "yo", (128, 8, 512))
                ss_ps = pst(es, "ss_ps", (128, 512))
                a_ps = pst(es, "a_ps", (128, 1536))
                g_ps = pst(es, "g_ps", (128, 1536))
                pp = [pst(es, "ppd", (128, 512))]
                xall = []
                mgroups = []
                for (isctx, s0, T, xo, ao) in segs_:
                    v = 2 if isctx else b
                    A_, B_ = modA[:, l, 1, v, :], modB[:, l, 1, v, :]
                    Sq = LC if isctx else S

                    def hap(kc, a, n, isctx=isctx):
                        return hcT[:, kc, a:a + n] if isctx else hT[:, kc, a:a + n]
                    lk, rk = ("xf", isctx, "l"), ("xf", isctx, "r")
                    nsegs = []
                    if s0 > 0:
                        P.op("dve", lambda e, xo=xo: e.tensor_copy(out=xf[:, :, xo:xo + 1], in_=xhalo[:, :, 0:1]), r=["xhalo"], w=[lk])
                    else:
                        P.op("dve", lambda e, xo=xo: e.memset(xf[:, :, xo:xo + 1], 0.0), w=[lk])
                    for a in range(s0, s0 + T, 512):
                        nn = min(512, s0 + T - a)
                        nsegs.append((a, nn, xo + 1 + a - s0))
                    if s0 + T < Sq:
                        nsegs.append((s0 + T, 1, xo + 1 + T))
                    else:
                        P.op("dve", lambda e, xo=xo, T=T: e.memset(xf[:, :, xo + 1 + T:xo + 2 + T], 0.0), w=[rk])
                    xkeys = []
                    for (a, nn, col) in nsegs:
                        key = ("xf", isctx, a)
                        xkeys.append(key)
                        tl = 4 if isctx else a // 512
                        norm_mod(lambda kc, a=a, nn=nn, hap=hap: hap(kc, a, nn), [hkey(tl, kc) for kc in range(8)], nn, A_, B_,
                                 lambda kc, col=col, nn=nn: xf[:, kc, col:col + nn], [key] * 8, ss_ps, "ss")
                    xall += xkeys + [lk, rk]
                    if s0 + T < Sq:
                        P.op("dve", lambda e, xo=xo, T=T: e.tensor_copy(out=xhalo[:, :, 0:1], in_=xf[:, :, xo + T:xo + T + 1]), r=xkeys, w=["xhalo"])
                    c0 = xo
                    while c0 < xo + T + 2:
                        c1 = min((c0 // 512 + 1) * 512, xo + T + 2)
                        mgroups.append((c0, c1))
                        c0 = c1
                if upto == "f_norm":
                    P.flush(); return
                for j in range(NCH):
                    wb_ = wub[j % 3]
                    wkey = ("wub", j % 3)
                    P.dma("pool", wb_[:, 0, :, :], wup_d[l, j], w=[wkey], slot=("wub", j % 3, 0))
                    P.dma("pool", wb_[:, 1, :, :], wup_d[l, NCH + j], w=[wkey], slot=("wub", j % 3, 1))
                    for part, (ps_, pkey, cdst, ckey) in enumerate(((a_ps, "a_ps", ca, "ca"), (g_ps, "g_ps", cg, "cg"))):
                        ch = part * NCH + j
                        for (c0, c1) in mgroups:
                            for kc in range(8):
                                P.op("pe", lambda e, ps_=ps_, kc=kc, c0=c0, c1=c1, part=part, wb_=wb_: e.matmul(
                                    ps_[:, c0:c1], wb_[:, part, kc, :], xf[:, kc, c0:c1], start=(kc == 0), stop=(kc == 7)),
                                    r=[wkey] + xall, w=[pkey])
                        w0, w1, w2 = (cwT[:, l, ch, i:i + 1] for i in range(3))
                        for (isctx, s0, T, xo, ao) in segs_:
                            P.op("act", lambda e, ps_=ps_, cdst=cdst, w1=w1, ch=ch, T=T, xo=xo, ao=ao: e.activation(
                                out=cdst[:, ao:ao + T], in_=ps_[:, xo + 1:xo + T + 1], func=AF.Identity, scale=w1, bias=cbT[:, l, ch:ch + 1]),
                                r=[pkey], w=[ckey])
                            P.op("dve", lambda e, ps_=ps_, cdst=cdst, w0=w0, T=T, xo=xo, ao=ao: e.scalar_tensor_tensor(
                                out=cdst[:, ao:ao + T], in0=ps_[:, xo:xo + T], scalar=w0, in1=cdst[:, ao:ao + T], op0=ALU.mult, op1=ALU.add),
                                r=[pkey, ckey], w=[ckey])
                            P.op("dve", lambda e, ps_=ps_, cdst=cdst, w2=w2, T=T, xo=xo, ao=ao: e.scalar_tensor_tensor(
                                out=cdst[:, ao:ao + T], in0=ps_[:, xo + 2:xo + T + 2], scalar=w2, in1=cdst[:, ao:ao + T], op0=ALU.mult, op1=ALU.add),
                                r=[pkey, ckey], w=[ckey])
                    P.op("act", lambda e: e.activation(out=cg[:, :W], in_=cg[:, :W], func=AF.Silu), r=["cg"], w=["cg"])
                    P.op("dve", lambda e, j=j: e.tensor_tensor(out=actT[:, j, :W], in0=cg[:, :W], in1=ca[:, :W], op=ALU.mult),
                         r=["cg", "ca"], w=[("actT", j)])
                    if upto == "f_up1":
                        P.flush(); return
                if upto == "f_up":
                    P.flush(); return
                pi = 0
                di = 0
                for (isctx, s0, T, xo, ao) in segs_:
                    for a in range(0, T, 512):
                        nn = min(512, T - a)
                        tile = 4 if isctx else (s0 + a) // 512

                        def get_ps(dc, a=a, nn=nn, ao=ao):
                            nonlocal pi, di
                            wd_ = wdb[di % 2]; wdk = ("wdb", di % 2); di += 1
                            P.dma("pool", wd_[:, 0:11, :], wdn_d[l, dc][:, 0:11, :], w=[wdk], slot=("wdb", (di - 1) % 2, 0))
                            P.dma("pool", wd_[:, 11:22, :], wdn_d[l, dc][:, 11:22, :], w=[wdk], slot=("wdb", (di - 1) % 2, 1))
                            ps = pp[0]; pk = ("pp", 0); pi += 1
                            for kc in range(NCH):
                                P.op("pe", lambda e, ps=ps, kc=kc, wd_=wd_: e.matmul(ps[:, :nn], wd_[:, kc, :], actT[:, kc, ao + a:ao + a + nn],
                                                                                     start=(kc == 0), stop=(kc == NCH - 1)),
                                     r=[wdk, ("actT", kc)], w=[pk])
                            return ps[:, :nn], pk
                        post_norm_residual(tile, l, 1, b, get_ps, ss_ps, "ss", yo)
                        if store and not isctx:
                            t0_ = tile * 512
                            P.dma("sp", out_d[b, :, :, t0_:t0_ + 512], hT[:, :, t0_:t0_ + 512], r=[hkey(tile, kc) for kc in range(8)], slot=("st", tile))
                P.flush()

    for b in range(nb):
        for kc in range(8):
            P.dma("sp", hT[:, kc, :], xT_d[b, :, kc, :], w=[hkey(t, kc) for t in range(4)], slot=("ld", kc))
        P.dma("sp", hcT[:, :, :], ctxT_d[b], w=[hkey(4, kc) for kc in range(8)], slot="ldc")
        P.flush()
        if "mix0a" in stages:
            mixer_even(b, 0, upto="a", halves=(0,))
        if "mix0b" in stages:
            mixer_even(b, 0, upto="b", halves=(0,))
        for st_ in ("c_u", "c_va", "c_gate"):
            if st_ in stages:
                mixer_even(b, 0, upto=st_, halves=())
        if "mix0ab" in stages:
            mixer_even(b, 0, upto="b")
        if "mix0c" in stages:
            mixer_even(b, 0, halves=())
        if "mix0" in stages:
            mixer_even(b, 0)
        for st_ in ("f_norm", "f_up1", "f_up"):
            if st_ in stages:
                ffn(b, 0, with_ctx=True, upto=st_)
        if "ffn0" in stages:
            ffn(b, 0, with_ctx=True)
        if "mix1" in stages:
            mixer_odd(b, 1)
        if "ffn1" in stages:
            ffn(b, 1, with_ctx=False, store=True)
        if dbg:
            P.dma("sp", hc_out[b], hcT[:, :, :], r=[hkey(4, kc) for kc in range(8)], slot="sthc")
        if "ffn1" not in stages:
            for kc in range(8):
                P.dma("sp", out_d[b, :, kc, :], hT[:, kc, :], r=[hkey(t, kc) for t in range(4)], slot=("st", kc))
        P.flush()
    ES.close()
    return nc


def _prep_shared(inp):
    f = lambda a: np.ascontiguousarray(np.asarray(a, dtype=np.float32))
    sh = {}
    w_ada = f(inp["w_ada"])
    sh["w_ada"] = np.ascontiguousarray(w_ada.reshape(2, 8, 128, 6 * D).transpose(0, 2, 1, 3))
    sh["b_adaT"] = np.ascontiguousarray(f(inp["b_ada"]).reshape(2, 48, 128).transpose(0, 2, 1))
    sh["gT"] = np.ascontiguousarray(f(inp["norm_g"]).reshape(2, 4, 8, 128).transpose(0, 1, 3, 2))
    sh["w_in"] = _lay_w(f(inp["w_in_ab"])[0])
    sh["w_out_ab"] = _lay_w(f(inp["w_out_ab"])[0])
    wqkv = f(inp["w_qkv_c"])[0]
    qcols = []
    for c in range(8):
        for s in range(2):
            h = HMAP[c][s]
            qcols.extend(range(h * 64, (h + 1) * 64))
    sh["w_q"] = _lay_w(wqkv[:, qcols])
    sh["w_k"] = _lay_w(wqkv[:, 1024:1280])
    sh["w_v"] = _lay_w(wqkv[:, 1280:1536])
    sh["w_out_c"] = _lay_w(f(inp["w_out_c"])[0][qcols, :])
    sh["w_sT"] = np.ascontiguousarray(f(inp["a_w_s"])[0].transpose(2, 0, 1))
    sh["b_s_bc"] = np.ascontiguousarray(np.broadcast_to(f(inp["a_b_s"])[0].reshape(1, 512), (128, 512)))
    sh["gv_bc"] = np.ascontiguousarray(np.broadcast_to(f(inp["a_v_g"])[0].reshape(1, 512), (128, 512)))
    bias, index, plan = _build_na_bias(f(inp["b_rpb"])[0])
    sh["bias_na"] = bias
    sh["qg2"] = np.ascontiguousarray(np.tile(f(inp["c_q_g"])[0], 2).reshape(128, 1))
    sh["kg2"] = np.ascontiguousarray(np.tile(f(inp["c_k_g"])[0], 2).reshape(128, 1))
    wup = f(inp["w_up"])
    sh["w_up"] = np.ascontiguousarray(wup.reshape(2, 8, 128, 2 * NCH, 128).transpose(0, 3, 2, 1, 4))
    sh["conv_wT"] = np.ascontiguousarray(f(inp["conv_w"]).reshape(2, 3, 2 * NCH, 128).transpose(0, 3, 2, 1))
    sh["conv_bT"] = np.ascontiguousarray(f(inp["conv_b"]).reshape(2, 2 * NCH, 128).transpose(0, 2, 1))
    wdn = f(inp["w_down"])
    sh["w_down"] = np.ascontiguousarray(wdn.reshape(2, NCH, 128, 8, 128).transpose(0, 3, 2, 1, 4))
    cosT, sinT, perm = _rope_tables()
    sh["cosT"], sh["sinT"], sh["perm"] = cosT, sinT, perm
    return sh, index, plan


def _lay_act(a):
    nb_, L, _ = a.shape
    return np.ascontiguousarray(a.transpose(0, 2, 1).reshape(nb_, 8, 128, L).transpose(0, 2, 1, 3))


def kernel(**inp):
    x = np.asarray(inp["x"], np.float32)
    c = np.asarray(inp["c"], np.float32)
    ctx = np.asarray(inp["ctx"], np.float32)
    c_ctx = np.asarray(inp["c_ctx"], np.float32)
    sh, index, plan = _prep_shared(inp)
    n_cores = 8
    nb = x.shape[0] // n_cores
    nc = build_program(sh["bias_na"].shape[0], index, plan, nb=nb)
    in_maps = []
    for i in range(n_cores):
        m = dict(sh)
        sl = slice(i * nb, (i + 1) * nb)
        m["xT"] = _lay_act(x[sl])
        m["ctxT"] = _lay_act(ctx[sl])
        cv = np.stack([c[i * nb], c[i * nb + 1], c_ctx], -1)
        m["cT"] = np.ascontiguousarray(cv.reshape(8, 128, 3).transpose(1, 0, 2))
        in_maps.append(m)
    res = run_bass_kernel_spmd(nc, in_maps, core_ids=list(range(n_cores)))
    out = np.empty_like(x)
    for i in range(n_cores):
        o = res.results[i]["outT"]
        out[i * nb:(i + 1) * nb] = o.transpose(0, 3, 2, 1).reshape(nb, S, D)
    return out
```
